# Optimizing a Trainium2 kernel written in Bass

```python
import jax, jax.numpy as jnp
from jax import lax
import numpy as np

D_MODEL = 2048
BATCH = 4
SEQ = 2048
DEPTH = 1
DEC_BATCH = 128
DEC_SEQ = 8
PAST_LEN = 16384
PAGE_SIZE = 128

D_POOL = D_MODEL // 2
POOL_WINDOWS = (2, 4, 8, 16)
N_POOL_GROUPS = len(POOL_WINDOWS)
POOL_GROUP_W = D_POOL // N_POOL_GROUPS
POOL_STATE = max(POOL_WINDOWS) - 1
D_CONV = D_MODEL // 2
CONV_WIDTH = 31
CONV_STATE = CONV_WIDTH - 1
D_PLE = 256
EPS = 1e-6
D_IN = 2 * D_POOL + 3 * D_CONV + 2 * D_MODEL

kernel_name = "gated_pool_conformer_hybrid_step"


def rms_norm(x, g):
    xf = x.astype(jnp.float32)
    y = xf * lax.rsqrt(jnp.mean(xf * xf, axis=-1, keepdims=True) + EPS)
    return (y * g.astype(jnp.float32)).astype(x.dtype)


def layer_norm(x, g, b):
    xf = x.astype(jnp.float32)
    mu = jnp.mean(xf, axis=-1, keepdims=True)
    xc = xf - mu
    y = xc * lax.rsqrt(jnp.mean(xc * xc, axis=-1, keepdims=True) + EPS)
    return (y * g.astype(jnp.float32) + b.astype(jnp.float32)).astype(x.dtype)


def pool_mixer(u, past, start_pos, w_pool, pool_scale):
    B, T, _ = u.shape
    L = POOL_STATE
    ext = jnp.concatenate([past.astype(u.dtype), u], axis=1)
    extf = ext.astype(jnp.float32)
    csum = jnp.concatenate([jnp.zeros((B, 1, D_POOL), jnp.float32), jnp.cumsum(extf, axis=1)], axis=1)
    end = csum[:, L + 1:]
    pos = start_pos + jnp.arange(T)
    outs = []
    for gi, w in enumerate(POOL_WINDOWS):
        sl = slice(gi * POOL_GROUP_W, (gi + 1) * POOL_GROUP_W)
        win_sum = end[..., sl] - csum[:, L + 1 - w:L + 1 - w + T, sl]
        count = jnp.minimum(pos + 1, w).astype(jnp.float32)
        outs.append(win_sum / count[None, :, None])
    pooled = jnp.concatenate(outs, axis=-1) - extf[:, L:]
    pooled = pooled.astype(u.dtype).reshape(B, T, N_POOL_GROUPS, POOL_GROUP_W)
    mixed = jnp.einsum('btgc,gcd->btgd', pooled, w_pool).reshape(B, T, D_POOL)
    return mixed * pool_scale, ext[:, -L:]


def conv_module(a, b, past, w_dw, b_dw, ln_g, ln_b):
    v = a * jax.nn.sigmoid(b)
    ext = jnp.concatenate([past.astype(v.dtype), v], axis=1)
    y = lax.conv_general_dilated(
        ext, w_dw[:, None, :].astype(ext.dtype), window_strides=(1,), padding='VALID',
        dimension_numbers=('NWC', 'WIO', 'NWC'), feature_group_count=D_CONV)
    y = y + b_dw
    y = jax.nn.silu(layer_norm(y, ln_g, ln_b))
    return y, ext[:, -CONV_STATE:]


def hybrid_layer(x, p, pool_past, conv_past, start_pos, g_pre, w_in, w_pool, pool_scale,
                 w_dw, b_dw, ln_g, ln_b, w_proj_pool, w_proj_conv, w_out, g_post,
                 w_ple, g_ple, w_ple_gate):
    h = rms_norm(x, g_pre)
    proj = h @ w_in
    o = 0
    u_a = proj[..., o:o + D_POOL]; o += D_POOL
    z_a = proj[..., o:o + D_POOL]; o += D_POOL
    a_b = proj[..., o:o + D_CONV]; o += D_CONV
    b_b = proj[..., o:o + D_CONV]; o += D_CONV
    z_b = proj[..., o:o + D_CONV]; o += D_CONV
    gate_a = proj[..., o:o + D_MODEL]; o += D_MODEL
    gate_b = proj[..., o:o + D_MODEL]
    ya, pool_new = pool_mixer(u_a, pool_past, start_pos, w_pool, pool_scale)
    ya = (ya * jax.nn.silu(z_a)) @ w_proj_pool
    yb, conv_new = conv_module(a_b, b_b, conv_past, w_dw, b_dw, ln_g, ln_b)
    yb = (yb * jax.nn.silu(z_b)) @ w_proj_conv
    m = jax.nn.sigmoid(gate_a) * ya + jax.nn.sigmoid(gate_b) * yb
    x = x + rms_norm(m @ w_out, g_post)
    e = rms_norm(p.astype(x.dtype) @ w_ple, g_ple)
    x = x + e * jax.nn.sigmoid(x @ w_ple_gate)
    return x, pool_new, conv_new


def setup_inputs(seed: int = 0) -> dict:
    key = jax.random.key(seed)
    ks = jax.random.split(key, 24)
    f32 = jnp.float32
    nrm = lambda k, s, sc: jax.random.normal(k, s, f32) * sc
    return {
        "x_prompt": nrm(ks[0], (BATCH, SEQ, D_MODEL), 1.0),
        "x_sample": nrm(ks[1], (DEC_BATCH, DEC_SEQ, D_MODEL), 1.0),
        "state_pool": nrm(ks[2], (DEPTH, DEC_BATCH, POOL_STATE, D_POOL), 1.0),
        "state_conv": nrm(ks[3], (DEPTH, DEC_BATCH, CONV_STATE, D_CONV), 0.5),
        "p_prompt": nrm(ks[4], (DEPTH, BATCH, SEQ, D_PLE), 1.0),
        "p_sample": nrm(ks[5], (DEPTH, DEC_BATCH, DEC_SEQ, D_PLE), 1.0),
        "g_pre": 1.0 + nrm(ks[6], (DEPTH, D_MODEL), 0.05),
        "w_in": nrm(ks[7], (DEPTH, D_MODEL, D_IN), D_MODEL ** -0.5),
        "w_pool": nrm(ks[8], (DEPTH, N_POOL_GROUPS, POOL_GROUP_W, POOL_GROUP_W), POOL_GROUP_W ** -0.5),
        "pool_scale": 1.0 + nrm(ks[9], (DEPTH, D_POOL), 0.1),
        "w_dw": nrm(ks[10], (DEPTH, CONV_WIDTH, D_CONV), CONV_WIDTH ** -0.5),
        "b_dw": nrm(ks[11], (DEPTH, D_CONV), 0.02),
        "ln_g": 1.0 + nrm(ks[12], (DEPTH, D_CONV), 0.05),
        "ln_b": nrm(ks[13], (DEPTH, D_CONV), 0.02),
        "w_proj_pool": nrm(ks[14], (DEPTH, D_POOL, D_MODEL), D_POOL ** -0.5),
        "w_proj_conv": nrm(ks[15], (DEPTH, D_CONV, D_MODEL), D_CONV ** -0.5),
        "w_out": nrm(ks[16], (DEPTH, D_MODEL, D_MODEL), D_MODEL ** -0.5),
        "g_post": 1.0 + nrm(ks[17], (DEPTH, D_MODEL), 0.05),
        "w_ple": nrm(ks[18], (DEPTH, D_PLE, D_MODEL), D_PLE ** -0.5),
        "g_ple": 1.0 + nrm(ks[19], (DEPTH, D_MODEL), 0.05),
        "w_ple_gate": nrm(ks[20], (DEPTH, D_MODEL, D_MODEL), D_MODEL ** -0.5),
    }


def reference(x_prompt, x_sample, state_pool, state_conv, p_prompt, p_sample, g_pre, w_in,
              w_pool, pool_scale, w_dw, b_dw, ln_g, ln_b, w_proj_pool, w_proj_conv, w_out,
              g_post, w_ple, g_ple, w_ple_gate):
    xp, xs = x_prompt, x_sample
    pool_p, conv_p, pool_s, conv_s = [], [], [], []
    for i in range(DEPTH):
        lw = (g_pre[i], w_in[i], w_pool[i], pool_scale[i], w_dw[i], b_dw[i], ln_g[i], ln_b[i],
              w_proj_pool[i], w_proj_conv[i], w_out[i], g_post[i], w_ple[i], g_ple[i], w_ple_gate[i])
        zp_pool = jnp.zeros((xp.shape[0], POOL_STATE, D_POOL), xp.dtype)
        zp_conv = jnp.zeros((xp.shape[0], CONV_STATE, D_CONV), xp.dtype)
        xp, npp, ncp = hybrid_layer(xp, p_prompt[i], zp_pool, zp_conv, 0, *lw)
        xs, nps, ncs = hybrid_layer(xs, p_sample[i], state_pool[i], state_conv[i], PAST_LEN, *lw)
        pool_p.append(npp); conv_p.append(ncp); pool_s.append(nps); conv_s.append(ncs)
    new_pool_prompt = jnp.stack(pool_p)
    new_conv_prompt = jnp.stack(conv_p)
    new_pool_sample = jnp.stack(pool_s)
    new_conv_sample = jnp.stack(conv_s)
    return (xp, xs, new_pool_prompt, new_conv_prompt, new_pool_sample, new_conv_sample)
```

```python
import numpy as np
import concourse.bass as bass
import concourse.mybir as mybir
from concourse.bass_utils import run_bass_kernel_spmd

F32 = mybir.dt.float32
BF16 = mybir.dt.bfloat16
AF = mybir.ActivationFunctionType
ALU = mybir.AluOpType

NCORES = 8
D = 2048
DP = 1024
NT = 9
W = 1184
T = 1152
EPS = 1e-6
SB_BASE = 16512
SB_END = 229376

CH_U, CH_ZA, CH_A, CH_B, CH_ZB, CH_GA, CH_GB = 0, 8, 16, 24, 32, 40, 56


class Sched:
    ENGS = ("pe", "act", "dve", "pool", "sp")
    GR = 1024

    def __init__(self):
        self.prog = {e: [] for e in self.ENGS}
        self.cnt = {}
        self.waited = {e: {} for e in self.ENGS}
        self.recs = {}
        self.buckets = {}
        self.semnames = set()
        self.final_waits = {}
        self.needed = set()

    def _granules(self, space, lo, hi):
        return [(space, g) for g in range(lo // self.GR, (hi - 1) // self.GR + 1)]

    def _add(self, key, val):
        old = self.recs.get(key)
        if old is None:
            for b in self._granules(key[0], key[1], key[2]):
                self.buckets.setdefault(b, set()).add(key)
        if old is None or old < val:
            self.recs[key] = val

    def _remove(self, key):
        del self.recs[key]
        for b in self._granules(key[0], key[1], key[2]):
            self.buckets[b].discard(key)

    def _overlaps(self, space, lo, hi):
        seen = set()
        for b in self._granules(space, lo, hi):
            for key in self.buckets.get(b, ()):
                if key in seen:
                    continue
                if key[1] < hi and lo < key[2]:
                    seen.add(key)
        return seen

    def _collect(self, eng_sem, is_pe, reads, writes):
        deps = {}
        cur = self.cnt.get(eng_sem, 0)

        def need(sem, val):
            if deps.get(sem, 0) < val:
                deps[sem] = val

        for (space, lo, hi) in reads:
            for key in self._overlaps(space, lo, hi):
                sem = key[3]
                if not key[4]:
                    if space == "ps" and sem != eng_sem:
                        need(sem, self.recs[key])
                    continue
                if sem == eng_sem:
                    if is_pe:
                        continue
                need(sem, self.recs[key])
        for (space, lo, hi) in writes:
            for key in self._overlaps(space, lo, hi):
                sem = key[3]
                if sem == eng_sem:
                    continue
                need(sem, self.recs[key])
        return deps

    def _record(self, sem, val, reads, writes):
        for (space, lo, hi) in writes:
            for key in list(self._overlaps(space, lo, hi)):
                if key[1] >= lo and key[2] <= hi:
                    self._remove(key)
            self._add((space, lo, hi, sem, True), val)
        for (space, lo, hi) in reads:
            self._add((space, lo, hi, sem, False), val)

    def _waits(self, eng, deps):
        out = []
        for sem, val in deps.items():
            if self.waited[eng].get(sem, 0) < val:
                self.waited[eng][sem] = val
                out.append((sem, val))
                self.needed.add((sem, val))
        return out

    def op(self, eng, reads, writes, fn):
        sem = "E_" + eng
        self.semnames.add(sem)
        deps = self._collect(sem, eng == "pe", reads, writes)
        waits = self._waits(eng, deps)
        val = self.cnt.get(sem, 0) + 1
        self.cnt[sem] = val
        self.prog[eng].append((waits, fn, (sem, val, False)))
        self._record(sem, val, reads, writes)

    def dma(self, queue, sem, reads, writes, fn, final=False, n=1):
        self.semnames.add(sem)
        deps = self._collect(sem, False, reads, writes)
        deps.pop(sem, None)
        waits = self._waits(queue, deps)
        val = self.cnt.get(sem, 0) + 16 * n
        self.cnt[sem] = val
        self.prog[queue].append((waits, fn, (sem, val, True)))
        self._record(sem, val, reads, writes)
        if final:
            self.final_waits[sem] = val

    def finalize(self):
        self.rank = {}
        by_sem = {}
        for (sem, val) in self.needed:
            if sem.startswith("E_"):
                by_sem.setdefault(sem, []).append(val)
        for sem, vals in by_sem.items():
            for r, v in enumerate(sorted(vals)):
                self.rank[(sem, v)] = r + 1

    def wait_value(self, sem, val):
        return self.rank[(sem, val)] if sem.startswith("E_") else val


class Buf:
    def __init__(self, t, space, addr, free_shape, esz):
        self.t = t
        self.space = space
        self.addr = addr
        self.shape = tuple(free_shape)
        self.esz = esz

    def __getitem__(self, k):
        return self.t[k]

    def reg(self, *idx):
        shp = self.shape
        idx = list(idx) + [None] * (len(shp) - len(idx))
        rngs = []
        for d, i in enumerate(idx):
            if i is None:
                rngs.append((0, shp[d]))
            elif isinstance(i, int):
                rngs.append((i, i + 1))
            else:
                rngs.append(i)
        strides = [1] * len(shp)
        for d in range(len(shp) - 2, -1, -1):
            strides[d] = strides[d + 1] * shp[d + 1]
        nd = len(shp)
        cut = nd
        while cut > 1 and rngs[cut - 1] == (0, shp[cut - 1]):
            cut -= 1
        out = []

        if self.space == "ps":
            return [("ps", self.addr, self.addr + 2048)]

        def rec(d, off):
            if d == cut - 1:
                lo = off + rngs[d][0] * strides[d]
                hi = off + rngs[d][1] * strides[d]
                out.append((self.space, self.addr + lo * self.esz, self.addr + hi * self.esz))
                return
            for i in range(rngs[d][0], rngs[d][1]):
                rec(d + 1, off + i * strides[d])

        rec(0, 0)
        return out


class Arena:
    def __init__(self, nc):
        self.nc = nc
        self.n = 0

    def at(self, name, addr, free_shape, dt, parts=128):
        esz = 4 if dt == F32 else 2
        size = int(np.prod(free_shape)) * esz
        assert addr % 32 == 0, (name, addr)
        assert SB_BASE <= addr and addr + size <= SB_END, (name, addr, size)
        self.n += 1
        t = self.nc.alloc_sbuf_tensor_at(f"{name}_{self.n}", [parts] + list(free_shape), dt, offset=addr)
        return Buf(t, "sb", addr, free_shape, esz)


def build_program(stop_after=None, dbg=None):
    nc = bass.Bass("TRN2", target_bir_lowering=False)
    S = Sched()
    A = Arena(nc)
    dbg = dbg or []

    def dram_in(name, shape):
        return nc.dram_tensor(name, list(shape), F32, kind="ExternalInput").ap()

    def dram_out(name, shape):
        return nc.dram_tensor(name, list(shape), F32, kind="ExternalOutput").ap()

    x_tok = dram_in("x_tok", [NT, 128, D])
    x_halo = dram_in("x_halo", [32, D])
    p_tok = dram_in("p_tok", [NT, 128, 256])
    st_conv = dram_in("st_conv", [16, 30, DP])
    st_pool = dram_in("st_pool", [16, 15, DP])
    cst_d = dram_in("cst", [128, 360])
    ident_d = dram_in("ident", [128, 128])
    w_in_d = dram_in("w_in_t", [72, 128, 16, 128])
    w_pool_d = dram_in("w_pool_t", [128, 4 * 2 * 256])
    w_pp_d = dram_in("w_pp_t", [16, 128, 8, 128])
    w_pc_d = dram_in("w_pc_t", [16, 128, 8, 128])
    w_out_d = dram_in("w_out_t", [8, 128, 16, 256])
    w_pg_d = dram_in("w_pg_t", [8, 128, 16, 256])
    w_ple_d = dram_in("w_ple_t", [128, 2 * D])
    gpost_d = dram_in("g_post_bc", [128, D])
    gple_d = dram_in("g_ple_bc", [128, D])

    y_tok = dram_out("y_tok", [NT, 128, D])
    ncs_new = dram_out("ncs_new", [128, DP])
    ncs_old = dram_out("ncs_old", [16, 22, DP])
    nps_new = dram_out("nps_new", [128, DP])
    nps_old = dram_out("nps_old", [16, 7, DP])
    ncp_o = dram_out("ncp", [32, DP])
    npp_o = dram_out("npp", [32, DP])
    dbg_out = {}

    a = SB_BASE
    C0 = a
    ident_bf = A.at("ident_bf", a, [128], BF16); a += 256
    ident_f = A.at("ident_f", a, [128], F32); a += 512
    ones_bf = A.at("ones_bf", a, [128], BF16); a += 256
    cst = A.at("cst", a, [360], F32); a += 1440
    wpool = A.at("wpool", a, [4, 2, 256], BF16); a += 4096
    stat = A.at("stat", a, [192], F32); a += 768
    pT = A.at("pT", a, [NT, 2, 128], BF16); a += NT * 2 * 128 * 2
    exs = A.at("exs", a, [2, 16, 23], F32); a += 2 * 16 * 23 * 4
    tmpf = A.at("tmpf", a, [16], F32); a += 64
    negh = A.at("negh", a, [8], F32); a += 32
    epsb = A.at("epsb", a, [8], F32); a += 32
    a = (a + 31) // 32 * 32
    R1 = a
    hT = A.at("hT", R1, [16, W], BF16); a += 16 * W * 2
    R2 = a
    vacc = [A.at(f"vacc{i}", R2 + i * W * 4, [W], F32) for i in range(8)]
    vall = A.at("vall", R2, [10, W], F32)
    a += 10 * W * 4
    R3 = a
    mT = A.at("mT", R3, [16, T], BF16); a += 16 * T * 2
    R4 = a
    ext_s = A.at("ext_s", R4, [8, 16, 38], BF16); a += 8 * 16 * 38 * 4
    v_bf = [A.at(f"v_bf{i}", R4 + 9728 + i * 2368, [W], BF16) for i in range(2)]
    vt = [A.at(f"vt{i}", R4 + 9728 + 4736 + i * 640, [160], F32) for i in range(2)]
    dgA = A.at("dgA", R2 + 8 * W * 4, [16, 128], BF16)
    dgB = A.at("dgB", R2 + 8 * W * 4 + 4096, [16, 128], BF16)
    wdwc = A.at("wdwc", R2 + 8 * W * 4 + 8192, [32], F32)
    dgA2 = A.at("dgA2", R3, [16, 128], BF16)
    dgB2 = A.at("dgB2", R3 + 4096, [16, 128], BF16)
    wdwc2 = A.at("wdwc2", R3 + 8192, [32], F32)
    R4b = a
    ya_in = A.at("ya_in", R4b, [8, T], BF16); a += 8 * T * 2
    R5 = a
    wsl = [A.at(f"wsl{i}", R5 + i * 4096, [2048], BF16) for i in range(4)]
    a += 4 * 4096
    R6 = a
    a += 18432
    assert a <= SB_END, a

    xs = [A.at(f"xs{i}", R3 + i * 8192, [D], F32) for i in range(3)]
    xs += [A.at(f"xs{3 + i}", R4b + i * 8192, [D], F32) for i in range(2)]
    pall = A.at("pall", R4b, [NT, 256], BF16)
    xb = [A.at(f"xb{i}", R3 + 24576 + i * 4096, [D], BF16) for i in range(2)]
    sqj = A.at("sqj", R3 + 32768, [D], BF16)
    silu_za = A.at("silu_za", R3, [4, T], BF16)
    pooled = A.at("pooled", R3 + 9216, [4, T], BF16)
    ubuf = A.at("ubuf", R3 + 18432, [W], F32)
    uscr = [A.at(f"uscr{i}", R3 + 18432 + (i + 1) * W * 4, [W], F32) for i in range(2)]
    assert 18432 + 3 * W * 4 <= 16 * T * 2
    ybf = [A.at(f"ybf{i}", R3 + i * 2304, [T], BF16) for i in range(2)]
    ysq = [A.at(f"ysq{i}", R3 + 4608 + i * 2304, [T], BF16) for i in range(2)]
    mean_sb = A.at("mean", R3 + 9216, [T], F32)
    rstd_sb = A.at("rstd", R3 + 9216 + 4608, [T], F32)
    sgl = [A.at(f"sgl{i}", R3 + 18432 + i * 4608, [T], F32) for i in range(2)]
    silu_zb = A.at("silu_zb", R4, [8, T], BF16)
    ext_u = A.at("ext_u", R6, [8, 16, 23], F32)
    sgt = [A.at(f"sgt{i}", R6 + 11776 + i * 2048, [512], F32) for i in range(3)]
    stg = A.at("stg", R6 + 11776, [DP], F32)
    sga = [A.at(f"sga{i}", R6 + i * 2304, [T], BF16) for i in range(2)]
    sgb = [A.at(f"sgb{i}", R6 + 4608 + i * 2304, [T], BF16) for i in range(2)]
    tA = A.at("tA", R6 + 9216, [T], F32)
    tB = A.at("tB", R6 + 9216 + 4608, [T], F32)
    TAIL = a
    so_s = [A.at(f"so_s{i}", TAIL + i * 512, [128], F32) for i in range(2)]
    so_p = [A.at(f"so_p{i}", TAIL + 1024 + i * 512, [128], F32, parts=32) for i in range(2)]
    assert TAIL + 2048 <= SB_END
    z = A.at("z", R1, [NT, D], F32)
    CZ = R1 + NT * D * 4
    CZ = (CZ + 31) // 32 * 32
    wple = A.at("wple", CZ, [2, D], BF16)
    assert CZ + 8192 <= R3
    c = R4
    wslc = [None, A.at("wslc1", c, [16, 256], BF16), A.at("wslc2", c + 8192, [16, 256], BF16)]; c += 2 * 8192
    x1b = [A.at(f"x1b{i}", c + i * 4096, [D], BF16) for i in range(2)]; c += 8192
    assert c == R4b + 5120
    wslc[0] = A.at("wslc0", c, [16, 256], BF16); c += 8192
    gpost = A.at("gpost", c, [D], F32); c += 8192
    gple = A.at("gple", c, [D], F32); c += 8192
    sgc = [A.at(f"sgc{i}", c + i * 1024, [256], F32) for i in range(2)]; c += 2048
    etc_ = [A.at(f"etc{i}", c + i * 1024, [256], F32) for i in range(2)]; c += 2048
    ptile = A.at("ptile", c, [256], F32); c += 1024
    pbt = A.at("pbt", c, [256], BF16); c += 512
    assert c <= SB_END, c

    banks = []
    for i in range(8):
        t = nc.alloc_psum_tensor(f"bank{i}", [128, 512], F32)
        banks.append(Buf(t, "ps", i * 2048, [512], 4))
    acc_rot = [0]

    def next_bank():
        b = banks[acc_rot[0] % 6]
        acc_rot[0] += 1
        return b

    TPB = (banks[6], banks[7])

    def bank_bf(b):
        return b.t.bitcast(BF16)

    def regs(*lists):
        out = []
        for l in lists:
            out.extend(l)
        return out

    S.dma("sp", "cst", [], regs(cst.reg(), ident_f.reg()),
          lambda e: [e.dma_start(out=cst[:], in_=cst_d), e.dma_start(out=ident_f[:], in_=ident_d)], n=2)
    S.dma("pool", "cstb", [], ident_bf.reg(), lambda e: e.dma_start(out=ident_bf[:], in_=ident_d))
    S.op("dve", [], ones_bf.reg(), lambda e: e.memset(ones_bf[:], 1.0))
    S.op("dve", [], negh.reg(), lambda e: e.memset(negh[:], -0.5))
    S.op("dve", [], epsb.reg(), lambda e: e.memset(epsb[:], EPS))
    S.op("dve", [], stat.reg(), lambda e: e.memset(stat[:], 0.0))

    GPRE = lambda k0, k1: cst[:, k0:k1]
    PSC = lambda d: cst[:, 16 + d:17 + d]
    BDW = lambda c_: cst[:, 24 + c_:25 + c_]
    LNG = lambda c_: cst[:, 32 + c_:33 + c_]
    LNB = lambda c_: cst[:, 40 + c_:41 + c_]
    INVC = lambda g: cst[:, 48 + g * 16:64 + g * 16]
    WDW = lambda c_, k: cst[:, 112 + c_ * 31 + k:113 + c_ * 31 + k]
    cst_r = cst.reg()

    wlist = []

    def w_in_src(ch):
        return (w_in_d[ch], lambda sl: sl.t[:].rearrange("p (k c) -> p k c", k=16), [16, 128])

    def w_pr_src(dram, G):
        return (dram[G], lambda sl: sl.t[:].rearrange("p (k c) -> p k c", k=8), [8, 256])

    order = []
    for c_ in range(8):
        order.append(("in", CH_B + c_)); order.append(("in", CH_A + c_))
    for g in range(4):
        order += [("in", CH_U + 2 * g), ("in", CH_ZA + 2 * g), ("in", CH_U + 2 * g + 1), ("in", CH_ZA + 2 * g + 1)]
    for c_ in range(8):
        order.append(("in", CH_ZB + c_))
    for j in range(16):
        order.append(("in", CH_GA + j))
        order.append(("pp", j))
        order.append(("in", CH_GB + j))
        order.append(("pc", j))
    wpos = {}
    for n, it in enumerate(order):
        wpos[it] = n
    w_issued = [0]

    W_LIMIT = [1]

    def w_issue_upto(n):
        while w_issued[0] <= min(n, len(order) - 1, W_LIMIT[0]):
            m = w_issued[0]
            kind, idx = order[m]
            sl = wsl[m % 4]
            if kind == "in":
                src, view = w_in_d[idx], sl.t[:].rearrange("p (k c) -> p k c", k=16)
            elif kind == "pp":
                src, view = w_pp_d[idx], sl.t[:, 0:1024].rearrange("p (k c) -> p k c", k=8)
            else:
                src, view = w_pc_d[idx], sl.t[:, 0:1024].rearrange("p (k c) -> p k c", k=8)
            S.dma("pool", f"w{m % 4}", [], sl.reg(),
                  lambda e, view=view, src=src: e.dma_start(out=view, in_=src))
            w_issued[0] += 1

    def wget(kind, idx):
        n = wpos[(kind, idx)]
        w_issue_upto(n + 3)
        sl = wsl[n % 4]
        if kind == "in":
            return sl, sl.t[:].rearrange("p (k c) -> p k c", k=16)
        return sl, sl.t[:, 0:1024].rearrange("p (k c) -> p k c", k=8)

    conv_q = []

    def conv_run(n):
        for _ in range(n):
            if not conv_q:
                return
            conv_q.pop(0)()

    def after_unit():
        conv_run(CONV_RATE[0])

    CONV_RATE = [0]

    xs_n = [0]

    def stage0(src_ap, nrows, col0, statcol):
        s = xs_n[0] % 5
        sb = xs_n[0] % 2
        xs_n[0] += 1
        xsl, xbl = xs[s], xb[sb]
        S.dma("sp", f"xs{s}", [], xsl.reg(), lambda e: e.dma_start(out=xsl[0:nrows, :], in_=src_ap))
        S.op("act", xsl.reg(), regs(sqj.reg(), stat.reg((statcol, statcol + 1))),
             lambda e: e.activation(out=sqj[0:nrows, :], in_=xsl[0:nrows, :], func=AF.Square,
                                    accum_out=stat[0:nrows, statcol:statcol + 1]))
        sc_ = stat[0:nrows, statcol:statcol + 1]
        sr_ = stat.reg((statcol, statcol + 1))
        S.op("act", regs(sr_, epsb.reg()), sr_,
             lambda e: e.activation(out=sc_, in_=sc_, func=AF.Sqrt, scale=1.0 / D, bias=epsb[0:nrows, 0:1]))
        S.op("dve", sr_, sr_, lambda e: e.reciprocal(out=sc_, in_=sc_))
        S.op("dve", regs(xsl.reg(), sr_), xbl.reg(),
             lambda e: e.tensor_scalar(out=xbl[0:nrows, :], in0=xsl[0:nrows, :], scalar1=sc_, scalar2=None, op0=ALU.mult))
        evs = []
        for h in range(2):
            tb = next_bank()
            tbv = bank_bf(tb)[:, 0:8 * nrows].rearrange("p (k t) -> p k t", k=8)

            def pe_fn(e, h=h, tbv=tbv):
                ins = None
                for kk in range(8):
                    k = h * 8 + kk
                    ins = e.transpose(tbv[:, kk, :], xbl[0:nrows, k * 128:(k + 1) * 128], ident_bf[0:nrows, 0:nrows])
                return ins
            S.op("pe", regs(xbl.reg(), ident_bf.reg()), tb.reg(), pe_fn)

            def ev(h=h, tb=tb, tbv=tbv):
                S.op("dve", regs(tb.reg(), cst_r), hT.reg((h * 8, h * 8 + 8), (col0, col0 + nrows)),
                     lambda e: e.tensor_tensor(
                         out=hT[:, h * 8:h * 8 + 8, col0:col0 + nrows], in0=tbv,
                         in1=GPRE(h * 8, h * 8 + 8).unsqueeze(2).broadcast_to([128, 8, nrows]), op=ALU.mult))
            evs.append(ev)
        return evs

    def state_T(src_dram, nb, nr, ext, j):
        xsl = stg
        rows = nb * nr
        src = src_dram[j * nb:(j + 1) * nb].rearrange("b r c -> (b r) c")
        S.dma("sp", "stg", [], xsl.reg((0, DP)), lambda e: e.dma_start(out=xsl[0:rows, 0:DP], in_=src))
        for h in range(2):
            tb = TPB[h]

            def pe_fn(e, h=h, tb=tb):
                ins = None
                for cc in range(4):
                    c_ = h * 4 + cc
                    ins = e.transpose(tb[:, cc * rows:(cc + 1) * rows], xsl[0:rows, c_ * 128:(c_ + 1) * 128],
                                      ident_f[0:rows, 0:rows])
                return ins
            S.op("pe", regs(xsl.reg((0, DP)), ident_f.reg()), tb.reg(), pe_fn)
            S.op("act", tb.reg(), ext.reg((h * 4, h * 4 + 4)),
                 lambda e, h=h, tb=tb: e.activation(
                     out=ext[:, h * 4:h * 4 + 4, j * nb:(j + 1) * nb, 0:nr],
                     in_=tb[:, 0:4 * rows].rearrange("p (c b r) -> p c b r", c=4, b=nb), func=AF.Copy))

    def p_load():
        S.dma("pool", "pall", [], pall.reg(), lambda e: e.dma_start(out=pall[:], in_=p_tok.rearrange("i p c -> p i c")))

    def p_T(i):
        tb = TPB[i % 2]
        tbv = bank_bf(tb)[:, 0:256].rearrange("p (k t) -> p k t", k=2)

        def pe_fn(e):
            ins = None
            for kk in range(2):
                ins = e.transpose(tbv[:, kk, :], pall[:, i, kk * 128:(kk + 1) * 128], ident_bf[:])
            return ins
        S.op("pe", regs(pall.reg(i), ident_bf.reg()), tb.reg(), pe_fn)
        S.op("dve", tb.reg(), pT.reg(i), lambda e: e.tensor_copy(out=pT[:, i, :, :], in_=tbv))

    BLK_H = [(0, 512), (512, 1024), (1024, W)]
    BLK_N = [(32, 544), (544, 1056), (1056, W)]

    def stage1(ch, halo, evac, blocks=(0, 1, 2)):
        sl, wv = wget("in", ch)
        for (c0, c1) in [(BLK_H if halo else BLK_N)[b_] for b_ in blocks]:
            n = c1 - c0
            bk = next_bank()

            def pe_fn(e, bk=bk, c0=c0, c1=c1, n=n):
                ins = None
                for k in range(16):
                    ins = e.matmul(bk[:, 0:n], wv[:, k, :], hT[:, k, c0:c1], start=(k == 0), stop=(k == 15))
                return ins
            S.op("pe", regs(sl.reg(), hT.reg(None, (c0, c1))), bk.reg((0, n)), pe_fn)
            evac(bk, c0, c1, n)
            after_unit()

    acc_of = {c_: vacc[c_] for c_ in range(8)}
    so_n = [0]
    CB = [(0, 512, 2), (512, 1024, 514), (1024, T, None)]

    def dg_set(c_):
        return (dgA2, dgB2, wdwc2) if c_ == 7 else (dgA, dgB, wdwc)

    def conv_prep(c_, slot):
        dA_, dB_, wc_ = dg_set(c_)
        S.op("dve", cst_r, wc_.reg(), lambda e: e.tensor_copy(out=wc_[:, 0:31], in_=cst[:, 112 + c_ * 31:143 + c_ * 31]))
        for (dg, k0, nk) in ((dA_, 0, 16), (dB_, 16, 15)):
            S.op("dve", regs(ident_bf.reg(), wc_.reg()), dg.reg((0, nk)),
                 lambda e, dg=dg, k0=k0, nk=nk: e.tensor_tensor(
                     out=dg[:, 0:nk, :], in0=ident_bf[:, :].unsqueeze(1).broadcast_to([128, nk, 128]),
                     in1=wc_[:, k0:k0 + nk].unsqueeze(2).broadcast_to([128, nk, 128]), op=ALU.mult))

    def conv_pe(c_, slot):
        vb, vtl, acc = v_bf[slot], vt[slot], vacc[c_]
        dgA, dgB, _ = dg_set(c_)
        S.op("act", vb.reg((1056, W)), ext_s.reg(c_),
             lambda e: e.activation(out=ext_s[:, c_, :, 30:38],
                                    in_=vb[:, 1056:W].rearrange("p (b t) -> p b t", b=16), func=AF.Copy))
        tb = TPB[c_ % 2]

        def pe_fn(e):
            e.transpose(tb[:, 0:128], vtl[:, 32:160], ident_f[:])
            return e.transpose(tb[0:32, 128:256], vtl[:, 0:32], ident_f[:])
        S.op("pe", regs(vtl.reg(), ident_f.reg()), tb.reg(), pe_fn)
        sl_ = so_n[0] % 2
        so_n[0] += 1
        ss_, sp_ = so_s[sl_], so_p[sl_]
        S.op("act", tb.reg(), ss_.reg(), lambda e: e.activation(out=ss_[:], in_=tb[:, 0:128], func=AF.Copy))
        S.op("act", tb.reg(), sp_.reg(), lambda e: e.activation(out=sp_[0:32, :], in_=tb[0:32, 128:256], func=AF.Copy))
        S.dma("sp", f"os{sl_}", ss_.reg(), [], lambda e: e.dma_start(out=ncs_new[:, c_ * 128:(c_ + 1) * 128], in_=ss_[:]),
              final=True)
        S.dma("sp", f"op{sl_}", sp_.reg(), [], lambda e: e.dma_start(out=ncp_o[:, c_ * 128:(c_ + 1) * 128],
                                                              in_=sp_[0:32, :]), final=True)
        KP = 19 if c_ < 7 else 31
        for k in range(KP, 31):
            i_ap = vb[:, 2 + k:1026 + k]
            i_rg = vb.reg((2 + k, 1026 + k))
            o_ap = acc[:, 0:1024]
            o_rg = acc.reg((0, 1024))
            if k == KP:
                S.op("dve", regs(i_rg, cst_r), o_rg,
                     lambda e, i_ap=i_ap, o_ap=o_ap, k=k: e.tensor_scalar(
                         out=o_ap, in0=i_ap, scalar1=WDW(c_, k), scalar2=BDW(c_), op0=ALU.mult, op1=ALU.add))
            else:
                S.op("dve", regs(i_rg, cst_r, o_rg), o_rg,
                     lambda e, i_ap=i_ap, o_ap=o_ap, k=k: e.scalar_tensor_tensor(
                         out=o_ap, in0=i_ap, scalar=WDW(c_, k), in1=o_ap, op0=ALU.mult, op1=ALU.add))
        for (t0, t1, voff) in CB:
            n = t1 - t0
            bk = next_bank()
            nk = KP if voff is not None else 31

            def pe_fn(e, bk=bk, t0=t0, n=n, voff=voff, nk=nk):
                ins = None
                for k in range(nk):
                    dg = dgA if k < 16 else dgB
                    if voff is not None:
                        ins = e.matmul(bk[:, 0:n], dg[:, k % 16, :], vb[:, voff + k:voff + k + n],
                                       start=(k == 0), stop=(k == nk - 1))
                    else:
                        ins = e.matmul(bk[:, 0:128].rearrange("p (b t) -> p b t", b=16), dg[:, k % 16, :],
                                       ext_s[:, c_, :, k:k + 8], start=(k == 0), stop=(k == nk - 1))
                return ins
            rd = regs(dgA.reg(), dgB.reg(), vb.reg() if voff is not None else ext_s.reg(c_))
            S.op("pe", rd, bk.reg(), pe_fn)
            if voff is not None and KP < 31:
                S.op("dve", regs(bk.reg(), acc.reg((t0, t1))), acc.reg((t0, t1)),
                     lambda e, bk=bk, t0=t0, t1=t1, n=n: e.tensor_tensor(out=acc[:, t0:t1], in0=bk[:, 0:n],
                                                                         in1=acc[:, t0:t1], op=ALU.add))
            else:
                S.op("act", regs(bk.reg(), cst_r), acc.reg((t0, t1)),
                     lambda e, bk=bk, t0=t0, t1=t1, n=n: e.activation(out=acc[:, t0:t1], in_=bk[:, 0:n],
                                                                      func=AF.Identity, bias=BDW(c_)))

    def a2_evs(c_):
        sg_ = vacc[c_]
        vb, vtl = v_bf[c_ % 2], vt[c_ % 2]

        def ev_b(bk, c0, c1, n):
            S.op("act", bk.reg(), sg_.reg((c0, c1)),
                 lambda e: e.activation(out=sg_[:, c0:c1], in_=bk[:, 0:n], func=AF.Sigmoid))

        def ev_a(bk, c0, c1, n):
            S.op("dve", regs(bk.reg(), sg_.reg((c0, c1))), vb.reg((c0, c1)),
                 lambda e: e.tensor_tensor(out=vb[:, c0:c1], in0=bk[:, 0:n], in1=sg_[:, c0:c1], op=ALU.mult))
            if c0 == 1024:
                S.op("dve", regs(bk.reg(), sg_.reg((c0, c1))), vtl.reg(),
                     lambda e: e.tensor_tensor(out=vtl[:, :], in0=bk[:, 0:n], in1=sg_[:, c0:c1], op=ALU.mult))
        return ev_b, ev_a

    early = stop_after != "A1"
    evb0, eva0 = a2_evs(0)

    def early_blocks(bl):
        if early:
            stage1(CH_B + 0, True, evb0, blocks=bl)
            stage1(CH_A + 0, True, eva0, blocks=bl)

    prev_evs = stage0(x_halo, 32, 0, 9)
    for i in range(NT):
        evs_ = stage0(x_tok[i], 128, 32 + 128 * i, i)
        for ev_ in prev_evs:
            ev_()
        prev_evs = evs_
        if early:
            if i == 4:
                stage1(CH_B + 0, True, evb0, blocks=[0])
            if i == 6:
                stage1(CH_A + 0, True, eva0, blocks=[0])
                W_LIMIT[0] = 3
                w_issue_upto(3)
    for ev_ in prev_evs:
        ev_()
    if early:
        stage1(CH_B + 0, True, evb0, blocks=[1])
        stage1(CH_A + 0, True, eva0, blocks=[1])
    early_blocks([2])
    W_LIMIT[0] = 10 ** 6
    S.dma("pool", "wpl", [], wpool.reg(),
          lambda e: e.dma_start(out=wpool[:], in_=w_pool_d.rearrange("p (g k c) -> p g k c", g=4, k=2)))
    p_load()
    for j_ in range(4):
        conv_q.append(lambda j_=j_: state_T(st_conv, 4, 30, ext_s, j_))
    for j_ in range(2):
        conv_q.append(lambda j_=j_: state_T(st_pool, 8, 15, ext_u, j_))
    conv_q.append(lambda: S.dma("sp", "out", [], [], lambda e: e.dma_start(out=ncs_old, in_=st_conv[:, 8:30, :]), final=True))
    conv_q.append(lambda: S.dma("sp", "out", [], [], lambda e: e.dma_start(out=nps_old, in_=st_pool[:, 8:15, :]), final=True))
    CONV_RATE[0] = 1

    pending = []
    for c_ in (range(8) if stop_after != "A1" else []):
        slot = c_ % 2
        if c_ > 0:
            ev_b, ev_a = a2_evs(c_)
            stage1(CH_B + c_, True, ev_b)
            if pending:
                conv_prep(*pending[0])
            stage1(CH_A + c_, True, ev_a)
        if pending:
            conv_run(10 ** 6)
            if c_ == 7:
                conv_prep(7, slot)
            conv_pe(*pending.pop(0))
        pending.append((c_, slot))
    if stop_after != "A1":
        conv_pe(*pending.pop(0))

    if stop_after not in ("A1", "A2"):
        for i in range(NT):
            conv_q.append(lambda i=i: p_T(i))
        CONV_RATE[0] = 1
        def za_part(g, cc):
            wwin = 2 ** (g + 1)
            ch = 2 * g + cc
            slot = (g % 2) * 2 + cc

            def ev_za(bk, c0, c1, n, slot=slot):
                st = sgt[acc_rot[0] % 3]
                S.op("act", bk.reg((0, n)), st.reg((0, n)),
                     lambda e: e.activation(out=st[:, 0:n], in_=bk[:, 0:n], func=AF.Sigmoid))
                S.op("dve", regs(bk.reg((0, n)), st.reg((0, n))), silu_za.reg(slot, (c0 - 32, c1 - 32)),
                     lambda e: e.tensor_tensor(out=silu_za[:, slot, c0 - 32:c1 - 32], in0=bk[:, 0:n],
                                               in1=st[:, 0:n], op=ALU.mult))
            stage1(CH_ZA + ch, False, ev_za)

        def u_part(g, cc):
            wwin = 2 ** (g + 1)
            ch = 2 * g + cc
            slot = (g % 2) * 2 + cc

            def ev_u(bk, c0, c1, n):
                S.op("act", bk.reg((0, n)), ubuf.reg((c0, c1)),
                     lambda e: e.activation(out=ubuf[:, c0:c1], in_=bk[:, 0:n], func=AF.Copy))
            stage1(CH_U + ch, True, ev_u)
            S.op("act", ubuf.reg((1056, W)), ext_u.reg(ch),
                 lambda e, ch=ch: e.activation(out=ext_u[:, ch, :, 15:23],
                                               in_=ubuf[:, 1056:W].rearrange("p (b t) -> p b t", b=16),
                                               func=AF.Copy))
            tb = TPB[ch % 2]

            def pe_fn(e, tb=tb):
                e.transpose(tb[:, 0:128], ubuf[:, 1056:W], ident_f[:])
                return e.transpose(tb[0:32, 128:256], ubuf[:, 1024:1056], ident_f[:])
            S.op("pe", regs(ubuf.reg((1024, W)), ident_f.reg()), tb.reg((0, 256)), pe_fn)
            sl_ = so_n[0] % 2
            so_n[0] += 1
            ss_, sp_ = so_s[sl_], so_p[sl_]
            S.op("act", tb.reg((0, 128)), ss_.reg(),
                 lambda e, tb=tb, ss_=ss_: e.activation(out=ss_[:], in_=tb[:, 0:128], func=AF.Copy))
            S.op("act", tb.reg((128, 256)), sp_.reg(),
                 lambda e, tb=tb, sp_=sp_: e.activation(out=sp_[0:32, :], in_=tb[0:32, 128:256], func=AF.Copy))
            S.dma("sp", f"os{sl_}", ss_.reg(), [],
                  lambda e, ch=ch, ss_=ss_: e.dma_start(out=nps_new[:, ch * 128:(ch + 1) * 128], in_=ss_[:]), final=True)
            S.dma("sp", f"op{sl_}", sp_.reg(), [],
                  lambda e, ch=ch, sp_=sp_: e.dma_start(out=npp_o[:, ch * 128:(ch + 1) * 128], in_=sp_[0:32, :]),
                  final=True)
            src = ubuf
            for l in range(1, g + 2):
                sh = 2 ** (l - 1)
                lo = 2 ** l
                dst = uscr[(l - 1) % 2]
                S.op("dve", src.reg((lo - sh, 1056)), dst.reg((lo, 1056)),
                     lambda e, src=src, dst=dst, lo=lo, sh=sh: e.tensor_tensor(
                         out=dst[:, lo:1056], in0=src[:, lo:1056], in1=src[:, lo - sh:1056 - sh], op=ALU.add))
                src = dst
            win = src
            ssrc_ap = ext_u[:, ch, :, :]
            ssrc_reg = ext_u.reg(ch)
            for l in range(1, g + 2):
                sh = 2 ** (l - 1)
                lo = 2 ** l - 1
                d = exs[:, (l - 1) % 2, :, :]
                dreg = exs.reg((l - 1) % 2)
                S.op("dve", ssrc_reg, dreg,
                     lambda e, s_=ssrc_ap, d=d, lo=lo, sh=sh: e.tensor_tensor(
                         out=d[:, :, lo:23], in0=s_[:, :, lo:23], in1=s_[:, :, lo - sh:23 - sh], op=ALU.add))
                ssrc_ap, ssrc_reg = d, dreg
            S.op("dve", regs(win.reg((32, 1056)), ubuf.reg((32, 1056))), pooled.reg(slot, (0, 1024)),
                 lambda e, win=win, slot=slot, wwin=wwin: e.scalar_tensor_tensor(
                     out=pooled[:, slot, 0:1024], in0=win[:, 32:1056], scalar=1.0 / wwin, in1=ubuf[:, 32:1056],
                     op0=ALU.mult, op1=ALU.subtract))
            S.op("dve", regs(ssrc_reg, ext_u.reg(ch)), pooled.reg(slot, (1024, T)),
                 lambda e, s_=ssrc_ap, slot=slot, wwin=wwin, ch=ch: e.scalar_tensor_tensor(
                     out=pooled[:, slot, 1024:T].rearrange("p (b t) -> p b t", b=16), in0=s_[:, :, 15:23],
                     scalar=1.0 / wwin, in1=ext_u[:, ch, :, 15:23], op0=ALU.mult, op1=ALU.subtract))
            S.op("dve", regs(win.reg((32, 48)), cst_r), tmpf.reg(),
                 lambda e, win=win, g=g: e.tensor_tensor(out=tmpf[:], in0=win[:, 32:48], in1=INVC(g), op=ALU.mult))
            S.op("dve", regs(tmpf.reg(), ubuf.reg((32, 48))), pooled.reg(slot, (0, 16)),
                 lambda e, slot=slot: e.tensor_tensor(out=pooled[:, slot, 0:16], in0=tmpf[:], in1=ubuf[:, 32:48],
                                                      op=ALU.subtract))

        def wpool_part(g):
            conv_run(10 ** 6)
            for dd in range(2):
                d = 2 * g + dd
                for (c0, c1) in [(0, 512), (512, 1024), (1024, T)]:
                    n = c1 - c0
                    bk = next_bank()

                    def pe_fn(e, bk=bk, c0=c0, c1=c1, n=n, dd=dd, g=g):
                        ins = None
                        for kc in range(2):
                            ins = e.matmul(bk[:, 0:n], wpool[:, g, kc, dd * 128:(dd + 1) * 128],
                                           pooled[:, (g % 2) * 2 + kc, c0:c1], start=(kc == 0), stop=(kc == 1))
                        return ins
                    S.op("pe", regs(wpool.reg(g), pooled.reg(((g % 2) * 2, (g % 2) * 2 + 2), (c0, c1))),
                         bk.reg((0, n)), pe_fn)
                    S.op("dve", regs(bk.reg((0, n)), silu_za.reg((g % 2) * 2 + dd, (c0, c1)), cst_r),
                         ya_in.reg(d, (c0, c1)),
                         lambda e, bk=bk, n=n, d=d, dd=dd, g=g, c0=c0, c1=c1: e.scalar_tensor_tensor(
                             out=ya_in[:, d, c0:c1], in0=bk[:, 0:n], scalar=PSC(d),
                             in1=silu_za[:, (g % 2) * 2 + dd, c0:c1], op0=ALU.mult, op1=ALU.mult))
                    after_unit()


        for g in range(4):
            u_part(g, 0)
            za_part(g, 0)
            u_part(g, 1)
            za_part(g, 1)
            if g > 0:
                wpool_part(g - 1)
        wpool_part(3)

    if stop_after not in ("A1", "A2", "A3"):
        TB = [(0, 512), (512, 1024), (1024, T)]
        for c_ in range(8):
            acc = acc_of[c_]
            s = c_ % 2
            S.op("dve", acc.reg((0, T)), ybf[s].reg(), lambda e, acc=acc, s=s: e.tensor_copy(
                out=ybf[s][:], in_=acc[:, 0:T]))
            S.op("act", acc.reg((0, T)), ysq[s].reg(), lambda e, acc=acc, s=s: e.activation(
                out=ysq[s][:], in_=acc[:, 0:T], func=AF.Square))

            def pe_fn(e, s=s, c_=c_):
                ins = None
                for bi, (c0, c1) in enumerate(TB):
                    n = c1 - c0
                    e.matmul(banks[bi][:, 0:n], ones_bf[:], ybf[s][:, c0:c1], start=(c_ == 0), stop=(c_ == 7))
                    ins = e.matmul(banks[3 + bi][:, 0:n], ones_bf[:], ysq[s][:, c0:c1], start=(c_ == 0), stop=(c_ == 7))
                return ins
            S.op("pe", regs(ybf[s].reg(), ysq[s].reg(), ones_bf.reg()),
                 regs(*[banks[b].reg() for b in range(6)]), pe_fn)
        for bi, (c0, c1) in enumerate(TB):
            n = c1 - c0
            S.op("act", banks[bi].reg((0, n)), mean_sb.reg((c0, c1)),
                 lambda e, bi=bi, c0=c0, c1=c1, n=n: e.activation(out=mean_sb[:, c0:c1], in_=banks[bi][:, 0:n],
                                                                  func=AF.Copy, scale=1.0 / DP))
        for bi, (c0, c1) in enumerate(TB):
            n = c1 - c0
            S.op("dve", mean_sb.reg((c0, c1)), rstd_sb.reg((c0, c1)),
                 lambda e, c0=c0, c1=c1: e.tensor_tensor(out=rstd_sb[:, c0:c1], in0=mean_sb[:, c0:c1],
                                                         in1=mean_sb[:, c0:c1], op=ALU.mult))
        for bi, (c0, c1) in enumerate(TB):
            n = c1 - c0
            S.op("dve", regs(banks[3 + bi].reg((0, n)), rstd_sb.reg((c0, c1))), rstd_sb.reg((c0, c1)),
                 lambda e, bi=bi, c0=c0, c1=c1, n=n: e.scalar_tensor_tensor(
                     out=rstd_sb[:, c0:c1], in0=banks[3 + bi][:, 0:n], scalar=1.0 / DP, in1=rstd_sb[:, c0:c1],
                     op0=ALU.mult, op1=ALU.subtract))
        S.op("act", regs(rstd_sb.reg(), epsb.reg()), rstd_sb.reg(),
             lambda e: e.activation(out=rstd_sb[:], in_=rstd_sb[:], func=AF.Sqrt, bias=epsb[:, 0:1]))
        S.op("dve", rstd_sb.reg(), rstd_sb.reg(), lambda e: e.reciprocal(out=rstd_sb[:], in_=rstd_sb[:]))
        acc_rot[0] = (acc_rot[0] + 5) // 6 * 6
        def zb_part(c_):
            def ev_zb(bk, c0, c1, n, c_=c_):
                st = sgt[acc_rot[0] % 3]
                S.op("act", bk.reg((0, n)), st.reg((0, n)),
                     lambda e: e.activation(out=st[:, 0:n], in_=bk[:, 0:n], func=AF.Sigmoid))
                S.op("dve", regs(bk.reg((0, n)), st.reg((0, n))), silu_zb.reg(c_, (c0 - 32, c1 - 32)),
                     lambda e: e.tensor_tensor(out=silu_zb[:, c_, c0 - 32:c1 - 32], in0=bk[:, 0:n],
                                               in1=st[:, 0:n], op=ALU.mult))
            stage1(CH_ZB + c_, False, ev_zb)


        zb_part(0)
        for c_ in range(8):
            if c_ + 1 < 8:
                zb_part(c_ + 1)
            acc = acc_of[c_]
            s = c_ % 2
            AT = acc[:, 0:T]
            S.op("dve", regs(acc.reg((0, T)), mean_sb.reg()), acc.reg((0, T)),
                 lambda e, AT=AT: e.tensor_tensor(out=AT, in0=AT, in1=mean_sb[:], op=ALU.subtract))
            S.op("dve", regs(acc.reg((0, T)), rstd_sb.reg()), acc.reg((0, T)),
                 lambda e, AT=AT: e.tensor_tensor(out=AT, in0=AT, in1=rstd_sb[:], op=ALU.mult))
            S.op("act", regs(acc.reg((0, T)), cst_r), sgl[s].reg(),
                 lambda e, AT=AT, s=s, c_=c_: e.activation(out=sgl[s][:], in_=AT, func=AF.Sigmoid,
                                                           scale=LNG(c_), bias=LNB(c_)))
            S.op("act", regs(acc.reg((0, T)), cst_r), acc.reg((0, T)),
                 lambda e, AT=AT, c_=c_: e.activation(out=AT, in_=AT, func=AF.Identity, scale=LNG(c_), bias=LNB(c_)))
            S.op("dve", regs(acc.reg((0, T)), sgl[s].reg()), acc.reg((0, T)),
                 lambda e, AT=AT, s=s: e.tensor_tensor(out=AT, in0=AT, in1=sgl[s][:], op=ALU.mult))
            S.op("dve", regs(acc.reg((0, T)), silu_zb.reg(c_)), silu_zb.reg(c_),
                 lambda e, AT=AT, c_=c_: e.tensor_tensor(out=silu_zb[:, c_, :], in0=AT, in1=silu_zb[:, c_, :],
                                                         op=ALU.mult))
    yb_in = silu_zb

    TB = [(0, 512), (512, 1024), (1024, T)]
    SS_E = 16

    def estat_unit(i, q):
        bk = next_bank()

        def pe_fn(e):
            ins = None
            for kk in range(2):
                ins = e.matmul(bk[:, :], pT[:, i, kk, :], wple[:, kk, q * 512:(q + 1) * 512],
                               start=(kk == 0), stop=(kk == 1))
            return ins
        S.op("pe", regs(pT.reg(i), wple.reg(None, (q * 512, q * 512 + 512))), bk.reg(), pe_fn)
        col = 128 + i * 4 + q
        S.op("act", bk.reg(), regs(bk.reg(), stat.reg((col, col + 1))),
             lambda e: e.activation(out=bk[:, :], in_=bk[:, :], func=AF.Square, accum_out=stat[:, col:col + 1]))

    if stop_after not in ("A1", "A2", "A3", "LN"):
        for j in range(16):
            if j == 2 and stop_after is None:
                S.dma("pool", "ccb", [], wple.reg(),
                      lambda e: e.dma_start(out=wple[:], in_=w_ple_d.rearrange("p (k c) -> p k c", k=2)))
                for i_ in range(NT):
                    for q_ in range(4):
                        conv_q.append(lambda i_=i_, q_=q_: estat_unit(i_, q_))
                CONV_RATE[0] = 1
            s = j % 2

            def ev_gate(dstb):
                def ev(bk, c0, c1, n):
                    S.op("act", bk.reg((0, n)), dstb.reg((c0 - 32, c1 - 32)),
                         lambda e: e.activation(out=dstb[:, c0 - 32:c1 - 32], in_=bk[:, 0:n], func=AF.Sigmoid))
                return ev

            def proj(kind, src_buf, gate_buf, final, j=j):
                sl, wv = wget(kind, j)
                for (c0, c1) in TB:
                    n = c1 - c0
                    bk = next_bank()

                    def pe_fn(e, bk=bk, c0=c0, c1=c1, n=n):
                        ins = None
                        for k in range(8):
                            ins = e.matmul(bk[:, 0:n], wv[:, k, :],
                                           src_buf[:, k, c0:c1], start=(k == 0), stop=(k == 7))
                        return ins
                    S.op("pe", regs(sl.reg(), src_buf.reg(None, (c0, c1))), bk.reg((0, n)), pe_fn)
                    if not final:
                        S.op("dve", regs(bk.reg((0, n)), gate_buf.reg((c0, c1))), tA.reg((c0, c1)),
                             lambda e, bk=bk, c0=c0, c1=c1, n=n: e.tensor_tensor(
                                 out=tA[:, c0:c1], in0=bk[:, 0:n], in1=gate_buf[:, c0:c1], op=ALU.mult))
                    else:
                        S.op("dve", regs(bk.reg((0, n)), gate_buf.reg((c0, c1))), tB.reg((c0, c1)),
                             lambda e, bk=bk, c0=c0, c1=c1, n=n: e.tensor_tensor(
                                 out=tB[:, c0:c1], in0=bk[:, 0:n], in1=gate_buf[:, c0:c1], op=ALU.mult))
                        S.op("dve", regs(tA.reg((c0, c1)), tB.reg((c0, c1))), mT.reg(j, (c0, c1)),
                             lambda e, c0=c0, c1=c1, j=j: e.tensor_tensor(out=mT[:, j, c0:c1], in0=tA[:, c0:c1],
                                                                          in1=tB[:, c0:c1], op=ALU.add))
            stage1(CH_GA + j, False, ev_gate(sga[s]))
            proj("pp", ya_in, sga[s], False)
            stage1(CH_GB + j, False, ev_gate(sgb[s]))
            proj("pc", yb_in, sgb[s], True)

    if stop_after is None:
        S.dma("sp", "cc", [], regs(gpost.reg(), gple.reg()),
              lambda e: [e.dma_start(out=gpost[:], in_=gpost_d), e.dma_start(out=gple[:], in_=gple_d)], n=2)

        corder = [("o", G, 0) for G in range(8)] + [("g", G, 0) for G in range(8)]
        c_issued = [0]

        C_LIMIT = [10 ** 6]

        def c_issue_upto(n):
            while c_issued[0] <= min(n, len(corder) - 1, C_LIMIT[0]):
                m = c_issued[0]
                kind, G, _rep = corder[m]
                sl = wslc[m % 3]
                src = w_out_d[G] if kind == "o" else w_pg_d[G]
                S.dma("pool", f"wc{m % 3}", [], sl.reg(), lambda e, sl=sl, src=src: e.dma_start(out=sl[:], in_=src))
                c_issued[0] += 1

        def cget(kind, G, rep):
            n = corder.index((kind, G, rep))
            c_issue_upto(n + 2)
            return wslc[n % 3]

        SS_Z = 32
        conv_run(10 ** 6)
        for i in range(NT):
            S.op("dve", stat.reg((128 + i * 4, 132 + i * 4)), stat.reg((SS_E + i, SS_E + i + 1)),
                 lambda e, i=i: e.tensor_reduce(out=stat[:, SS_E + i:SS_E + i + 1], in_=stat[:, 128 + i * 4:132 + i * 4],
                                                axis=mybir.AxisListType.X, op=ALU.add))
            S.op("act", regs(stat.reg((SS_E + i, SS_E + i + 1)), epsb.reg()), stat.reg((SS_E + i, SS_E + i + 1)),
                 lambda e, i=i: e.activation(out=stat[:, SS_E + i:SS_E + i + 1], in_=stat[:, SS_E + i:SS_E + i + 1],
                                             func=AF.Sqrt, scale=1.0 / D, bias=epsb[:, 0:1]))
            S.op("dve", stat.reg((SS_E + i, SS_E + i + 1)), stat.reg((SS_E + i, SS_E + i + 1)),
                 lambda e, i=i: e.reciprocal(out=stat[:, SS_E + i:SS_E + i + 1], in_=stat[:, SS_E + i:SS_E + i + 1]))

        HA, HB = [0, 1, 2, 3, 4], [5, 6, 7, 8]

        def c1_unit(G, i, sl):
            bk = next_bank()

            def pe_fn(e):
                ins = None
                for k in range(16):
                    ins = e.matmul(bk[:, 0:256], mT[:, k, i * 128:(i + 1) * 128], sl[:, k, :],
                                   start=(k == 0), stop=(k == 15))
                return ins
            S.op("pe", regs(sl.reg(), mT.reg(None, (i * 128, i * 128 + 128))), bk.reg(), pe_fn)
            S.op("act", bk.reg(), z.reg(i, (G * 256, G * 256 + 256)),
                 lambda e: e.activation(out=z[:, i, G * 256:(G + 1) * 256], in_=bk[:, 0:256], func=AF.Copy))
            col = SS_Z + i * 8 + G
            S.op("act", bk.reg(), regs(etc_[0].reg(), stat.reg((col, col + 1))),
                 lambda e: e.activation(out=etc_[0][:], in_=bk[:, 0:256], func=AF.Square,
                                        accum_out=stat[:, col:col + 1]))

        def x1_chain(i):
            c0 = SS_Z + i * 8
            rc = 25 + (i % 2)
            S.op("dve", stat.reg((c0, c0 + 8)), stat.reg((rc, rc + 1)),
                 lambda e: e.tensor_reduce(out=stat[:, rc:rc + 1], in_=stat[:, c0:c0 + 8],
                                           axis=mybir.AxisListType.X, op=ALU.add))
            S.op("act", regs(stat.reg((rc, rc + 1)), epsb.reg()), stat.reg((rc, rc + 1)),
                 lambda e: e.activation(out=stat[:, rc:rc + 1], in_=stat[:, rc:rc + 1], func=AF.Sqrt, scale=1.0 / D,
                                        bias=epsb[:, 0:1]))
            S.op("dve", stat.reg((rc, rc + 1)), stat.reg((rc, rc + 1)),
                 lambda e: e.reciprocal(out=stat[:, rc:rc + 1], in_=stat[:, rc:rc + 1]))
            S.op("dve", regs(z.reg(i), stat.reg((rc, rc + 1)), gpost.reg()), z.reg(i),
                 lambda e: e.scalar_tensor_tensor(out=z[:, i, :], in0=z[:, i, :], scalar=stat[:, rc:rc + 1],
                                                  in1=gpost[:], op0=ALU.mult, op1=ALU.mult))
            S.dma("pool", f"xa{i}", z.reg(i), z.reg(i),
                  lambda e: e.dma_start(out=z[:, i, :], in_=x_tok[i], accum_op=ALU.add))

        def x1_cast(i):
            xbl = x1b[i % 2]
            S.op("act", z.reg(i), xbl.reg(), lambda e: e.activation(out=xbl[:], in_=z[:, i, :], func=AF.Copy))

        def x1_transposes(i):
            xbl = x1b[i % 2]
            for h in range(2):
                tb = TPB[h]
                tbv = bank_bf(tb)[:, 0:1024].rearrange("p (k t) -> p k t", k=8)

                def pe_fn(e, h=h, tbv=tbv):
                    ins = None
                    for kk in range(8):
                        k = h * 8 + kk
                        ins = e.transpose(tbv[:, kk, :], xbl[:, k * 128:(k + 1) * 128], ident_bf[:])
                    return ins
                S.op("pe", regs(xbl.reg(), ident_bf.reg()), tb.reg(), pe_fn)
                if h == 0:
                    S.op("dve", tb.reg(), mT.reg((h * 8, h * 8 + 8), (i * 128, i * 128 + 128)),
                         lambda e, h=h, tbv=tbv: e.tensor_copy(out=mT[:, h * 8:h * 8 + 8, i * 128:(i + 1) * 128], in_=tbv))
                else:
                    S.op("act", tb.reg(), mT.reg((h * 8, h * 8 + 8), (i * 128, i * 128 + 128)),
                         lambda e, h=h, tbv=tbv: e.activation(out=mT[:, h * 8:h * 8 + 8, i * 128:(i + 1) * 128], in_=tbv,
                                                              func=AF.Copy))

        def c2_unit(G, i, sl):
            bk = next_bank()
            bk2 = next_bank()
            s = i % 2

            def pe_fn(e):
                ins = None
                for k in range(16):
                    ins = e.matmul(bk[:, 0:256], mT[:, k, i * 128:(i + 1) * 128], sl[:, k, :],
                                   start=(k == 0), stop=(k == 15))
                return ins
            S.op("pe", regs(sl.reg(), mT.reg(None, (i * 128, i * 128 + 128))), bk.reg(), pe_fn)

            def pe_fn2(e):
                ins = None
                for kk in range(2):
                    ins = e.matmul(bk2[:, 0:256], pT[:, i, kk, :], wple[:, kk, G * 256:(G + 1) * 256],
                                   start=(kk == 0), stop=(kk == 1))
                return ins
            S.op("pe", regs(pT.reg(i), wple.reg(None, (G * 256, G * 256 + 256))), bk2.reg(), pe_fn2)
            S.op("act", bk.reg(), sgc[s].reg(),
                 lambda e: e.activation(out=sgc[s][:], in_=bk[:, 0:256], func=AF.Sigmoid))
            S.op("dve", regs(bk2.reg(), stat.reg((SS_E + i, SS_E + i + 1)), gple.reg((G * 256, G * 256 + 256))),
                 etc_[s].reg(),
                 lambda e: e.scalar_tensor_tensor(
                     out=etc_[s][:], in0=bk2[:, 0:256], scalar=stat[:, SS_E + i:SS_E + i + 1],
                     in1=gple[:, G * 256:(G + 1) * 256], op0=ALU.mult, op1=ALU.mult))
            S.op("dve", regs(etc_[s].reg(), sgc[s].reg()), etc_[s].reg(),
                 lambda e: e.tensor_tensor(out=etc_[s][:], in0=etc_[s][:], in1=sgc[s][:], op=ALU.mult))
            S.op("dve", regs(etc_[s].reg(), z.reg(i, (G * 256, G * 256 + 256))), z.reg(i, (G * 256, G * 256 + 256)),
                 lambda e: e.tensor_tensor(out=z[:, i, G * 256:(G + 1) * 256],
                                           in0=z[:, i, G * 256:(G + 1) * 256], in1=etc_[s][:], op=ALU.add))
            S.dma("sp", "out", z.reg(i, (G * 256, G * 256 + 256)), [],
                  lambda e: e.dma_start(out=y_tok[i, :, G * 256:(G + 1) * 256],
                                        in_=z[:, i, G * 256:(G + 1) * 256]), final=True)

        ALLT = list(range(NT))
        for G in range(6):
            sl = cget("o", G, 0)
            for i in ALLT:
                c1_unit(G, i, sl)
        C_LIMIT[0] = 8
        sl6 = cget("o", 6, 0)
        sl7 = cget("o", 7, 0)
        slg0 = cget("g", 0, 0)
        done0 = []
        for st_ in range(NT + 3):
            if st_ < NT:
                c1_unit(6, st_, sl6)
                c1_unit(7, st_, sl7)
                x1_chain(st_)
            else:
                for i_ in (2 * (st_ - NT), 2 * (st_ - NT) + 1):
                    c2_unit(0, i_, slg0)
                    done0.append(i_)
            if 0 <= st_ - 3 < NT:
                x1_transposes(st_ - 3)
            if 0 <= st_ - 2 < NT:
                x1_cast(st_ - 2)
        C_LIMIT[0] = 10 ** 6
        for i in ALLT:
            if i not in done0:
                c2_unit(0, i, slg0)
        for G in range(1, 8):
            sl = cget("g", G, 0)
            for i in ALLT:
                c2_unit(G, i, sl)

    LB = dict(locals())
    for (name, buf, shape) in dbg:
        bufobj = LB[buf] if isinstance(buf, str) else buf
        if isinstance(bufobj, (list, tuple)):
            for bi_, b_ in enumerate(bufobj):
                dt_ = dram_out(f"dbg_{name}{bi_}", [128] + list(shape))
                S.dma("sp" if b_.esz == 4 else "pool", "dbgo", b_.reg(), [],
                      lambda e, dt_=dt_, b_=b_: e.dma_start(out=dt_, in_=b_[:]), final=True)
            continue
        dt_ = dram_out("dbg_" + name, [128] + list(shape))
        S.dma("pool", "dbgo", bufobj.reg(), [], lambda e, dt_=dt_, bufobj=bufobj: e.dma_start(out=dt_, in_=bufobj[:]),
              final=True)

    fin = dict(S.final_waits)
    S.prog["sp"].append(([(s, v) for s, v in fin.items()], None, None))

    semnames = sorted(S.semnames)
    sems = {}
    import contextlib
    with contextlib.ExitStack() as es:
        for sname in semnames:
            sems[sname] = es.enter_context(nc.semaphore(sname))
        block = es.enter_context(nc.Block())

        S.finalize()

        def run(e, engname):
            for waits, fn, inc in S.prog[engname]:
                for (s_, v) in waits:
                    e.wait_ge(sems[s_], S.wait_value(s_, v))
                if fn is None:
                    continue
                ins = fn(e)
                sem_, val_, is_dma = inc
                if is_dma:
                    if isinstance(ins, (list, tuple)):
                        for i_ in ins:
                            i_.then_inc(sems[sem_], 16)
                    else:
                        ins.then_inc(sems[sem_], 16)
                elif (sem_, val_) in S.rank:
                    ins.then_inc(sems[sem_], 1)

        @block.sync
        def _(e):
            run(e, "sp")

        @block.gpsimd
        def _(e):
            run(e, "pool")

        @block.scalar
        def _(e):
            run(e, "act")

        @block.vector
        def _(e):
            run(e, "dve")

        @block.tensor
        def _(e):
            run(e, "pe")
    return nc


def _prep_shared(inp):
    f = np.float32
    sh = {}
    w_in = np.asarray(inp["w_in"][0], f)
    sh["w_in_t"] = np.ascontiguousarray(w_in.reshape(16, 128, 72, 128).transpose(2, 1, 0, 3))
    w_pool = np.asarray(inp["w_pool"][0], f)
    sh["w_pool_t"] = np.ascontiguousarray(w_pool.reshape(4, 2, 128, 256).transpose(2, 0, 1, 3)).reshape(128, 2048)
    for nm, key in (("w_pp_t", "w_proj_pool"), ("w_pc_t", "w_proj_conv")):
        w = np.asarray(inp[key][0], f)
        sh[nm] = np.ascontiguousarray(w.reshape(8, 128, 16, 128).transpose(2, 1, 0, 3))
    for nm, key in (("w_out_t", "w_out"), ("w_pg_t", "w_ple_gate")):
        w = np.asarray(inp[key][0], f)
        sh[nm] = np.ascontiguousarray(w.reshape(16, 128, 8, 256).transpose(2, 1, 0, 3))
    w_ple = np.asarray(inp["w_ple"][0], f)
    sh["w_ple_t"] = np.ascontiguousarray(w_ple.reshape(2, 128, 2048).transpose(1, 0, 2)).reshape(128, 4096)
    sh["g_post_bc"] = np.ascontiguousarray(np.broadcast_to(np.asarray(inp["g_post"][0], f)[None, :], (128, D)))
    sh["g_ple_bc"] = np.ascontiguousarray(np.broadcast_to(np.asarray(inp["g_ple"][0], f)[None, :], (128, D)))
    sh["ident"] = np.eye(128, dtype=f)
    cst = np.zeros((128, 360), f)
    cst[:, 0:16] = np.asarray(inp["g_pre"][0], f).reshape(16, 128).T
    cst[:, 16:24] = np.asarray(inp["pool_scale"][0], f).reshape(8, 128).T
    cst[:, 24:32] = np.asarray(inp["b_dw"][0], f).reshape(8, 128).T
    cst[:, 32:40] = np.asarray(inp["ln_g"][0], f).reshape(8, 128).T
    cst[:, 40:48] = np.asarray(inp["ln_b"][0], f).reshape(8, 128).T
    wdw = np.asarray(inp["w_dw"][0], f)
    cst[:, 112:360] = wdw.reshape(31, 8, 128).transpose(2, 1, 0).reshape(128, 248)
    return sh, cst


def _in_maps(inp):
    f = np.float32
    sh, cst0 = _prep_shared(inp)
    xp = np.asarray(inp["x_prompt"], f)
    xsamp = np.asarray(inp["x_sample"], f)
    pp = np.asarray(inp["p_prompt"], f)[0]
    ps = np.asarray(inp["p_sample"], f)[0]
    stc = np.asarray(inp["state_conv"], f)[0]
    stp = np.asarray(inp["state_pool"], f)[0]
    maps = []
    for r in range(NCORES):
        b, half = r // 2, r % 2
        st = half * 1024
        m = dict(sh)
        xt = np.empty((NT, 128, D), f)
        xt[0:8] = xp[b, st:st + 1024].reshape(8, 128, D)
        xt[8] = xsamp[16 * r:16 * r + 16].reshape(128, D)
        m["x_tok"] = xt
        m["x_halo"] = np.ascontiguousarray(xp[b, st - 32:st]) if half else np.zeros((32, D), f)
        pt = np.empty((NT, 128, 256), f)
        pt[0:8] = pp[b, st:st + 1024].reshape(8, 128, 256)
        pt[8] = ps[16 * r:16 * r + 16].reshape(128, 256)
        m["p_tok"] = pt
        m["st_conv"] = np.ascontiguousarray(stc[16 * r:16 * r + 16])
        m["st_pool"] = np.ascontiguousarray(stp[16 * r:16 * r + 16])
        cst = cst0.copy()
        for g, wd in enumerate((2, 4, 8, 16)):
            pos = st + np.arange(16)
            cst[:, 48 + g * 16:64 + g * 16] = (1.0 / np.minimum(pos + 1, wd)).astype(f)[None, :]
        m["cst"] = cst
        maps.append(m)
    return maps


_NC_CACHE = {}


def kernel(**inputs):
    if "nc" not in _NC_CACHE:
        _NC_CACHE["nc"] = build_program()
    nc = _NC_CACHE["nc"]
    maps = _in_maps(inputs)
    res = run_bass_kernel_spmd(nc, maps, core_ids=list(range(NCORES)))
    R = res.results
    f = np.float32
    y_prompt = np.empty((4, 2048, D), f)
    y_sample = np.empty((128, 8, D), f)
    npp = np.empty((1, 4, 15, DP), f)
    ncp = np.empty((1, 4, 30, DP), f)
    nps = np.empty((1, 128, 15, DP), f)
    ncs = np.empty((1, 128, 30, DP), f)
    for r in range(NCORES):
        b, half = r // 2, r % 2
        st = half * 1024
        yt = R[r]["y_tok"]
        y_prompt[b, st:st + 1024] = yt[0:8].reshape(1024, D)
        y_sample[16 * r:16 * r + 16] = yt[8].reshape(16, 8, D)
        ncs[0, 16 * r:16 * r + 16, 0:22] = R[r]["ncs_old"]
        ncs[0, 16 * r:16 * r + 16, 22:30] = R[r]["ncs_new"].reshape(16, 8, DP)
        nps[0, 16 * r:16 * r + 16, 0:7] = R[r]["nps_old"]
        nps[0, 16 * r:16 * r + 16, 7:15] = R[r]["nps_new"].reshape(16, 8, DP)
        if half:
            ncp[0, b] = R[r]["ncp"][2:32]
            npp[0, b] = R[r]["npp"][17:32]
    return (y_prompt, y_sample, npp, ncp, nps, ncs)
```

```python
import numpy as np
import concourse.bass as bass
import concourse.mybir as mybir
from concourse.bass_utils import run_bass_kernel_spmd

F32 = mybir.dt.float32
BF16 = mybir.dt.bfloat16
AF = mybir.ActivationFunctionType
ALU = mybir.AluOpType

NCORES = 8
D = 2048
DP = 1024
NT = 9
W = 1184
T = 1152
EPS = 1e-6
SB_BASE = 16512
SB_END = 229376

CH_U, CH_ZA, CH_A, CH_B, CH_ZB, CH_GA, CH_GB = 0, 8, 16, 24, 32, 40, 56


class Sched:
    ENGS = ("pe", "act", "dve", "pool", "sp")
    GR = 1024

    def __init__(self):
        self.prog = {e: [] for e in self.ENGS}
        self.cnt = {}
        self.waited = {e: {} for e in self.ENGS}
        self.recs = {}
        self.buckets = {}
        self.semnames = set()
        self.final_waits = {}
        self.needed = set()

    def _granules(self, space, lo, hi):
        return [(space, g) for g in range(lo // self.GR, (hi - 1) // self.GR + 1)]

    def _add(self, key, val):
        old = self.recs.get(key)
        if old is None:
            for b in self._granules(key[0], key[1], key[2]):
                self.buckets.setdefault(b, set()).add(key)
        if old is None or old < val:
            self.recs[key] = val

    def _remove(self, key):
        del self.recs[key]
        for b in self._granules(key[0], key[1], key[2]):
            self.buckets[b].discard(key)

    def _overlaps(self, space, lo, hi):
        seen = set()
        for b in self._granules(space, lo, hi):
            for key in self.buckets.get(b, ()):
                if key in seen:
                    continue
                if key[1] < hi and lo < key[2]:
                    seen.add(key)
        return seen

    def _collect(self, eng_sem, is_pe, reads, writes):
        deps = {}
        cur = self.cnt.get(eng_sem, 0)

        def need(sem, val):
            if deps.get(sem, 0) < val:
                deps[sem] = val

        for (space, lo, hi) in reads:
            for key in self._overlaps(space, lo, hi):
                sem = key[3]
                if not key[4]:
                    if space == "ps" and sem != eng_sem:
                        need(sem, self.recs[key])
                    continue
                if sem == eng_sem:
                    if is_pe:
                        continue
                need(sem, self.recs[key])
        for (space, lo, hi) in writes:
            for key in self._overlaps(space, lo, hi):
                sem = key[3]
                if sem == eng_sem:
                    continue
                need(sem, self.recs[key])
        return deps

    def _record(self, sem, val, reads, writes):
        for (space, lo, hi) in writes:
            for key in list(self._overlaps(space, lo, hi)):
                if key[1] >= lo and key[2] <= hi:
                    self._remove(key)
            self._add((space, lo, hi, sem, True), val)
        for (space, lo, hi) in reads:
            self._add((space, lo, hi, sem, False), val)

    def _waits(self, eng, deps):
        out = []
        for sem, val in deps.items():
            if self.waited[eng].get(sem, 0) < val:
                self.waited[eng][sem] = val
                out.append((sem, val))
                self.needed.add((sem, val))
        return out

    def op(self, eng, reads, writes, fn):
        sem = "E_" + eng
        self.semnames.add(sem)
        deps = self._collect(sem, eng == "pe", reads, writes)
        waits = self._waits(eng, deps)
        val = self.cnt.get(sem, 0) + 1
        self.cnt[sem] = val
        self.prog[eng].append((waits, fn, (sem, val, False)))
        self._record(sem, val, reads, writes)

    def dma(self, queue, sem, reads, writes, fn, final=False, n=1):
        self.semnames.add(sem)
        deps = self._collect(sem, False, reads, writes)
        deps.pop(sem, None)
        waits = self._waits(queue, deps)
        val = self.cnt.get(sem, 0) + 16 * n
        self.cnt[sem] = val
        self.prog[queue].append((waits, fn, (sem, val, True)))
        self._record(sem, val, reads, writes)
        if final:
            self.final_waits[sem] = val

    def finalize(self):
        self.rank = {}
        by_sem = {}
        for (sem, val) in self.needed:
            if sem.startswith("E_"):
                by_sem.setdefault(sem, []).append(val)
        for sem, vals in by_sem.items():
            for r, v in enumerate(sorted(vals)):
                self.rank[(sem, v)] = r + 1

    def wait_value(self, sem, val):
        return self.rank[(sem, val)] if sem.startswith("E_") else val


class Buf:
    def __init__(self, t, space, addr, free_shape, esz):
        self.t = t
        self.space = space
        self.addr = addr
        self.shape = tuple(free_shape)
        self.esz = esz

    def __getitem__(self, k):
        return self.t[k]

    def reg(self, *idx):
        shp = self.shape
        idx = list(idx) + [None] * (len(shp) - len(idx))
        rngs = []
        for d, i in enumerate(idx):
            if i is None:
                rngs.append((0, shp[d]))
            elif isinstance(i, int):
                rngs.append((i, i + 1))
            else:
                rngs.append(i)
        strides = [1] * len(shp)
        for d in range(len(shp) - 2, -1, -1):
            strides[d] = strides[d + 1] * shp[d + 1]
        nd = len(shp)
        cut = nd
        while cut > 1 and rngs[cut - 1] == (0, shp[cut - 1]):
            cut -= 1
        out = []

        if self.space == "ps":
            return [("ps", self.addr, self.addr + 2048)]

        def rec(d, off):
            if d == cut - 1:
                lo = off + rngs[d][0] * strides[d]
                hi = off + rngs[d][1] * strides[d]
                out.append((self.space, self.addr + lo * self.esz, self.addr + hi * self.esz))
                return
            for i in range(rngs[d][0], rngs[d][1]):
                rec(d + 1, off + i * strides[d])

        rec(0, 0)
        return out


class Arena:
    def __init__(self, nc):
        self.nc = nc
        self.n = 0

    def at(self, name, addr, free_shape, dt, parts=128):
        esz = 4 if dt == F32 else 2
        size = int(np.prod(free_shape)) * esz
        assert addr % 32 == 0, (name, addr)
        assert SB_BASE <= addr and addr + size <= SB_END, (name, addr, size)
        self.n += 1
        t = self.nc.alloc_sbuf_tensor_at(f"{name}_{self.n}", [parts] + list(free_shape), dt, offset=addr)
        return Buf(t, "sb", addr, free_shape, esz)


def build_program(stop_after=None, dbg=None):
    nc = bass.Bass("TRN2", target_bir_lowering=False)
    S = Sched()
    A = Arena(nc)
    dbg = dbg or []

    def dram_in(name, shape):
        return nc.dram_tensor(name, list(shape), F32, kind="ExternalInput").ap()

    def dram_out(name, shape):
        return nc.dram_tensor(name, list(shape), F32, kind="ExternalOutput").ap()

    x_tok = dram_in("x_tok", [NT, 128, D])
    x_halo = dram_in("x_halo", [32, D])
    p_tok = dram_in("p_tok", [NT, 128, 256])
    st_conv = dram_in("st_conv", [16, 30, DP])
    st_pool = dram_in("st_pool", [16, 15, DP])
    cst_d = dram_in("cst", [128, 360])
    ident_d = dram_in("ident", [128, 128])
    w_in_d = dram_in("w_in_t", [72, 128, 16, 128])
    w_pool_d = dram_in("w_pool_t", [128, 4 * 2 * 256])
    w_pp_d = dram_in("w_pp_t", [16, 128, 8, 128])
    w_pc_d = dram_in("w_pc_t", [16, 128, 8, 128])
    w_out_d = dram_in("w_out_t", [8, 128, 16, 256])
    w_pg_d = dram_in("w_pg_t", [8, 128, 16, 256])
    w_ple_d = dram_in("w_ple_t", [128, 2 * D])
    gpost_d = dram_in("g_post_bc", [128, D])
    gple_d = dram_in("g_ple_bc", [128, D])

    y_tok = dram_out("y_tok", [NT, 128, D])
    ncs_new = dram_out("ncs_new", [128, DP])
    ncs_old = dram_out("ncs_old", [16, 22, DP])
    nps_new = dram_out("nps_new", [128, DP])
    nps_old = dram_out("nps_old", [16, 7, DP])
    ncp_o = dram_out("ncp", [32, DP])
    npp_o = dram_out("npp", [32, DP])
    dbg_out = {}

    a = SB_BASE
    C0 = a
    ident_bf = A.at("ident_bf", a, [128], BF16); a += 256
    ident_f = A.at("ident_f", a, [128], F32); a += 512
    ones_bf = A.at("ones_bf", a, [128], BF16); a += 256
    cst = A.at("cst", a, [360], F32); a += 1440
    wpool = A.at("wpool", a, [4, 2, 256], BF16); a += 4096
    stat = A.at("stat", a, [192], F32); a += 768
    pT = A.at("pT", a, [NT, 2, 128], BF16); a += NT * 2 * 128 * 2
    exs = A.at("exs", a, [2, 16, 23], F32); a += 2 * 16 * 23 * 4
    tmpf = A.at("tmpf", a, [16], F32); a += 64
    negh = A.at("negh", a, [8], F32); a += 32
    epsb = A.at("epsb", a, [8], F32); a += 32
    a = (a + 31) // 32 * 32
    R1 = a
    hT = A.at("hT", R1, [16, W], BF16); a += 16 * W * 2
    R2 = a
    vacc = [A.at(f"vacc{i}", R2 + i * W * 4, [W], F32) for i in range(8)]
    vall = A.at("vall", R2, [10, W], F32)
    a += 10 * W * 4
    R3 = a
    mT = A.at("mT", R3, [16, T], BF16); a += 16 * T * 2
    R4 = a
    ext_s = A.at("ext_s", R4, [8, 16, 38], BF16); a += 8 * 16 * 38 * 4
    v_bf = [A.at(f"v_bf{i}", R4 + 9728 + i * 2368, [W], BF16) for i in range(2)]
    vt = [A.at(f"vt{i}", R4 + 9728 + 4736 + i * 640, [160], F32) for i in range(2)]
    dgA = A.at("dgA", R2 + 8 * W * 4, [16, 128], BF16)
    dgB = A.at("dgB", R2 + 8 * W * 4 + 4096, [16, 128], BF16)
    wdwc = A.at("wdwc", R2 + 8 * W * 4 + 8192, [32], F32)
    dgA2 = A.at("dgA2", R3, [16, 128], BF16)
    dgB2 = A.at("dgB2", R3 + 4096, [16, 128], BF16)
    wdwc2 = A.at("wdwc2", R3 + 8192, [32], F32)
    R4b = a
    ya_in = A.at("ya_in", R4b, [8, T], BF16); a += 8 * T * 2
    R5 = a
    wsl = [A.at(f"wsl{i}", R5 + i * 4096, [2048], BF16) for i in range(4)]
    a += 4 * 4096
    R6 = a
    a += 18432
    assert a <= SB_END, a

    xs = [A.at(f"xs{i}", R3 + i * 8192, [D], F32) for i in range(3)]
    xs += [A.at(f"xs{3 + i}", R4b + i * 8192, [D], F32) for i in range(2)]
    pall = A.at("pall", R4b, [NT, 256], BF16)
    xb = [A.at(f"xb{i}", R3 + 24576 + i * 4096, [D], BF16) for i in range(2)]
    sqj = A.at("sqj", R3 + 32768, [D], BF16)
    silu_za = A.at("silu_za", R3, [4, T], BF16)
    pooled = A.at("pooled", R3 + 9216, [4, T], BF16)
    ubuf = A.at("ubuf", R3 + 18432, [W], F32)
    uscr = [A.at(f"uscr{i}", R3 + 18432 + (i + 1) * W * 4, [W], F32) for i in range(2)]
    assert 18432 + 3 * W * 4 <= 16 * T * 2
    ybf = [A.at(f"ybf{i}", R3 + i * 2304, [T], BF16) for i in range(2)]
    ysq = [A.at(f"ysq{i}", R3 + 4608 + i * 2304, [T], BF16) for i in range(2)]
    mean_sb = A.at("mean", R3 + 9216, [T], F32)
    rstd_sb = A.at("rstd", R3 + 9216 + 4608, [T], F32)
    sgl = [A.at(f"sgl{i}", R3 + 18432 + i * 4608, [T], F32) for i in range(2)]
    silu_zb = A.at("silu_zb", R4, [8, T], BF16)
    ext_u = A.at("ext_u", R6, [8, 16, 23], F32)
    sgt = [A.at(f"sgt{i}", R6 + 11776 + i * 2048, [512], F32) for i in range(3)]
    stg = A.at("stg", R6 + 11776, [DP], F32)
    sga = [A.at(f"sga{i}", R6 + i * 2304, [T], BF16) for i in range(2)]
    sgb = [A.at(f"sgb{i}", R6 + 4608 + i * 2304, [T], BF16) for i in range(2)]
    tA = A.at("tA", R6 + 9216, [T], F32)
    tA2 = [tA, A.at("tA1", R2, [T], F32)]
    tB = A.at("tB", R6 + 9216 + 4608, [T], F32)
    TAIL = a
    so_s = [A.at(f"so_s{i}", TAIL + i * 512, [128], F32) for i in range(2)]
    so_p = [A.at(f"so_p{i}", TAIL + 1024 + i * 512, [128], F32, parts=32) for i in range(2)]
    assert TAIL + 2048 <= SB_END
    z = A.at("z", R1, [NT, D], F32)
    CZ = R1 + NT * D * 4
    CZ = (CZ + 31) // 32 * 32
    wple = A.at("wple", CZ, [2, D], BF16)
    assert CZ + 8192 <= R3
    c = R4
    wslc = [None, A.at("wslc1", c, [16, 256], BF16), A.at("wslc2", c + 8192, [16, 256], BF16)]; c += 2 * 8192
    x1b = [A.at(f"x1b{i}", c + i * 4096, [D], BF16) for i in range(2)]; c += 8192
    assert c == R4b + 5120
    wslc[0] = A.at("wslc0", c, [16, 256], BF16); c += 8192
    gpost = A.at("gpost", c, [D], F32); c += 8192
    gple = A.at("gple", c, [D], F32); c += 8192
    sgc = [A.at(f"sgc{i}", c + i * 1024, [256], F32) for i in range(2)]; c += 2048
    etc_ = [A.at(f"etc{i}", c + i * 1024, [256], F32) for i in range(2)]; c += 2048
    ptile = A.at("ptile", c, [256], F32); c += 1024
    pbt = A.at("pbt", c, [256], BF16); c += 512
    assert c <= SB_END, c

    banks = []
    for i in range(8):
        t = nc.alloc_psum_tensor(f"bank{i}", [128, 512], F32)
        banks.append(Buf(t, "ps", i * 2048, [512], 4))
    acc_rot = [0]

    def next_bank():
        b = banks[acc_rot[0] % 6]
        acc_rot[0] += 1
        return b

    TPB = (banks[6], banks[7])

    def bank_bf(b):
        return b.t.bitcast(BF16)

    def regs(*lists):
        out = []
        for l in lists:
            out.extend(l)
        return out

    S.dma("sp", "cst", [], regs(cst.reg(), ident_f.reg()),
          lambda e: [e.dma_start(out=cst[:], in_=cst_d), e.dma_start(out=ident_f[:], in_=ident_d)], n=2)
    S.dma("pool", "cstb", [], ident_bf.reg(), lambda e: e.dma_start(out=ident_bf[:], in_=ident_d))
    S.op("dve", [], ones_bf.reg(), lambda e: e.memset(ones_bf[:], 1.0))
    S.op("dve", [], negh.reg(), lambda e: e.memset(negh[:], -0.5))
    S.op("dve", [], epsb.reg(), lambda e: e.memset(epsb[:], EPS))
    S.op("dve", [], stat.reg(), lambda e: e.memset(stat[:], 0.0))

    GPRE = lambda k0, k1: cst[:, k0:k1]
    PSC = lambda d: cst[:, 16 + d:17 + d]
    BDW = lambda c_: cst[:, 24 + c_:25 + c_]
    LNG = lambda c_: cst[:, 32 + c_:33 + c_]
    LNB = lambda c_: cst[:, 40 + c_:41 + c_]
    INVC = lambda g: cst[:, 48 + g * 16:64 + g * 16]
    WDW = lambda c_, k: cst[:, 112 + c_ * 31 + k:113 + c_ * 31 + k]
    cst_r = cst.reg()

    wlist = []

    def w_in_src(ch):
        return (w_in_d[ch], lambda sl: sl.t[:].rearrange("p (k c) -> p k c", k=16), [16, 128])

    def w_pr_src(dram, G):
        return (dram[G], lambda sl: sl.t[:].rearrange("p (k c) -> p k c", k=8), [8, 256])

    order = []
    for c_ in range(8):
        order.append(("in", CH_B + c_)); order.append(("in", CH_A + c_))
    for g in range(4):
        order += [("in", CH_U + 2 * g), ("in", CH_ZA + 2 * g), ("in", CH_U + 2 * g + 1), ("in", CH_ZA + 2 * g + 1)]
    for c_ in range(8):
        order.append(("in", CH_ZB + c_))
    for jp in range(0, 16, 2):
        for j in (jp, jp + 1):
            order.append(("in", CH_GA + j))
            order.append(("pp", j))
            order.append(("in", CH_GB + j))
        order.append(("pc", jp))
        order.append(("pc", jp + 1))
    wpos = {}
    for n, it in enumerate(order):
        wpos[it] = n
    w_issued = [0]

    W_LIMIT = [1]

    def w_issue_upto(n):
        while w_issued[0] <= min(n, len(order) - 1, W_LIMIT[0]):
            m = w_issued[0]
            kind, idx = order[m]
            sl = wsl[m % 4]
            if kind == "in":
                src, view = w_in_d[idx], sl.t[:].rearrange("p (k c) -> p k c", k=16)
            elif kind == "pp":
                src, view = w_pp_d[idx], sl.t[:, 0:1024].rearrange("p (k c) -> p k c", k=8)
            else:
                src, view = w_pc_d[idx], sl.t[:, 0:1024].rearrange("p (k c) -> p k c", k=8)
            S.dma("pool", f"w{m % 4}", [], sl.reg(),
                  lambda e, view=view, src=src: e.dma_start(out=view, in_=src))
            w_issued[0] += 1

    def wget(kind, idx):
        n = wpos[(kind, idx)]
        w_issue_upto(n + 3)
        sl = wsl[n % 4]
        if kind == "in":
            return sl, sl.t[:].rearrange("p (k c) -> p k c", k=16)
        return sl, sl.t[:, 0:1024].rearrange("p (k c) -> p k c", k=8)

    conv_q = []

    def conv_run(n):
        for _ in range(n):
            if not conv_q:
                return
            conv_q.pop(0)()

    def after_unit():
        conv_run(CONV_RATE[0])

    CONV_RATE = [0]

    xs_n = [0]

    def stage0(src_ap, nrows, col0, statcol):
        s = xs_n[0] % 5
        sb = xs_n[0] % 2
        xs_n[0] += 1
        xsl, xbl = xs[s], xb[sb]
        S.dma("sp", f"xs{s}", [], xsl.reg(), lambda e: e.dma_start(out=xsl[0:nrows, :], in_=src_ap))
        S.op("act", xsl.reg(), regs(sqj.reg(), stat.reg((statcol, statcol + 1))),
             lambda e: e.activation(out=sqj[0:nrows, :], in_=xsl[0:nrows, :], func=AF.Square,
                                    accum_out=stat[0:nrows, statcol:statcol + 1]))
        sc_ = stat[0:nrows, statcol:statcol + 1]
        sr_ = stat.reg((statcol, statcol + 1))
        S.op("act", regs(sr_, epsb.reg()), sr_,
             lambda e: e.activation(out=sc_, in_=sc_, func=AF.Sqrt, scale=1.0 / D, bias=epsb[0:nrows, 0:1]))
        S.op("dve", sr_, sr_, lambda e: e.reciprocal(out=sc_, in_=sc_))
        S.op("dve", regs(xsl.reg(), sr_), xbl.reg(),
             lambda e: e.tensor_scalar(out=xbl[0:nrows, :], in0=xsl[0:nrows, :], scalar1=sc_, scalar2=None, op0=ALU.mult))
        evs = []
        for h in range(2):
            tb = next_bank()
            tbv = bank_bf(tb)[:, 0:8 * nrows].rearrange("p (k t) -> p k t", k=8)

            def pe_fn(e, h=h, tbv=tbv):
                ins = None
                for kk in range(8):
                    k = h * 8 + kk
                    ins = e.transpose(tbv[:, kk, :], xbl[0:nrows, k * 128:(k + 1) * 128], ident_bf[0:nrows, 0:nrows])
                return ins
            S.op("pe", regs(xbl.reg(), ident_bf.reg()), tb.reg(), pe_fn)

            def ev(h=h, tb=tb, tbv=tbv):
                S.op("dve", regs(tb.reg(), cst_r), hT.reg((h * 8, h * 8 + 8), (col0, col0 + nrows)),
                     lambda e: e.tensor_tensor(
                         out=hT[:, h * 8:h * 8 + 8, col0:col0 + nrows], in0=tbv,
                         in1=GPRE(h * 8, h * 8 + 8).unsqueeze(2).broadcast_to([128, 8, nrows]), op=ALU.mult))
            evs.append(ev)
        return evs

    def state_T(src_dram, nb, nr, ext, j):
        xsl = stg
        rows = nb * nr
        src = src_dram[j * nb:(j + 1) * nb].rearrange("b r c -> (b r) c")
        S.dma("sp", "stg", [], xsl.reg((0, DP)), lambda e: e.dma_start(out=xsl[0:rows, 0:DP], in_=src))
        for h in range(2):
            tb = TPB[h]

            def pe_fn(e, h=h, tb=tb):
                ins = None
                for cc in range(4):
                    c_ = h * 4 + cc
                    ins = e.transpose(tb[:, cc * rows:(cc + 1) * rows], xsl[0:rows, c_ * 128:(c_ + 1) * 128],
                                      ident_f[0:rows, 0:rows])
                return ins
            S.op("pe", regs(xsl.reg((0, DP)), ident_f.reg()), tb.reg(), pe_fn)
            S.op("act", tb.reg(), ext.reg((h * 4, h * 4 + 4)),
                 lambda e, h=h, tb=tb: e.activation(
                     out=ext[:, h * 4:h * 4 + 4, j * nb:(j + 1) * nb, 0:nr],
                     in_=tb[:, 0:4 * rows].rearrange("p (c b r) -> p c b r", c=4, b=nb), func=AF.Copy))

    def p_load():
        S.dma("pool", "pall", [], pall.reg(), lambda e: e.dma_start(out=pall[:], in_=p_tok.rearrange("i p c -> p i c")))

    def p_T(i):
        tb = TPB[i % 2]
        tbv = bank_bf(tb)[:, 0:256].rearrange("p (k t) -> p k t", k=2)

        def pe_fn(e):
            ins = None
            for kk in range(2):
                ins = e.transpose(tbv[:, kk, :], pall[:, i, kk * 128:(kk + 1) * 128], ident_bf[:])
            return ins
        S.op("pe", regs(pall.reg(i), ident_bf.reg()), tb.reg(), pe_fn)
        S.op("dve", tb.reg(), pT.reg(i), lambda e: e.tensor_copy(out=pT[:, i, :, :], in_=tbv))

    BLK_H = [(0, 512), (512, 1024), (1024, W)]
    BLK_N = [(32, 544), (544, 1056), (1056, W)]

    def stage1(ch, halo, evac, blocks=(0, 1, 2)):
        sl, wv = wget("in", ch)
        for (c0, c1) in [(BLK_H if halo else BLK_N)[b_] for b_ in blocks]:
            n = c1 - c0
            bk = next_bank()

            def pe_fn(e, bk=bk, c0=c0, c1=c1, n=n):
                ins = None
                for k in range(16):
                    ins = e.matmul(bk[:, 0:n], wv[:, k, :], hT[:, k, c0:c1], start=(k == 0), stop=(k == 15))
                return ins
            S.op("pe", regs(sl.reg(), hT.reg(None, (c0, c1))), bk.reg((0, n)), pe_fn)
            evac(bk, c0, c1, n)
            after_unit()

    acc_of = {c_: vacc[c_] for c_ in range(8)}
    so_n = [0]
    CB = [(0, 512, 2), (512, 1024, 514), (1024, T, None)]

    def dg_set(c_):
        return (dgA2, dgB2, wdwc2) if c_ == 7 else (dgA, dgB, wdwc)

    def conv_prep(c_, slot):
        dA_, dB_, wc_ = dg_set(c_)
        S.op("dve", cst_r, wc_.reg(), lambda e: e.tensor_copy(out=wc_[:, 0:31], in_=cst[:, 112 + c_ * 31:143 + c_ * 31]))
        for (dg, k0, nk) in ((dA_, 0, 16), (dB_, 16, 15)):
            S.op("dve", regs(ident_bf.reg(), wc_.reg()), dg.reg((0, nk)),
                 lambda e, dg=dg, k0=k0, nk=nk: e.tensor_tensor(
                     out=dg[:, 0:nk, :], in0=ident_bf[:, :].unsqueeze(1).broadcast_to([128, nk, 128]),
                     in1=wc_[:, k0:k0 + nk].unsqueeze(2).broadcast_to([128, nk, 128]), op=ALU.mult))

    def conv_pe(c_, slot):
        vb, vtl, acc = v_bf[slot], vt[slot], vacc[c_]
        dgA, dgB, _ = dg_set(c_)
        S.op("act", vb.reg((1056, W)), ext_s.reg(c_),
             lambda e: e.activation(out=ext_s[:, c_, :, 30:38],
                                    in_=vb[:, 1056:W].rearrange("p (b t) -> p b t", b=16), func=AF.Copy))
        tb = TPB[c_ % 2]

        def pe_fn(e):
            e.transpose(tb[:, 0:128], vtl[:, 32:160], ident_f[:])
            return e.transpose(tb[0:32, 128:256], vtl[:, 0:32], ident_f[:])
        S.op("pe", regs(vtl.reg(), ident_f.reg()), tb.reg(), pe_fn)
        sl_ = so_n[0] % 2
        so_n[0] += 1
        ss_, sp_ = so_s[sl_], so_p[sl_]
        S.op("act", tb.reg(), ss_.reg(), lambda e: e.activation(out=ss_[:], in_=tb[:, 0:128], func=AF.Copy))
        S.op("act", tb.reg(), sp_.reg(), lambda e: e.activation(out=sp_[0:32, :], in_=tb[0:32, 128:256], func=AF.Copy))
        S.dma("sp", f"os{sl_}", ss_.reg(), [], lambda e: e.dma_start(out=ncs_new[:, c_ * 128:(c_ + 1) * 128], in_=ss_[:]),
              final=True)
        S.dma("sp", f"op{sl_}", sp_.reg(), [], lambda e: e.dma_start(out=ncp_o[:, c_ * 128:(c_ + 1) * 128],
                                                              in_=sp_[0:32, :]), final=True)
        KP = 19 if c_ < 7 else 31
        for k in range(KP, 31):
            i_ap = vb[:, 2 + k:1026 + k]
            i_rg = vb.reg((2 + k, 1026 + k))
            o_ap = acc[:, 0:1024]
            o_rg = acc.reg((0, 1024))
            if k == KP:
                S.op("dve", regs(i_rg, cst_r), o_rg,
                     lambda e, i_ap=i_ap, o_ap=o_ap, k=k: e.tensor_scalar(
                         out=o_ap, in0=i_ap, scalar1=WDW(c_, k), scalar2=BDW(c_), op0=ALU.mult, op1=ALU.add))
            else:
                S.op("dve", regs(i_rg, cst_r, o_rg), o_rg,
                     lambda e, i_ap=i_ap, o_ap=o_ap, k=k: e.scalar_tensor_tensor(
                         out=o_ap, in0=i_ap, scalar=WDW(c_, k), in1=o_ap, op0=ALU.mult, op1=ALU.add))
        for (t0, t1, voff) in CB:
            n = t1 - t0
            bk = next_bank()
            nk = KP if voff is not None else 31

            def pe_fn(e, bk=bk, t0=t0, n=n, voff=voff, nk=nk):
                ins = None
                for k in range(nk):
                    dg = dgA if k < 16 else dgB
                    if voff is not None:
                        ins = e.matmul(bk[:, 0:n], dg[:, k % 16, :], vb[:, voff + k:voff + k + n],
                                       start=(k == 0), stop=(k == nk - 1))
                    else:
                        ins = e.matmul(bk[:, 0:128].rearrange("p (b t) -> p b t", b=16), dg[:, k % 16, :],
                                       ext_s[:, c_, :, k:k + 8], start=(k == 0), stop=(k == nk - 1))
                return ins
            rd = regs(dgA.reg(), dgB.reg(), vb.reg() if voff is not None else ext_s.reg(c_))
            S.op("pe", rd, bk.reg(), pe_fn)
            if voff is not None and KP < 31:
                S.op("dve", regs(bk.reg(), acc.reg((t0, t1))), acc.reg((t0, t1)),
                     lambda e, bk=bk, t0=t0, t1=t1, n=n: e.tensor_tensor(out=acc[:, t0:t1], in0=bk[:, 0:n],
                                                                         in1=acc[:, t0:t1], op=ALU.add))
            else:
                S.op("act", regs(bk.reg(), cst_r), acc.reg((t0, t1)),
                     lambda e, bk=bk, t0=t0, t1=t1, n=n: e.activation(out=acc[:, t0:t1], in_=bk[:, 0:n],
                                                                      func=AF.Identity, bias=BDW(c_)))

    def a2_evs(c_):
        sg_ = vacc[c_]
        vb, vtl = v_bf[c_ % 2], vt[c_ % 2]

        def ev_b(bk, c0, c1, n):
            S.op("act", bk.reg(), sg_.reg((c0, c1)),
                 lambda e: e.activation(out=sg_[:, c0:c1], in_=bk[:, 0:n], func=AF.Sigmoid))

        def ev_a(bk, c0, c1, n):
            S.op("dve", regs(bk.reg(), sg_.reg((c0, c1))), vb.reg((c0, c1)),
                 lambda e: e.tensor_tensor(out=vb[:, c0:c1], in0=bk[:, 0:n], in1=sg_[:, c0:c1], op=ALU.mult))
            if c0 == 1024:
                S.op("dve", regs(bk.reg(), sg_.reg((c0, c1))), vtl.reg(),
                     lambda e: e.tensor_tensor(out=vtl[:, :], in0=bk[:, 0:n], in1=sg_[:, c0:c1], op=ALU.mult))
        return ev_b, ev_a

    early = stop_after != "A1"
    evb0, eva0 = a2_evs(0)

    def early_blocks(bl):
        if early:
            stage1(CH_B + 0, True, evb0, blocks=bl)
            stage1(CH_A + 0, True, eva0, blocks=bl)

    prev_evs = stage0(x_halo, 32, 0, 9)
    for i in range(NT):
        evs_ = stage0(x_tok[i], 128, 32 + 128 * i, i)
        for ev_ in prev_evs:
            ev_()
        prev_evs = evs_
        if early:
            if i == 4:
                stage1(CH_B + 0, True, evb0, blocks=[0])
            if i == 6:
                stage1(CH_A + 0, True, eva0, blocks=[0])
    for ev_ in prev_evs:
        ev_()
    if early:
        stage1(CH_B + 0, True, evb0, blocks=[1])
        stage1(CH_A + 0, True, eva0, blocks=[1])
    early_blocks([2])
    W_LIMIT[0] = 10 ** 6
    S.dma("pool", "wpl", [], wpool.reg(),
          lambda e: e.dma_start(out=wpool[:], in_=w_pool_d.rearrange("p (g k c) -> p g k c", g=4, k=2)))
    p_load()
    for j_ in range(4):
        conv_q.append(lambda j_=j_: state_T(st_conv, 4, 30, ext_s, j_))
    for j_ in range(2):
        conv_q.append(lambda j_=j_: state_T(st_pool, 8, 15, ext_u, j_))
    conv_q.append(lambda: S.dma("sp", "out", [], [], lambda e: e.dma_start(out=ncs_old, in_=st_conv[:, 8:30, :]), final=True))
    conv_q.append(lambda: S.dma("sp", "out", [], [], lambda e: e.dma_start(out=nps_old, in_=st_pool[:, 8:15, :]), final=True))
    CONV_RATE[0] = 1

    pending = []
    for c_ in (range(8) if stop_after != "A1" else []):
        slot = c_ % 2
        if c_ > 0:
            ev_b, ev_a = a2_evs(c_)
            stage1(CH_B + c_, True, ev_b)
            if pending:
                conv_prep(*pending[0])
            stage1(CH_A + c_, True, ev_a)
        if pending:
            conv_run(10 ** 6)
            if c_ == 7:
                conv_prep(7, slot)
            conv_pe(*pending.pop(0))
        pending.append((c_, slot))
    if stop_after != "A1":
        conv_pe(*pending.pop(0))

    if stop_after not in ("A1", "A2"):
        for i in range(NT):
            conv_q.append(lambda i=i: p_T(i))
        CONV_RATE[0] = 1
        def za_part(g, cc):
            wwin = 2 ** (g + 1)
            ch = 2 * g + cc
            slot = (g % 2) * 2 + cc

            def ev_za(bk, c0, c1, n, slot=slot):
                st = sgt[acc_rot[0] % 3]
                S.op("act", bk.reg((0, n)), st.reg((0, n)),
                     lambda e: e.activation(out=st[:, 0:n], in_=bk[:, 0:n], func=AF.Sigmoid))
                S.op("dve", regs(bk.reg((0, n)), st.reg((0, n))), silu_za.reg(slot, (c0 - 32, c1 - 32)),
                     lambda e: e.tensor_tensor(out=silu_za[:, slot, c0 - 32:c1 - 32], in0=bk[:, 0:n],
                                               in1=st[:, 0:n], op=ALU.mult))
            stage1(CH_ZA + ch, False, ev_za)

        def u_part(g, cc):
            wwin = 2 ** (g + 1)
            ch = 2 * g + cc
            slot = (g % 2) * 2 + cc

            def ev_u(bk, c0, c1, n):
                S.op("act", bk.reg((0, n)), ubuf.reg((c0, c1)),
                     lambda e: e.activation(out=ubuf[:, c0:c1], in_=bk[:, 0:n], func=AF.Copy))
            stage1(CH_U + ch, True, ev_u)
            S.op("act", ubuf.reg((1056, W)), ext_u.reg(ch),
                 lambda e, ch=ch: e.activation(out=ext_u[:, ch, :, 15:23],
                                               in_=ubuf[:, 1056:W].rearrange("p (b t) -> p b t", b=16),
                                               func=AF.Copy))
            tb = TPB[ch % 2]

            def pe_fn(e, tb=tb):
                e.transpose(tb[:, 0:128], ubuf[:, 1056:W], ident_f[:])
                return e.transpose(tb[0:32, 128:256], ubuf[:, 1024:1056], ident_f[:])
            S.op("pe", regs(ubuf.reg((1024, W)), ident_f.reg()), tb.reg((0, 256)), pe_fn)
            sl_ = so_n[0] % 2
            so_n[0] += 1
            ss_, sp_ = so_s[sl_], so_p[sl_]
            S.op("act", tb.reg((0, 128)), ss_.reg(),
                 lambda e, tb=tb, ss_=ss_: e.activation(out=ss_[:], in_=tb[:, 0:128], func=AF.Copy))
            S.op("act", tb.reg((128, 256)), sp_.reg(),
                 lambda e, tb=tb, sp_=sp_: e.activation(out=sp_[0:32, :], in_=tb[0:32, 128:256], func=AF.Copy))
            S.dma("sp", f"os{sl_}", ss_.reg(), [],
                  lambda e, ch=ch, ss_=ss_: e.dma_start(out=nps_new[:, ch * 128:(ch + 1) * 128], in_=ss_[:]), final=True)
            S.dma("sp", f"op{sl_}", sp_.reg(), [],
                  lambda e, ch=ch, sp_=sp_: e.dma_start(out=npp_o[:, ch * 128:(ch + 1) * 128], in_=sp_[0:32, :]),
                  final=True)
            src = ubuf
            for l in range(1, g + 2):
                sh = 2 ** (l - 1)
                lo = 2 ** l
                dst = uscr[(l - 1) % 2]
                S.op("dve", src.reg((lo - sh, 1056)), dst.reg((lo, 1056)),
                     lambda e, src=src, dst=dst, lo=lo, sh=sh: e.tensor_tensor(
                         out=dst[:, lo:1056], in0=src[:, lo:1056], in1=src[:, lo - sh:1056 - sh], op=ALU.add))
                src = dst
            win = src
            ssrc_ap = ext_u[:, ch, :, :]
            ssrc_reg = ext_u.reg(ch)
            for l in range(1, g + 2):
                sh = 2 ** (l - 1)
                lo = 2 ** l - 1
                d = exs[:, (l - 1) % 2, :, :]
                dreg = exs.reg((l - 1) % 2)
                S.op("dve", ssrc_reg, dreg,
                     lambda e, s_=ssrc_ap, d=d, lo=lo, sh=sh: e.tensor_tensor(
                         out=d[:, :, lo:23], in0=s_[:, :, lo:23], in1=s_[:, :, lo - sh:23 - sh], op=ALU.add))
                ssrc_ap, ssrc_reg = d, dreg
            S.op("dve", regs(win.reg((32, 1056)), ubuf.reg((32, 1056))), pooled.reg(slot, (0, 1024)),
                 lambda e, win=win, slot=slot, wwin=wwin: e.scalar_tensor_tensor(
                     out=pooled[:, slot, 0:1024], in0=win[:, 32:1056], scalar=1.0 / wwin, in1=ubuf[:, 32:1056],
                     op0=ALU.mult, op1=ALU.subtract))
            S.op("dve", regs(ssrc_reg, ext_u.reg(ch)), pooled.reg(slot, (1024, T)),
                 lambda e, s_=ssrc_ap, slot=slot, wwin=wwin, ch=ch: e.scalar_tensor_tensor(
                     out=pooled[:, slot, 1024:T].rearrange("p (b t) -> p b t", b=16), in0=s_[:, :, 15:23],
                     scalar=1.0 / wwin, in1=ext_u[:, ch, :, 15:23], op0=ALU.mult, op1=ALU.subtract))
            S.op("dve", regs(win.reg((32, 48)), cst_r), tmpf.reg(),
                 lambda e, win=win, g=g: e.tensor_tensor(out=tmpf[:], in0=win[:, 32:48], in1=INVC(g), op=ALU.mult))
            S.op("dve", regs(tmpf.reg(), ubuf.reg((32, 48))), pooled.reg(slot, (0, 16)),
                 lambda e, slot=slot: e.tensor_tensor(out=pooled[:, slot, 0:16], in0=tmpf[:], in1=ubuf[:, 32:48],
                                                      op=ALU.subtract))

        def wpool_part(g):
            conv_run(10 ** 6)
            for dd in range(2):
                d = 2 * g + dd
                for (c0, c1) in [(0, 512), (512, 1024), (1024, T)]:
                    n = c1 - c0
                    bk = next_bank()

                    def pe_fn(e, bk=bk, c0=c0, c1=c1, n=n, dd=dd, g=g):
                        ins = None
                        for kc in range(2):
                            ins = e.matmul(bk[:, 0:n], wpool[:, g, kc, dd * 128:(dd + 1) * 128],
                                           pooled[:, (g % 2) * 2 + kc, c0:c1], start=(kc == 0), stop=(kc == 1))
                        return ins
                    S.op("pe", regs(wpool.reg(g), pooled.reg(((g % 2) * 2, (g % 2) * 2 + 2), (c0, c1))),
                         bk.reg((0, n)), pe_fn)
                    S.op("dve", regs(bk.reg((0, n)), silu_za.reg((g % 2) * 2 + dd, (c0, c1)), cst_r),
                         ya_in.reg(d, (c0, c1)),
                         lambda e, bk=bk, n=n, d=d, dd=dd, g=g, c0=c0, c1=c1: e.scalar_tensor_tensor(
                             out=ya_in[:, d, c0:c1], in0=bk[:, 0:n], scalar=PSC(d),
                             in1=silu_za[:, (g % 2) * 2 + dd, c0:c1], op0=ALU.mult, op1=ALU.mult))
                    after_unit()


        for g in range(4):
            u_part(g, 0)
            za_part(g, 0)
            u_part(g, 1)
            za_part(g, 1)
            if g > 0:
                wpool_part(g - 1)
        wpool_part(3)

    if stop_after not in ("A1", "A2", "A3"):
        TB = [(0, 512), (512, 1024), (1024, T)]
        for c_ in range(8):
            acc = acc_of[c_]
            s = c_ % 2
            S.op("dve", acc.reg((0, T)), ybf[s].reg(), lambda e, acc=acc, s=s: e.tensor_copy(
                out=ybf[s][:], in_=acc[:, 0:T]))
            S.op("act", acc.reg((0, T)), ysq[s].reg(), lambda e, acc=acc, s=s: e.activation(
                out=ysq[s][:], in_=acc[:, 0:T], func=AF.Square))

            def pe_fn(e, s=s, c_=c_):
                ins = None
                for bi, (c0, c1) in enumerate(TB):
                    n = c1 - c0
                    e.matmul(banks[bi][:, 0:n], ones_bf[:], ybf[s][:, c0:c1], start=(c_ == 0), stop=(c_ == 7))
                    ins = e.matmul(banks[3 + bi][:, 0:n], ones_bf[:], ysq[s][:, c0:c1], start=(c_ == 0), stop=(c_ == 7))
                return ins
            S.op("pe", regs(ybf[s].reg(), ysq[s].reg(), ones_bf.reg()),
                 regs(*[banks[b].reg() for b in range(6)]), pe_fn)
        for bi, (c0, c1) in enumerate(TB):
            n = c1 - c0
            S.op("act", banks[bi].reg((0, n)), mean_sb.reg((c0, c1)),
                 lambda e, bi=bi, c0=c0, c1=c1, n=n: e.activation(out=mean_sb[:, c0:c1], in_=banks[bi][:, 0:n],
                                                                  func=AF.Copy, scale=1.0 / DP))
        for bi, (c0, c1) in enumerate(TB):
            n = c1 - c0
            S.op("dve", mean_sb.reg((c0, c1)), rstd_sb.reg((c0, c1)),
                 lambda e, c0=c0, c1=c1: e.tensor_tensor(out=rstd_sb[:, c0:c1], in0=mean_sb[:, c0:c1],
                                                         in1=mean_sb[:, c0:c1], op=ALU.mult))
        for bi, (c0, c1) in enumerate(TB):
            n = c1 - c0
            S.op("dve", regs(banks[3 + bi].reg((0, n)), rstd_sb.reg((c0, c1))), rstd_sb.reg((c0, c1)),
                 lambda e, bi=bi, c0=c0, c1=c1, n=n: e.scalar_tensor_tensor(
                     out=rstd_sb[:, c0:c1], in0=banks[3 + bi][:, 0:n], scalar=1.0 / DP, in1=rstd_sb[:, c0:c1],
                     op0=ALU.mult, op1=ALU.subtract))
        S.op("act", regs(rstd_sb.reg(), epsb.reg()), rstd_sb.reg(),
             lambda e: e.activation(out=rstd_sb[:], in_=rstd_sb[:], func=AF.Sqrt, bias=epsb[:, 0:1]))
        S.op("dve", rstd_sb.reg(), rstd_sb.reg(), lambda e: e.reciprocal(out=rstd_sb[:], in_=rstd_sb[:]))
        acc_rot[0] = (acc_rot[0] + 5) // 6 * 6
        def zb_part(c_):
            def ev_zb(bk, c0, c1, n, c_=c_):
                st = sgt[acc_rot[0] % 3]
                S.op("act", bk.reg((0, n)), st.reg((0, n)),
                     lambda e: e.activation(out=st[:, 0:n], in_=bk[:, 0:n], func=AF.Sigmoid))
                S.op("dve", regs(bk.reg((0, n)), st.reg((0, n))), silu_zb.reg(c_, (c0 - 32, c1 - 32)),
                     lambda e: e.tensor_tensor(out=silu_zb[:, c_, c0 - 32:c1 - 32], in0=bk[:, 0:n],
                                               in1=st[:, 0:n], op=ALU.mult))
            stage1(CH_ZB + c_, False, ev_zb)


        zb_part(0)
        for c_ in range(8):
            if c_ + 1 < 8:
                zb_part(c_ + 1)
            acc = acc_of[c_]
            s = c_ % 2
            AT = acc[:, 0:T]
            S.op("dve", regs(acc.reg((0, T)), mean_sb.reg()), acc.reg((0, T)),
                 lambda e, AT=AT: e.tensor_tensor(out=AT, in0=AT, in1=mean_sb[:], op=ALU.subtract))
            S.op("dve", regs(acc.reg((0, T)), rstd_sb.reg()), acc.reg((0, T)),
                 lambda e, AT=AT: e.tensor_tensor(out=AT, in0=AT, in1=rstd_sb[:], op=ALU.mult))
            S.op("act", regs(acc.reg((0, T)), cst_r), sgl[s].reg(),
                 lambda e, AT=AT, s=s, c_=c_: e.activation(out=sgl[s][:], in_=AT, func=AF.Sigmoid,
                                                           scale=LNG(c_), bias=LNB(c_)))
            S.op("act", regs(acc.reg((0, T)), cst_r), acc.reg((0, T)),
                 lambda e, AT=AT, c_=c_: e.activation(out=AT, in_=AT, func=AF.Identity, scale=LNG(c_), bias=LNB(c_)))
            S.op("dve", regs(acc.reg((0, T)), sgl[s].reg()), acc.reg((0, T)),
                 lambda e, AT=AT, s=s: e.tensor_tensor(out=AT, in0=AT, in1=sgl[s][:], op=ALU.mult))
            S.op("dve", regs(acc.reg((0, T)), silu_zb.reg(c_)), silu_zb.reg(c_),
                 lambda e, AT=AT, c_=c_: e.tensor_tensor(out=silu_zb[:, c_, :], in0=AT, in1=silu_zb[:, c_, :],
                                                         op=ALU.mult))
    yb_in = silu_zb

    TB = [(0, 512), (512, 1024), (1024, T)]
    SS_E = 16

    def estat_unit(i, q):
        bk = next_bank()

        def pe_fn(e):
            ins = None
            for kk in range(2):
                ins = e.matmul(bk[:, :], pT[:, i, kk, :], wple[:, kk, q * 512:(q + 1) * 512],
                               start=(kk == 0), stop=(kk == 1))
            return ins
        S.op("pe", regs(pT.reg(i), wple.reg(None, (q * 512, q * 512 + 512))), bk.reg(), pe_fn)
        col = 128 + i * 4 + q
        S.op("act", bk.reg(), regs(bk.reg(), stat.reg((col, col + 1))),
             lambda e: e.activation(out=bk[:, :], in_=bk[:, :], func=AF.Square, accum_out=stat[:, col:col + 1]))

    if stop_after not in ("A1", "A2", "A3", "LN"):
        for j in range(16):
            if j == 2 and stop_after is None:
                S.dma("pool", "ccb", [], wple.reg(),
                      lambda e: e.dma_start(out=wple[:], in_=w_ple_d.rearrange("p (k c) -> p k c", k=2)))
                for i_ in range(NT):
                    for q_ in range(4):
                        conv_q.append(lambda i_=i_, q_=q_: estat_unit(i_, q_))
                CONV_RATE[0] = 1
            s = j % 2

            def ev_gate(dstb):
                def ev(bk, c0, c1, n):
                    S.op("act", bk.reg((0, n)), dstb.reg((c0 - 32, c1 - 32)),
                         lambda e: e.activation(out=dstb[:, c0 - 32:c1 - 32], in_=bk[:, 0:n], func=AF.Sigmoid))
                return ev

            def proj(kind, src_buf, gate_buf, final, j=j):
                tA = tA2[j % 2]
                sl, wv = wget(kind, j)
                for (c0, c1) in TB:
                    n = c1 - c0
                    bk = next_bank()

                    def pe_fn(e, bk=bk, c0=c0, c1=c1, n=n):
                        ins = None
                        for k in range(8):
                            ins = e.matmul(bk[:, 0:n], wv[:, k, :],
                                           src_buf[:, k, c0:c1], start=(k == 0), stop=(k == 7))
                        return ins
                    S.op("pe", regs(sl.reg(), src_buf.reg(None, (c0, c1))), bk.reg((0, n)), pe_fn)
                    if not final:
                        S.op("dve", regs(bk.reg((0, n)), gate_buf.reg((c0, c1))), tA.reg((c0, c1)),
                             lambda e, bk=bk, c0=c0, c1=c1, n=n: e.tensor_tensor(
                                 out=tA[:, c0:c1], in0=bk[:, 0:n], in1=gate_buf[:, c0:c1], op=ALU.mult))
                    else:
                        S.op("dve", regs(bk.reg((0, n)), gate_buf.reg((c0, c1))), tB.reg((c0, c1)),
                             lambda e, bk=bk, c0=c0, c1=c1, n=n: e.tensor_tensor(
                                 out=tB[:, c0:c1], in0=bk[:, 0:n], in1=gate_buf[:, c0:c1], op=ALU.mult))
                        S.op("dve", regs(tA.reg((c0, c1)), tB.reg((c0, c1))), mT.reg(j, (c0, c1)),
                             lambda e, c0=c0, c1=c1, j=j: e.tensor_tensor(out=mT[:, j, c0:c1], in0=tA[:, c0:c1],
                                                                          in1=tB[:, c0:c1], op=ALU.add))
            stage1(CH_GA + j, False, ev_gate(sga[s]))
            proj("pp", ya_in, sga[s], False)
            stage1(CH_GB + j, False, ev_gate(sgb[s]))
            if j % 2 == 1:
                proj("pc", yb_in, sgb[0], True, j=j - 1)
                proj("pc", yb_in, sgb[1], True, j=j)

    if stop_after is None:
        S.dma("sp", "cc", [], regs(gpost.reg(), gple.reg()),
              lambda e: [e.dma_start(out=gpost[:], in_=gpost_d), e.dma_start(out=gple[:], in_=gple_d)], n=2)

        corder = [("o", G, 0) for G in range(8)] + [("g", G, 0) for G in range(8)]
        c_issued = [0]

        C_LIMIT = [10 ** 6]

        def c_issue_upto(n):
            while c_issued[0] <= min(n, len(corder) - 1, C_LIMIT[0]):
                m = c_issued[0]
                kind, G, _rep = corder[m]
                sl = wslc[m % 3]
                src = w_out_d[G] if kind == "o" else w_pg_d[G]
                S.dma("pool", f"wc{m % 3}", [], sl.reg(), lambda e, sl=sl, src=src: e.dma_start(out=sl[:], in_=src))
                c_issued[0] += 1

        def cget(kind, G, rep):
            n = corder.index((kind, G, rep))
            c_issue_upto(n + 2)
            return wslc[n % 3]

        SS_Z = 32
        conv_run(10 ** 6)
        for i in range(NT):
            S.op("dve", stat.reg((128 + i * 4, 132 + i * 4)), stat.reg((SS_E + i, SS_E + i + 1)),
                 lambda e, i=i: e.tensor_reduce(out=stat[:, SS_E + i:SS_E + i + 1], in_=stat[:, 128 + i * 4:132 + i * 4],
                                                axis=mybir.AxisListType.X, op=ALU.add))
            S.op("act", regs(stat.reg((SS_E + i, SS_E + i + 1)), epsb.reg()), stat.reg((SS_E + i, SS_E + i + 1)),
                 lambda e, i=i: e.activation(out=stat[:, SS_E + i:SS_E + i + 1], in_=stat[:, SS_E + i:SS_E + i + 1],
                                             func=AF.Sqrt, scale=1.0 / D, bias=epsb[:, 0:1]))
            S.op("dve", stat.reg((SS_E + i, SS_E + i + 1)), stat.reg((SS_E + i, SS_E + i + 1)),
                 lambda e, i=i: e.reciprocal(out=stat[:, SS_E + i:SS_E + i + 1], in_=stat[:, SS_E + i:SS_E + i + 1]))

        HA, HB = [0, 1, 2, 3, 4], [5, 6, 7, 8]

        def c1_unit(G, i, sl):
            bk = next_bank()

            def pe_fn(e):
                ins = None
                for k in range(16):
                    ins = e.matmul(bk[:, 0:256], mT[:, k, i * 128:(i + 1) * 128], sl[:, k, :],
                                   start=(k == 0), stop=(k == 15))
                return ins
            S.op("pe", regs(sl.reg(), mT.reg(None, (i * 128, i * 128 + 128))), bk.reg(), pe_fn)
            S.op("act", bk.reg(), z.reg(i, (G * 256, G * 256 + 256)),
                 lambda e: e.activation(out=z[:, i, G * 256:(G + 1) * 256], in_=bk[:, 0:256], func=AF.Copy))
            col = SS_Z + i * 8 + G
            S.op("act", bk.reg(), regs(etc_[0].reg(), stat.reg((col, col + 1))),
                 lambda e: e.activation(out=etc_[0][:], in_=bk[:, 0:256], func=AF.Square,
                                        accum_out=stat[:, col:col + 1]))

        def x1_chain(i):
            c0 = SS_Z + i * 8
            rc = 25 + (i % 2)
            S.op("dve", stat.reg((c0, c0 + 8)), stat.reg((rc, rc + 1)),
                 lambda e: e.tensor_reduce(out=stat[:, rc:rc + 1], in_=stat[:, c0:c0 + 8],
                                           axis=mybir.AxisListType.X, op=ALU.add))
            S.op("act", regs(stat.reg((rc, rc + 1)), epsb.reg()), stat.reg((rc, rc + 1)),
                 lambda e: e.activation(out=stat[:, rc:rc + 1], in_=stat[:, rc:rc + 1], func=AF.Sqrt, scale=1.0 / D,
                                        bias=epsb[:, 0:1]))
            S.op("dve", stat.reg((rc, rc + 1)), stat.reg((rc, rc + 1)),
                 lambda e: e.reciprocal(out=stat[:, rc:rc + 1], in_=stat[:, rc:rc + 1]))
            S.op("dve", regs(z.reg(i), stat.reg((rc, rc + 1)), gpost.reg()), z.reg(i),
                 lambda e: e.scalar_tensor_tensor(out=z[:, i, :], in0=z[:, i, :], scalar=stat[:, rc:rc + 1],
                                                  in1=gpost[:], op0=ALU.mult, op1=ALU.mult))
            S.dma("pool", f"xa{i}", z.reg(i), z.reg(i),
                  lambda e: e.dma_start(out=z[:, i, :], in_=x_tok[i], accum_op=ALU.add))

        def x1_cast(i):
            xbl = x1b[i % 2]
            S.op("act", z.reg(i), xbl.reg(), lambda e: e.activation(out=xbl[:], in_=z[:, i, :], func=AF.Copy))

        def x1_transposes(i):
            xbl = x1b[i % 2]
            for h in range(2):
                tb = TPB[h]
                tbv = bank_bf(tb)[:, 0:1024].rearrange("p (k t) -> p k t", k=8)

                def pe_fn(e, h=h, tbv=tbv):
                    ins = None
                    for kk in range(8):
                        k = h * 8 + kk
                        ins = e.transpose(tbv[:, kk, :], xbl[:, k * 128:(k + 1) * 128], ident_bf[:])
                    return ins
                S.op("pe", regs(xbl.reg(), ident_bf.reg()), tb.reg(), pe_fn)
                if h == 0:
                    S.op("dve", tb.reg(), mT.reg((h * 8, h * 8 + 8), (i * 128, i * 128 + 128)),
                         lambda e, h=h, tbv=tbv: e.tensor_copy(out=mT[:, h * 8:h * 8 + 8, i * 128:(i + 1) * 128], in_=tbv))
                else:
                    S.op("act", tb.reg(), mT.reg((h * 8, h * 8 + 8), (i * 128, i * 128 + 128)),
                         lambda e, h=h, tbv=tbv: e.activation(out=mT[:, h * 8:h * 8 + 8, i * 128:(i + 1) * 128], in_=tbv,
                                                              func=AF.Copy))

        def c2_unit(G, i, sl):
            bk = next_bank()
            bk2 = next_bank()
            s = i % 2

            def pe_fn(e):
                ins = None
                for k in range(16):
                    ins = e.matmul(bk[:, 0:256], mT[:, k, i * 128:(i + 1) * 128], sl[:, k, :],
                                   start=(k == 0), stop=(k == 15))
                return ins
            S.op("pe", regs(sl.reg(), mT.reg(None, (i * 128, i * 128 + 128))), bk.reg(), pe_fn)

            def pe_fn2(e):
                ins = None
                for kk in range(2):
                    ins = e.matmul(bk2[:, 0:256], pT[:, i, kk, :], wple[:, kk, G * 256:(G + 1) * 256],
                                   start=(kk == 0), stop=(kk == 1))
                return ins
            S.op("pe", regs(pT.reg(i), wple.reg(None, (G * 256, G * 256 + 256))), bk2.reg(), pe_fn2)
            S.op("act", bk.reg(), sgc[s].reg(),
                 lambda e: e.activation(out=sgc[s][:], in_=bk[:, 0:256], func=AF.Sigmoid))
            S.op("dve", regs(bk2.reg(), stat.reg((SS_E + i, SS_E + i + 1)), gple.reg((G * 256, G * 256 + 256))),
                 etc_[s].reg(),
                 lambda e: e.scalar_tensor_tensor(
                     out=etc_[s][:], in0=bk2[:, 0:256], scalar=stat[:, SS_E + i:SS_E + i + 1],
                     in1=gple[:, G * 256:(G + 1) * 256], op0=ALU.mult, op1=ALU.mult))
            S.op("dve", regs(etc_[s].reg(), sgc[s].reg()), etc_[s].reg(),
                 lambda e: e.tensor_tensor(out=etc_[s][:], in0=etc_[s][:], in1=sgc[s][:], op=ALU.mult))
            S.op("dve", regs(etc_[s].reg(), z.reg(i, (G * 256, G * 256 + 256))), z.reg(i, (G * 256, G * 256 + 256)),
                 lambda e: e.tensor_tensor(out=z[:, i, G * 256:(G + 1) * 256],
                                           in0=z[:, i, G * 256:(G + 1) * 256], in1=etc_[s][:], op=ALU.add))
            S.dma("sp", "out", z.reg(i, (G * 256, G * 256 + 256)), [],
                  lambda e: e.dma_start(out=y_tok[i, :, G * 256:(G + 1) * 256],
                                        in_=z[:, i, G * 256:(G + 1) * 256]), final=True)

        ALLT = list(range(NT))
        for G in range(6):
            sl = cget("o", G, 0)
            for i in ALLT:
                c1_unit(G, i, sl)
        C_LIMIT[0] = 8
        sl6 = cget("o", 6, 0)
        sl7 = cget("o", 7, 0)
        slg0 = cget("g", 0, 0)
        done0 = []
        for st_ in range(NT + 3):
            if st_ < NT:
                c1_unit(6, st_, sl6)
                c1_unit(7, st_, sl7)
                x1_chain(st_)
            else:
                for i_ in (2 * (st_ - NT), 2 * (st_ - NT) + 1):
                    c2_unit(0, i_, slg0)
                    done0.append(i_)
            if 0 <= st_ - 3 < NT:
                x1_transposes(st_ - 3)
            if 0 <= st_ - 2 < NT:
                x1_cast(st_ - 2)
        C_LIMIT[0] = 10 ** 6
        for i in ALLT:
            if i not in done0:
                c2_unit(0, i, slg0)
        for G in range(1, 8):
            sl = cget("g", G, 0)
            for i in ALLT:
                c2_unit(G, i, sl)

    LB = dict(locals())
    for (name, buf, shape) in dbg:
        bufobj = LB[buf] if isinstance(buf, str) else buf
        if isinstance(bufobj, (list, tuple)):
            for bi_, b_ in enumerate(bufobj):
                dt_ = dram_out(f"dbg_{name}{bi_}", [128] + list(shape))
                S.dma("sp" if b_.esz == 4 else "pool", "dbgo", b_.reg(), [],
                      lambda e, dt_=dt_, b_=b_: e.dma_start(out=dt_, in_=b_[:]), final=True)
            continue
        dt_ = dram_out("dbg_" + name, [128] + list(shape))
        S.dma("pool", "dbgo", bufobj.reg(), [], lambda e, dt_=dt_, bufobj=bufobj: e.dma_start(out=dt_, in_=bufobj[:]),
              final=True)

    fin = dict(S.final_waits)
    S.prog["sp"].append(([(s, v) for s, v in fin.items()], None, None))

    semnames = sorted(S.semnames)
    sems = {}
    import contextlib
    with contextlib.ExitStack() as es:
        for sname in semnames:
            sems[sname] = es.enter_context(nc.semaphore(sname))
        block = es.enter_context(nc.Block())

        S.finalize()

        def run(e, engname):
            for waits, fn, inc in S.prog[engname]:
                for (s_, v) in waits:
                    e.wait_ge(sems[s_], S.wait_value(s_, v))
                if fn is None:
                    continue
                ins = fn(e)
                sem_, val_, is_dma = inc
                if is_dma:
                    if isinstance(ins, (list, tuple)):
                        for i_ in ins:
                            i_.then_inc(sems[sem_], 16)
                    else:
                        ins.then_inc(sems[sem_], 16)
                elif (sem_, val_) in S.rank:
                    ins.then_inc(sems[sem_], 1)

        @block.sync
        def _(e):
            run(e, "sp")

        @block.gpsimd
        def _(e):
            run(e, "pool")

        @block.scalar
        def _(e):
            run(e, "act")

        @block.vector
        def _(e):
            run(e, "dve")

        @block.tensor
        def _(e):
            run(e, "pe")
    return nc


def _prep_shared(inp):
    f = np.float32
    sh = {}
    w_in = np.asarray(inp["w_in"][0], f)
    sh["w_in_t"] = np.ascontiguousarray(w_in.reshape(16, 128, 72, 128).transpose(2, 1, 0, 3))
    w_pool = np.asarray(inp["w_pool"][0], f)
    sh["w_pool_t"] = np.ascontiguousarray(w_pool.reshape(4, 2, 128, 256).transpose(2, 0, 1, 3)).reshape(128, 2048)
    for nm, key in (("w_pp_t", "w_proj_pool"), ("w_pc_t", "w_proj_conv")):
        w = np.asarray(inp[key][0], f)
        sh[nm] = np.ascontiguousarray(w.reshape(8, 128, 16, 128).transpose(2, 1, 0, 3))
    for nm, key in (("w_out_t", "w_out"), ("w_pg_t", "w_ple_gate")):
        w = np.asarray(inp[key][0], f)
        sh[nm] = np.ascontiguousarray(w.reshape(16, 128, 8, 256).transpose(2, 1, 0, 3))
    w_ple = np.asarray(inp["w_ple"][0], f)
    sh["w_ple_t"] = np.ascontiguousarray(w_ple.reshape(2, 128, 2048).transpose(1, 0, 2)).reshape(128, 4096)
    sh["g_post_bc"] = np.ascontiguousarray(np.broadcast_to(np.asarray(inp["g_post"][0], f)[None, :], (128, D)))
    sh["g_ple_bc"] = np.ascontiguousarray(np.broadcast_to(np.asarray(inp["g_ple"][0], f)[None, :], (128, D)))
    sh["ident"] = np.eye(128, dtype=f)
    cst = np.zeros((128, 360), f)
    cst[:, 0:16] = np.asarray(inp["g_pre"][0], f).reshape(16, 128).T
    cst[:, 16:24] = np.asarray(inp["pool_scale"][0], f).reshape(8, 128).T
    cst[:, 24:32] = np.asarray(inp["b_dw"][0], f).reshape(8, 128).T
    cst[:, 32:40] = np.asarray(inp["ln_g"][0], f).reshape(8, 128).T
    cst[:, 40:48] = np.asarray(inp["ln_b"][0], f).reshape(8, 128).T
    wdw = np.asarray(inp["w_dw"][0], f)
    cst[:, 112:360] = wdw.reshape(31, 8, 128).transpose(2, 1, 0).reshape(128, 248)
    return sh, cst


def _in_maps(inp):
    f = np.float32
    sh, cst0 = _prep_shared(inp)
    xp = np.asarray(inp["x_prompt"], f)
    xsamp = np.asarray(inp["x_sample"], f)
    pp = np.asarray(inp["p_prompt"], f)[0]
    ps = np.asarray(inp["p_sample"], f)[0]
    stc = np.asarray(inp["state_conv"], f)[0]
    stp = np.asarray(inp["state_pool"], f)[0]
    maps = []
    for r in range(NCORES):
        b, half = r // 2, r % 2
        st = half * 1024
        m = dict(sh)
        xt = np.empty((NT, 128, D), f)
        xt[0:8] = xp[b, st:st + 1024].reshape(8, 128, D)
        xt[8] = xsamp[16 * r:16 * r + 16].reshape(128, D)
        m["x_tok"] = xt
        m["x_halo"] = np.ascontiguousarray(xp[b, st - 32:st]) if half else np.zeros((32, D), f)
        pt = np.empty((NT, 128, 256), f)
        pt[0:8] = pp[b, st:st + 1024].reshape(8, 128, 256)
        pt[8] = ps[16 * r:16 * r + 16].reshape(128, 256)
        m["p_tok"] = pt
        m["st_conv"] = np.ascontiguousarray(stc[16 * r:16 * r + 16])
        m["st_pool"] = np.ascontiguousarray(stp[16 * r:16 * r + 16])
        cst = cst0.copy()
        for g, wd in enumerate((2, 4, 8, 16)):
            pos = st + np.arange(16)
            cst[:, 48 + g * 16:64 + g * 16] = (1.0 / np.minimum(pos + 1, wd)).astype(f)[None, :]
        m["cst"] = cst
        maps.append(m)
    return maps


_NC_CACHE = {}


def kernel(**inputs):
    if "nc" not in _NC_CACHE:
        _NC_CACHE["nc"] = build_program()
    nc = _NC_CACHE["nc"]
    maps = _in_maps(inputs)
    res = run_bass_kernel_spmd(nc, maps, core_ids=list(range(NCORES)))
    R = res.results
    f = np.float32
    y_prompt = np.empty((4, 2048, D), f)
    y_sample = np.empty((128, 8, D), f)
    npp = np.empty((1, 4, 15, DP), f)
    ncp = np.empty((1, 4, 30, DP), f)
    nps = np.empty((1, 128, 15, DP), f)
    ncs = np.empty((1, 128, 30, DP), f)
    for r in range(NCORES):
        b, half = r // 2, r % 2
        st = half * 1024
        yt = R[r]["y_tok"]
        y_prompt[b, st:st + 1024] = yt[0:8].reshape(1024, D)
        y_sample[16 * r:16 * r + 16] = yt[8].reshape(16, 8, D)
        ncs[0, 16 * r:16 * r + 16, 0:22] = R[r]["ncs_old"]
        ncs[0, 16 * r:16 * r + 16, 22:30] = R[r]["ncs_new"].reshape(16, 8, DP)
        nps[0, 16 * r:16 * r + 16, 0:7] = R[r]["nps_old"]
        nps[0, 16 * r:16 * r + 16, 7:15] = R[r]["nps_new"].reshape(16, 8, DP)
        if half:
            ncp[0, b] = R[r]["ncp"][2:32]
            npp[0, b] = R[r]["npp"][17:32]
    return (y_prompt, y_sample, npp, ncp, nps, ncs)
```

```python
import numpy as np
import concourse.bass as bass
import concourse.mybir as mybir
from concourse.bass_utils import run_bass_kernel_spmd

F32 = mybir.dt.float32
BF16 = mybir.dt.bfloat16
AF = mybir.ActivationFunctionType
ALU = mybir.AluOpType

NCORES = 8
D = 2048
DP = 1024
NT = 9
W = 1184
T = 1152
EPS = 1e-6
SB_BASE = 16512
SB_END = 229376

CH_U, CH_ZA, CH_A, CH_B, CH_ZB, CH_GA, CH_GB = 0, 8, 16, 24, 32, 40, 56


class Sched:
    ENGS = ("pe", "act", "dve", "pool", "sp")
    GR = 1024

    def __init__(self):
        self.prog = {e: [] for e in self.ENGS}
        self.cnt = {}
        self.waited = {e: {} for e in self.ENGS}
        self.recs = {}
        self.buckets = {}
        self.semnames = set()
        self.final_waits = {}
        self.needed = set()

    def _granules(self, space, lo, hi):
        return [(space, g) for g in range(lo // self.GR, (hi - 1) // self.GR + 1)]

    def _add(self, key, val):
        old = self.recs.get(key)
        if old is None:
            for b in self._granules(key[0], key[1], key[2]):
                self.buckets.setdefault(b, set()).add(key)
        if old is None or old < val:
            self.recs[key] = val

    def _remove(self, key):
        del self.recs[key]
        for b in self._granules(key[0], key[1], key[2]):
            self.buckets[b].discard(key)

    def _overlaps(self, space, lo, hi):
        seen = set()
        for b in self._granules(space, lo, hi):
            for key in self.buckets.get(b, ()):
                if key in seen:
                    continue
                if key[1] < hi and lo < key[2]:
                    seen.add(key)
        return seen

    def _collect(self, eng_sem, is_pe, reads, writes):
        deps = {}
        cur = self.cnt.get(eng_sem, 0)

        def need(sem, val):
            if deps.get(sem, 0) < val:
                deps[sem] = val

        for (space, lo, hi) in reads:
            for key in self._overlaps(space, lo, hi):
                sem = key[3]
                if not key[4]:
                    if space == "ps" and sem != eng_sem:
                        need(sem, self.recs[key])
                    continue
                if sem == eng_sem:
                    if is_pe:
                        continue
                need(sem, self.recs[key])
        for (space, lo, hi) in writes:
            for key in self._overlaps(space, lo, hi):
                sem = key[3]
                if sem == eng_sem:
                    continue
                need(sem, self.recs[key])
        return deps

    def _record(self, sem, val, reads, writes):
        for (space, lo, hi) in writes:
            for key in list(self._overlaps(space, lo, hi)):
                if key[1] >= lo and key[2] <= hi:
                    self._remove(key)
            self._add((space, lo, hi, sem, True), val)
        for (space, lo, hi) in reads:
            self._add((space, lo, hi, sem, False), val)

    def _waits(self, eng, deps):
        out = []
        for sem, val in deps.items():
            if self.waited[eng].get(sem, 0) < val:
                self.waited[eng][sem] = val
                out.append((sem, val))
                self.needed.add((sem, val))
        return out

    def op(self, eng, reads, writes, fn):
        sem = "E_" + eng
        self.semnames.add(sem)
        deps = self._collect(sem, eng == "pe", reads, writes)
        waits = self._waits(eng, deps)
        val = self.cnt.get(sem, 0) + 1
        self.cnt[sem] = val
        self.prog[eng].append((waits, fn, (sem, val, False)))
        self._record(sem, val, reads, writes)

    def dma(self, queue, sem, reads, writes, fn, final=False, n=1):
        self.semnames.add(sem)
        deps = self._collect(sem, False, reads, writes)
        deps.pop(sem, None)
        waits = self._waits(queue, deps)
        val = self.cnt.get(sem, 0) + 16 * n
        self.cnt[sem] = val
        self.prog[queue].append((waits, fn, (sem, val, True)))
        self._record(sem, val, reads, writes)
        if final:
            self.final_waits[sem] = val

    def finalize(self):
        self.rank = {}
        by_sem = {}
        for (sem, val) in self.needed:
            if sem.startswith("E_"):
                by_sem.setdefault(sem, []).append(val)
        for sem, vals in by_sem.items():
            for r, v in enumerate(sorted(vals)):
                self.rank[(sem, v)] = r + 1

    def wait_value(self, sem, val):
        return self.rank[(sem, val)] if sem.startswith("E_") else val


class Buf:
    def __init__(self, t, space, addr, free_shape, esz):
        self.t = t
        self.space = space
        self.addr = addr
        self.shape = tuple(free_shape)
        self.esz = esz

    def __getitem__(self, k):
        return self.t[k]

    def reg(self, *idx):
        shp = self.shape
        idx = list(idx) + [None] * (len(shp) - len(idx))
        rngs = []
        for d, i in enumerate(idx):
            if i is None:
                rngs.append((0, shp[d]))
            elif isinstance(i, int):
                rngs.append((i, i + 1))
            else:
                rngs.append(i)
        strides = [1] * len(shp)
        for d in range(len(shp) - 2, -1, -1):
            strides[d] = strides[d + 1] * shp[d + 1]
        nd = len(shp)
        cut = nd
        while cut > 1 and rngs[cut - 1] == (0, shp[cut - 1]):
            cut -= 1
        out = []

        if self.space == "ps":
            return [("ps", self.addr, self.addr + 2048)]

        def rec(d, off):
            if d == cut - 1:
                lo = off + rngs[d][0] * strides[d]
                hi = off + rngs[d][1] * strides[d]
                out.append((self.space, self.addr + lo * self.esz, self.addr + hi * self.esz))
                return
            for i in range(rngs[d][0], rngs[d][1]):
                rec(d + 1, off + i * strides[d])

        rec(0, 0)
        return out


class Arena:
    def __init__(self, nc):
        self.nc = nc
        self.n = 0

    def at(self, name, addr, free_shape, dt, parts=128):
        esz = 4 if dt == F32 else 2
        size = int(np.prod(free_shape)) * esz
        assert addr % 32 == 0, (name, addr)
        assert SB_BASE <= addr and addr + size <= SB_END, (name, addr, size)
        self.n += 1
        t = self.nc.alloc_sbuf_tensor_at(f"{name}_{self.n}", [parts] + list(free_shape), dt, offset=addr)
        return Buf(t, "sb", addr, free_shape, esz)


def build_program(stop_after=None, dbg=None):
    nc = bass.Bass("TRN2", target_bir_lowering=False)
    S = Sched()
    A = Arena(nc)
    dbg = dbg or []

    def dram_in(name, shape):
        return nc.dram_tensor(name, list(shape), F32, kind="ExternalInput").ap()

    def dram_out(name, shape):
        return nc.dram_tensor(name, list(shape), F32, kind="ExternalOutput").ap()

    x_tok = dram_in("x_tok", [NT, 128, D])
    x_halo = dram_in("x_halo", [32, D])
    p_tok = dram_in("p_tok", [NT, 128, 256])
    st_conv = dram_in("st_conv", [16, 30, DP])
    st_pool = dram_in("st_pool", [16, 15, DP])
    cst_d = dram_in("cst", [128, 360])
    ident_d = dram_in("ident", [128, 128])
    w_in_d = dram_in("w_in_t", [72, 128, 16, 128])
    w_pool_d = dram_in("w_pool_t", [128, 4 * 2 * 256])
    w_pp_d = dram_in("w_pp_t", [16, 128, 8, 128])
    w_pc_d = dram_in("w_pc_t", [16, 128, 8, 128])
    w_out_d = dram_in("w_out_t", [8, 128, 16, 256])
    w_pg_d = dram_in("w_pg_t", [8, 128, 16, 256])
    w_ple_d = dram_in("w_ple_t", [128, 2 * D])
    gpost_d = dram_in("g_post_bc", [128, D])
    gple_d = dram_in("g_ple_bc", [128, D])

    y_tok = dram_out("y_tok", [NT, 128, D])
    ncs_new = dram_out("ncs_new", [128, DP])
    ncs_old = dram_out("ncs_old", [16, 22, DP])
    nps_new = dram_out("nps_new", [128, DP])
    nps_old = dram_out("nps_old", [16, 7, DP])
    ncp_o = dram_out("ncp", [32, DP])
    npp_o = dram_out("npp", [32, DP])
    dbg_out = {}

    a = SB_BASE
    C0 = a
    ident_bf = A.at("ident_bf", a, [128], BF16); a += 256
    ident_f = A.at("ident_f", a, [128], F32); a += 512
    ones_bf = A.at("ones_bf", a, [128], BF16); a += 256
    cst = A.at("cst", a, [360], F32); a += 1440
    wpool = A.at("wpool", a, [4, 2, 256], BF16); a += 4096
    stat = A.at("stat", a, [192], F32); a += 768
    pT = A.at("pT", a, [NT, 2, 128], BF16); a += NT * 2 * 128 * 2
    exs = A.at("exs", a, [2, 16, 23], F32); a += 2 * 16 * 23 * 4
    tmpf = A.at("tmpf", a, [16], F32); a += 64
    negh = A.at("negh", a, [8], F32); a += 32
    epsb = A.at("epsb", a, [8], F32); a += 32
    a = (a + 31) // 32 * 32
    R1 = a
    hT = A.at("hT", R1, [16, W], BF16); a += 16 * W * 2
    R2 = a
    vacc = [A.at(f"vacc{i}", R2 + i * W * 4, [W], F32) for i in range(8)]
    vall = A.at("vall", R2, [10, W], F32)
    a += 10 * W * 4
    R3 = a
    mT = A.at("mT", R3, [16, T], BF16); a += 16 * T * 2
    R4 = a
    ext_s = A.at("ext_s", R4, [8, 16, 38], BF16); a += 8 * 16 * 38 * 4
    v_bf = [A.at(f"v_bf{i}", R4 + 9728 + i * 2368, [W], BF16) for i in range(2)]
    vt = [A.at(f"vt{i}", R4 + 9728 + 4736 + i * 640, [160], F32) for i in range(2)]
    dgA = A.at("dgA", R2 + 8 * W * 4, [16, 128], BF16)
    dgB = A.at("dgB", R2 + 8 * W * 4 + 4096, [16, 128], BF16)
    wdwc = A.at("wdwc", R2 + 8 * W * 4 + 8192, [32], F32)
    dgA2 = A.at("dgA2", R3, [16, 128], BF16)
    dgB2 = A.at("dgB2", R3 + 4096, [16, 128], BF16)
    wdwc2 = A.at("wdwc2", R3 + 8192, [32], F32)
    R4b = a
    ya_in = A.at("ya_in", R4b, [8, T], BF16); a += 8 * T * 2
    R5 = a
    wsl = [A.at(f"wsl{i}", R5 + i * 4096, [2048], BF16) for i in range(4)]
    a += 4 * 4096
    R6 = a
    a += 18432
    assert a <= SB_END, a

    xs = [A.at(f"xs{i}", R3 + i * 8192, [D], F32) for i in range(3)]
    xs += [A.at(f"xs{3 + i}", R4b + i * 8192, [D], F32) for i in range(2)]
    pall = A.at("pall", R4b, [NT, 256], BF16)
    xb = [A.at(f"xb{i}", R3 + 24576 + i * 4096, [D], BF16) for i in range(2)]
    sqj = A.at("sqj", R3 + 32768, [D], BF16)
    silu_za = A.at("silu_za", R3, [4, T], BF16)
    pooled = A.at("pooled", R3 + 9216, [4, T], BF16)
    ubuf = A.at("ubuf", R3 + 18432, [W], F32)
    uscr = [A.at(f"uscr{i}", R3 + 18432 + (i + 1) * W * 4, [W], F32) for i in range(2)]
    assert 18432 + 3 * W * 4 <= 16 * T * 2
    ybf = [A.at(f"ybf{i}", R3 + i * 2304, [T], BF16) for i in range(2)]
    ysq = [A.at(f"ysq{i}", R3 + 4608 + i * 2304, [T], BF16) for i in range(2)]
    mean_sb = A.at("mean", R3 + 9216, [T], F32)
    rstd_sb = A.at("rstd", R3 + 9216 + 4608, [T], F32)
    sgl = [A.at(f"sgl{i}", R3 + 18432 + i * 4608, [T], F32) for i in range(2)]
    silu_zb = A.at("silu_zb", R4, [8, T], BF16)
    ext_u = A.at("ext_u", R6, [8, 16, 23], F32)
    sgt = [A.at(f"sgt{i}", R6 + 11776 + i * 2048, [512], F32) for i in range(3)]
    stg = A.at("stg", R6 + 11776, [DP], F32)
    sga = [A.at(f"sga{i}", R6 + i * 2304, [T], BF16) for i in range(2)]
    sgb = [A.at(f"sgb{i}", R6 + 4608 + i * 2304, [T], BF16) for i in range(2)]
    tA = A.at("tA", R6 + 9216, [T], F32)
    tA2 = [tA, A.at("tA1", R2, [T], F32)]
    tB = A.at("tB", R6 + 9216 + 4608, [T], F32)
    TAIL = a
    so_s = [A.at(f"so_s{i}", TAIL + i * 512, [128], F32) for i in range(2)]
    so_p = [A.at(f"so_p{i}", TAIL + 1024 + i * 512, [128], F32, parts=32) for i in range(2)]
    assert TAIL + 2048 <= SB_END
    z = A.at("z", R1, [NT, D], F32)
    CZ = R1 + NT * D * 4
    CZ = (CZ + 31) // 32 * 32
    wple = A.at("wple", CZ, [2, D], BF16)
    assert CZ + 8192 <= R3
    c = R4
    wslc = [None, A.at("wslc1", c, [16, 256], BF16), A.at("wslc2", c + 8192, [16, 256], BF16)]; c += 2 * 8192
    x1b = [A.at(f"x1b{i}", c + i * 4096, [D], BF16) for i in range(2)]; c += 8192
    assert c == R4b + 5120
    wslc[0] = A.at("wslc0", c, [16, 256], BF16); c += 8192
    gpost = A.at("gpost", c, [D], F32); c += 8192
    gple = A.at("gple", c, [D], F32); c += 8192
    sgc = [A.at(f"sgc{i}", c + i * 1024, [256], F32) for i in range(2)]; c += 2048
    etc_ = [A.at(f"etc{i}", c + i * 1024, [256], F32) for i in range(2)]; c += 2048
    ptile = A.at("ptile", c, [256], F32); c += 1024
    pbt = A.at("pbt", c, [256], BF16); c += 512
    assert c <= SB_END, c

    banks = []
    for i in range(8):
        t = nc.alloc_psum_tensor(f"bank{i}", [128, 512], F32)
        banks.append(Buf(t, "ps", i * 2048, [512], 4))
    acc_rot = [0]

    def next_bank():
        b = banks[acc_rot[0] % 6]
        acc_rot[0] += 1
        return b

    TPB = (banks[6], banks[7])

    def bank_bf(b):
        return b.t.bitcast(BF16)

    def regs(*lists):
        out = []
        for l in lists:
            out.extend(l)
        return out

    S.dma("sp", "cst", [], regs(cst.reg(), ident_f.reg()),
          lambda e: [e.dma_start(out=cst[:], in_=cst_d), e.dma_start(out=ident_f[:], in_=ident_d)], n=2)
    S.dma("pool", "cstb", [], ident_bf.reg(), lambda e: e.dma_start(out=ident_bf[:], in_=ident_d))
    S.op("dve", [], ones_bf.reg(), lambda e: e.memset(ones_bf[:], 1.0))
    S.op("dve", [], negh.reg(), lambda e: e.memset(negh[:], -0.5))
    S.op("dve", [], epsb.reg(), lambda e: e.memset(epsb[:], EPS))
    S.op("dve", [], stat.reg(), lambda e: e.memset(stat[:], 0.0))

    GPRE = lambda k0, k1: cst[:, k0:k1]
    PSC = lambda d: cst[:, 16 + d:17 + d]
    BDW = lambda c_: cst[:, 24 + c_:25 + c_]
    LNG = lambda c_: cst[:, 32 + c_:33 + c_]
    LNB = lambda c_: cst[:, 40 + c_:41 + c_]
    INVC = lambda g: cst[:, 48 + g * 16:64 + g * 16]
    WDW = lambda c_, k: cst[:, 112 + c_ * 31 + k:113 + c_ * 31 + k]
    cst_r = cst.reg()

    wlist = []

    def w_in_src(ch):
        return (w_in_d[ch], lambda sl: sl.t[:].rearrange("p (k c) -> p k c", k=16), [16, 128])

    def w_pr_src(dram, G):
        return (dram[G], lambda sl: sl.t[:].rearrange("p (k c) -> p k c", k=8), [8, 256])

    order = []
    for c_ in range(8):
        order.append(("in", CH_B + c_)); order.append(("in", CH_A + c_))
    for g in range(4):
        order += [("in", CH_U + 2 * g), ("in", CH_ZA + 2 * g), ("in", CH_U + 2 * g + 1), ("in", CH_ZA + 2 * g + 1)]
    for c_ in range(8):
        order.append(("in", CH_ZB + c_))
    for jp in range(0, 16, 2):
        for j in (jp, jp + 1):
            order.append(("in", CH_GA + j))
            order.append(("pp", j))
            order.append(("in", CH_GB + j))
        order.append(("pc", jp))
        order.append(("pc", jp + 1))
    wpos = {}
    for n, it in enumerate(order):
        wpos[it] = n
    w_issued = [0]

    W_LIMIT = [1]

    def w_issue_upto(n):
        while w_issued[0] <= min(n, len(order) - 1, W_LIMIT[0]):
            m = w_issued[0]
            kind, idx = order[m]
            sl = wsl[m % 4]
            if kind == "in":
                src, view = w_in_d[idx], sl.t[:].rearrange("p (k c) -> p k c", k=16)
            elif kind == "pp":
                src, view = w_pp_d[idx], sl.t[:, 0:1024].rearrange("p (k c) -> p k c", k=8)
            else:
                src, view = w_pc_d[idx], sl.t[:, 0:1024].rearrange("p (k c) -> p k c", k=8)
            S.dma("pool", f"w{m % 4}", [], sl.reg(),
                  lambda e, view=view, src=src: e.dma_start(out=view, in_=src))
            w_issued[0] += 1

    def wget(kind, idx):
        n = wpos[(kind, idx)]
        w_issue_upto(n + 3)
        sl = wsl[n % 4]
        if kind == "in":
            return sl, sl.t[:].rearrange("p (k c) -> p k c", k=16)
        return sl, sl.t[:, 0:1024].rearrange("p (k c) -> p k c", k=8)

    conv_q = []

    def conv_run(n):
        for _ in range(n):
            if not conv_q:
                return
            conv_q.pop(0)()

    def after_unit():
        conv_run(CONV_RATE[0])

    CONV_RATE = [0]

    xs_n = [0]

    def stage0(src_ap, nrows, col0, statcol):
        s = xs_n[0] % 5
        sb = xs_n[0] % 2
        xs_n[0] += 1
        xsl, xbl = xs[s], xb[sb]
        S.dma("sp", f"xs{s}", [], xsl.reg(), lambda e: e.dma_start(out=xsl[0:nrows, :], in_=src_ap))
        S.op("act", xsl.reg(), regs(sqj.reg(), stat.reg((statcol, statcol + 1))),
             lambda e: e.activation(out=sqj[0:nrows, :], in_=xsl[0:nrows, :], func=AF.Square,
                                    accum_out=stat[0:nrows, statcol:statcol + 1]))
        sc_ = stat[0:nrows, statcol:statcol + 1]
        sr_ = stat.reg((statcol, statcol + 1))
        S.op("act", regs(sr_, epsb.reg()), sr_,
             lambda e: e.activation(out=sc_, in_=sc_, func=AF.Sqrt, scale=1.0 / D, bias=epsb[0:nrows, 0:1]))
        S.op("dve", sr_, sr_, lambda e: e.reciprocal(out=sc_, in_=sc_))
        S.op("dve", regs(xsl.reg(), sr_), xbl.reg(),
             lambda e: e.tensor_scalar(out=xbl[0:nrows, :], in0=xsl[0:nrows, :], scalar1=sc_, scalar2=None, op0=ALU.mult))
        evs = []
        for h in range(2):
            tb = next_bank()
            tbv = bank_bf(tb)[:, 0:8 * nrows].rearrange("p (k t) -> p k t", k=8)

            def pe_fn(e, h=h, tbv=tbv):
                ins = None
                for kk in range(8):
                    k = h * 8 + kk
                    ins = e.transpose(tbv[:, kk, :], xbl[0:nrows, k * 128:(k + 1) * 128], ident_bf[0:nrows, 0:nrows])
                return ins
            S.op("pe", regs(xbl.reg(), ident_bf.reg()), tb.reg(), pe_fn)

            def ev(h=h, tb=tb, tbv=tbv):
                S.op("dve", regs(tb.reg(), cst_r), hT.reg((h * 8, h * 8 + 8), (col0, col0 + nrows)),
                     lambda e: e.tensor_tensor(
                         out=hT[:, h * 8:h * 8 + 8, col0:col0 + nrows], in0=tbv,
                         in1=GPRE(h * 8, h * 8 + 8).unsqueeze(2).broadcast_to([128, 8, nrows]), op=ALU.mult))
            evs.append(ev)
        return evs

    def state_T(src_dram, nb, nr, ext, j):
        xsl = stg
        rows = nb * nr
        src = src_dram[j * nb:(j + 1) * nb].rearrange("b r c -> (b r) c")
        S.dma("sp", "stg", [], xsl.reg((0, DP)), lambda e: e.dma_start(out=xsl[0:rows, 0:DP], in_=src))
        for h in range(2):
            tb = TPB[h]

            def pe_fn(e, h=h, tb=tb):
                ins = None
                for cc in range(4):
                    c_ = h * 4 + cc
                    ins = e.transpose(tb[:, cc * rows:(cc + 1) * rows], xsl[0:rows, c_ * 128:(c_ + 1) * 128],
                                      ident_f[0:rows, 0:rows])
                return ins
            S.op("pe", regs(xsl.reg((0, DP)), ident_f.reg()), tb.reg(), pe_fn)
            S.op("act", tb.reg(), ext.reg((h * 4, h * 4 + 4)),
                 lambda e, h=h, tb=tb: e.activation(
                     out=ext[:, h * 4:h * 4 + 4, j * nb:(j + 1) * nb, 0:nr],
                     in_=tb[:, 0:4 * rows].rearrange("p (c b r) -> p c b r", c=4, b=nb), func=AF.Copy))

    def p_load():
        S.dma("pool", "pall", [], pall.reg(), lambda e: e.dma_start(out=pall[:], in_=p_tok.rearrange("i p c -> p i c")))

    def p_T(i):
        tb = TPB[i % 2]
        tbv = bank_bf(tb)[:, 0:256].rearrange("p (k t) -> p k t", k=2)

        def pe_fn(e):
            ins = None
            for kk in range(2):
                ins = e.transpose(tbv[:, kk, :], pall[:, i, kk * 128:(kk + 1) * 128], ident_bf[:])
            return ins
        S.op("pe", regs(pall.reg(i), ident_bf.reg()), tb.reg(), pe_fn)
        S.op("dve", tb.reg(), pT.reg(i), lambda e: e.tensor_copy(out=pT[:, i, :, :], in_=tbv))

    BLK_H = [(0, 512), (512, 1024), (1024, W)]
    BLK_N = [(32, 544), (544, 1056), (1056, W)]

    def stage1(ch, halo, evac, blocks=(0, 1, 2)):
        sl, wv = wget("in", ch)
        for (c0, c1) in [(BLK_H if halo else BLK_N)[b_] for b_ in blocks]:
            n = c1 - c0
            bk = next_bank()

            def pe_fn(e, bk=bk, c0=c0, c1=c1, n=n):
                ins = None
                for k in range(16):
                    ins = e.matmul(bk[:, 0:n], wv[:, k, :], hT[:, k, c0:c1], start=(k == 0), stop=(k == 15))
                return ins
            S.op("pe", regs(sl.reg(), hT.reg(None, (c0, c1))), bk.reg((0, n)), pe_fn)
            evac(bk, c0, c1, n)
            after_unit()

    acc_of = {c_: vacc[c_] for c_ in range(8)}
    so_n = [0]
    CB = [(0, 512, 2), (512, 1024, 514), (1024, T, None)]

    def dg_set(c_):
        return (dgA2, dgB2, wdwc2) if c_ == 7 else (dgA, dgB, wdwc)

    def conv_prep(c_, slot):
        dA_, dB_, wc_ = dg_set(c_)
        S.op("dve", cst_r, wc_.reg(), lambda e: e.tensor_copy(out=wc_[:, 0:31], in_=cst[:, 112 + c_ * 31:143 + c_ * 31]))
        for (dg, k0, nk) in ((dA_, 0, 16), (dB_, 16, 15)):
            S.op("dve", regs(ident_bf.reg(), wc_.reg()), dg.reg((0, nk)),
                 lambda e, dg=dg, k0=k0, nk=nk: e.tensor_tensor(
                     out=dg[:, 0:nk, :], in0=ident_bf[:, :].unsqueeze(1).broadcast_to([128, nk, 128]),
                     in1=wc_[:, k0:k0 + nk].unsqueeze(2).broadcast_to([128, nk, 128]), op=ALU.mult))

    def conv_pe(c_, slot):
        vb, vtl, acc = v_bf[slot], vt[slot], vacc[c_]
        dgA, dgB, _ = dg_set(c_)
        S.op("act", vb.reg((1056, W)), ext_s.reg(c_),
             lambda e: e.activation(out=ext_s[:, c_, :, 30:38],
                                    in_=vb[:, 1056:W].rearrange("p (b t) -> p b t", b=16), func=AF.Copy))
        tb = TPB[c_ % 2]

        def pe_fn(e):
            e.transpose(tb[:, 0:128], vtl[:, 32:160], ident_f[:])
            return e.transpose(tb[0:32, 128:256], vtl[:, 0:32], ident_f[:])
        S.op("pe", regs(vtl.reg(), ident_f.reg()), tb.reg(), pe_fn)
        sl_ = so_n[0] % 2
        so_n[0] += 1
        ss_, sp_ = so_s[sl_], so_p[sl_]
        S.op("act", tb.reg(), ss_.reg(), lambda e: e.activation(out=ss_[:], in_=tb[:, 0:128], func=AF.Copy))
        S.op("act", tb.reg(), sp_.reg(), lambda e: e.activation(out=sp_[0:32, :], in_=tb[0:32, 128:256], func=AF.Copy))
        S.dma("sp", f"os{sl_}", ss_.reg(), [], lambda e: e.dma_start(out=ncs_new[:, c_ * 128:(c_ + 1) * 128], in_=ss_[:]),
              final=True)
        S.dma("sp", f"op{sl_}", sp_.reg(), [], lambda e: e.dma_start(out=ncp_o[:, c_ * 128:(c_ + 1) * 128],
                                                              in_=sp_[0:32, :]), final=True)
        KP = 19 if c_ < 7 else 31
        for k in range(KP, 31):
            i_ap = vb[:, 2 + k:1026 + k]
            i_rg = vb.reg((2 + k, 1026 + k))
            o_ap = acc[:, 0:1024]
            o_rg = acc.reg((0, 1024))
            if k == KP:
                S.op("dve", regs(i_rg, cst_r), o_rg,
                     lambda e, i_ap=i_ap, o_ap=o_ap, k=k: e.tensor_scalar(
                         out=o_ap, in0=i_ap, scalar1=WDW(c_, k), scalar2=BDW(c_), op0=ALU.mult, op1=ALU.add))
            else:
                S.op("dve", regs(i_rg, cst_r, o_rg), o_rg,
                     lambda e, i_ap=i_ap, o_ap=o_ap, k=k: e.scalar_tensor_tensor(
                         out=o_ap, in0=i_ap, scalar=WDW(c_, k), in1=o_ap, op0=ALU.mult, op1=ALU.add))
        for (t0, t1, voff) in CB:
            n = t1 - t0
            bk = next_bank()
            nk = KP if voff is not None else 31

            def pe_fn(e, bk=bk, t0=t0, n=n, voff=voff, nk=nk):
                ins = None
                for k in range(nk):
                    dg = dgA if k < 16 else dgB
                    if voff is not None:
                        ins = e.matmul(bk[:, 0:n], dg[:, k % 16, :], vb[:, voff + k:voff + k + n],
                                       start=(k == 0), stop=(k == nk - 1))
                    else:
                        ins = e.matmul(bk[:, 0:128].rearrange("p (b t) -> p b t", b=16), dg[:, k % 16, :],
                                       ext_s[:, c_, :, k:k + 8], start=(k == 0), stop=(k == nk - 1))
                return ins
            rd = regs(dgA.reg(), dgB.reg(), vb.reg() if voff is not None else ext_s.reg(c_))
            S.op("pe", rd, bk.reg(), pe_fn)
            if voff is not None and KP < 31:
                S.op("dve", regs(bk.reg(), acc.reg((t0, t1))), acc.reg((t0, t1)),
                     lambda e, bk=bk, t0=t0, t1=t1, n=n: e.tensor_tensor(out=acc[:, t0:t1], in0=bk[:, 0:n],
                                                                         in1=acc[:, t0:t1], op=ALU.add))
            else:
                S.op("act", regs(bk.reg(), cst_r), acc.reg((t0, t1)),
                     lambda e, bk=bk, t0=t0, t1=t1, n=n: e.activation(out=acc[:, t0:t1], in_=bk[:, 0:n],
                                                                      func=AF.Identity, bias=BDW(c_)))

    def a2_evs(c_):
        sg_ = vacc[c_]
        vb, vtl = v_bf[c_ % 2], vt[c_ % 2]

        def ev_b(bk, c0, c1, n):
            S.op("act", bk.reg(), sg_.reg((c0, c1)),
                 lambda e: e.activation(out=sg_[:, c0:c1], in_=bk[:, 0:n], func=AF.Sigmoid))

        def ev_a(bk, c0, c1, n):
            S.op("dve", regs(bk.reg(), sg_.reg((c0, c1))), vb.reg((c0, c1)),
                 lambda e: e.tensor_tensor(out=vb[:, c0:c1], in0=bk[:, 0:n], in1=sg_[:, c0:c1], op=ALU.mult))
            if c0 == 1024:
                S.op("dve", regs(bk.reg(), sg_.reg((c0, c1))), vtl.reg(),
                     lambda e: e.tensor_tensor(out=vtl[:, :], in0=bk[:, 0:n], in1=sg_[:, c0:c1], op=ALU.mult))
        return ev_b, ev_a

    early = stop_after != "A1"
    evb0, eva0 = a2_evs(0)

    def early_blocks(bl):
        if early:
            stage1(CH_B + 0, True, evb0, blocks=bl)
            stage1(CH_A + 0, True, eva0, blocks=bl)

    prev_evs = stage0(x_halo, 32, 0, 9)
    for i in range(NT):
        evs_ = stage0(x_tok[i], 128, 32 + 128 * i, i)
        for ev_ in prev_evs:
            ev_()
        prev_evs = evs_
        if early:
            if i == 4:
                stage1(CH_B + 0, True, evb0, blocks=[0])
            if i == 6:
                stage1(CH_A + 0, True, eva0, blocks=[0])
    for ev_ in prev_evs:
        ev_()
    if early:
        stage1(CH_B + 0, True, evb0, blocks=[1])
        stage1(CH_A + 0, True, eva0, blocks=[1])
    early_blocks([2])
    W_LIMIT[0] = 10 ** 6
    S.dma("pool", "wpl", [], wpool.reg(),
          lambda e: e.dma_start(out=wpool[:], in_=w_pool_d.rearrange("p (g k c) -> p g k c", g=4, k=2)))
    p_load()
    for j_ in range(4):
        conv_q.append(lambda j_=j_: state_T(st_conv, 4, 30, ext_s, j_))
    for j_ in range(2):
        conv_q.append(lambda j_=j_: state_T(st_pool, 8, 15, ext_u, j_))
    conv_q.append(lambda: S.dma("sp", "out", [], [], lambda e: e.dma_start(out=ncs_old, in_=st_conv[:, 8:30, :]), final=True))
    conv_q.append(lambda: S.dma("sp", "out", [], [], lambda e: e.dma_start(out=nps_old, in_=st_pool[:, 8:15, :]), final=True))
    CONV_RATE[0] = 1

    pending = []
    for c_ in (range(8) if stop_after != "A1" else []):
        slot = c_ % 2
        if c_ > 0:
            ev_b, ev_a = a2_evs(c_)
            stage1(CH_B + c_, True, ev_b)
            if pending:
                conv_prep(*pending[0])
            stage1(CH_A + c_, True, ev_a)
        if pending:
            conv_run(10 ** 6)
            if c_ == 7:
                conv_prep(7, slot)
            conv_pe(*pending.pop(0))
        pending.append((c_, slot))
    if stop_after != "A1":
        conv_pe(*pending.pop(0))

    if stop_after not in ("A1", "A2"):
        for i in range(NT):
            conv_q.append(lambda i=i: p_T(i))
        CONV_RATE[0] = 1
        def za_part(g, cc):
            wwin = 2 ** (g + 1)
            ch = 2 * g + cc
            slot = (g % 2) * 2 + cc

            def ev_za(bk, c0, c1, n, slot=slot):
                st = sgt[acc_rot[0] % 3]
                S.op("act", bk.reg((0, n)), st.reg((0, n)),
                     lambda e: e.activation(out=st[:, 0:n], in_=bk[:, 0:n], func=AF.Sigmoid))
                S.op("dve", regs(bk.reg((0, n)), st.reg((0, n))), silu_za.reg(slot, (c0 - 32, c1 - 32)),
                     lambda e: e.tensor_tensor(out=silu_za[:, slot, c0 - 32:c1 - 32], in0=bk[:, 0:n],
                                               in1=st[:, 0:n], op=ALU.mult))
            stage1(CH_ZA + ch, False, ev_za)

        def u_part(g, cc):
            wwin = 2 ** (g + 1)
            ch = 2 * g + cc
            slot = (g % 2) * 2 + cc

            def ev_u(bk, c0, c1, n):
                S.op("act", bk.reg((0, n)), ubuf.reg((c0, c1)),
                     lambda e: e.activation(out=ubuf[:, c0:c1], in_=bk[:, 0:n], func=AF.Copy))
            stage1(CH_U + ch, True, ev_u)
            S.op("act", ubuf.reg((1056, W)), ext_u.reg(ch),
                 lambda e, ch=ch: e.activation(out=ext_u[:, ch, :, 15:23],
                                               in_=ubuf[:, 1056:W].rearrange("p (b t) -> p b t", b=16),
                                               func=AF.Copy))
            tb = TPB[ch % 2]

            def pe_fn(e, tb=tb):
                e.transpose(tb[:, 0:128], ubuf[:, 1056:W], ident_f[:])
                return e.transpose(tb[0:32, 128:256], ubuf[:, 1024:1056], ident_f[:])
            S.op("pe", regs(ubuf.reg((1024, W)), ident_f.reg()), tb.reg((0, 256)), pe_fn)
            sl_ = so_n[0] % 2
            so_n[0] += 1
            ss_, sp_ = so_s[sl_], so_p[sl_]
            S.op("act", tb.reg((0, 128)), ss_.reg(),
                 lambda e, tb=tb, ss_=ss_: e.activation(out=ss_[:], in_=tb[:, 0:128], func=AF.Copy))
            S.op("act", tb.reg((128, 256)), sp_.reg(),
                 lambda e, tb=tb, sp_=sp_: e.activation(out=sp_[0:32, :], in_=tb[0:32, 128:256], func=AF.Copy))
            S.dma("sp", f"os{sl_}", ss_.reg(), [],
                  lambda e, ch=ch, ss_=ss_: e.dma_start(out=nps_new[:, ch * 128:(ch + 1) * 128], in_=ss_[:]), final=True)
            S.dma("sp", f"op{sl_}", sp_.reg(), [],
                  lambda e, ch=ch, sp_=sp_: e.dma_start(out=npp_o[:, ch * 128:(ch + 1) * 128], in_=sp_[0:32, :]),
                  final=True)
            src = ubuf
            for l in range(1, g + 2):
                sh = 2 ** (l - 1)
                lo = 2 ** l
                dst = uscr[(l - 1) % 2]
                S.op("dve", src.reg((lo - sh, 1056)), dst.reg((lo, 1056)),
                     lambda e, src=src, dst=dst, lo=lo, sh=sh: e.tensor_tensor(
                         out=dst[:, lo:1056], in0=src[:, lo:1056], in1=src[:, lo - sh:1056 - sh], op=ALU.add))
                src = dst
            win = src
            ssrc_ap = ext_u[:, ch, :, :]
            ssrc_reg = ext_u.reg(ch)
            for l in range(1, g + 2):
                sh = 2 ** (l - 1)
                lo = 2 ** l - 1
                d = exs[:, (l - 1) % 2, :, :]
                dreg = exs.reg((l - 1) % 2)
                S.op("dve", ssrc_reg, dreg,
                     lambda e, s_=ssrc_ap, d=d, lo=lo, sh=sh: e.tensor_tensor(
                         out=d[:, :, lo:23], in0=s_[:, :, lo:23], in1=s_[:, :, lo - sh:23 - sh], op=ALU.add))
                ssrc_ap, ssrc_reg = d, dreg
            S.op("dve", regs(win.reg((32, 1056)), ubuf.reg((32, 1056))), pooled.reg(slot, (0, 1024)),
                 lambda e, win=win, slot=slot, wwin=wwin: e.scalar_tensor_tensor(
                     out=pooled[:, slot, 0:1024], in0=win[:, 32:1056], scalar=1.0 / wwin, in1=ubuf[:, 32:1056],
                     op0=ALU.mult, op1=ALU.subtract))
            S.op("dve", regs(ssrc_reg, ext_u.reg(ch)), pooled.reg(slot, (1024, T)),
                 lambda e, s_=ssrc_ap, slot=slot, wwin=wwin, ch=ch: e.scalar_tensor_tensor(
                     out=pooled[:, slot, 1024:T].rearrange("p (b t) -> p b t", b=16), in0=s_[:, :, 15:23],
                     scalar=1.0 / wwin, in1=ext_u[:, ch, :, 15:23], op0=ALU.mult, op1=ALU.subtract))
            S.op("dve", regs(win.reg((32, 48)), cst_r), tmpf.reg(),
                 lambda e, win=win, g=g: e.tensor_tensor(out=tmpf[:], in0=win[:, 32:48], in1=INVC(g), op=ALU.mult))
            S.op("dve", regs(tmpf.reg(), ubuf.reg((32, 48))), pooled.reg(slot, (0, 16)),
                 lambda e, slot=slot: e.tensor_tensor(out=pooled[:, slot, 0:16], in0=tmpf[:], in1=ubuf[:, 32:48],
                                                      op=ALU.subtract))

        def wpool_part(g):
            conv_run(10 ** 6)
            for dd in range(2):
                d = 2 * g + dd
                for (c0, c1) in [(0, 512), (512, 1024), (1024, T)]:
                    n = c1 - c0
                    bk = next_bank()

                    def pe_fn(e, bk=bk, c0=c0, c1=c1, n=n, dd=dd, g=g):
                        ins = None
                        for kc in range(2):
                            ins = e.matmul(bk[:, 0:n], wpool[:, g, kc, dd * 128:(dd + 1) * 128],
                                           pooled[:, (g % 2) * 2 + kc, c0:c1], start=(kc == 0), stop=(kc == 1))
                        return ins
                    S.op("pe", regs(wpool.reg(g), pooled.reg(((g % 2) * 2, (g % 2) * 2 + 2), (c0, c1))),
                         bk.reg((0, n)), pe_fn)
                    S.op("dve", regs(bk.reg((0, n)), silu_za.reg((g % 2) * 2 + dd, (c0, c1)), cst_r),
                         ya_in.reg(d, (c0, c1)),
                         lambda e, bk=bk, n=n, d=d, dd=dd, g=g, c0=c0, c1=c1: e.scalar_tensor_tensor(
                             out=ya_in[:, d, c0:c1], in0=bk[:, 0:n], scalar=PSC(d),
                             in1=silu_za[:, (g % 2) * 2 + dd, c0:c1], op0=ALU.mult, op1=ALU.mult))
                    after_unit()


        for g in range(4):
            u_part(g, 0)
            za_part(g, 0)
            u_part(g, 1)
            za_part(g, 1)
            if g > 0:
                wpool_part(g - 1)
        wpool_part(3)

    if stop_after not in ("A1", "A2", "A3"):
        TB = [(0, 512), (512, 1024), (1024, T)]
        for c_ in range(8):
            acc = acc_of[c_]
            s = c_ % 2
            S.op("dve", acc.reg((0, T)), ybf[s].reg(), lambda e, acc=acc, s=s: e.tensor_copy(
                out=ybf[s][:], in_=acc[:, 0:T]))
            S.op("act", acc.reg((0, T)), ysq[s].reg(), lambda e, acc=acc, s=s: e.activation(
                out=ysq[s][:], in_=acc[:, 0:T], func=AF.Square))

            def pe_fn(e, s=s, c_=c_):
                ins = None
                for bi, (c0, c1) in enumerate(TB):
                    n = c1 - c0
                    e.matmul(banks[bi][:, 0:n], ones_bf[:], ybf[s][:, c0:c1], start=(c_ == 0), stop=(c_ == 7))
                    ins = e.matmul(banks[3 + bi][:, 0:n], ones_bf[:], ysq[s][:, c0:c1], start=(c_ == 0), stop=(c_ == 7))
                return ins
            S.op("pe", regs(ybf[s].reg(), ysq[s].reg(), ones_bf.reg()),
                 regs(*[banks[b].reg() for b in range(6)]), pe_fn)
        for bi, (c0, c1) in enumerate(TB):
            n = c1 - c0
            S.op("act", banks[bi].reg((0, n)), mean_sb.reg((c0, c1)),
                 lambda e, bi=bi, c0=c0, c1=c1, n=n: e.activation(out=mean_sb[:, c0:c1], in_=banks[bi][:, 0:n],
                                                                  func=AF.Copy, scale=1.0 / DP))
        for bi, (c0, c1) in enumerate(TB):
            n = c1 - c0
            S.op("dve", mean_sb.reg((c0, c1)), rstd_sb.reg((c0, c1)),
                 lambda e, c0=c0, c1=c1: e.tensor_tensor(out=rstd_sb[:, c0:c1], in0=mean_sb[:, c0:c1],
                                                         in1=mean_sb[:, c0:c1], op=ALU.mult))
        for bi, (c0, c1) in enumerate(TB):
            n = c1 - c0
            S.op("dve", regs(banks[3 + bi].reg((0, n)), rstd_sb.reg((c0, c1))), rstd_sb.reg((c0, c1)),
                 lambda e, bi=bi, c0=c0, c1=c1, n=n: e.scalar_tensor_tensor(
                     out=rstd_sb[:, c0:c1], in0=banks[3 + bi][:, 0:n], scalar=1.0 / DP, in1=rstd_sb[:, c0:c1],
                     op0=ALU.mult, op1=ALU.subtract))
        S.op("act", regs(rstd_sb.reg(), epsb.reg()), rstd_sb.reg(),
             lambda e: e.activation(out=rstd_sb[:], in_=rstd_sb[:], func=AF.Sqrt, bias=epsb[:, 0:1]))
        S.op("dve", rstd_sb.reg(), rstd_sb.reg(), lambda e: e.reciprocal(out=rstd_sb[:], in_=rstd_sb[:]))
        acc_rot[0] = (acc_rot[0] + 5) // 6 * 6
        def zb_part(c_):
            def ev_zb(bk, c0, c1, n, c_=c_):
                st = sgt[acc_rot[0] % 3]
                S.op("act", bk.reg((0, n)), st.reg((0, n)),
                     lambda e: e.activation(out=st[:, 0:n], in_=bk[:, 0:n], func=AF.Sigmoid))
                S.op("dve", regs(bk.reg((0, n)), st.reg((0, n))), silu_zb.reg(c_, (c0 - 32, c1 - 32)),
                     lambda e: e.tensor_tensor(out=silu_zb[:, c_, c0 - 32:c1 - 32], in0=bk[:, 0:n],
                                               in1=st[:, 0:n], op=ALU.mult))
            stage1(CH_ZB + c_, False, ev_zb)


        def norm_a(c_):
            acc = acc_of[c_]
            s = c_ % 2
            AT = acc[:, 0:T]
            S.op("dve", regs(acc.reg((0, T)), mean_sb.reg()), acc.reg((0, T)),
                 lambda e: e.tensor_tensor(out=AT, in0=AT, in1=mean_sb[:], op=ALU.subtract))
            S.op("dve", regs(acc.reg((0, T)), rstd_sb.reg()), acc.reg((0, T)),
                 lambda e: e.tensor_tensor(out=AT, in0=AT, in1=rstd_sb[:], op=ALU.mult))
            S.op("act", regs(acc.reg((0, T)), cst_r), sgl[s].reg(),
                 lambda e: e.activation(out=sgl[s][:], in_=AT, func=AF.Sigmoid, scale=LNG(c_), bias=LNB(c_)))
            S.op("act", regs(acc.reg((0, T)), cst_r), acc.reg((0, T)),
                 lambda e: e.activation(out=AT, in_=AT, func=AF.Identity, scale=LNG(c_), bias=LNB(c_)))

        def norm_b(c_):
            acc = acc_of[c_]
            s = c_ % 2
            AT = acc[:, 0:T]
            S.op("dve", regs(acc.reg((0, T)), sgl[s].reg()), acc.reg((0, T)),
                 lambda e: e.tensor_tensor(out=AT, in0=AT, in1=sgl[s][:], op=ALU.mult))
            S.op("dve", regs(acc.reg((0, T)), silu_zb.reg(c_)), silu_zb.reg(c_),
                 lambda e: e.tensor_tensor(out=silu_zb[:, c_, :], in0=AT, in1=silu_zb[:, c_, :], op=ALU.mult))

        zb_part(0)
        for c_ in range(8):
            if c_ + 1 < 8:
                zb_part(c_ + 1)
            norm_a(c_)
            if c_ >= 1:
                norm_b(c_ - 1)
        norm_b(7)
    yb_in = silu_zb

    TB = [(0, 512), (512, 1024), (1024, T)]
    SS_E = 16

    def estat_unit(i, q):
        bk = next_bank()

        def pe_fn(e):
            ins = None
            for kk in range(2):
                ins = e.matmul(bk[:, :], pT[:, i, kk, :], wple[:, kk, q * 512:(q + 1) * 512],
                               start=(kk == 0), stop=(kk == 1))
            return ins
        S.op("pe", regs(pT.reg(i), wple.reg(None, (q * 512, q * 512 + 512))), bk.reg(), pe_fn)
        col = 128 + i * 4 + q
        S.op("act", bk.reg(), regs(bk.reg(), stat.reg((col, col + 1))),
             lambda e: e.activation(out=bk[:, :], in_=bk[:, :], func=AF.Square, accum_out=stat[:, col:col + 1]))

    if stop_after not in ("A1", "A2", "A3", "LN"):
        for j in range(16):
            if j == 2 and stop_after is None:
                S.dma("pool", "ccb", [], wple.reg(),
                      lambda e: e.dma_start(out=wple[:], in_=w_ple_d.rearrange("p (k c) -> p k c", k=2)))
                for i_ in range(NT):
                    for q_ in range(4):
                        conv_q.append(lambda i_=i_, q_=q_: estat_unit(i_, q_))
                CONV_RATE[0] = 1
            s = j % 2

            def ev_gate(dstb):
                def ev(bk, c0, c1, n):
                    S.op("act", bk.reg((0, n)), dstb.reg((c0 - 32, c1 - 32)),
                         lambda e: e.activation(out=dstb[:, c0 - 32:c1 - 32], in_=bk[:, 0:n], func=AF.Sigmoid))
                return ev

            def proj(kind, src_buf, gate_buf, final, j=j):
                tA = tA2[j % 2]
                sl, wv = wget(kind, j)
                for (c0, c1) in TB:
                    n = c1 - c0
                    bk = next_bank()

                    def pe_fn(e, bk=bk, c0=c0, c1=c1, n=n):
                        ins = None
                        for k in range(8):
                            ins = e.matmul(bk[:, 0:n], wv[:, k, :],
                                           src_buf[:, k, c0:c1], start=(k == 0), stop=(k == 7))
                        return ins
                    S.op("pe", regs(sl.reg(), src_buf.reg(None, (c0, c1))), bk.reg((0, n)), pe_fn)
                    if not final:
                        S.op("dve", regs(bk.reg((0, n)), gate_buf.reg((c0, c1))), tA.reg((c0, c1)),
                             lambda e, bk=bk, c0=c0, c1=c1, n=n: e.tensor_tensor(
                                 out=tA[:, c0:c1], in0=bk[:, 0:n], in1=gate_buf[:, c0:c1], op=ALU.mult))
                    else:
                        S.op("dve", regs(bk.reg((0, n)), gate_buf.reg((c0, c1))), tB.reg((c0, c1)),
                             lambda e, bk=bk, c0=c0, c1=c1, n=n: e.tensor_tensor(
                                 out=tB[:, c0:c1], in0=bk[:, 0:n], in1=gate_buf[:, c0:c1], op=ALU.mult))
                        S.op("dve", regs(tA.reg((c0, c1)), tB.reg((c0, c1))), mT.reg(j, (c0, c1)),
                             lambda e, c0=c0, c1=c1, j=j: e.tensor_tensor(out=mT[:, j, c0:c1], in0=tA[:, c0:c1],
                                                                          in1=tB[:, c0:c1], op=ALU.add))
            stage1(CH_GA + j, False, ev_gate(sga[s]))
            proj("pp", ya_in, sga[s], False)
            stage1(CH_GB + j, False, ev_gate(sgb[s]))
            if j % 2 == 1:
                proj("pc", yb_in, sgb[0], True, j=j - 1)
                proj("pc", yb_in, sgb[1], True, j=j)

    if stop_after is None:
        S.dma("sp", "cc", [], regs(gpost.reg(), gple.reg()),
              lambda e: [e.dma_start(out=gpost[:], in_=gpost_d), e.dma_start(out=gple[:], in_=gple_d)], n=2)

        corder = [("o", G, 0) for G in range(8)] + [("g", G, 0) for G in range(8)]
        c_issued = [0]

        C_LIMIT = [10 ** 6]

        def c_issue_upto(n):
            while c_issued[0] <= min(n, len(corder) - 1, C_LIMIT[0]):
                m = c_issued[0]
                kind, G, _rep = corder[m]
                sl = wslc[m % 3]
                src = w_out_d[G] if kind == "o" else w_pg_d[G]
                S.dma("pool", f"wc{m % 3}", [], sl.reg(), lambda e, sl=sl, src=src: e.dma_start(out=sl[:], in_=src))
                c_issued[0] += 1

        def cget(kind, G, rep):
            n = corder.index((kind, G, rep))
            c_issue_upto(n + 2)
            return wslc[n % 3]

        SS_Z = 32
        conv_run(10 ** 6)
        for i in range(NT):
            S.op("dve", stat.reg((128 + i * 4, 132 + i * 4)), stat.reg((SS_E + i, SS_E + i + 1)),
                 lambda e, i=i: e.tensor_reduce(out=stat[:, SS_E + i:SS_E + i + 1], in_=stat[:, 128 + i * 4:132 + i * 4],
                                                axis=mybir.AxisListType.X, op=ALU.add))
            S.op("act", regs(stat.reg((SS_E + i, SS_E + i + 1)), epsb.reg()), stat.reg((SS_E + i, SS_E + i + 1)),
                 lambda e, i=i: e.activation(out=stat[:, SS_E + i:SS_E + i + 1], in_=stat[:, SS_E + i:SS_E + i + 1],
                                             func=AF.Sqrt, scale=1.0 / D, bias=epsb[:, 0:1]))
            S.op("dve", stat.reg((SS_E + i, SS_E + i + 1)), stat.reg((SS_E + i, SS_E + i + 1)),
                 lambda e, i=i: e.reciprocal(out=stat[:, SS_E + i:SS_E + i + 1], in_=stat[:, SS_E + i:SS_E + i + 1]))

        HA, HB = [0, 1, 2, 3, 4], [5, 6, 7, 8]

        def c1_unit(G, i, sl):
            bk = next_bank()

            def pe_fn(e):
                ins = None
                for k in range(16):
                    ins = e.matmul(bk[:, 0:256], mT[:, k, i * 128:(i + 1) * 128], sl[:, k, :],
                                   start=(k == 0), stop=(k == 15))
                return ins
            S.op("pe", regs(sl.reg(), mT.reg(None, (i * 128, i * 128 + 128))), bk.reg(), pe_fn)
            S.op("act", bk.reg(), z.reg(i, (G * 256, G * 256 + 256)),
                 lambda e: e.activation(out=z[:, i, G * 256:(G + 1) * 256], in_=bk[:, 0:256], func=AF.Copy))
            col = SS_Z + i * 8 + G
            S.op("act", bk.reg(), regs(etc_[0].reg(), stat.reg((col, col + 1))),
                 lambda e: e.activation(out=etc_[0][:], in_=bk[:, 0:256], func=AF.Square,
                                        accum_out=stat[:, col:col + 1]))

        def x1_chain(i):
            c0 = SS_Z + i * 8
            rc = 25 + (i % 2)
            S.op("dve", stat.reg((c0, c0 + 8)), stat.reg((rc, rc + 1)),
                 lambda e: e.tensor_reduce(out=stat[:, rc:rc + 1], in_=stat[:, c0:c0 + 8],
                                           axis=mybir.AxisListType.X, op=ALU.add))
            S.op("act", regs(stat.reg((rc, rc + 1)), epsb.reg()), stat.reg((rc, rc + 1)),
                 lambda e: e.activation(out=stat[:, rc:rc + 1], in_=stat[:, rc:rc + 1], func=AF.Sqrt, scale=1.0 / D,
                                        bias=epsb[:, 0:1]))
            S.op("dve", stat.reg((rc, rc + 1)), stat.reg((rc, rc + 1)),
                 lambda e: e.reciprocal(out=stat[:, rc:rc + 1], in_=stat[:, rc:rc + 1]))
            S.op("dve", regs(z.reg(i), stat.reg((rc, rc + 1)), gpost.reg()), z.reg(i),
                 lambda e: e.scalar_tensor_tensor(out=z[:, i, :], in0=z[:, i, :], scalar=stat[:, rc:rc + 1],
                                                  in1=gpost[:], op0=ALU.mult, op1=ALU.mult))
            S.dma("pool", f"xa{i}", z.reg(i), z.reg(i),
                  lambda e: e.dma_start(out=z[:, i, :], in_=x_tok[i], accum_op=ALU.add))

        def x1_cast(i):
            xbl = x1b[i % 2]
            S.op("act", z.reg(i), xbl.reg(), lambda e: e.activation(out=xbl[:], in_=z[:, i, :], func=AF.Copy))

        def x1_transposes(i):
            xbl = x1b[i % 2]
            for h in range(2):
                tb = TPB[h]
                tbv = bank_bf(tb)[:, 0:1024].rearrange("p (k t) -> p k t", k=8)

                def pe_fn(e, h=h, tbv=tbv):
                    ins = None
                    for kk in range(8):
                        k = h * 8 + kk
                        ins = e.transpose(tbv[:, kk, :], xbl[:, k * 128:(k + 1) * 128], ident_bf[:])
                    return ins
                S.op("pe", regs(xbl.reg(), ident_bf.reg()), tb.reg(), pe_fn)
                if h == 0:
                    S.op("dve", tb.reg(), mT.reg((h * 8, h * 8 + 8), (i * 128, i * 128 + 128)),
                         lambda e, h=h, tbv=tbv: e.tensor_copy(out=mT[:, h * 8:h * 8 + 8, i * 128:(i + 1) * 128], in_=tbv))
                else:
                    S.op("act", tb.reg(), mT.reg((h * 8, h * 8 + 8), (i * 128, i * 128 + 128)),
                         lambda e, h=h, tbv=tbv: e.activation(out=mT[:, h * 8:h * 8 + 8, i * 128:(i + 1) * 128], in_=tbv,
                                                              func=AF.Copy))

        def c2_unit(G, i, sl):
            bk = next_bank()
            bk2 = next_bank()
            s = i % 2

            def pe_fn(e):
                ins = None
                for k in range(16):
                    ins = e.matmul(bk[:, 0:256], mT[:, k, i * 128:(i + 1) * 128], sl[:, k, :],
                                   start=(k == 0), stop=(k == 15))
                return ins
            S.op("pe", regs(sl.reg(), mT.reg(None, (i * 128, i * 128 + 128))), bk.reg(), pe_fn)

            def pe_fn2(e):
                ins = None
                for kk in range(2):
                    ins = e.matmul(bk2[:, 0:256], pT[:, i, kk, :], wple[:, kk, G * 256:(G + 1) * 256],
                                   start=(kk == 0), stop=(kk == 1))
                return ins
            S.op("pe", regs(pT.reg(i), wple.reg(None, (G * 256, G * 256 + 256))), bk2.reg(), pe_fn2)
            S.op("act", bk.reg(), sgc[s].reg(),
                 lambda e: e.activation(out=sgc[s][:], in_=bk[:, 0:256], func=AF.Sigmoid))
            S.op("dve", regs(bk2.reg(), stat.reg((SS_E + i, SS_E + i + 1)), gple.reg((G * 256, G * 256 + 256))),
                 etc_[s].reg(),
                 lambda e: e.scalar_tensor_tensor(
                     out=etc_[s][:], in0=bk2[:, 0:256], scalar=stat[:, SS_E + i:SS_E + i + 1],
                     in1=gple[:, G * 256:(G + 1) * 256], op0=ALU.mult, op1=ALU.mult))
            S.op("dve", regs(etc_[s].reg(), sgc[s].reg()), etc_[s].reg(),
                 lambda e: e.tensor_tensor(out=etc_[s][:], in0=etc_[s][:], in1=sgc[s][:], op=ALU.mult))
            S.op("dve", regs(etc_[s].reg(), z.reg(i, (G * 256, G * 256 + 256))), z.reg(i, (G * 256, G * 256 + 256)),
                 lambda e: e.tensor_tensor(out=z[:, i, G * 256:(G + 1) * 256],
                                           in0=z[:, i, G * 256:(G + 1) * 256], in1=etc_[s][:], op=ALU.add))
            S.dma("sp", "out", z.reg(i, (G * 256, G * 256 + 256)), [],
                  lambda e: e.dma_start(out=y_tok[i, :, G * 256:(G + 1) * 256],
                                        in_=z[:, i, G * 256:(G + 1) * 256]), final=True)

        ALLT = list(range(NT))
        for G in range(6):
            sl = cget("o", G, 0)
            for i in ALLT:
                c1_unit(G, i, sl)
        C_LIMIT[0] = 8
        sl6 = cget("o", 6, 0)
        sl7 = cget("o", 7, 0)
        slg0 = cget("g", 0, 0)
        done0 = []
        for st_ in range(NT + 3):
            if st_ < NT:
                c1_unit(6, st_, sl6)
                c1_unit(7, st_, sl7)
                x1_chain(st_)
            else:
                for i_ in (2 * (st_ - NT), 2 * (st_ - NT) + 1):
                    c2_unit(0, i_, slg0)
                    done0.append(i_)
            if 0 <= st_ - 3 < NT:
                x1_transposes(st_ - 3)
            if 0 <= st_ - 2 < NT:
                x1_cast(st_ - 2)
        C_LIMIT[0] = 10 ** 6
        for i in ALLT:
            if i not in done0:
                c2_unit(0, i, slg0)
        for G in range(1, 8):
            sl = cget("g", G, 0)
            for i in ALLT:
                c2_unit(G, i, sl)

    LB = dict(locals())
    for (name, buf, shape) in dbg:
        bufobj = LB[buf] if isinstance(buf, str) else buf
        if isinstance(bufobj, (list, tuple)):
            for bi_, b_ in enumerate(bufobj):
                dt_ = dram_out(f"dbg_{name}{bi_}", [128] + list(shape))
                S.dma("sp" if b_.esz == 4 else "pool", "dbgo", b_.reg(), [],
                      lambda e, dt_=dt_, b_=b_: e.dma_start(out=dt_, in_=b_[:]), final=True)
            continue
        dt_ = dram_out("dbg_" + name, [128] + list(shape))
        S.dma("pool", "dbgo", bufobj.reg(), [], lambda e, dt_=dt_, bufobj=bufobj: e.dma_start(out=dt_, in_=bufobj[:]),
              final=True)

    fin = dict(S.final_waits)
    S.prog["sp"].append(([(s, v) for s, v in fin.items()], None, None))

    semnames = sorted(S.semnames)
    sems = {}
    import contextlib
    with contextlib.ExitStack() as es:
        for sname in semnames:
            sems[sname] = es.enter_context(nc.semaphore(sname))
        block = es.enter_context(nc.Block())

        S.finalize()

        def run(e, engname):
            for waits, fn, inc in S.prog[engname]:
                for (s_, v) in waits:
                    e.wait_ge(sems[s_], S.wait_value(s_, v))
                if fn is None:
                    continue
                ins = fn(e)
                sem_, val_, is_dma = inc
                if is_dma:
                    if isinstance(ins, (list, tuple)):
                        for i_ in ins:
                            i_.then_inc(sems[sem_], 16)
                    else:
                        ins.then_inc(sems[sem_], 16)
                elif (sem_, val_) in S.rank:
                    ins.then_inc(sems[sem_], 1)

        @block.sync
        def _(e):
            run(e, "sp")

        @block.gpsimd
        def _(e):
            run(e, "pool")

        @block.scalar
        def _(e):
            run(e, "act")

        @block.vector
        def _(e):
            run(e, "dve")

        @block.tensor
        def _(e):
            run(e, "pe")
    return nc


def _prep_shared(inp):
    f = np.float32
    sh = {}
    w_in = np.asarray(inp["w_in"][0], f)
    sh["w_in_t"] = np.ascontiguousarray(w_in.reshape(16, 128, 72, 128).transpose(2, 1, 0, 3))
    w_pool = np.asarray(inp["w_pool"][0], f)
    sh["w_pool_t"] = np.ascontiguousarray(w_pool.reshape(4, 2, 128, 256).transpose(2, 0, 1, 3)).reshape(128, 2048)
    for nm, key in (("w_pp_t", "w_proj_pool"), ("w_pc_t", "w_proj_conv")):
        w = np.asarray(inp[key][0], f)
        sh[nm] = np.ascontiguousarray(w.reshape(8, 128, 16, 128).transpose(2, 1, 0, 3))
    for nm, key in (("w_out_t", "w_out"), ("w_pg_t", "w_ple_gate")):
        w = np.asarray(inp[key][0], f)
        sh[nm] = np.ascontiguousarray(w.reshape(16, 128, 8, 256).transpose(2, 1, 0, 3))
    w_ple = np.asarray(inp["w_ple"][0], f)
    sh["w_ple_t"] = np.ascontiguousarray(w_ple.reshape(2, 128, 2048).transpose(1, 0, 2)).reshape(128, 4096)
    sh["g_post_bc"] = np.ascontiguousarray(np.broadcast_to(np.asarray(inp["g_post"][0], f)[None, :], (128, D)))
    sh["g_ple_bc"] = np.ascontiguousarray(np.broadcast_to(np.asarray(inp["g_ple"][0], f)[None, :], (128, D)))
    sh["ident"] = np.eye(128, dtype=f)
    cst = np.zeros((128, 360), f)
    cst[:, 0:16] = np.asarray(inp["g_pre"][0], f).reshape(16, 128).T
    cst[:, 16:24] = np.asarray(inp["pool_scale"][0], f).reshape(8, 128).T
    cst[:, 24:32] = np.asarray(inp["b_dw"][0], f).reshape(8, 128).T
    cst[:, 32:40] = np.asarray(inp["ln_g"][0], f).reshape(8, 128).T
    cst[:, 40:48] = np.asarray(inp["ln_b"][0], f).reshape(8, 128).T
    wdw = np.asarray(inp["w_dw"][0], f)
    cst[:, 112:360] = wdw.reshape(31, 8, 128).transpose(2, 1, 0).reshape(128, 248)
    return sh, cst


def _in_maps(inp):
    f = np.float32
    sh, cst0 = _prep_shared(inp)
    xp = np.asarray(inp["x_prompt"], f)
    xsamp = np.asarray(inp["x_sample"], f)
    pp = np.asarray(inp["p_prompt"], f)[0]
    ps = np.asarray(inp["p_sample"], f)[0]
    stc = np.asarray(inp["state_conv"], f)[0]
    stp = np.asarray(inp["state_pool"], f)[0]
    maps = []
    for r in range(NCORES):
        b, half = r // 2, r % 2
        st = half * 1024
        m = dict(sh)
        xt = np.empty((NT, 128, D), f)
        xt[0:8] = xp[b, st:st + 1024].reshape(8, 128, D)
        xt[8] = xsamp[16 * r:16 * r + 16].reshape(128, D)
        m["x_tok"] = xt
        m["x_halo"] = np.ascontiguousarray(xp[b, st - 32:st]) if half else np.zeros((32, D), f)
        pt = np.empty((NT, 128, 256), f)
        pt[0:8] = pp[b, st:st + 1024].reshape(8, 128, 256)
        pt[8] = ps[16 * r:16 * r + 16].reshape(128, 256)
        m["p_tok"] = pt
        m["st_conv"] = np.ascontiguousarray(stc[16 * r:16 * r + 16])
        m["st_pool"] = np.ascontiguousarray(stp[16 * r:16 * r + 16])
        cst = cst0.copy()
        for g, wd in enumerate((2, 4, 8, 16)):
            pos = st + np.arange(16)
            cst[:, 48 + g * 16:64 + g * 16] = (1.0 / np.minimum(pos + 1, wd)).astype(f)[None, :]
        m["cst"] = cst
        maps.append(m)
    return maps


_NC_CACHE = {}


def kernel(**inputs):
    if "nc" not in _NC_CACHE:
        _NC_CACHE["nc"] = build_program()
    nc = _NC_CACHE["nc"]
    maps = _in_maps(inputs)
    res = run_bass_kernel_spmd(nc, maps, core_ids=list(range(NCORES)))
    R = res.results
    f = np.float32
    y_prompt = np.empty((4, 2048, D), f)
    y_sample = np.empty((128, 8, D), f)
    npp = np.empty((1, 4, 15, DP), f)
    ncp = np.empty((1, 4, 30, DP), f)
    nps = np.empty((1, 128, 15, DP), f)
    ncs = np.empty((1, 128, 30, DP), f)
    for r in range(NCORES):
        b, half = r // 2, r % 2
        st = half * 1024
        yt = R[r]["y_tok"]
        y_prompt[b, st:st + 1024] = yt[0:8].reshape(1024, D)
        y_sample[16 * r:16 * r + 16] = yt[8].reshape(16, 8, D)
        ncs[0, 16 * r:16 * r + 16, 0:22] = R[r]["ncs_old"]
        ncs[0, 16 * r:16 * r + 16, 22:30] = R[r]["ncs_new"].reshape(16, 8, DP)
        nps[0, 16 * r:16 * r + 16, 0:7] = R[r]["nps_old"]
        nps[0, 16 * r:16 * r + 16, 7:15] = R[r]["nps_new"].reshape(16, 8, DP)
        if half:
            ncp[0, b] = R[r]["ncp"][2:32]
            npp[0, b] = R[r]["npp"][17:32]
    return (y_prompt, y_sample, npp, ncp, nps, ncs)
```

```python
import numpy as np
import concourse.bass as bass
import concourse.mybir as mybir
from concourse.bass_utils import run_bass_kernel_spmd

F32 = mybir.dt.float32
BF16 = mybir.dt.bfloat16
AF = mybir.ActivationFunctionType
ALU = mybir.AluOpType

NCORES = 8
D = 2048
DP = 1024
NT = 9
W = 1184
T = 1152
EPS = 1e-6
SB_BASE = 16512
SB_END = 229376

CH_U, CH_ZA, CH_A, CH_B, CH_ZB, CH_GA, CH_GB = 0, 8, 16, 24, 32, 40, 56


class Sched:
    ENGS = ("pe", "act", "dve", "pool", "sp")
    GR = 1024

    def __init__(self):
        self.prog = {e: [] for e in self.ENGS}
        self.cnt = {}
        self.waited = {e: {} for e in self.ENGS}
        self.recs = {}
        self.buckets = {}
        self.semnames = set()
        self.final_waits = {}
        self.needed = set()

    def _granules(self, space, lo, hi):
        return [(space, g) for g in range(lo // self.GR, (hi - 1) // self.GR + 1)]

    def _add(self, key, val):
        old = self.recs.get(key)
        if old is None:
            for b in self._granules(key[0], key[1], key[2]):
                self.buckets.setdefault(b, set()).add(key)
        if old is None or old < val:
            self.recs[key] = val

    def _remove(self, key):
        del self.recs[key]
        for b in self._granules(key[0], key[1], key[2]):
            self.buckets[b].discard(key)

    def _overlaps(self, space, lo, hi):
        seen = set()
        for b in self._granules(space, lo, hi):
            for key in self.buckets.get(b, ()):
                if key in seen:
                    continue
                if key[1] < hi and lo < key[2]:
                    seen.add(key)
        return seen

    def _collect(self, eng_sem, is_pe, reads, writes):
        deps = {}
        cur = self.cnt.get(eng_sem, 0)

        def need(sem, val):
            if deps.get(sem, 0) < val:
                deps[sem] = val

        for (space, lo, hi) in reads:
            for key in self._overlaps(space, lo, hi):
                sem = key[3]
                if not key[4]:
                    if space == "ps" and sem != eng_sem:
                        need(sem, self.recs[key])
                    continue
                if sem == eng_sem:
                    if is_pe:
                        continue
                need(sem, self.recs[key])
        for (space, lo, hi) in writes:
            for key in self._overlaps(space, lo, hi):
                sem = key[3]
                if sem == eng_sem:
                    continue
                need(sem, self.recs[key])
        return deps

    def _record(self, sem, val, reads, writes):
        for (space, lo, hi) in writes:
            for key in list(self._overlaps(space, lo, hi)):
                if key[1] >= lo and key[2] <= hi:
                    self._remove(key)
            self._add((space, lo, hi, sem, True), val)
        for (space, lo, hi) in reads:
            self._add((space, lo, hi, sem, False), val)

    def _waits(self, eng, deps):
        out = []
        for sem, val in deps.items():
            if self.waited[eng].get(sem, 0) < val:
                self.waited[eng][sem] = val
                out.append((sem, val))
                self.needed.add((sem, val))
        return out

    def op(self, eng, reads, writes, fn):
        sem = "E_" + eng
        self.semnames.add(sem)
        deps = self._collect(sem, eng == "pe", reads, writes)
        waits = self._waits(eng, deps)
        val = self.cnt.get(sem, 0) + 1
        self.cnt[sem] = val
        self.prog[eng].append((waits, fn, (sem, val, False)))
        self._record(sem, val, reads, writes)

    def dma(self, queue, sem, reads, writes, fn, final=False, n=1):
        self.semnames.add(sem)
        deps = self._collect(sem, False, reads, writes)
        deps.pop(sem, None)
        waits = self._waits(queue, deps)
        val = self.cnt.get(sem, 0) + 16 * n
        self.cnt[sem] = val
        self.prog[queue].append((waits, fn, (sem, val, True)))
        self._record(sem, val, reads, writes)
        if final:
            self.final_waits[sem] = val

    def finalize(self):
        self.rank = {}
        by_sem = {}
        for (sem, val) in self.needed:
            if sem.startswith("E_"):
                by_sem.setdefault(sem, []).append(val)
        for sem, vals in by_sem.items():
            for r, v in enumerate(sorted(vals)):
                self.rank[(sem, v)] = r + 1

    def wait_value(self, sem, val):
        return self.rank[(sem, val)] if sem.startswith("E_") else val


class Buf:
    def __init__(self, t, space, addr, free_shape, esz):
        self.t = t
        self.space = space
        self.addr = addr
        self.shape = tuple(free_shape)
        self.esz = esz

    def __getitem__(self, k):
        return self.t[k]

    def reg(self, *idx):
        shp = self.shape
        idx = list(idx) + [None] * (len(shp) - len(idx))
        rngs = []
        for d, i in enumerate(idx):
            if i is None:
                rngs.append((0, shp[d]))
            elif isinstance(i, int):
                rngs.append((i, i + 1))
            else:
                rngs.append(i)
        strides = [1] * len(shp)
        for d in range(len(shp) - 2, -1, -1):
            strides[d] = strides[d + 1] * shp[d + 1]
        nd = len(shp)
        cut = nd
        while cut > 1 and rngs[cut - 1] == (0, shp[cut - 1]):
            cut -= 1
        out = []

        if self.space == "ps":
            return [("ps", self.addr, self.addr + 2048)]

        def rec(d, off):
            if d == cut - 1:
                lo = off + rngs[d][0] * strides[d]
                hi = off + rngs[d][1] * strides[d]
                out.append((self.space, self.addr + lo * self.esz, self.addr + hi * self.esz))
                return
            for i in range(rngs[d][0], rngs[d][1]):
                rec(d + 1, off + i * strides[d])

        rec(0, 0)
        return out


class Arena:
    def __init__(self, nc):
        self.nc = nc
        self.n = 0

    def at(self, name, addr, free_shape, dt, parts=128):
        esz = 4 if dt == F32 else 2
        size = int(np.prod(free_shape)) * esz
        assert addr % 32 == 0, (name, addr)
        assert SB_BASE <= addr and addr + size <= SB_END, (name, addr, size)
        self.n += 1
        t = self.nc.alloc_sbuf_tensor_at(f"{name}_{self.n}", [parts] + list(free_shape), dt, offset=addr)
        return Buf(t, "sb", addr, free_shape, esz)


def build_program(stop_after=None, dbg=None):
    nc = bass.Bass("TRN2", target_bir_lowering=False)
    S = Sched()
    A = Arena(nc)
    dbg = dbg or []

    def dram_in(name, shape):
        return nc.dram_tensor(name, list(shape), F32, kind="ExternalInput").ap()

    def dram_out(name, shape):
        return nc.dram_tensor(name, list(shape), F32, kind="ExternalOutput").ap()

    x_tok = dram_in("x_tok", [NT, 128, D])
    x_halo = dram_in("x_halo", [32, D])
    p_tok = dram_in("p_tok", [NT, 128, 256])
    st_conv = dram_in("st_conv", [16, 30, DP])
    st_pool = dram_in("st_pool", [16, 15, DP])
    cst_d = dram_in("cst", [128, 360])
    ident_d = dram_in("ident", [128, 128])
    w_in_d = dram_in("w_in_t", [72, 128, 16, 128])
    w_pool_d = dram_in("w_pool_t", [128, 4 * 2 * 256])
    w_pp_d = dram_in("w_pp_t", [16, 128, 8, 128])
    w_pc_d = dram_in("w_pc_t", [16, 128, 8, 128])
    w_out_d = dram_in("w_out_t", [8, 128, 16, 256])
    w_pg_d = dram_in("w_pg_t", [8, 128, 16, 256])
    w_ple_d = dram_in("w_ple_t", [128, 2 * D])
    gpost_d = dram_in("g_post_bc", [128, D])
    gple_d = dram_in("g_ple_bc", [128, D])

    y_tok = dram_out("y_tok", [NT, 128, D])
    ncs_new = dram_out("ncs_new", [128, DP])
    ncs_old = dram_out("ncs_old", [16, 22, DP])
    nps_new = dram_out("nps_new", [128, DP])
    nps_old = dram_out("nps_old", [16, 7, DP])
    ncp_o = dram_out("ncp", [32, DP])
    npp_o = dram_out("npp", [32, DP])
    dbg_out = {}

    a = SB_BASE
    C0 = a
    ident_bf = A.at("ident_bf", a, [128], BF16); a += 256
    ident_f = A.at("ident_f", a, [128], F32); a += 512
    ones_bf = A.at("ones_bf", a, [128], BF16); a += 256
    cst = A.at("cst", a, [360], F32); a += 1440
    wpool = A.at("wpool", a, [4, 2, 256], BF16); a += 4096
    stat = A.at("stat", a, [192], F32); a += 768
    pT = A.at("pT", a, [NT, 2, 128], BF16); a += NT * 2 * 128 * 2
    exs = A.at("exs", a, [2, 16, 23], F32); a += 2 * 16 * 23 * 4
    tmpf = A.at("tmpf", a, [16], F32); a += 64
    negh = A.at("negh", a, [8], F32); a += 32
    epsb = A.at("epsb", a, [8], F32); a += 32
    a = (a + 31) // 32 * 32
    R1 = a
    hT = A.at("hT", R1, [16, W], BF16); a += 16 * W * 2
    R2 = a
    vacc = [A.at(f"vacc{i}", R2 + i * W * 4, [W], F32) for i in range(8)]
    vall = A.at("vall", R2, [10, W], F32)
    a += 10 * W * 4
    R3 = a
    mT = A.at("mT", R3, [16, T], BF16); a += 16 * T * 2
    R4 = a
    ext_s = A.at("ext_s", R4, [8, 16, 38], BF16); a += 8 * 16 * 38 * 4
    v_bf = [A.at(f"v_bf{i}", R4 + 9728 + i * 2368, [W], BF16) for i in range(2)]
    vt = [A.at(f"vt{i}", R4 + 9728 + 4736 + i * 640, [160], F32) for i in range(2)]
    dgA = A.at("dgA", R2 + 8 * W * 4, [16, 128], BF16)
    dgB = A.at("dgB", R2 + 8 * W * 4 + 4096, [16, 128], BF16)
    wdwc = A.at("wdwc", R2 + 8 * W * 4 + 8192, [32], F32)
    dgA2 = A.at("dgA2", R3, [16, 128], BF16)
    dgB2 = A.at("dgB2", R3 + 4096, [16, 128], BF16)
    wdwc2 = A.at("wdwc2", R3 + 8192, [32], F32)
    R4b = a
    ya_in = A.at("ya_in", R4b, [8, T], BF16); a += 8 * T * 2
    R5 = a
    wsl = [A.at(f"wsl{i}", R5 + i * 4096, [2048], BF16) for i in range(4)]
    a += 4 * 4096
    R6 = a
    a += 18432
    assert a <= SB_END, a

    xs = [A.at(f"xs{i}", R3 + i * 8192, [D], F32) for i in range(3)]
    xs += [A.at(f"xs{3 + i}", R4b + i * 8192, [D], F32) for i in range(2)]
    pall = A.at("pall", R4b, [NT, 256], BF16)
    xb = [A.at(f"xb{i}", R3 + 24576 + i * 4096, [D], BF16) for i in range(2)]
    sqj = A.at("sqj", R3 + 32768, [D], BF16)
    silu_za = A.at("silu_za", R3, [4, T], BF16)
    pooled = A.at("pooled", R3 + 9216, [4, T], BF16)
    ubuf = A.at("ubuf", R3 + 18432, [W], F32)
    uscr = [A.at(f"uscr{i}", R3 + 18432 + (i + 1) * W * 4, [W], F32) for i in range(2)]
    assert 18432 + 3 * W * 4 <= 16 * T * 2
    ybf = [A.at(f"ybf{i}", R3 + i * 2304, [T], BF16) for i in range(2)]
    ysq = [A.at(f"ysq{i}", R3 + 4608 + i * 2304, [T], BF16) for i in range(2)]
    mean_sb = A.at("mean", R3 + 9216, [T], F32)
    rstd_sb = A.at("rstd", R3 + 9216 + 4608, [T], F32)
    sgl = [A.at(f"sgl{i}", R3 + 18432 + i * 4608, [T], F32) for i in range(2)]
    silu_zb = A.at("silu_zb", R4, [8, T], BF16)
    ext_u = A.at("ext_u", R6, [8, 16, 23], F32)
    sgt = [A.at(f"sgt{i}", R6 + 11776 + i * 2048, [512], F32) for i in range(3)]
    stg = A.at("stg", R6 + 11776, [DP], F32)
    stgs = [A.at(f"stgs{i}", R3 + 12288 + i * 4096, [DP], F32) for i in range(6)]
    stg_n = [0]
    sga = [A.at(f"sga{i}", R6 + i * 2304, [T], BF16) for i in range(2)]
    sgb = [A.at(f"sgb{i}", R6 + 4608 + i * 2304, [T], BF16) for i in range(2)]
    tA = A.at("tA", R6 + 9216, [T], F32)
    tA2 = [tA, A.at("tA1", R2, [T], F32)]
    tB = A.at("tB", R6 + 9216 + 4608, [T], F32)
    TAIL = a
    so_s = [A.at(f"so_s{i}", TAIL + i * 512, [128], F32) for i in range(2)]
    so_p = [A.at(f"so_p{i}", TAIL + 1024 + i * 512, [128], F32, parts=32) for i in range(2)]
    assert TAIL + 2048 <= SB_END
    z = A.at("z", R1, [NT, D], F32)
    CZ = R1 + NT * D * 4
    CZ = (CZ + 31) // 32 * 32
    wple = A.at("wple", CZ, [2, D], BF16)
    assert CZ + 8192 <= R3
    c = R4
    wslc = [None, A.at("wslc1", c, [16, 256], BF16), A.at("wslc2", c + 8192, [16, 256], BF16)]; c += 2 * 8192
    x1b = [A.at(f"x1b{i}", c + i * 4096, [D], BF16) for i in range(2)]; c += 8192
    assert c == R4b + 5120
    wslc[0] = A.at("wslc0", c, [16, 256], BF16); c += 8192
    gpost = A.at("gpost", c, [D], F32); c += 8192
    gple = A.at("gple", c, [D], F32); c += 8192
    sgc = [A.at(f"sgc{i}", c + i * 1024, [256], F32) for i in range(2)]; c += 2048
    etc_ = [A.at(f"etc{i}", c + i * 1024, [256], F32) for i in range(2)]; c += 2048
    ptile = A.at("ptile", c, [256], F32); c += 1024
    pbt = A.at("pbt", c, [256], BF16); c += 512
    assert c <= SB_END, c

    banks = []
    for i in range(8):
        t = nc.alloc_psum_tensor(f"bank{i}", [128, 512], F32)
        banks.append(Buf(t, "ps", i * 2048, [512], 4))
    acc_rot = [0]

    def next_bank():
        b = banks[acc_rot[0] % 6]
        acc_rot[0] += 1
        return b

    TPB = (banks[6], banks[7])

    def bank_bf(b):
        return b.t.bitcast(BF16)

    def regs(*lists):
        out = []
        for l in lists:
            out.extend(l)
        return out

    S.dma("sp", "cst", [], regs(cst.reg(), ident_f.reg()),
          lambda e: [e.dma_start(out=cst[:], in_=cst_d), e.dma_start(out=ident_f[:], in_=ident_d)], n=2)
    S.dma("pool", "cstb", [], ident_bf.reg(), lambda e: e.dma_start(out=ident_bf[:], in_=ident_d))
    S.op("dve", [], ones_bf.reg(), lambda e: e.memset(ones_bf[:], 1.0))
    S.op("dve", [], negh.reg(), lambda e: e.memset(negh[:], -0.5))
    S.op("dve", [], epsb.reg(), lambda e: e.memset(epsb[:], EPS))
    S.op("dve", [], stat.reg(), lambda e: e.memset(stat[:], 0.0))

    GPRE = lambda k0, k1: cst[:, k0:k1]
    PSC = lambda d: cst[:, 16 + d:17 + d]
    BDW = lambda c_: cst[:, 24 + c_:25 + c_]
    LNG = lambda c_: cst[:, 32 + c_:33 + c_]
    LNB = lambda c_: cst[:, 40 + c_:41 + c_]
    INVC = lambda g: cst[:, 48 + g * 16:64 + g * 16]
    WDW = lambda c_, k: cst[:, 112 + c_ * 31 + k:113 + c_ * 31 + k]
    cst_r = cst.reg()

    wlist = []

    def w_in_src(ch):
        return (w_in_d[ch], lambda sl: sl.t[:].rearrange("p (k c) -> p k c", k=16), [16, 128])

    def w_pr_src(dram, G):
        return (dram[G], lambda sl: sl.t[:].rearrange("p (k c) -> p k c", k=8), [8, 256])

    order = []
    for c_ in range(8):
        order.append(("in", CH_B + c_)); order.append(("in", CH_A + c_))
    for g in range(4):
        order += [("in", CH_U + 2 * g), ("in", CH_ZA + 2 * g), ("in", CH_U + 2 * g + 1), ("in", CH_ZA + 2 * g + 1)]
    for c_ in range(8):
        order.append(("in", CH_ZB + c_))
    for jp in range(0, 16, 2):
        for j in (jp, jp + 1):
            order.append(("in", CH_GA + j))
            order.append(("pp", j))
            order.append(("in", CH_GB + j))
        order.append(("pc", jp))
        order.append(("pc", jp + 1))
    wpos = {}
    for n, it in enumerate(order):
        wpos[it] = n
    w_issued = [0]

    W_LIMIT = [1]

    def w_issue_upto(n):
        while w_issued[0] <= min(n, len(order) - 1, W_LIMIT[0]):
            m = w_issued[0]
            kind, idx = order[m]
            sl = wsl[m % 4]
            if kind == "in":
                src, view = w_in_d[idx], sl.t[:].rearrange("p (k c) -> p k c", k=16)
            elif kind == "pp":
                src, view = w_pp_d[idx], sl.t[:, 0:1024].rearrange("p (k c) -> p k c", k=8)
            else:
                src, view = w_pc_d[idx], sl.t[:, 0:1024].rearrange("p (k c) -> p k c", k=8)
            S.dma("pool", f"w{m % 4}", [], sl.reg(),
                  lambda e, view=view, src=src: e.dma_start(out=view, in_=src))
            w_issued[0] += 1

    def wget(kind, idx):
        n = wpos[(kind, idx)]
        w_issue_upto(n + 3)
        sl = wsl[n % 4]
        if kind == "in":
            return sl, sl.t[:].rearrange("p (k c) -> p k c", k=16)
        return sl, sl.t[:, 0:1024].rearrange("p (k c) -> p k c", k=8)

    conv_q = []

    def conv_run(n):
        for _ in range(n):
            if not conv_q:
                return
            conv_q.pop(0)()

    def after_unit():
        conv_run(CONV_RATE[0])

    CONV_RATE = [0]

    xs_n = [0]

    def stage0(src_ap, nrows, col0, statcol):
        s = xs_n[0] % 5
        sb = xs_n[0] % 2
        xs_n[0] += 1
        xsl, xbl = xs[s], xb[sb]
        S.dma("sp", f"xs{s}", [], xsl.reg(), lambda e: e.dma_start(out=xsl[0:nrows, :], in_=src_ap))
        S.op("act", xsl.reg(), regs(sqj.reg(), stat.reg((statcol, statcol + 1))),
             lambda e: e.activation(out=sqj[0:nrows, :], in_=xsl[0:nrows, :], func=AF.Square,
                                    accum_out=stat[0:nrows, statcol:statcol + 1]))
        sc_ = stat[0:nrows, statcol:statcol + 1]
        sr_ = stat.reg((statcol, statcol + 1))
        S.op("act", regs(sr_, epsb.reg()), sr_,
             lambda e: e.activation(out=sc_, in_=sc_, func=AF.Sqrt, scale=1.0 / D, bias=epsb[0:nrows, 0:1]))
        S.op("dve", sr_, sr_, lambda e: e.reciprocal(out=sc_, in_=sc_))
        S.op("dve", regs(xsl.reg(), sr_), xbl.reg(),
             lambda e: e.tensor_scalar(out=xbl[0:nrows, :], in0=xsl[0:nrows, :], scalar1=sc_, scalar2=None, op0=ALU.mult))
        evs = []
        for h in range(2):
            tb = next_bank()
            tbv = bank_bf(tb)[:, 0:8 * nrows].rearrange("p (k t) -> p k t", k=8)

            def pe_fn(e, h=h, tbv=tbv):
                ins = None
                for kk in range(8):
                    k = h * 8 + kk
                    ins = e.transpose(tbv[:, kk, :], xbl[0:nrows, k * 128:(k + 1) * 128], ident_bf[0:nrows, 0:nrows])
                return ins
            S.op("pe", regs(xbl.reg(), ident_bf.reg()), tb.reg(), pe_fn)

            def ev(h=h, tb=tb, tbv=tbv):
                S.op("dve", regs(tb.reg(), cst_r), hT.reg((h * 8, h * 8 + 8), (col0, col0 + nrows)),
                     lambda e: e.tensor_tensor(
                         out=hT[:, h * 8:h * 8 + 8, col0:col0 + nrows], in0=tbv,
                         in1=GPRE(h * 8, h * 8 + 8).unsqueeze(2).broadcast_to([128, 8, nrows]), op=ALU.mult))
            evs.append(ev)
        return evs

    def state_T(src_dram, nb, nr, ext, j):
        si_ = stg_n[0]
        stg_n[0] += 1
        xsl = stgs[si_]
        rows = nb * nr
        src = src_dram[j * nb:(j + 1) * nb].rearrange("b r c -> (b r) c")
        S.dma("sp", f"stg{si_}", [], xsl.reg((0, DP)), lambda e: e.dma_start(out=xsl[0:rows, 0:DP], in_=src))
        for h in range(2):
            tb = TPB[h]

            def pe_fn(e, h=h, tb=tb):
                ins = None
                for cc in range(4):
                    c_ = h * 4 + cc
                    ins = e.transpose(tb[:, cc * rows:(cc + 1) * rows], xsl[0:rows, c_ * 128:(c_ + 1) * 128],
                                      ident_f[0:rows, 0:rows])
                return ins
            S.op("pe", regs(xsl.reg((0, DP)), ident_f.reg()), tb.reg(), pe_fn)
            S.op("act", tb.reg(), ext.reg((h * 4, h * 4 + 4)),
                 lambda e, h=h, tb=tb: e.activation(
                     out=ext[:, h * 4:h * 4 + 4, j * nb:(j + 1) * nb, 0:nr],
                     in_=tb[:, 0:4 * rows].rearrange("p (c b r) -> p c b r", c=4, b=nb), func=AF.Copy))

    def p_load():
        S.dma("pool", "pall", [], pall.reg(), lambda e: e.dma_start(out=pall[:], in_=p_tok.rearrange("i p c -> p i c")))

    def p_T(i):
        tb = TPB[i % 2]
        tbv = bank_bf(tb)[:, 0:256].rearrange("p (k t) -> p k t", k=2)

        def pe_fn(e):
            ins = None
            for kk in range(2):
                ins = e.transpose(tbv[:, kk, :], pall[:, i, kk * 128:(kk + 1) * 128], ident_bf[:])
            return ins
        S.op("pe", regs(pall.reg(i), ident_bf.reg()), tb.reg(), pe_fn)
        S.op("dve", tb.reg(), pT.reg(i), lambda e: e.tensor_copy(out=pT[:, i, :, :], in_=tbv))

    BLK_H = [(0, 512), (512, 1024), (1024, W)]
    BLK_N = [(32, 544), (544, 1056), (1056, W)]

    def stage1(ch, halo, evac, blocks=(0, 1, 2)):
        sl, wv = wget("in", ch)
        for (c0, c1) in [(BLK_H if halo else BLK_N)[b_] for b_ in blocks]:
            n = c1 - c0
            bk = next_bank()

            def pe_fn(e, bk=bk, c0=c0, c1=c1, n=n):
                ins = None
                for k in range(16):
                    ins = e.matmul(bk[:, 0:n], wv[:, k, :], hT[:, k, c0:c1], start=(k == 0), stop=(k == 15))
                return ins
            S.op("pe", regs(sl.reg(), hT.reg(None, (c0, c1))), bk.reg((0, n)), pe_fn)
            evac(bk, c0, c1, n)
            after_unit()

    acc_of = {c_: vacc[c_] for c_ in range(8)}
    so_n = [0]
    CB = [(0, 512, 2), (512, 1024, 514), (1024, T, None)]

    def dg_set(c_):
        return (dgA2, dgB2, wdwc2) if c_ == 7 else (dgA, dgB, wdwc)

    def conv_prep(c_, slot):
        dA_, dB_, wc_ = dg_set(c_)
        S.op("dve", cst_r, wc_.reg(), lambda e: e.tensor_copy(out=wc_[:, 0:31], in_=cst[:, 112 + c_ * 31:143 + c_ * 31]))
        for (dg, k0, nk) in ((dA_, 0, 16), (dB_, 16, 15)):
            S.op("dve", regs(ident_bf.reg(), wc_.reg()), dg.reg((0, nk)),
                 lambda e, dg=dg, k0=k0, nk=nk: e.tensor_tensor(
                     out=dg[:, 0:nk, :], in0=ident_bf[:, :].unsqueeze(1).broadcast_to([128, nk, 128]),
                     in1=wc_[:, k0:k0 + nk].unsqueeze(2).broadcast_to([128, nk, 128]), op=ALU.mult))

    def conv_pe(c_, slot):
        vb, vtl, acc = v_bf[slot], vt[slot], vacc[c_]
        dgA, dgB, _ = dg_set(c_)
        S.op("act", vb.reg((1056, W)), ext_s.reg(c_),
             lambda e: e.activation(out=ext_s[:, c_, :, 30:38],
                                    in_=vb[:, 1056:W].rearrange("p (b t) -> p b t", b=16), func=AF.Copy))
        tb = TPB[c_ % 2]

        def pe_fn(e):
            e.transpose(tb[:, 0:128], vtl[:, 32:160], ident_f[:])
            return e.transpose(tb[0:32, 128:256], vtl[:, 0:32], ident_f[:])
        S.op("pe", regs(vtl.reg(), ident_f.reg()), tb.reg(), pe_fn)
        sl_ = so_n[0] % 2
        so_n[0] += 1
        ss_, sp_ = so_s[sl_], so_p[sl_]
        S.op("act", tb.reg(), ss_.reg(), lambda e: e.activation(out=ss_[:], in_=tb[:, 0:128], func=AF.Copy))
        S.op("act", tb.reg(), sp_.reg(), lambda e: e.activation(out=sp_[0:32, :], in_=tb[0:32, 128:256], func=AF.Copy))
        S.dma("sp", f"os{sl_}", ss_.reg(), [], lambda e: e.dma_start(out=ncs_new[:, c_ * 128:(c_ + 1) * 128], in_=ss_[:]),
              final=True)
        S.dma("sp", f"op{sl_}", sp_.reg(), [], lambda e: e.dma_start(out=ncp_o[:, c_ * 128:(c_ + 1) * 128],
                                                              in_=sp_[0:32, :]), final=True)
        KP = 19 if c_ < 7 else 31
        for k in range(KP, 31):
            i_ap = vb[:, 2 + k:1026 + k]
            i_rg = vb.reg((2 + k, 1026 + k))
            o_ap = acc[:, 0:1024]
            o_rg = acc.reg((0, 1024))
            if k == KP:
                S.op("dve", regs(i_rg, cst_r), o_rg,
                     lambda e, i_ap=i_ap, o_ap=o_ap, k=k: e.tensor_scalar(
                         out=o_ap, in0=i_ap, scalar1=WDW(c_, k), scalar2=BDW(c_), op0=ALU.mult, op1=ALU.add))
            else:
                S.op("dve", regs(i_rg, cst_r, o_rg), o_rg,
                     lambda e, i_ap=i_ap, o_ap=o_ap, k=k: e.scalar_tensor_tensor(
                         out=o_ap, in0=i_ap, scalar=WDW(c_, k), in1=o_ap, op0=ALU.mult, op1=ALU.add))
        for (t0, t1, voff) in CB:
            n = t1 - t0
            bk = next_bank()
            nk = KP if voff is not None else 31

            def pe_fn(e, bk=bk, t0=t0, n=n, voff=voff, nk=nk):
                ins = None
                for k in range(nk):
                    dg = dgA if k < 16 else dgB
                    if voff is not None:
                        ins = e.matmul(bk[:, 0:n], dg[:, k % 16, :], vb[:, voff + k:voff + k + n],
                                       start=(k == 0), stop=(k == nk - 1))
                    else:
                        ins = e.matmul(bk[:, 0:128].rearrange("p (b t) -> p b t", b=16), dg[:, k % 16, :],
                                       ext_s[:, c_, :, k:k + 8], start=(k == 0), stop=(k == nk - 1))
                return ins
            rd = regs(dgA.reg(), dgB.reg(), vb.reg() if voff is not None else ext_s.reg(c_))
            S.op("pe", rd, bk.reg(), pe_fn)
            if voff is not None and KP < 31:
                S.op("dve", regs(bk.reg(), acc.reg((t0, t1))), acc.reg((t0, t1)),
                     lambda e, bk=bk, t0=t0, t1=t1, n=n: e.tensor_tensor(out=acc[:, t0:t1], in0=bk[:, 0:n],
                                                                         in1=acc[:, t0:t1], op=ALU.add))
            else:
                S.op("act", regs(bk.reg(), cst_r), acc.reg((t0, t1)),
                     lambda e, bk=bk, t0=t0, t1=t1, n=n: e.activation(out=acc[:, t0:t1], in_=bk[:, 0:n],
                                                                      func=AF.Identity, bias=BDW(c_)))

    def a2_evs(c_):
        sg_ = vacc[c_]
        vb, vtl = v_bf[c_ % 2], vt[c_ % 2]

        def ev_b(bk, c0, c1, n):
            S.op("act", bk.reg(), sg_.reg((c0, c1)),
                 lambda e: e.activation(out=sg_[:, c0:c1], in_=bk[:, 0:n], func=AF.Sigmoid))

        def ev_a(bk, c0, c1, n):
            S.op("dve", regs(bk.reg(), sg_.reg((c0, c1))), vb.reg((c0, c1)),
                 lambda e: e.tensor_tensor(out=vb[:, c0:c1], in0=bk[:, 0:n], in1=sg_[:, c0:c1], op=ALU.mult))
            if c0 == 1024:
                S.op("dve", regs(bk.reg(), sg_.reg((c0, c1))), vtl.reg(),
                     lambda e: e.tensor_tensor(out=vtl[:, :], in0=bk[:, 0:n], in1=sg_[:, c0:c1], op=ALU.mult))
        return ev_b, ev_a

    early = stop_after != "A1"
    evb0, eva0 = a2_evs(0)

    def early_blocks(bl):
        if early:
            stage1(CH_B + 0, True, evb0, blocks=bl)
            stage1(CH_A + 0, True, eva0, blocks=bl)

    prev_evs = stage0(x_halo, 32, 0, 9)
    for i in range(NT):
        evs_ = stage0(x_tok[i], 128, 32 + 128 * i, i)
        for ev_ in prev_evs:
            ev_()
        prev_evs = evs_
        if early:
            if i == 4:
                stage1(CH_B + 0, True, evb0, blocks=[0])
            if i == 6:
                stage1(CH_A + 0, True, eva0, blocks=[0])
    for ev_ in prev_evs:
        ev_()
    if early:
        stage1(CH_B + 0, True, evb0, blocks=[1])
        stage1(CH_A + 0, True, eva0, blocks=[1])
    early_blocks([2])
    W_LIMIT[0] = 10 ** 6
    S.dma("pool", "wpl", [], wpool.reg(),
          lambda e: e.dma_start(out=wpool[:], in_=w_pool_d.rearrange("p (g k c) -> p g k c", g=4, k=2)))
    p_load()
    for j_ in range(4):
        conv_q.append(lambda j_=j_: state_T(st_conv, 4, 30, ext_s, j_))
    for j_ in range(2):
        conv_q.append(lambda j_=j_: state_T(st_pool, 8, 15, ext_u, j_))
    conv_q.append(lambda: S.dma("sp", "out", [], [], lambda e: e.dma_start(out=ncs_old, in_=st_conv[:, 8:30, :]), final=True))
    conv_q.append(lambda: S.dma("sp", "out", [], [], lambda e: e.dma_start(out=nps_old, in_=st_pool[:, 8:15, :]), final=True))
    CONV_RATE[0] = 1

    pending = []
    for c_ in (range(8) if stop_after != "A1" else []):
        slot = c_ % 2
        if c_ > 0:
            ev_b, ev_a = a2_evs(c_)
            stage1(CH_B + c_, True, ev_b)
            if pending:
                conv_prep(*pending[0])
            stage1(CH_A + c_, True, ev_a)
        if pending:
            conv_run(10 ** 6)
            if c_ == 7:
                conv_prep(7, slot)
            conv_pe(*pending.pop(0))
        pending.append((c_, slot))
    if stop_after != "A1":
        conv_pe(*pending.pop(0))

    if stop_after not in ("A1", "A2"):
        for i in range(NT):
            conv_q.append(lambda i=i: p_T(i))
        CONV_RATE[0] = 1
        def za_part(g, cc):
            wwin = 2 ** (g + 1)
            ch = 2 * g + cc
            slot = (g % 2) * 2 + cc

            def ev_za(bk, c0, c1, n, slot=slot):
                st = sgt[acc_rot[0] % 3]
                S.op("act", bk.reg((0, n)), st.reg((0, n)),
                     lambda e: e.activation(out=st[:, 0:n], in_=bk[:, 0:n], func=AF.Sigmoid))
                S.op("dve", regs(bk.reg((0, n)), st.reg((0, n))), silu_za.reg(slot, (c0 - 32, c1 - 32)),
                     lambda e: e.tensor_tensor(out=silu_za[:, slot, c0 - 32:c1 - 32], in0=bk[:, 0:n],
                                               in1=st[:, 0:n], op=ALU.mult))
            stage1(CH_ZA + ch, False, ev_za)

        def u_part(g, cc):
            wwin = 2 ** (g + 1)
            ch = 2 * g + cc
            slot = (g % 2) * 2 + cc

            def ev_u(bk, c0, c1, n):
                S.op("act", bk.reg((0, n)), ubuf.reg((c0, c1)),
                     lambda e: e.activation(out=ubuf[:, c0:c1], in_=bk[:, 0:n], func=AF.Copy))
            stage1(CH_U + ch, True, ev_u)
            S.op("act", ubuf.reg((1056, W)), ext_u.reg(ch),
                 lambda e, ch=ch: e.activation(out=ext_u[:, ch, :, 15:23],
                                               in_=ubuf[:, 1056:W].rearrange("p (b t) -> p b t", b=16),
                                               func=AF.Copy))
            tb = TPB[ch % 2]

            def pe_fn(e, tb=tb):
                e.transpose(tb[:, 0:128], ubuf[:, 1056:W], ident_f[:])
                return e.transpose(tb[0:32, 128:256], ubuf[:, 1024:1056], ident_f[:])
            S.op("pe", regs(ubuf.reg((1024, W)), ident_f.reg()), tb.reg((0, 256)), pe_fn)
            sl_ = so_n[0] % 2
            so_n[0] += 1
            ss_, sp_ = so_s[sl_], so_p[sl_]
            S.op("act", tb.reg((0, 128)), ss_.reg(),
                 lambda e, tb=tb, ss_=ss_: e.activation(out=ss_[:], in_=tb[:, 0:128], func=AF.Copy))
            S.op("act", tb.reg((128, 256)), sp_.reg(),
                 lambda e, tb=tb, sp_=sp_: e.activation(out=sp_[0:32, :], in_=tb[0:32, 128:256], func=AF.Copy))
            S.dma("sp", f"os{sl_}", ss_.reg(), [],
                  lambda e, ch=ch, ss_=ss_: e.dma_start(out=nps_new[:, ch * 128:(ch + 1) * 128], in_=ss_[:]), final=True)
            S.dma("sp", f"op{sl_}", sp_.reg(), [],
                  lambda e, ch=ch, sp_=sp_: e.dma_start(out=npp_o[:, ch * 128:(ch + 1) * 128], in_=sp_[0:32, :]),
                  final=True)
            src = ubuf
            for l in range(1, g + 2):
                sh = 2 ** (l - 1)
                lo = 2 ** l
                dst = uscr[(l - 1) % 2]
                S.op("dve", src.reg((lo - sh, 1056)), dst.reg((lo, 1056)),
                     lambda e, src=src, dst=dst, lo=lo, sh=sh: e.tensor_tensor(
                         out=dst[:, lo:1056], in0=src[:, lo:1056], in1=src[:, lo - sh:1056 - sh], op=ALU.add))
                src = dst
            win = src
            ssrc_ap = ext_u[:, ch, :, :]
            ssrc_reg = ext_u.reg(ch)
            for l in range(1, g + 2):
                sh = 2 ** (l - 1)
                lo = 2 ** l - 1
                d = exs[:, (l - 1) % 2, :, :]
                dreg = exs.reg((l - 1) % 2)
                S.op("dve", ssrc_reg, dreg,
                     lambda e, s_=ssrc_ap, d=d, lo=lo, sh=sh: e.tensor_tensor(
                         out=d[:, :, lo:23], in0=s_[:, :, lo:23], in1=s_[:, :, lo - sh:23 - sh], op=ALU.add))
                ssrc_ap, ssrc_reg = d, dreg
            S.op("dve", regs(win.reg((32, 1056)), ubuf.reg((32, 1056))), pooled.reg(slot, (0, 1024)),
                 lambda e, win=win, slot=slot, wwin=wwin: e.scalar_tensor_tensor(
                     out=pooled[:, slot, 0:1024], in0=win[:, 32:1056], scalar=1.0 / wwin, in1=ubuf[:, 32:1056],
                     op0=ALU.mult, op1=ALU.subtract))
            S.op("dve", regs(ssrc_reg, ext_u.reg(ch)), pooled.reg(slot, (1024, T)),
                 lambda e, s_=ssrc_ap, slot=slot, wwin=wwin, ch=ch: e.scalar_tensor_tensor(
                     out=pooled[:, slot, 1024:T].rearrange("p (b t) -> p b t", b=16), in0=s_[:, :, 15:23],
                     scalar=1.0 / wwin, in1=ext_u[:, ch, :, 15:23], op0=ALU.mult, op1=ALU.subtract))
            S.op("dve", regs(win.reg((32, 48)), cst_r), tmpf.reg(),
                 lambda e, win=win, g=g: e.tensor_tensor(out=tmpf[:], in0=win[:, 32:48], in1=INVC(g), op=ALU.mult))
            S.op("dve", regs(tmpf.reg(), ubuf.reg((32, 48))), pooled.reg(slot, (0, 16)),
                 lambda e, slot=slot: e.tensor_tensor(out=pooled[:, slot, 0:16], in0=tmpf[:], in1=ubuf[:, 32:48],
                                                      op=ALU.subtract))

        def wpool_part(g):
            conv_run(10 ** 6)
            for dd in range(2):
                d = 2 * g + dd
                for (c0, c1) in [(0, 512), (512, 1024), (1024, T)]:
                    n = c1 - c0
                    bk = next_bank()

                    def pe_fn(e, bk=bk, c0=c0, c1=c1, n=n, dd=dd, g=g):
                        ins = None
                        for kc in range(2):
                            ins = e.matmul(bk[:, 0:n], wpool[:, g, kc, dd * 128:(dd + 1) * 128],
                                           pooled[:, (g % 2) * 2 + kc, c0:c1], start=(kc == 0), stop=(kc == 1))
                        return ins
                    S.op("pe", regs(wpool.reg(g), pooled.reg(((g % 2) * 2, (g % 2) * 2 + 2), (c0, c1))),
                         bk.reg((0, n)), pe_fn)
                    S.op("dve", regs(bk.reg((0, n)), silu_za.reg((g % 2) * 2 + dd, (c0, c1)), cst_r),
                         ya_in.reg(d, (c0, c1)),
                         lambda e, bk=bk, n=n, d=d, dd=dd, g=g, c0=c0, c1=c1: e.scalar_tensor_tensor(
                             out=ya_in[:, d, c0:c1], in0=bk[:, 0:n], scalar=PSC(d),
                             in1=silu_za[:, (g % 2) * 2 + dd, c0:c1], op0=ALU.mult, op1=ALU.mult))
                    after_unit()


        for g in range(4):
            u_part(g, 0)
            za_part(g, 0)
            u_part(g, 1)
            za_part(g, 1)
            if g > 0:
                wpool_part(g - 1)
        wpool_part(3)

    if stop_after not in ("A1", "A2", "A3"):
        TB = [(0, 512), (512, 1024), (1024, T)]
        for c_ in range(8):
            acc = acc_of[c_]
            s = c_ % 2
            S.op("dve", acc.reg((0, T)), ybf[s].reg(), lambda e, acc=acc, s=s: e.tensor_copy(
                out=ybf[s][:], in_=acc[:, 0:T]))
            S.op("act", acc.reg((0, T)), ysq[s].reg(), lambda e, acc=acc, s=s: e.activation(
                out=ysq[s][:], in_=acc[:, 0:T], func=AF.Square))

            def pe_fn(e, s=s, c_=c_):
                ins = None
                for bi, (c0, c1) in enumerate(TB):
                    n = c1 - c0
                    e.matmul(banks[bi][:, 0:n], ones_bf[:], ybf[s][:, c0:c1], start=(c_ == 0), stop=(c_ == 7))
                    ins = e.matmul(banks[3 + bi][:, 0:n], ones_bf[:], ysq[s][:, c0:c1], start=(c_ == 0), stop=(c_ == 7))
                return ins
            S.op("pe", regs(ybf[s].reg(), ysq[s].reg(), ones_bf.reg()),
                 regs(*[banks[b].reg() for b in range(6)]), pe_fn)
        for bi, (c0, c1) in enumerate(TB):
            n = c1 - c0
            S.op("act", banks[bi].reg((0, n)), mean_sb.reg((c0, c1)),
                 lambda e, bi=bi, c0=c0, c1=c1, n=n: e.activation(out=mean_sb[:, c0:c1], in_=banks[bi][:, 0:n],
                                                                  func=AF.Copy, scale=1.0 / DP))
        for bi, (c0, c1) in enumerate(TB):
            n = c1 - c0
            S.op("dve", mean_sb.reg((c0, c1)), rstd_sb.reg((c0, c1)),
                 lambda e, c0=c0, c1=c1: e.tensor_tensor(out=rstd_sb[:, c0:c1], in0=mean_sb[:, c0:c1],
                                                         in1=mean_sb[:, c0:c1], op=ALU.mult))
        for bi, (c0, c1) in enumerate(TB):
            n = c1 - c0
            S.op("dve", regs(banks[3 + bi].reg((0, n)), rstd_sb.reg((c0, c1))), rstd_sb.reg((c0, c1)),
                 lambda e, bi=bi, c0=c0, c1=c1, n=n: e.scalar_tensor_tensor(
                     out=rstd_sb[:, c0:c1], in0=banks[3 + bi][:, 0:n], scalar=1.0 / DP, in1=rstd_sb[:, c0:c1],
                     op0=ALU.mult, op1=ALU.subtract))
        S.op("act", regs(rstd_sb.reg(), epsb.reg()), rstd_sb.reg(),
             lambda e: e.activation(out=rstd_sb[:], in_=rstd_sb[:], func=AF.Sqrt, bias=epsb[:, 0:1]))
        S.op("dve", rstd_sb.reg(), rstd_sb.reg(), lambda e: e.reciprocal(out=rstd_sb[:], in_=rstd_sb[:]))
        acc_rot[0] = (acc_rot[0] + 5) // 6 * 6
        def zb_part(c_):
            def ev_zb(bk, c0, c1, n, c_=c_):
                st = sgt[acc_rot[0] % 3]
                S.op("act", bk.reg((0, n)), st.reg((0, n)),
                     lambda e: e.activation(out=st[:, 0:n], in_=bk[:, 0:n], func=AF.Sigmoid))
                S.op("dve", regs(bk.reg((0, n)), st.reg((0, n))), silu_zb.reg(c_, (c0 - 32, c1 - 32)),
                     lambda e: e.tensor_tensor(out=silu_zb[:, c_, c0 - 32:c1 - 32], in0=bk[:, 0:n],
                                               in1=st[:, 0:n], op=ALU.mult))
            stage1(CH_ZB + c_, False, ev_zb)


        def norm_a(c_):
            acc = acc_of[c_]
            s = c_ % 2
            AT = acc[:, 0:T]
            S.op("dve", regs(acc.reg((0, T)), mean_sb.reg()), acc.reg((0, T)),
                 lambda e: e.tensor_tensor(out=AT, in0=AT, in1=mean_sb[:], op=ALU.subtract))
            S.op("dve", regs(acc.reg((0, T)), rstd_sb.reg()), acc.reg((0, T)),
                 lambda e: e.tensor_tensor(out=AT, in0=AT, in1=rstd_sb[:], op=ALU.mult))
            S.op("act", regs(acc.reg((0, T)), cst_r), sgl[s].reg(),
                 lambda e: e.activation(out=sgl[s][:], in_=AT, func=AF.Sigmoid, scale=LNG(c_), bias=LNB(c_)))
            S.op("act", regs(acc.reg((0, T)), cst_r), acc.reg((0, T)),
                 lambda e: e.activation(out=AT, in_=AT, func=AF.Identity, scale=LNG(c_), bias=LNB(c_)))

        def norm_b(c_):
            acc = acc_of[c_]
            s = c_ % 2
            AT = acc[:, 0:T]
            S.op("dve", regs(acc.reg((0, T)), sgl[s].reg()), acc.reg((0, T)),
                 lambda e: e.tensor_tensor(out=AT, in0=AT, in1=sgl[s][:], op=ALU.mult))
            S.op("dve", regs(acc.reg((0, T)), silu_zb.reg(c_)), silu_zb.reg(c_),
                 lambda e: e.tensor_tensor(out=silu_zb[:, c_, :], in0=AT, in1=silu_zb[:, c_, :], op=ALU.mult))

        zb_part(0)
        for c_ in range(8):
            if c_ + 1 < 8:
                zb_part(c_ + 1)
            norm_a(c_)
            if c_ >= 1:
                norm_b(c_ - 1)
        norm_b(7)
    yb_in = silu_zb

    TB = [(0, 512), (512, 1024), (1024, T)]
    SS_E = 16

    def estat_unit(i, q):
        bk = next_bank()

        def pe_fn(e):
            ins = None
            for kk in range(2):
                ins = e.matmul(bk[:, :], pT[:, i, kk, :], wple[:, kk, q * 512:(q + 1) * 512],
                               start=(kk == 0), stop=(kk == 1))
            return ins
        S.op("pe", regs(pT.reg(i), wple.reg(None, (q * 512, q * 512 + 512))), bk.reg(), pe_fn)
        col = 128 + i * 4 + q
        S.op("act", bk.reg(), regs(bk.reg(), stat.reg((col, col + 1))),
             lambda e: e.activation(out=bk[:, :], in_=bk[:, :], func=AF.Square, accum_out=stat[:, col:col + 1]))

    if stop_after not in ("A1", "A2", "A3", "LN"):
        for j in range(16):
            if j == 2 and stop_after is None:
                S.dma("pool", "ccb", [], wple.reg(),
                      lambda e: e.dma_start(out=wple[:], in_=w_ple_d.rearrange("p (k c) -> p k c", k=2)))
                for i_ in range(NT):
                    for q_ in range(4):
                        conv_q.append(lambda i_=i_, q_=q_: estat_unit(i_, q_))
                CONV_RATE[0] = 1
            s = j % 2

            def ev_gate(dstb):
                def ev(bk, c0, c1, n):
                    S.op("act", bk.reg((0, n)), dstb.reg((c0 - 32, c1 - 32)),
                         lambda e: e.activation(out=dstb[:, c0 - 32:c1 - 32], in_=bk[:, 0:n], func=AF.Sigmoid))
                return ev

            def proj(kind, src_buf, gate_buf, final, j=j):
                tA = tA2[j % 2]
                sl, wv = wget(kind, j)
                for (c0, c1) in TB:
                    n = c1 - c0
                    bk = next_bank()

                    def pe_fn(e, bk=bk, c0=c0, c1=c1, n=n):
                        ins = None
                        for k in range(8):
                            ins = e.matmul(bk[:, 0:n], wv[:, k, :],
                                           src_buf[:, k, c0:c1], start=(k == 0), stop=(k == 7))
                        return ins
                    S.op("pe", regs(sl.reg(), src_buf.reg(None, (c0, c1))), bk.reg((0, n)), pe_fn)
                    if not final:
                        S.op("dve", regs(bk.reg((0, n)), gate_buf.reg((c0, c1))), tA.reg((c0, c1)),
                             lambda e, bk=bk, c0=c0, c1=c1, n=n: e.tensor_tensor(
                                 out=tA[:, c0:c1], in0=bk[:, 0:n], in1=gate_buf[:, c0:c1], op=ALU.mult))
                    else:
                        S.op("dve", regs(bk.reg((0, n)), gate_buf.reg((c0, c1))), tB.reg((c0, c1)),
                             lambda e, bk=bk, c0=c0, c1=c1, n=n: e.tensor_tensor(
                                 out=tB[:, c0:c1], in0=bk[:, 0:n], in1=gate_buf[:, c0:c1], op=ALU.mult))
                        S.op("dve", regs(tA.reg((c0, c1)), tB.reg((c0, c1))), mT.reg(j, (c0, c1)),
                             lambda e, c0=c0, c1=c1, j=j: e.tensor_tensor(out=mT[:, j, c0:c1], in0=tA[:, c0:c1],
                                                                          in1=tB[:, c0:c1], op=ALU.add))
            stage1(CH_GA + j, False, ev_gate(sga[s]))
            proj("pp", ya_in, sga[s], False)
            stage1(CH_GB + j, False, ev_gate(sgb[s]))
            if j % 2 == 1:
                proj("pc", yb_in, sgb[0], True, j=j - 1)
                proj("pc", yb_in, sgb[1], True, j=j)

    if stop_after is None:
        S.dma("sp", "cc", [], regs(gpost.reg(), gple.reg()),
              lambda e: [e.dma_start(out=gpost[:], in_=gpost_d), e.dma_start(out=gple[:], in_=gple_d)], n=2)

        corder = [("o", G, 0) for G in range(8)] + [("g", G, 0) for G in range(8)]
        c_issued = [0]

        C_LIMIT = [10 ** 6]

        def c_issue_upto(n):
            while c_issued[0] <= min(n, len(corder) - 1, C_LIMIT[0]):
                m = c_issued[0]
                kind, G, _rep = corder[m]
                sl = wslc[m % 3]
                src = w_out_d[G] if kind == "o" else w_pg_d[G]
                S.dma("pool", f"wc{m % 3}", [], sl.reg(), lambda e, sl=sl, src=src: e.dma_start(out=sl[:], in_=src))
                c_issued[0] += 1

        def cget(kind, G, rep):
            n = corder.index((kind, G, rep))
            c_issue_upto(n + 2)
            return wslc[n % 3]

        SS_Z = 32
        conv_run(10 ** 6)
        for i in range(NT):
            S.op("dve", stat.reg((128 + i * 4, 132 + i * 4)), stat.reg((SS_E + i, SS_E + i + 1)),
                 lambda e, i=i: e.tensor_reduce(out=stat[:, SS_E + i:SS_E + i + 1], in_=stat[:, 128 + i * 4:132 + i * 4],
                                                axis=mybir.AxisListType.X, op=ALU.add))
            S.op("act", regs(stat.reg((SS_E + i, SS_E + i + 1)), epsb.reg()), stat.reg((SS_E + i, SS_E + i + 1)),
                 lambda e, i=i: e.activation(out=stat[:, SS_E + i:SS_E + i + 1], in_=stat[:, SS_E + i:SS_E + i + 1],
                                             func=AF.Sqrt, scale=1.0 / D, bias=epsb[:, 0:1]))
            S.op("dve", stat.reg((SS_E + i, SS_E + i + 1)), stat.reg((SS_E + i, SS_E + i + 1)),
                 lambda e, i=i: e.reciprocal(out=stat[:, SS_E + i:SS_E + i + 1], in_=stat[:, SS_E + i:SS_E + i + 1]))

        HA, HB = [0, 1, 2, 3, 4], [5, 6, 7, 8]

        def c1_unit(G, i, sl):
            bk = next_bank()

            def pe_fn(e):
                ins = None
                for k in range(16):
                    ins = e.matmul(bk[:, 0:256], mT[:, k, i * 128:(i + 1) * 128], sl[:, k, :],
                                   start=(k == 0), stop=(k == 15))
                return ins
            S.op("pe", regs(sl.reg(), mT.reg(None, (i * 128, i * 128 + 128))), bk.reg(), pe_fn)
            S.op("act", bk.reg(), z.reg(i, (G * 256, G * 256 + 256)),
                 lambda e: e.activation(out=z[:, i, G * 256:(G + 1) * 256], in_=bk[:, 0:256], func=AF.Copy))
            col = SS_Z + i * 8 + G
            S.op("act", bk.reg(), regs(etc_[0].reg(), stat.reg((col, col + 1))),
                 lambda e: e.activation(out=etc_[0][:], in_=bk[:, 0:256], func=AF.Square,
                                        accum_out=stat[:, col:col + 1]))

        def x1_chain(i):
            c0 = SS_Z + i * 8
            rc = 25 + (i % 2)
            S.op("dve", stat.reg((c0, c0 + 8)), stat.reg((rc, rc + 1)),
                 lambda e: e.tensor_reduce(out=stat[:, rc:rc + 1], in_=stat[:, c0:c0 + 8],
                                           axis=mybir.AxisListType.X, op=ALU.add))
            S.op("act", regs(stat.reg((rc, rc + 1)), epsb.reg()), stat.reg((rc, rc + 1)),
                 lambda e: e.activation(out=stat[:, rc:rc + 1], in_=stat[:, rc:rc + 1], func=AF.Sqrt, scale=1.0 / D,
                                        bias=epsb[:, 0:1]))
            S.op("dve", stat.reg((rc, rc + 1)), stat.reg((rc, rc + 1)),
                 lambda e: e.reciprocal(out=stat[:, rc:rc + 1], in_=stat[:, rc:rc + 1]))
            S.op("dve", regs(z.reg(i), stat.reg((rc, rc + 1)), gpost.reg()), z.reg(i),
                 lambda e: e.scalar_tensor_tensor(out=z[:, i, :], in0=z[:, i, :], scalar=stat[:, rc:rc + 1],
                                                  in1=gpost[:], op0=ALU.mult, op1=ALU.mult))
            S.dma("pool", f"xa{i}", z.reg(i), z.reg(i),
                  lambda e: e.dma_start(out=z[:, i, :], in_=x_tok[i], accum_op=ALU.add))

        def x1_cast(i):
            xbl = x1b[i % 2]
            S.op("act", z.reg(i), xbl.reg(), lambda e: e.activation(out=xbl[:], in_=z[:, i, :], func=AF.Copy))

        def x1_transposes(i):
            xbl = x1b[i % 2]
            for h in range(2):
                tb = TPB[h]
                tbv = bank_bf(tb)[:, 0:1024].rearrange("p (k t) -> p k t", k=8)

                def pe_fn(e, h=h, tbv=tbv):
                    ins = None
                    for kk in range(8):
                        k = h * 8 + kk
                        ins = e.transpose(tbv[:, kk, :], xbl[:, k * 128:(k + 1) * 128], ident_bf[:])
                    return ins
                S.op("pe", regs(xbl.reg(), ident_bf.reg()), tb.reg(), pe_fn)
                if h == 0:
                    S.op("dve", tb.reg(), mT.reg((h * 8, h * 8 + 8), (i * 128, i * 128 + 128)),
                         lambda e, h=h, tbv=tbv: e.tensor_copy(out=mT[:, h * 8:h * 8 + 8, i * 128:(i + 1) * 128], in_=tbv))
                else:
                    S.op("act", tb.reg(), mT.reg((h * 8, h * 8 + 8), (i * 128, i * 128 + 128)),
                         lambda e, h=h, tbv=tbv: e.activation(out=mT[:, h * 8:h * 8 + 8, i * 128:(i + 1) * 128], in_=tbv,
                                                              func=AF.Copy))

        def c2_unit(G, i, sl):
            bk = next_bank()
            bk2 = next_bank()
            s = i % 2

            def pe_fn(e):
                ins = None
                for k in range(16):
                    ins = e.matmul(bk[:, 0:256], mT[:, k, i * 128:(i + 1) * 128], sl[:, k, :],
                                   start=(k == 0), stop=(k == 15))
                return ins
            S.op("pe", regs(sl.reg(), mT.reg(None, (i * 128, i * 128 + 128))), bk.reg(), pe_fn)

            def pe_fn2(e):
                ins = None
                for kk in range(2):
                    ins = e.matmul(bk2[:, 0:256], pT[:, i, kk, :], wple[:, kk, G * 256:(G + 1) * 256],
                                   start=(kk == 0), stop=(kk == 1))
                return ins
            S.op("pe", regs(pT.reg(i), wple.reg(None, (G * 256, G * 256 + 256))), bk2.reg(), pe_fn2)
            S.op("act", bk.reg(), sgc[s].reg(),
                 lambda e: e.activation(out=sgc[s][:], in_=bk[:, 0:256], func=AF.Sigmoid))
            S.op("dve", regs(bk2.reg(), stat.reg((SS_E + i, SS_E + i + 1)), gple.reg((G * 256, G * 256 + 256))),
                 etc_[s].reg(),
                 lambda e: e.scalar_tensor_tensor(
                     out=etc_[s][:], in0=bk2[:, 0:256], scalar=stat[:, SS_E + i:SS_E + i + 1],
                     in1=gple[:, G * 256:(G + 1) * 256], op0=ALU.mult, op1=ALU.mult))
            S.op("dve", regs(etc_[s].reg(), sgc[s].reg()), etc_[s].reg(),
                 lambda e: e.tensor_tensor(out=etc_[s][:], in0=etc_[s][:], in1=sgc[s][:], op=ALU.mult))
            S.op("dve", regs(etc_[s].reg(), z.reg(i, (G * 256, G * 256 + 256))), z.reg(i, (G * 256, G * 256 + 256)),
                 lambda e: e.tensor_tensor(out=z[:, i, G * 256:(G + 1) * 256],
                                           in0=z[:, i, G * 256:(G + 1) * 256], in1=etc_[s][:], op=ALU.add))
            S.dma("sp", "out", z.reg(i, (G * 256, G * 256 + 256)), [],
                  lambda e: e.dma_start(out=y_tok[i, :, G * 256:(G + 1) * 256],
                                        in_=z[:, i, G * 256:(G + 1) * 256]), final=True)

        ALLT = list(range(NT))
        for G in range(6):
            sl = cget("o", G, 0)
            for i in ALLT:
                c1_unit(G, i, sl)
        C_LIMIT[0] = 8
        sl6 = cget("o", 6, 0)
        sl7 = cget("o", 7, 0)
        slg0 = cget("g", 0, 0)
        done0 = []
        for st_ in range(NT + 3):
            if st_ < NT:
                c1_unit(6, st_, sl6)
                c1_unit(7, st_, sl7)
                x1_chain(st_)
            else:
                for i_ in (2 * (st_ - NT), 2 * (st_ - NT) + 1):
                    c2_unit(0, i_, slg0)
                    done0.append(i_)
            if 0 <= st_ - 3 < NT:
                x1_transposes(st_ - 3)
            if 0 <= st_ - 2 < NT:
                x1_cast(st_ - 2)
        C_LIMIT[0] = 10 ** 6
        for i in ALLT:
            if i not in done0:
                c2_unit(0, i, slg0)
        for G in range(1, 8):
            sl = cget("g", G, 0)
            for i in ALLT:
                c2_unit(G, i, sl)

    LB = dict(locals())
    for (name, buf, shape) in dbg:
        bufobj = LB[buf] if isinstance(buf, str) else buf
        if isinstance(bufobj, (list, tuple)):
            for bi_, b_ in enumerate(bufobj):
                dt_ = dram_out(f"dbg_{name}{bi_}", [128] + list(shape))
                S.dma("sp" if b_.esz == 4 else "pool", "dbgo", b_.reg(), [],
                      lambda e, dt_=dt_, b_=b_: e.dma_start(out=dt_, in_=b_[:]), final=True)
            continue
        dt_ = dram_out("dbg_" + name, [128] + list(shape))
        S.dma("pool", "dbgo", bufobj.reg(), [], lambda e, dt_=dt_, bufobj=bufobj: e.dma_start(out=dt_, in_=bufobj[:]),
              final=True)

    fin = dict(S.final_waits)
    S.prog["sp"].append(([(s, v) for s, v in fin.items()], None, None))

    semnames = sorted(S.semnames)
    sems = {}
    import contextlib
    with contextlib.ExitStack() as es:
        for sname in semnames:
            sems[sname] = es.enter_context(nc.semaphore(sname))
        block = es.enter_context(nc.Block())

        S.finalize()

        def run(e, engname):
            for waits, fn, inc in S.prog[engname]:
                for (s_, v) in waits:
                    e.wait_ge(sems[s_], S.wait_value(s_, v))
                if fn is None:
                    continue
                ins = fn(e)
                sem_, val_, is_dma = inc
                if is_dma:
                    if isinstance(ins, (list, tuple)):
                        for i_ in ins:
                            i_.then_inc(sems[sem_], 16)
                    else:
                        ins.then_inc(sems[sem_], 16)
                elif (sem_, val_) in S.rank:
                    ins.then_inc(sems[sem_], 1)

        @block.sync
        def _(e):
            run(e, "sp")

        @block.gpsimd
        def _(e):
            run(e, "pool")

        @block.scalar
        def _(e):
            run(e, "act")

        @block.vector
        def _(e):
            run(e, "dve")

        @block.tensor
        def _(e):
            run(e, "pe")
    return nc


def _prep_shared(inp):
    f = np.float32
    sh = {}
    w_in = np.asarray(inp["w_in"][0], f)
    sh["w_in_t"] = np.ascontiguousarray(w_in.reshape(16, 128, 72, 128).transpose(2, 1, 0, 3))
    w_pool = np.asarray(inp["w_pool"][0], f)
    sh["w_pool_t"] = np.ascontiguousarray(w_pool.reshape(4, 2, 128, 256).transpose(2, 0, 1, 3)).reshape(128, 2048)
    for nm, key in (("w_pp_t", "w_proj_pool"), ("w_pc_t", "w_proj_conv")):
        w = np.asarray(inp[key][0], f)
        sh[nm] = np.ascontiguousarray(w.reshape(8, 128, 16, 128).transpose(2, 1, 0, 3))
    for nm, key in (("w_out_t", "w_out"), ("w_pg_t", "w_ple_gate")):
        w = np.asarray(inp[key][0], f)
        sh[nm] = np.ascontiguousarray(w.reshape(16, 128, 8, 256).transpose(2, 1, 0, 3))
    w_ple = np.asarray(inp["w_ple"][0], f)
    sh["w_ple_t"] = np.ascontiguousarray(w_ple.reshape(2, 128, 2048).transpose(1, 0, 2)).reshape(128, 4096)
    sh["g_post_bc"] = np.ascontiguousarray(np.broadcast_to(np.asarray(inp["g_post"][0], f)[None, :], (128, D)))
    sh["g_ple_bc"] = np.ascontiguousarray(np.broadcast_to(np.asarray(inp["g_ple"][0], f)[None, :], (128, D)))
    sh["ident"] = np.eye(128, dtype=f)
    cst = np.zeros((128, 360), f)
    cst[:, 0:16] = np.asarray(inp["g_pre"][0], f).reshape(16, 128).T
    cst[:, 16:24] = np.asarray(inp["pool_scale"][0], f).reshape(8, 128).T
    cst[:, 24:32] = np.asarray(inp["b_dw"][0], f).reshape(8, 128).T
    cst[:, 32:40] = np.asarray(inp["ln_g"][0], f).reshape(8, 128).T
    cst[:, 40:48] = np.asarray(inp["ln_b"][0], f).reshape(8, 128).T
    wdw = np.asarray(inp["w_dw"][0], f)
    cst[:, 112:360] = wdw.reshape(31, 8, 128).transpose(2, 1, 0).reshape(128, 248)
    return sh, cst


def _in_maps(inp):
    f = np.float32
    sh, cst0 = _prep_shared(inp)
    xp = np.asarray(inp["x_prompt"], f)
    xsamp = np.asarray(inp["x_sample"], f)
    pp = np.asarray(inp["p_prompt"], f)[0]
    ps = np.asarray(inp["p_sample"], f)[0]
    stc = np.asarray(inp["state_conv"], f)[0]
    stp = np.asarray(inp["state_pool"], f)[0]
    maps = []
    for r in range(NCORES):
        b, half = r // 2, r % 2
        st = half * 1024
        m = dict(sh)
        xt = np.empty((NT, 128, D), f)
        xt[0:8] = xp[b, st:st + 1024].reshape(8, 128, D)
        xt[8] = xsamp[16 * r:16 * r + 16].reshape(128, D)
        m["x_tok"] = xt
        m["x_halo"] = np.ascontiguousarray(xp[b, st - 32:st]) if half else np.zeros((32, D), f)
        pt = np.empty((NT, 128, 256), f)
        pt[0:8] = pp[b, st:st + 1024].reshape(8, 128, 256)
        pt[8] = ps[16 * r:16 * r + 16].reshape(128, 256)
        m["p_tok"] = pt
        m["st_conv"] = np.ascontiguousarray(stc[16 * r:16 * r + 16])
        m["st_pool"] = np.ascontiguousarray(stp[16 * r:16 * r + 16])
        cst = cst0.copy()
        for g, wd in enumerate((2, 4, 8, 16)):
            pos = st + np.arange(16)
            cst[:, 48 + g * 16:64 + g * 16] = (1.0 / np.minimum(pos + 1, wd)).astype(f)[None, :]
        m["cst"] = cst
        maps.append(m)
    return maps


_NC_CACHE = {}


def kernel(**inputs):
    if "nc" not in _NC_CACHE:
        _NC_CACHE["nc"] = build_program()
    nc = _NC_CACHE["nc"]
    maps = _in_maps(inputs)
    res = run_bass_kernel_spmd(nc, maps, core_ids=list(range(NCORES)))
    R = res.results
    f = np.float32
    y_prompt = np.empty((4, 2048, D), f)
    y_sample = np.empty((128, 8, D), f)
    npp = np.empty((1, 4, 15, DP), f)
    ncp = np.empty((1, 4, 30, DP), f)
    nps = np.empty((1, 128, 15, DP), f)
    ncs = np.empty((1, 128, 30, DP), f)
    for r in range(NCORES):
        b, half = r // 2, r % 2
        st = half * 1024
        yt = R[r]["y_tok"]
        y_prompt[b, st:st + 1024] = yt[0:8].reshape(1024, D)
        y_sample[16 * r:16 * r + 16] = yt[8].reshape(16, 8, D)
        ncs[0, 16 * r:16 * r + 16, 0:22] = R[r]["ncs_old"]
        ncs[0, 16 * r:16 * r + 16, 22:30] = R[r]["ncs_new"].reshape(16, 8, DP)
        nps[0, 16 * r:16 * r + 16, 0:7] = R[r]["nps_old"]
        nps[0, 16 * r:16 * r + 16, 7:15] = R[r]["nps_new"].reshape(16, 8, DP)
        if half:
            ncp[0, b] = R[r]["ncp"][2:32]
            npp[0, b] = R[r]["npp"][17:32]
    return (y_prompt, y_sample, npp, ncp, nps, ncs)
```

```python
import numpy as np
import concourse.bass as bass
import concourse.mybir as mybir
from concourse.bass_utils import run_bass_kernel_spmd

F32 = mybir.dt.float32
BF16 = mybir.dt.bfloat16
AF = mybir.ActivationFunctionType
ALU = mybir.AluOpType

NCORES = 8
D = 2048
DP = 1024
NT = 9
W = 1184
T = 1152
EPS = 1e-6
SB_BASE = 16512
SB_END = 229376

CH_U, CH_ZA, CH_A, CH_B, CH_ZB, CH_GA, CH_GB = 0, 8, 16, 24, 32, 40, 56


class Sched:
    ENGS = ("pe", "act", "dve", "pool", "sp")
    GR = 1024

    def __init__(self):
        self.prog = {e: [] for e in self.ENGS}
        self.cnt = {}
        self.waited = {e: {} for e in self.ENGS}
        self.recs = {}
        self.buckets = {}
        self.semnames = set()
        self.final_waits = {}
        self.needed = set()

    def _granules(self, space, lo, hi):
        return [(space, g) for g in range(lo // self.GR, (hi - 1) // self.GR + 1)]

    def _add(self, key, val):
        old = self.recs.get(key)
        if old is None:
            for b in self._granules(key[0], key[1], key[2]):
                self.buckets.setdefault(b, set()).add(key)
        if old is None or old < val:
            self.recs[key] = val

    def _remove(self, key):
        del self.recs[key]
        for b in self._granules(key[0], key[1], key[2]):
            self.buckets[b].discard(key)

    def _overlaps(self, space, lo, hi):
        seen = set()
        for b in self._granules(space, lo, hi):
            for key in self.buckets.get(b, ()):
                if key in seen:
                    continue
                if key[1] < hi and lo < key[2]:
                    seen.add(key)
        return seen

    def _collect(self, eng_sem, is_pe, reads, writes):
        deps = {}
        cur = self.cnt.get(eng_sem, 0)

        def need(sem, val):
            if deps.get(sem, 0) < val:
                deps[sem] = val

        for (space, lo, hi) in reads:
            for key in self._overlaps(space, lo, hi):
                sem = key[3]
                if not key[4]:
                    if space == "ps" and sem != eng_sem:
                        need(sem, self.recs[key])
                    continue
                if sem == eng_sem:
                    if is_pe:
                        continue
                need(sem, self.recs[key])
        for (space, lo, hi) in writes:
            for key in self._overlaps(space, lo, hi):
                sem = key[3]
                if sem == eng_sem:
                    continue
                need(sem, self.recs[key])
        return deps

    def _record(self, sem, val, reads, writes):
        for (space, lo, hi) in writes:
            for key in list(self._overlaps(space, lo, hi)):
                if key[1] >= lo and key[2] <= hi:
                    self._remove(key)
            self._add((space, lo, hi, sem, True), val)
        for (space, lo, hi) in reads:
            self._add((space, lo, hi, sem, False), val)

    def _waits(self, eng, deps):
        out = []
        for sem, val in deps.items():
            if self.waited[eng].get(sem, 0) < val:
                self.waited[eng][sem] = val
                out.append((sem, val))
                self.needed.add((sem, val))
        return out

    def op(self, eng, reads, writes, fn):
        sem = "E_" + eng
        self.semnames.add(sem)
        deps = self._collect(sem, eng == "pe", reads, writes)
        waits = self._waits(eng, deps)
        val = self.cnt.get(sem, 0) + 1
        self.cnt[sem] = val
        self.prog[eng].append((waits, fn, (sem, val, False)))
        self._record(sem, val, reads, writes)

    def dma(self, queue, sem, reads, writes, fn, final=False, n=1):
        self.semnames.add(sem)
        deps = self._collect(sem, False, reads, writes)
        deps.pop(sem, None)
        waits = self._waits(queue, deps)
        val = self.cnt.get(sem, 0) + 16 * n
        self.cnt[sem] = val
        self.prog[queue].append((waits, fn, (sem, val, True)))
        self._record(sem, val, reads, writes)
        if final:
            self.final_waits[sem] = val

    def finalize(self):
        self.rank = {}
        by_sem = {}
        for (sem, val) in self.needed:
            if sem.startswith("E_"):
                by_sem.setdefault(sem, []).append(val)
        for sem, vals in by_sem.items():
            for r, v in enumerate(sorted(vals)):
                self.rank[(sem, v)] = r + 1

    def wait_value(self, sem, val):
        return self.rank[(sem, val)] if sem.startswith("E_") else val


class Buf:
    def __init__(self, t, space, addr, free_shape, esz):
        self.t = t
        self.space = space
        self.addr = addr
        self.shape = tuple(free_shape)
        self.esz = esz

    def __getitem__(self, k):
        return self.t[k]

    def reg(self, *idx):
        shp = self.shape
        idx = list(idx) + [None] * (len(shp) - len(idx))
        rngs = []
        for d, i in enumerate(idx):
            if i is None:
                rngs.append((0, shp[d]))
            elif isinstance(i, int):
                rngs.append((i, i + 1))
            else:
                rngs.append(i)
        strides = [1] * len(shp)
        for d in range(len(shp) - 2, -1, -1):
            strides[d] = strides[d + 1] * shp[d + 1]
        nd = len(shp)
        cut = nd
        while cut > 1 and rngs[cut - 1] == (0, shp[cut - 1]):
            cut -= 1
        out = []

        if self.space == "ps":
            return [("ps", self.addr, self.addr + 2048)]

        def rec(d, off):
            if d == cut - 1:
                lo = off + rngs[d][0] * strides[d]
                hi = off + rngs[d][1] * strides[d]
                out.append((self.space, self.addr + lo * self.esz, self.addr + hi * self.esz))
                return
            for i in range(rngs[d][0], rngs[d][1]):
                rec(d + 1, off + i * strides[d])

        rec(0, 0)
        return out


class Arena:
    def __init__(self, nc):
        self.nc = nc
        self.n = 0

    def at(self, name, addr, free_shape, dt, parts=128):
        esz = 4 if dt == F32 else 2
        size = int(np.prod(free_shape)) * esz
        assert addr % 32 == 0, (name, addr)
        assert SB_BASE <= addr and addr + size <= SB_END, (name, addr, size)
        self.n += 1
        t = self.nc.alloc_sbuf_tensor_at(f"{name}_{self.n}", [parts] + list(free_shape), dt, offset=addr)
        return Buf(t, "sb", addr, free_shape, esz)


def build_program(stop_after=None, dbg=None):
    nc = bass.Bass("TRN2", target_bir_lowering=False)
    S = Sched()
    A = Arena(nc)
    dbg = dbg or []

    def dram_in(name, shape):
        return nc.dram_tensor(name, list(shape), F32, kind="ExternalInput").ap()

    def dram_out(name, shape):
        return nc.dram_tensor(name, list(shape), F32, kind="ExternalOutput").ap()

    x_tok = dram_in("x_tok", [NT, 128, D])
    x_halo = dram_in("x_halo", [32, D])
    p_tok = dram_in("p_tok", [NT, 128, 256])
    st_conv = dram_in("st_conv", [16, 30, DP])
    st_pool = dram_in("st_pool", [16, 15, DP])
    cst_d = dram_in("cst", [128, 360])
    ident_d = dram_in("ident", [128, 128])
    w_in_d = dram_in("w_in_t", [72, 128, 16, 128])
    w_pool_d = dram_in("w_pool_t", [128, 4 * 2 * 256])
    w_pp_d = dram_in("w_pp_t", [16, 128, 8, 128])
    w_pc_d = dram_in("w_pc_t", [16, 128, 8, 128])
    w_out_d = dram_in("w_out_t", [8, 128, 16, 256])
    w_pg_d = dram_in("w_pg_t", [8, 128, 16, 256])
    w_ple_d = dram_in("w_ple_t", [128, 2 * D])
    gpost_d = dram_in("g_post_bc", [128, D])
    gple_d = dram_in("g_ple_bc", [128, D])

    y_tok = dram_out("y_tok", [NT, 128, D])
    ncs_new = dram_out("ncs_new", [128, DP])
    ncs_old = dram_out("ncs_old", [16, 22, DP])
    nps_new = dram_out("nps_new", [128, DP])
    nps_old = dram_out("nps_old", [16, 7, DP])
    ncp_o = dram_out("ncp", [32, DP])
    npp_o = dram_out("npp", [32, DP])
    dbg_out = {}

    a = SB_BASE
    C0 = a
    ident_bf = A.at("ident_bf", a, [128], BF16); a += 256
    ident_f = A.at("ident_f", a, [128], F32); a += 512
    ones_bf = A.at("ones_bf", a, [128], BF16); a += 256
    cst = A.at("cst", a, [360], F32); a += 1440
    wpool = A.at("wpool", a, [4, 2, 256], BF16); a += 4096
    stat = A.at("stat", a, [192], F32); a += 768
    pT = A.at("pT", a, [NT, 2, 128], BF16); a += NT * 2 * 128 * 2
    exs = A.at("exs", a, [2, 16, 23], F32); a += 2 * 16 * 23 * 4
    tmpf = A.at("tmpf", a, [16], F32); a += 64
    negh = A.at("negh", a, [8], F32); a += 32
    epsb = A.at("epsb", a, [8], F32); a += 32
    a = (a + 31) // 32 * 32
    R1 = a
    hT = A.at("hT", R1, [16, W], BF16); a += 16 * W * 2
    R2 = a
    vacc = [A.at(f"vacc{i}", R2 + i * W * 4, [W], F32) for i in range(8)]
    vall = A.at("vall", R2, [10, W], F32)
    a += 10 * W * 4
    R3 = a
    mT = A.at("mT", R3, [16, T], BF16); a += 16 * T * 2
    R4 = a
    ext_s = A.at("ext_s", R4, [8, 16, 38], BF16); a += 8 * 16 * 38 * 4
    v_bf = [A.at(f"v_bf{i}", R4 + 9728 + i * 2368, [W], BF16) for i in range(2)]
    vt = [A.at(f"vt{i}", R4 + 9728 + 4736 + i * 640, [160], F32) for i in range(2)]
    dgA = A.at("dgA", R2 + 8 * W * 4, [16, 128], BF16)
    dgB = A.at("dgB", R2 + 8 * W * 4 + 4096, [16, 128], BF16)
    wdwc = A.at("wdwc", R2 + 8 * W * 4 + 8192, [32], F32)
    dgA2 = A.at("dgA2", R3, [16, 128], BF16)
    dgB2 = A.at("dgB2", R3 + 4096, [16, 128], BF16)
    wdwc2 = A.at("wdwc2", R3 + 8192, [32], F32)
    R4b = a
    ya_in = A.at("ya_in", R4b, [8, T], BF16); a += 8 * T * 2
    R5 = a
    wsl = [A.at(f"wsl{i}", R5 + i * 4096, [2048], BF16) for i in range(4)]
    a += 4 * 4096
    R6 = a
    a += 18432
    assert a <= SB_END, a

    xs = [A.at(f"xs{i}", R3 + i * 8192, [D], F32) for i in range(3)]
    xs += [A.at(f"xs{3 + i}", R4b + i * 8192, [D], F32) for i in range(2)]
    pall = A.at("pall", R4b, [NT, 256], BF16)
    xb = [A.at(f"xb{i}", R3 + 24576 + i * 4096, [D], BF16) for i in range(2)]
    sqj = A.at("sqj", R3 + 32768, [D], BF16)
    silu_za = A.at("silu_za", R3, [4, T], BF16)
    pooled = A.at("pooled", R3 + 9216, [4, T], BF16)
    ubuf = A.at("ubuf", R3 + 18432, [W], F32)
    uscr = [A.at(f"uscr{i}", R3 + 18432 + (i + 1) * W * 4, [W], F32) for i in range(2)]
    assert 18432 + 3 * W * 4 <= 16 * T * 2
    ybf = [A.at(f"ybf{i}", R3 + i * 2304, [T], BF16) for i in range(2)]
    ysq = [A.at(f"ysq{i}", R3 + 4608 + i * 2304, [T], BF16) for i in range(2)]
    mean_sb = A.at("mean", R3 + 9216, [T], F32)
    rstd_sb = A.at("rstd", R3 + 9216 + 4608, [T], F32)
    sgl = [A.at(f"sgl{i}", R3 + 18432 + i * 4608, [T], F32) for i in range(2)]
    silu_zb = A.at("silu_zb", R4, [8, T], BF16)
    ext_u = A.at("ext_u", R6, [8, 16, 23], F32)
    sgt = [A.at(f"sgt{i}", R6 + 11776 + i * 2048, [512], F32) for i in range(3)]
    stg = A.at("stg", R6 + 11776, [DP], F32)
    stgs = [A.at(f"stgs{i}", R3 + 12288 + i * 4096, [DP], F32) for i in range(6)]
    stg_n = [0]
    sga = [A.at(f"sga{i}", R6 + i * 2304, [T], BF16) for i in range(2)]
    sgb = [A.at(f"sgb{i}", R6 + 4608 + i * 2304, [T], BF16) for i in range(2)]
    tA = A.at("tA", R6 + 9216, [T], F32)
    tA2 = [tA, A.at("tA1", R2, [T], F32)]
    tB = A.at("tB", R6 + 9216 + 4608, [T], F32)
    TAIL = a
    so_s = [A.at(f"so_s{i}", TAIL + i * 512, [128], F32) for i in range(2)]
    so_p = [A.at(f"so_p{i}", TAIL + 1024 + i * 512, [128], F32, parts=32) for i in range(2)]
    assert TAIL + 2048 <= SB_END
    z = A.at("z", R1, [NT, D], F32)
    CZ = R1 + NT * D * 4
    CZ = (CZ + 31) // 32 * 32
    wple = A.at("wple", CZ, [2, D], BF16)
    assert CZ + 8192 <= R3
    c = R4
    wslc = [None, A.at("wslc1", c, [16, 256], BF16), A.at("wslc2", c + 8192, [16, 256], BF16)]; c += 2 * 8192
    x1b = [A.at(f"x1b{i}", c + i * 4096, [D], BF16) for i in range(2)]; c += 8192
    assert c == R4b + 5120
    wslc[0] = A.at("wslc0", c, [16, 256], BF16); c += 8192
    gpost = A.at("gpost", c, [D], F32); c += 8192
    gple = A.at("gple", c, [D], F32); c += 8192
    sgc = [A.at(f"sgc{i}", c + i * 1024, [256], F32) for i in range(2)]; c += 2048
    etc_ = [A.at(f"etc{i}", c + i * 1024, [256], F32) for i in range(2)]; c += 2048
    ptile = A.at("ptile", c, [256], F32); c += 1024
    pbt = A.at("pbt", c, [256], BF16); c += 512
    assert c <= SB_END, c

    banks = []
    for i in range(8):
        t = nc.alloc_psum_tensor(f"bank{i}", [128, 512], F32)
        banks.append(Buf(t, "ps", i * 2048, [512], 4))
    acc_rot = [0]

    def next_bank():
        b = banks[acc_rot[0] % 6]
        acc_rot[0] += 1
        return b

    TPB = (banks[6], banks[7])

    def bank_bf(b):
        return b.t.bitcast(BF16)

    def regs(*lists):
        out = []
        for l in lists:
            out.extend(l)
        return out

    S.dma("sp", "cst", [], regs(cst.reg(), ident_f.reg()),
          lambda e: [e.dma_start(out=cst[:], in_=cst_d), e.dma_start(out=ident_f[:], in_=ident_d)], n=2)
    S.dma("pool", "cstb", [], ident_bf.reg(), lambda e: e.dma_start(out=ident_bf[:], in_=ident_d))
    S.op("dve", [], ones_bf.reg(), lambda e: e.memset(ones_bf[:], 1.0))
    S.op("dve", [], negh.reg(), lambda e: e.memset(negh[:], -0.5))
    S.op("dve", [], epsb.reg(), lambda e: e.memset(epsb[:], EPS))
    S.op("act", epsb.reg(), negh.reg((4, 8)), lambda e: e.activation(out=negh[:, 4:8], in_=epsb[:, 0:4], func=AF.Square))
    S.op("dve", [], stat.reg(), lambda e: e.memset(stat[:], 0.0))

    GPRE = lambda k0, k1: cst[:, k0:k1]
    PSC = lambda d: cst[:, 16 + d:17 + d]
    BDW = lambda c_: cst[:, 24 + c_:25 + c_]
    LNG = lambda c_: cst[:, 32 + c_:33 + c_]
    LNB = lambda c_: cst[:, 40 + c_:41 + c_]
    INVC = lambda g: cst[:, 48 + g * 16:64 + g * 16]
    WDW = lambda c_, k: cst[:, 112 + c_ * 31 + k:113 + c_ * 31 + k]
    cst_r = cst.reg()

    wlist = []

    def w_in_src(ch):
        return (w_in_d[ch], lambda sl: sl.t[:].rearrange("p (k c) -> p k c", k=16), [16, 128])

    def w_pr_src(dram, G):
        return (dram[G], lambda sl: sl.t[:].rearrange("p (k c) -> p k c", k=8), [8, 256])

    order = []
    for c_ in range(8):
        order.append(("in", CH_B + c_)); order.append(("in", CH_A + c_))
    for g in range(4):
        order += [("in", CH_U + 2 * g), ("in", CH_ZA + 2 * g), ("in", CH_U + 2 * g + 1), ("in", CH_ZA + 2 * g + 1)]
    for c_ in range(8):
        order.append(("in", CH_ZB + c_))
    for jp in range(0, 16, 2):
        for j in (jp, jp + 1):
            order.append(("in", CH_GA + j))
            order.append(("pp", j))
            order.append(("in", CH_GB + j))
        order.append(("pc", jp))
        order.append(("pc", jp + 1))
    wpos = {}
    for n, it in enumerate(order):
        wpos[it] = n
    w_issued = [0]

    W_LIMIT = [1]

    def w_issue_upto(n):
        while w_issued[0] <= min(n, len(order) - 1, W_LIMIT[0]):
            m = w_issued[0]
            kind, idx = order[m]
            sl = wsl[m % 4]
            if kind == "in":
                src, view = w_in_d[idx], sl.t[:].rearrange("p (k c) -> p k c", k=16)
            elif kind == "pp":
                src, view = w_pp_d[idx], sl.t[:, 0:1024].rearrange("p (k c) -> p k c", k=8)
            else:
                src, view = w_pc_d[idx], sl.t[:, 0:1024].rearrange("p (k c) -> p k c", k=8)
            S.dma("pool", f"w{m % 4}", [], sl.reg(),
                  lambda e, view=view, src=src: e.dma_start(out=view, in_=src))
            w_issued[0] += 1

    def wget(kind, idx):
        n = wpos[(kind, idx)]
        w_issue_upto(n + 3)
        sl = wsl[n % 4]
        if kind == "in":
            return sl, sl.t[:].rearrange("p (k c) -> p k c", k=16)
        return sl, sl.t[:, 0:1024].rearrange("p (k c) -> p k c", k=8)

    conv_q = []

    def conv_run(n):
        for _ in range(n):
            if not conv_q:
                return
            conv_q.pop(0)()

    def after_unit():
        conv_run(CONV_RATE[0])

    CONV_RATE = [0]

    xs_n = [0]

    def stage0(src_ap, nrows, col0, statcol):
        s = xs_n[0] % 5
        sb = xs_n[0] % 2
        xs_n[0] += 1
        xsl, xbl = xs[s], xb[sb]
        S.dma("sp", f"xs{s}", [], xsl.reg(), lambda e: e.dma_start(out=xsl[0:nrows, :], in_=src_ap))
        S.op("act", xsl.reg(), regs(sqj.reg(), stat.reg((statcol, statcol + 1))),
             lambda e: e.activation(out=sqj[0:nrows, :], in_=xsl[0:nrows, :], func=AF.Square,
                                    accum_out=stat[0:nrows, statcol:statcol + 1]))
        sc_ = stat[0:nrows, statcol:statcol + 1]
        sr_ = stat.reg((statcol, statcol + 1))
        S.op("act", regs(sr_, epsb.reg()), sr_,
             lambda e: e.activation(out=sc_, in_=sc_, func=AF.Sqrt, scale=1.0 / D, bias=epsb[0:nrows, 0:1]))
        S.op("dve", sr_, sr_, lambda e: e.reciprocal(out=sc_, in_=sc_))
        S.op("dve", regs(xsl.reg(), sr_), xbl.reg(),
             lambda e: e.tensor_scalar(out=xbl[0:nrows, :], in0=xsl[0:nrows, :], scalar1=sc_, scalar2=None, op0=ALU.mult))
        evs = []
        for h in range(2):
            tb = next_bank()
            tbv = bank_bf(tb)[:, 0:8 * nrows].rearrange("p (k t) -> p k t", k=8)

            def pe_fn(e, h=h, tbv=tbv):
                ins = None
                for kk in range(8):
                    k = h * 8 + kk
                    ins = e.transpose(tbv[:, kk, :], xbl[0:nrows, k * 128:(k + 1) * 128], ident_bf[0:nrows, 0:nrows])
                return ins
            S.op("pe", regs(xbl.reg(), ident_bf.reg()), tb.reg(), pe_fn)

            def ev(h=h, tb=tb, tbv=tbv):
                S.op("dve", regs(tb.reg(), cst_r), hT.reg((h * 8, h * 8 + 8), (col0, col0 + nrows)),
                     lambda e: e.tensor_tensor(
                         out=hT[:, h * 8:h * 8 + 8, col0:col0 + nrows], in0=tbv,
                         in1=GPRE(h * 8, h * 8 + 8).unsqueeze(2).broadcast_to([128, 8, nrows]), op=ALU.mult))
            evs.append(ev)
        return evs

    def state_T(src_dram, nb, nr, ext, j):
        si_ = stg_n[0]
        stg_n[0] += 1
        xsl = stgs[si_]
        rows = nb * nr
        src = src_dram[j * nb:(j + 1) * nb].rearrange("b r c -> (b r) c")
        S.dma("sp", f"stg{si_}", [], xsl.reg((0, DP)), lambda e: e.dma_start(out=xsl[0:rows, 0:DP], in_=src))
        for h in range(2):
            tb = TPB[h]

            def pe_fn(e, h=h, tb=tb):
                ins = None
                for cc in range(4):
                    c_ = h * 4 + cc
                    ins = e.transpose(tb[:, cc * rows:(cc + 1) * rows], xsl[0:rows, c_ * 128:(c_ + 1) * 128],
                                      ident_f[0:rows, 0:rows])
                return ins
            S.op("pe", regs(xsl.reg((0, DP)), ident_f.reg()), tb.reg(), pe_fn)
            S.op("act", tb.reg(), ext.reg((h * 4, h * 4 + 4)),
                 lambda e, h=h, tb=tb: e.activation(
                     out=ext[:, h * 4:h * 4 + 4, j * nb:(j + 1) * nb, 0:nr],
                     in_=tb[:, 0:4 * rows].rearrange("p (c b r) -> p c b r", c=4, b=nb), func=AF.Copy))

    def p_load():
        S.dma("pool", "pall", [], pall.reg(), lambda e: e.dma_start(out=pall[:], in_=p_tok.rearrange("i p c -> p i c")))

    def p_T(i):
        tb = TPB[i % 2]
        tbv = bank_bf(tb)[:, 0:256].rearrange("p (k t) -> p k t", k=2)

        def pe_fn(e):
            ins = None
            for kk in range(2):
                ins = e.transpose(tbv[:, kk, :], pall[:, i, kk * 128:(kk + 1) * 128], ident_bf[:])
            return ins
        S.op("pe", regs(pall.reg(i), ident_bf.reg()), tb.reg(), pe_fn)
        S.op("dve", tb.reg(), pT.reg(i), lambda e: e.tensor_copy(out=pT[:, i, :, :], in_=tbv))

    BLK_H = [(0, 512), (512, 1024), (1024, W)]
    BLK_N = [(32, 544), (544, 1056), (1056, W)]

    def stage1(ch, halo, evac, blocks=(0, 1, 2)):
        sl, wv = wget("in", ch)
        for (c0, c1) in [(BLK_H if halo else BLK_N)[b_] for b_ in blocks]:
            n = c1 - c0
            bk = next_bank()

            def pe_fn(e, bk=bk, c0=c0, c1=c1, n=n):
                ins = None
                for k in range(16):
                    ins = e.matmul(bk[:, 0:n], wv[:, k, :], hT[:, k, c0:c1], start=(k == 0), stop=(k == 15))
                return ins
            S.op("pe", regs(sl.reg(), hT.reg(None, (c0, c1))), bk.reg((0, n)), pe_fn)
            evac(bk, c0, c1, n)
            after_unit()

    acc_of = {c_: vacc[c_] for c_ in range(8)}
    so_n = [0]
    CB = [(0, 512, 2), (512, 1024, 514), (1024, T, None)]

    def dg_set(c_):
        return (dgA2, dgB2, wdwc2) if c_ == 7 else (dgA, dgB, wdwc)

    def conv_prep(c_, slot):
        dA_, dB_, wc_ = dg_set(c_)
        S.op("dve", cst_r, wc_.reg(), lambda e: e.tensor_copy(out=wc_[:, 0:31], in_=cst[:, 112 + c_ * 31:143 + c_ * 31]))
        for (dg, k0, nk) in ((dA_, 0, 16), (dB_, 16, 15)):
            S.op("dve", regs(ident_bf.reg(), wc_.reg()), dg.reg((0, nk)),
                 lambda e, dg=dg, k0=k0, nk=nk: e.tensor_tensor(
                     out=dg[:, 0:nk, :], in0=ident_bf[:, :].unsqueeze(1).broadcast_to([128, nk, 128]),
                     in1=wc_[:, k0:k0 + nk].unsqueeze(2).broadcast_to([128, nk, 128]), op=ALU.mult))

    def conv_pe(c_, slot):
        vb, vtl, acc = v_bf[slot], vt[slot], vacc[c_]
        dgA, dgB, _ = dg_set(c_)
        S.op("act", vb.reg((1056, W)), ext_s.reg(c_),
             lambda e: e.activation(out=ext_s[:, c_, :, 30:38],
                                    in_=vb[:, 1056:W].rearrange("p (b t) -> p b t", b=16), func=AF.Copy))
        tb = TPB[c_ % 2]

        def pe_fn(e):
            e.transpose(tb[:, 0:128], vtl[:, 32:160], ident_f[:])
            return e.transpose(tb[0:32, 128:256], vtl[:, 0:32], ident_f[:])
        S.op("pe", regs(vtl.reg(), ident_f.reg()), tb.reg(), pe_fn)
        sl_ = so_n[0] % 2
        so_n[0] += 1
        ss_, sp_ = so_s[sl_], so_p[sl_]
        S.op("act", tb.reg(), ss_.reg(), lambda e: e.activation(out=ss_[:], in_=tb[:, 0:128], func=AF.Copy))
        S.op("act", tb.reg(), sp_.reg(), lambda e: e.activation(out=sp_[0:32, :], in_=tb[0:32, 128:256], func=AF.Copy))
        S.dma("sp", f"os{sl_}", ss_.reg(), [], lambda e: e.dma_start(out=ncs_new[:, c_ * 128:(c_ + 1) * 128], in_=ss_[:]),
              final=True)
        S.dma("sp", f"op{sl_}", sp_.reg(), [], lambda e: e.dma_start(out=ncp_o[:, c_ * 128:(c_ + 1) * 128],
                                                              in_=sp_[0:32, :]), final=True)
        KP = 19 if c_ < 7 else 31
        for k in range(KP, 31):
            i_ap = vb[:, 2 + k:1026 + k]
            i_rg = vb.reg((2 + k, 1026 + k))
            o_ap = acc[:, 0:1024]
            o_rg = acc.reg((0, 1024))
            if k == KP:
                S.op("dve", regs(i_rg, cst_r), o_rg,
                     lambda e, i_ap=i_ap, o_ap=o_ap, k=k: e.tensor_scalar(
                         out=o_ap, in0=i_ap, scalar1=WDW(c_, k), scalar2=BDW(c_), op0=ALU.mult, op1=ALU.add))
            else:
                S.op("dve", regs(i_rg, cst_r, o_rg), o_rg,
                     lambda e, i_ap=i_ap, o_ap=o_ap, k=k: e.scalar_tensor_tensor(
                         out=o_ap, in0=i_ap, scalar=WDW(c_, k), in1=o_ap, op0=ALU.mult, op1=ALU.add))
        for (t0, t1, voff) in CB:
            n = t1 - t0
            bk = next_bank()
            nk = KP if voff is not None else 31

            def pe_fn(e, bk=bk, t0=t0, n=n, voff=voff, nk=nk):
                ins = None
                for k in range(nk):
                    dg = dgA if k < 16 else dgB
                    if voff is not None:
                        ins = e.matmul(bk[:, 0:n], dg[:, k % 16, :], vb[:, voff + k:voff + k + n],
                                       start=(k == 0), stop=(k == nk - 1))
                    else:
                        ins = e.matmul(bk[:, 0:128].rearrange("p (b t) -> p b t", b=16), dg[:, k % 16, :],
                                       ext_s[:, c_, :, k:k + 8], start=(k == 0), stop=(k == nk - 1))
                return ins
            rd = regs(dgA.reg(), dgB.reg(), vb.reg() if voff is not None else ext_s.reg(c_))
            S.op("pe", rd, bk.reg(), pe_fn)
            if voff is not None and KP < 31:
                S.op("dve", regs(bk.reg(), acc.reg((t0, t1))), acc.reg((t0, t1)),
                     lambda e, bk=bk, t0=t0, t1=t1, n=n: e.tensor_tensor(out=acc[:, t0:t1], in0=bk[:, 0:n],
                                                                         in1=acc[:, t0:t1], op=ALU.add))
            else:
                S.op("act", regs(bk.reg(), cst_r), acc.reg((t0, t1)),
                     lambda e, bk=bk, t0=t0, t1=t1, n=n: e.activation(out=acc[:, t0:t1], in_=bk[:, 0:n],
                                                                      func=AF.Identity, bias=BDW(c_)))

    def a2_evs(c_):
        sg_ = vacc[c_]
        vb, vtl = v_bf[c_ % 2], vt[c_ % 2]

        def ev_b(bk, c0, c1, n):
            S.op("act", bk.reg(), sg_.reg((c0, c1)),
                 lambda e: e.activation(out=sg_[:, c0:c1], in_=bk[:, 0:n], func=AF.Sigmoid))

        def ev_a(bk, c0, c1, n):
            S.op("dve", regs(bk.reg(), sg_.reg((c0, c1))), vb.reg((c0, c1)),
                 lambda e: e.tensor_tensor(out=vb[:, c0:c1], in0=bk[:, 0:n], in1=sg_[:, c0:c1], op=ALU.mult))
            if c0 == 1024:
                S.op("dve", regs(bk.reg(), sg_.reg((c0, c1))), vtl.reg(),
                     lambda e: e.tensor_tensor(out=vtl[:, :], in0=bk[:, 0:n], in1=sg_[:, c0:c1], op=ALU.mult))
        return ev_b, ev_a

    early = stop_after != "A1"
    evb0, eva0 = a2_evs(0)

    def early_blocks(bl):
        if early:
            stage1(CH_B + 0, True, evb0, blocks=bl)
            stage1(CH_A + 0, True, eva0, blocks=bl)

    prev_evs = stage0(x_halo, 32, 0, 9)
    for i in range(NT):
        evs_ = stage0(x_tok[i], 128, 32 + 128 * i, i)
        for ev_ in prev_evs:
            ev_()
        prev_evs = evs_
        if early:
            if i == 4:
                stage1(CH_B + 0, True, evb0, blocks=[0])
            if i == 6:
                stage1(CH_A + 0, True, eva0, blocks=[0])
    for ev_ in prev_evs:
        ev_()
    if early:
        stage1(CH_B + 0, True, evb0, blocks=[1])
        stage1(CH_A + 0, True, eva0, blocks=[1])
    early_blocks([2])
    W_LIMIT[0] = 10 ** 6
    S.dma("pool", "wpl", [], wpool.reg(),
          lambda e: e.dma_start(out=wpool[:], in_=w_pool_d.rearrange("p (g k c) -> p g k c", g=4, k=2)))
    p_load()
    for j_ in range(4):
        conv_q.append(lambda j_=j_: state_T(st_conv, 4, 30, ext_s, j_))
    for j_ in range(2):
        conv_q.append(lambda j_=j_: state_T(st_pool, 8, 15, ext_u, j_))
    conv_q.append(lambda: S.dma("sp", "out", [], [], lambda e: e.dma_start(out=ncs_old, in_=st_conv[:, 8:30, :]), final=True))
    conv_q.append(lambda: S.dma("sp", "out", [], [], lambda e: e.dma_start(out=nps_old, in_=st_pool[:, 8:15, :]), final=True))
    CONV_RATE[0] = 1

    pending = []
    for c_ in (range(8) if stop_after != "A1" else []):
        slot = c_ % 2
        if c_ > 0:
            ev_b, ev_a = a2_evs(c_)
            stage1(CH_B + c_, True, ev_b)
            if pending:
                conv_prep(*pending[0])
            stage1(CH_A + c_, True, ev_a)
        if pending:
            conv_run(10 ** 6)
            if c_ == 7:
                conv_prep(7, slot)
            conv_pe(*pending.pop(0))
        pending.append((c_, slot))
    if stop_after != "A1":
        conv_pe(*pending.pop(0))

    if stop_after not in ("A1", "A2"):
        for i in range(NT):
            conv_q.append(lambda i=i: p_T(i))
        CONV_RATE[0] = 1
        def za_part(g, cc):
            wwin = 2 ** (g + 1)
            ch = 2 * g + cc
            slot = (g % 2) * 2 + cc

            def ev_za(bk, c0, c1, n, slot=slot):
                st = sgt[acc_rot[0] % 3]
                S.op("act", bk.reg((0, n)), st.reg((0, n)),
                     lambda e: e.activation(out=st[:, 0:n], in_=bk[:, 0:n], func=AF.Sigmoid))
                S.op("dve", regs(bk.reg((0, n)), st.reg((0, n))), silu_za.reg(slot, (c0 - 32, c1 - 32)),
                     lambda e: e.tensor_tensor(out=silu_za[:, slot, c0 - 32:c1 - 32], in0=bk[:, 0:n],
                                               in1=st[:, 0:n], op=ALU.mult))
            stage1(CH_ZA + ch, False, ev_za)

        def u_part(g, cc):
            wwin = 2 ** (g + 1)
            ch = 2 * g + cc
            slot = (g % 2) * 2 + cc

            def ev_u(bk, c0, c1, n):
                S.op("act", bk.reg((0, n)), ubuf.reg((c0, c1)),
                     lambda e: e.activation(out=ubuf[:, c0:c1], in_=bk[:, 0:n], func=AF.Copy))
            stage1(CH_U + ch, True, ev_u)
            S.op("act", ubuf.reg((1056, W)), ext_u.reg(ch),
                 lambda e, ch=ch: e.activation(out=ext_u[:, ch, :, 15:23],
                                               in_=ubuf[:, 1056:W].rearrange("p (b t) -> p b t", b=16),
                                               func=AF.Copy))
            tb = TPB[ch % 2]

            def pe_fn(e, tb=tb):
                e.transpose(tb[:, 0:128], ubuf[:, 1056:W], ident_f[:])
                return e.transpose(tb[0:32, 128:256], ubuf[:, 1024:1056], ident_f[:])
            S.op("pe", regs(ubuf.reg((1024, W)), ident_f.reg()), tb.reg((0, 256)), pe_fn)
            sl_ = so_n[0] % 2
            so_n[0] += 1
            ss_, sp_ = so_s[sl_], so_p[sl_]
            S.op("act", tb.reg((0, 128)), ss_.reg(),
                 lambda e, tb=tb, ss_=ss_: e.activation(out=ss_[:], in_=tb[:, 0:128], func=AF.Copy))
            S.op("act", tb.reg((128, 256)), sp_.reg(),
                 lambda e, tb=tb, sp_=sp_: e.activation(out=sp_[0:32, :], in_=tb[0:32, 128:256], func=AF.Copy))
            S.dma("sp", f"os{sl_}", ss_.reg(), [],
                  lambda e, ch=ch, ss_=ss_: e.dma_start(out=nps_new[:, ch * 128:(ch + 1) * 128], in_=ss_[:]), final=True)
            S.dma("sp", f"op{sl_}", sp_.reg(), [],
                  lambda e, ch=ch, sp_=sp_: e.dma_start(out=npp_o[:, ch * 128:(ch + 1) * 128], in_=sp_[0:32, :]),
                  final=True)
            src = ubuf
            for l in range(1, g + 2):
                sh = 2 ** (l - 1)
                lo = 2 ** l
                dst = uscr[(l - 1) % 2]
                S.op("dve", src.reg((lo - sh, 1056)), dst.reg((lo, 1056)),
                     lambda e, src=src, dst=dst, lo=lo, sh=sh: e.tensor_tensor(
                         out=dst[:, lo:1056], in0=src[:, lo:1056], in1=src[:, lo - sh:1056 - sh], op=ALU.add))
                src = dst
            win = src
            ssrc_ap = ext_u[:, ch, :, :]
            ssrc_reg = ext_u.reg(ch)
            for l in range(1, g + 2):
                sh = 2 ** (l - 1)
                lo = 2 ** l - 1
                d = exs[:, (l - 1) % 2, :, :]
                dreg = exs.reg((l - 1) % 2)
                S.op("dve", ssrc_reg, dreg,
                     lambda e, s_=ssrc_ap, d=d, lo=lo, sh=sh: e.tensor_tensor(
                         out=d[:, :, lo:23], in0=s_[:, :, lo:23], in1=s_[:, :, lo - sh:23 - sh], op=ALU.add))
                ssrc_ap, ssrc_reg = d, dreg
            S.op("dve", regs(win.reg((32, 1056)), ubuf.reg((32, 1056))), pooled.reg(slot, (0, 1024)),
                 lambda e, win=win, slot=slot, wwin=wwin: e.scalar_tensor_tensor(
                     out=pooled[:, slot, 0:1024], in0=win[:, 32:1056], scalar=1.0 / wwin, in1=ubuf[:, 32:1056],
                     op0=ALU.mult, op1=ALU.subtract))
            S.op("dve", regs(ssrc_reg, ext_u.reg(ch)), pooled.reg(slot, (1024, T)),
                 lambda e, s_=ssrc_ap, slot=slot, wwin=wwin, ch=ch: e.scalar_tensor_tensor(
                     out=pooled[:, slot, 1024:T].rearrange("p (b t) -> p b t", b=16), in0=s_[:, :, 15:23],
                     scalar=1.0 / wwin, in1=ext_u[:, ch, :, 15:23], op0=ALU.mult, op1=ALU.subtract))
            S.op("dve", regs(win.reg((32, 48)), cst_r), tmpf.reg(),
                 lambda e, win=win, g=g: e.tensor_tensor(out=tmpf[:], in0=win[:, 32:48], in1=INVC(g), op=ALU.mult))
            S.op("dve", regs(tmpf.reg(), ubuf.reg((32, 48))), pooled.reg(slot, (0, 16)),
                 lambda e, slot=slot: e.tensor_tensor(out=pooled[:, slot, 0:16], in0=tmpf[:], in1=ubuf[:, 32:48],
                                                      op=ALU.subtract))

        def wpool_part(g):
            conv_run(10 ** 6)
            for dd in range(2):
                d = 2 * g + dd
                for (c0, c1) in [(0, 512), (512, 1024), (1024, T)]:
                    n = c1 - c0
                    bk = next_bank()

                    def pe_fn(e, bk=bk, c0=c0, c1=c1, n=n, dd=dd, g=g):
                        ins = None
                        for kc in range(2):
                            ins = e.matmul(bk[:, 0:n], wpool[:, g, kc, dd * 128:(dd + 1) * 128],
                                           pooled[:, (g % 2) * 2 + kc, c0:c1], start=(kc == 0), stop=(kc == 1))
                        return ins
                    S.op("pe", regs(wpool.reg(g), pooled.reg(((g % 2) * 2, (g % 2) * 2 + 2), (c0, c1))),
                         bk.reg((0, n)), pe_fn)
                    S.op("dve", regs(bk.reg((0, n)), silu_za.reg((g % 2) * 2 + dd, (c0, c1)), cst_r),
                         ya_in.reg(d, (c0, c1)),
                         lambda e, bk=bk, n=n, d=d, dd=dd, g=g, c0=c0, c1=c1: e.scalar_tensor_tensor(
                             out=ya_in[:, d, c0:c1], in0=bk[:, 0:n], scalar=PSC(d),
                             in1=silu_za[:, (g % 2) * 2 + dd, c0:c1], op0=ALU.mult, op1=ALU.mult))
                    after_unit()


        for g in range(4):
            u_part(g, 0)
            za_part(g, 0)
            u_part(g, 1)
            za_part(g, 1)
            if g > 0:
                wpool_part(g - 1)
        wpool_part(3)

    if stop_after not in ("A1", "A2", "A3"):
        TB = [(0, 512), (512, 1024), (1024, T)]
        for c_ in range(8):
            acc = acc_of[c_]
            s = c_ % 2
            S.op("dve", acc.reg((0, T)), ybf[s].reg(), lambda e, acc=acc, s=s: e.tensor_copy(
                out=ybf[s][:], in_=acc[:, 0:T]))
            S.op("act", acc.reg((0, T)), ysq[s].reg(), lambda e, acc=acc, s=s: e.activation(
                out=ysq[s][:], in_=acc[:, 0:T], func=AF.Square))

            def pe_fn(e, s=s, c_=c_):
                ins = None
                for bi, (c0, c1) in enumerate(TB):
                    n = c1 - c0
                    e.matmul(banks[bi][:, 0:n], ones_bf[:], ybf[s][:, c0:c1], start=(c_ == 0), stop=(c_ == 7))
                    ins = e.matmul(banks[3 + bi][:, 0:n], ones_bf[:], ysq[s][:, c0:c1], start=(c_ == 0), stop=(c_ == 7))
                return ins
            S.op("pe", regs(ybf[s].reg(), ysq[s].reg(), ones_bf.reg()),
                 regs(*[banks[b].reg() for b in range(6)]), pe_fn)
        for bi, (c0, c1) in enumerate(TB):
            n = c1 - c0
            S.op("act", banks[bi].reg((0, n)), mean_sb.reg((c0, c1)),
                 lambda e, bi=bi, c0=c0, c1=c1, n=n: e.activation(out=mean_sb[:, c0:c1], in_=banks[bi][:, 0:n],
                                                                  func=AF.Copy, scale=1.0 / DP))
        for bi, (c0, c1) in enumerate(TB):
            n = c1 - c0
            S.op("dve", mean_sb.reg((c0, c1)), rstd_sb.reg((c0, c1)),
                 lambda e, c0=c0, c1=c1: e.tensor_tensor(out=rstd_sb[:, c0:c1], in0=mean_sb[:, c0:c1],
                                                         in1=mean_sb[:, c0:c1], op=ALU.mult))
        for bi, (c0, c1) in enumerate(TB):
            n = c1 - c0
            S.op("dve", regs(banks[3 + bi].reg((0, n)), rstd_sb.reg((c0, c1))), rstd_sb.reg((c0, c1)),
                 lambda e, bi=bi, c0=c0, c1=c1, n=n: e.scalar_tensor_tensor(
                     out=rstd_sb[:, c0:c1], in0=banks[3 + bi][:, 0:n], scalar=1.0 / DP, in1=rstd_sb[:, c0:c1],
                     op0=ALU.mult, op1=ALU.subtract))
        S.op("act", regs(rstd_sb.reg(), epsb.reg()), rstd_sb.reg(),
             lambda e: e.activation(out=rstd_sb[:], in_=rstd_sb[:], func=AF.Sqrt, bias=epsb[:, 0:1]))
        S.op("dve", rstd_sb.reg(), rstd_sb.reg(), lambda e: e.reciprocal(out=rstd_sb[:], in_=rstd_sb[:]))
        acc_rot[0] = (acc_rot[0] + 5) // 6 * 6
        def zb_part(c_):
            def ev_zb(bk, c0, c1, n, c_=c_):
                st = sgt[acc_rot[0] % 3]
                S.op("act", bk.reg((0, n)), st.reg((0, n)),
                     lambda e: e.activation(out=st[:, 0:n], in_=bk[:, 0:n], func=AF.Sigmoid))
                S.op("dve", regs(bk.reg((0, n)), st.reg((0, n))), silu_zb.reg(c_, (c0 - 32, c1 - 32)),
                     lambda e: e.tensor_tensor(out=silu_zb[:, c_, c0 - 32:c1 - 32], in0=bk[:, 0:n],
                                               in1=st[:, 0:n], op=ALU.mult))
            stage1(CH_ZB + c_, False, ev_zb)


        def norm_a(c_):
            acc = acc_of[c_]
            s = c_ % 2
            AT = acc[:, 0:T]
            S.op("dve", regs(acc.reg((0, T)), mean_sb.reg()), acc.reg((0, T)),
                 lambda e: e.tensor_tensor(out=AT, in0=AT, in1=mean_sb[:], op=ALU.subtract))
            S.op("dve", regs(acc.reg((0, T)), rstd_sb.reg()), acc.reg((0, T)),
                 lambda e: e.tensor_tensor(out=AT, in0=AT, in1=rstd_sb[:], op=ALU.mult))
            S.op("act", regs(acc.reg((0, T)), cst_r), sgl[s].reg(),
                 lambda e: e.activation(out=sgl[s][:], in_=AT, func=AF.Sigmoid, scale=LNG(c_), bias=LNB(c_)))
            S.op("act", regs(acc.reg((0, T)), cst_r), acc.reg((0, T)),
                 lambda e: e.activation(out=AT, in_=AT, func=AF.Identity, scale=LNG(c_), bias=LNB(c_)))

        def norm_b(c_):
            acc = acc_of[c_]
            s = c_ % 2
            AT = acc[:, 0:T]
            S.op("dve", regs(acc.reg((0, T)), sgl[s].reg()), acc.reg((0, T)),
                 lambda e: e.tensor_tensor(out=AT, in0=AT, in1=sgl[s][:], op=ALU.mult))
            S.op("dve", regs(acc.reg((0, T)), silu_zb.reg(c_)), silu_zb.reg(c_),
                 lambda e: e.tensor_tensor(out=silu_zb[:, c_, :], in0=AT, in1=silu_zb[:, c_, :], op=ALU.mult))

        zb_part(0)
        for c_ in range(8):
            if c_ + 1 < 8:
                zb_part(c_ + 1)
            norm_a(c_)
            if c_ >= 1:
                norm_b(c_ - 1)
        norm_b(7)
    yb_in = silu_zb

    TB = [(0, 512), (512, 1024), (1024, T)]
    SS_E = 16

    def estat_unit(i, q):
        bk = next_bank()

        def pe_fn(e):
            ins = None
            for kk in range(2):
                ins = e.matmul(bk[:, :], pT[:, i, kk, :], wple[:, kk, q * 512:(q + 1) * 512],
                               start=(kk == 0), stop=(kk == 1))
            return ins
        S.op("pe", regs(pT.reg(i), wple.reg(None, (q * 512, q * 512 + 512))), bk.reg(), pe_fn)
        col = 128 + i * 4 + q
        S.op("act", bk.reg(), regs(bk.reg(), stat.reg((col, col + 1))),
             lambda e: e.activation(out=bk[:, :], in_=bk[:, :], func=AF.Square, accum_out=stat[:, col:col + 1]))

    if stop_after not in ("A1", "A2", "A3", "LN"):
        for j in range(16):
            if j == 2 and stop_after is None:
                S.dma("pool", "ccb", [], wple.reg(),
                      lambda e: e.dma_start(out=wple[:], in_=w_ple_d.rearrange("p (k c) -> p k c", k=2)))
                for i_ in range(NT):
                    for q_ in range(4):
                        conv_q.append(lambda i_=i_, q_=q_: estat_unit(i_, q_))
                CONV_RATE[0] = 1
            s = j % 2

            def ev_gate(dstb):
                def ev(bk, c0, c1, n):
                    S.op("act", bk.reg((0, n)), dstb.reg((c0 - 32, c1 - 32)),
                         lambda e: e.activation(out=dstb[:, c0 - 32:c1 - 32], in_=bk[:, 0:n], func=AF.Sigmoid))
                return ev

            def proj(kind, src_buf, gate_buf, final, j=j):
                tA = tA2[j % 2]
                sl, wv = wget(kind, j)
                for (c0, c1) in TB:
                    n = c1 - c0
                    bk = next_bank()

                    def pe_fn(e, bk=bk, c0=c0, c1=c1, n=n):
                        ins = None
                        for k in range(8):
                            ins = e.matmul(bk[:, 0:n], wv[:, k, :],
                                           src_buf[:, k, c0:c1], start=(k == 0), stop=(k == 7))
                        return ins
                    S.op("pe", regs(sl.reg(), src_buf.reg(None, (c0, c1))), bk.reg((0, n)), pe_fn)
                    if not final:
                        S.op("dve", regs(bk.reg((0, n)), gate_buf.reg((c0, c1))), tA.reg((c0, c1)),
                             lambda e, bk=bk, c0=c0, c1=c1, n=n: e.tensor_tensor(
                                 out=tA[:, c0:c1], in0=bk[:, 0:n], in1=gate_buf[:, c0:c1], op=ALU.mult))
                    else:
                        S.op("dve", regs(bk.reg((0, n)), gate_buf.reg((c0, c1))), tB.reg((c0, c1)),
                             lambda e, bk=bk, c0=c0, c1=c1, n=n: e.tensor_tensor(
                                 out=tB[:, c0:c1], in0=bk[:, 0:n], in1=gate_buf[:, c0:c1], op=ALU.mult))
                        S.op("dve", regs(tA.reg((c0, c1)), tB.reg((c0, c1))), mT.reg(j, (c0, c1)),
                             lambda e, c0=c0, c1=c1, j=j: e.tensor_tensor(out=mT[:, j, c0:c1], in0=tA[:, c0:c1],
                                                                          in1=tB[:, c0:c1], op=ALU.add))
            stage1(CH_GA + j, False, ev_gate(sga[s]))
            proj("pp", ya_in, sga[s], False)
            stage1(CH_GB + j, False, ev_gate(sgb[s]))
            if j % 2 == 1:
                proj("pc", yb_in, sgb[0], True, j=j - 1)
                proj("pc", yb_in, sgb[1], True, j=j)

    if stop_after is None:
        S.dma("sp", "cc", [], regs(gpost.reg(), gple.reg()),
              lambda e: [e.dma_start(out=gpost[:], in_=gpost_d), e.dma_start(out=gple[:], in_=gple_d)], n=2)

        corder = [("o", G, 0) for G in range(8)] + [("g", G, 0) for G in range(8)]
        c_issued = [0]

        C_LIMIT = [10 ** 6]

        def c_issue_upto(n):
            while c_issued[0] <= min(n, len(corder) - 1, C_LIMIT[0]):
                m = c_issued[0]
                kind, G, _rep = corder[m]
                sl = wslc[m % 3]
                src = w_out_d[G] if kind == "o" else w_pg_d[G]
                S.dma("pool", f"wc{m % 3}", [], sl.reg(), lambda e, sl=sl, src=src: e.dma_start(out=sl[:], in_=src))
                c_issued[0] += 1

        def cget(kind, G, rep):
            n = corder.index((kind, G, rep))
            c_issue_upto(n + 2)
            return wslc[n % 3]

        SS_Z = 32
        conv_run(10 ** 6)
        for i in range(NT):
            S.op("dve", stat.reg((128 + i * 4, 132 + i * 4)), stat.reg((SS_E + i, SS_E + i + 1)),
                 lambda e, i=i: e.tensor_reduce(out=stat[:, SS_E + i:SS_E + i + 1], in_=stat[:, 128 + i * 4:132 + i * 4],
                                                axis=mybir.AxisListType.X, op=ALU.add))
            S.op("act", regs(stat.reg((SS_E + i, SS_E + i + 1)), epsb.reg()), stat.reg((SS_E + i, SS_E + i + 1)),
                 lambda e, i=i: e.activation(out=stat[:, SS_E + i:SS_E + i + 1], in_=stat[:, SS_E + i:SS_E + i + 1],
                                             func=AF.Sqrt, scale=1.0 / D, bias=epsb[:, 0:1]))
            S.op("dve", stat.reg((SS_E + i, SS_E + i + 1)), stat.reg((SS_E + i, SS_E + i + 1)),
                 lambda e, i=i: e.reciprocal(out=stat[:, SS_E + i:SS_E + i + 1], in_=stat[:, SS_E + i:SS_E + i + 1]))

        HA, HB = [0, 1, 2, 3, 4], [5, 6, 7, 8]

        def c1_unit(G, i, sl):
            bk = next_bank()

            def pe_fn(e):
                ins = None
                for k in range(16):
                    ins = e.matmul(bk[:, 0:256], mT[:, k, i * 128:(i + 1) * 128], sl[:, k, :],
                                   start=(k == 0), stop=(k == 15))
                return ins
            S.op("pe", regs(sl.reg(), mT.reg(None, (i * 128, i * 128 + 128))), bk.reg(), pe_fn)
            S.op("act", bk.reg(), z.reg(i, (G * 256, G * 256 + 256)),
                 lambda e: e.activation(out=z[:, i, G * 256:(G + 1) * 256], in_=bk[:, 0:256], func=AF.Copy))
            col = SS_Z + i * 8 + G
            S.op("act", bk.reg(), regs(etc_[0].reg(), stat.reg((col, col + 1))),
                 lambda e: e.activation(out=etc_[0][:], in_=bk[:, 0:256], func=AF.Square,
                                        accum_out=stat[:, col:col + 1]))

        def x1_chain(i):
            c0 = SS_Z + i * 8
            rc = 25 + (i % 2)
            S.op("dve", stat.reg((c0, c0 + 8)), stat.reg((rc, rc + 1)),
                 lambda e: e.tensor_reduce(out=stat[:, rc:rc + 1], in_=stat[:, c0:c0 + 8],
                                           axis=mybir.AxisListType.X, op=ALU.add))
            S.op("act", regs(stat.reg((rc, rc + 1)), epsb.reg()), stat.reg((rc, rc + 1)),
                 lambda e: e.activation(out=stat[:, rc:rc + 1], in_=stat[:, rc:rc + 1], func=AF.Sqrt, scale=1.0 / D,
                                        bias=epsb[:, 0:1]))
            S.op("dve", stat.reg((rc, rc + 1)), stat.reg((rc, rc + 1)),
                 lambda e: e.reciprocal(out=stat[:, rc:rc + 1], in_=stat[:, rc:rc + 1]))
            S.op("dve", regs(z.reg(i), stat.reg((rc, rc + 1)), gpost.reg()), z.reg(i),
                 lambda e: e.scalar_tensor_tensor(out=z[:, i, :], in0=z[:, i, :], scalar=stat[:, rc:rc + 1],
                                                  in1=gpost[:], op0=ALU.mult, op1=ALU.mult))
            S.dma("pool", f"xa{i}", z.reg(i), z.reg(i),
                  lambda e: e.dma_start(out=z[:, i, :], in_=x_tok[i], accum_op=ALU.add))

        def x1_cast(i):
            xbl = x1b[i % 2]
            S.op("act", z.reg(i), xbl.reg(), lambda e: e.activation(out=xbl[:], in_=z[:, i, :], func=AF.Copy))

        def x1_transposes(i):
            xbl = x1b[i % 2]
            for h in range(2):
                tb = TPB[h]
                tbv = bank_bf(tb)[:, 0:1024].rearrange("p (k t) -> p k t", k=8)

                def pe_fn(e, h=h, tbv=tbv):
                    ins = None
                    for kk in range(8):
                        k = h * 8 + kk
                        ins = e.transpose(tbv[:, kk, :], xbl[:, k * 128:(k + 1) * 128], ident_bf[:])
                    return ins
                S.op("pe", regs(xbl.reg(), ident_bf.reg()), tb.reg(), pe_fn)
                if h == 0:
                    S.op("dve", tb.reg(), mT.reg((h * 8, h * 8 + 8), (i * 128, i * 128 + 128)),
                         lambda e, h=h, tbv=tbv: e.tensor_copy(out=mT[:, h * 8:h * 8 + 8, i * 128:(i + 1) * 128], in_=tbv))
                else:
                    S.op("act", tb.reg(), mT.reg((h * 8, h * 8 + 8), (i * 128, i * 128 + 128)),
                         lambda e, h=h, tbv=tbv: e.activation(out=mT[:, h * 8:h * 8 + 8, i * 128:(i + 1) * 128], in_=tbv,
                                                              func=AF.Copy))

        def c2_unit(G, i, sl):
            bk = next_bank()
            bk2 = next_bank()
            s = i % 2

            def pe_fn(e):
                ins = None
                for k in range(16):
                    ins = e.matmul(bk[:, 0:256], mT[:, k, i * 128:(i + 1) * 128], sl[:, k, :],
                                   start=(k == 0), stop=(k == 15))
                return ins
            S.op("pe", regs(sl.reg(), mT.reg(None, (i * 128, i * 128 + 128))), bk.reg(), pe_fn)

            def pe_fn2(e):
                ins = None
                for kk in range(2):
                    ins = e.matmul(bk2[:, 0:256], pT[:, i, kk, :], wple[:, kk, G * 256:(G + 1) * 256],
                                   start=(kk == 0), stop=(kk == 1))
                return ins
            S.op("pe", regs(pT.reg(i), wple.reg(None, (G * 256, G * 256 + 256))), bk2.reg(), pe_fn2)
            S.op("act", bk.reg(), sgc[s].reg(),
                 lambda e: e.activation(out=sgc[s][:], in_=bk[:, 0:256], func=AF.Sigmoid))
            S.op("dve", regs(bk2.reg(), stat.reg((SS_E + i, SS_E + i + 1)), gple.reg((G * 256, G * 256 + 256))),
                 etc_[s].reg(),
                 lambda e: e.scalar_tensor_tensor(
                     out=etc_[s][:], in0=bk2[:, 0:256], scalar=stat[:, SS_E + i:SS_E + i + 1],
                     in1=gple[:, G * 256:(G + 1) * 256], op0=ALU.mult, op1=ALU.mult))
            S.op("dve", regs(etc_[s].reg(), sgc[s].reg()), etc_[s].reg(),
                 lambda e: e.tensor_tensor(out=etc_[s][:], in0=etc_[s][:], in1=sgc[s][:], op=ALU.mult))
            S.op("dve", regs(etc_[s].reg(), z.reg(i, (G * 256, G * 256 + 256))), z.reg(i, (G * 256, G * 256 + 256)),
                 lambda e: e.tensor_tensor(out=z[:, i, G * 256:(G + 1) * 256],
                                           in0=z[:, i, G * 256:(G + 1) * 256], in1=etc_[s][:], op=ALU.add))
            S.dma("sp", "out", z.reg(i, (G * 256, G * 256 + 256)), [],
                  lambda e: e.dma_start(out=y_tok[i, :, G * 256:(G + 1) * 256],
                                        in_=z[:, i, G * 256:(G + 1) * 256]), final=True)

        ALLT = list(range(NT))
        for G in range(6):
            sl = cget("o", G, 0)
            for i in ALLT:
                c1_unit(G, i, sl)
        C_LIMIT[0] = 8
        sl6 = cget("o", 6, 0)
        sl7 = cget("o", 7, 0)
        slg0 = cget("g", 0, 0)
        done0 = []
        for st_ in range(NT + 3):
            if st_ < NT:
                c1_unit(6, st_, sl6)
                c1_unit(7, st_, sl7)
                x1_chain(st_)
            else:
                for i_ in (2 * (st_ - NT), 2 * (st_ - NT) + 1):
                    c2_unit(0, i_, slg0)
                    done0.append(i_)
            if 0 <= st_ - 3 < NT:
                x1_transposes(st_ - 3)
            if 0 <= st_ - 2 < NT:
                x1_cast(st_ - 2)
        C_LIMIT[0] = 10 ** 6
        for i in ALLT:
            if i not in done0:
                c2_unit(0, i, slg0)
        for G in range(1, 8):
            sl = cget("g", G, 0)
            for i in ALLT:
                c2_unit(G, i, sl)

    LB = dict(locals())
    for (name, buf, shape) in dbg:
        bufobj = LB[buf] if isinstance(buf, str) else buf
        if isinstance(bufobj, (list, tuple)):
            for bi_, b_ in enumerate(bufobj):
                dt_ = dram_out(f"dbg_{name}{bi_}", [128] + list(shape))
                S.dma("sp" if b_.esz == 4 else "pool", "dbgo", b_.reg(), [],
                      lambda e, dt_=dt_, b_=b_: e.dma_start(out=dt_, in_=b_[:]), final=True)
            continue
        dt_ = dram_out("dbg_" + name, [128] + list(shape))
        S.dma("pool", "dbgo", bufobj.reg(), [], lambda e, dt_=dt_, bufobj=bufobj: e.dma_start(out=dt_, in_=bufobj[:]),
              final=True)

    fin = dict(S.final_waits)
    S.prog["sp"].append(([(s, v) for s, v in fin.items()], None, None))

    semnames = sorted(S.semnames)
    sems = {}
    import contextlib
    with contextlib.ExitStack() as es:
        for sname in semnames:
            sems[sname] = es.enter_context(nc.semaphore(sname))
        block = es.enter_context(nc.Block())

        S.finalize()

        def run(e, engname):
            for waits, fn, inc in S.prog[engname]:
                for (s_, v) in waits:
                    e.wait_ge(sems[s_], S.wait_value(s_, v))
                if fn is None:
                    continue
                ins = fn(e)
                sem_, val_, is_dma = inc
                if is_dma:
                    if isinstance(ins, (list, tuple)):
                        for i_ in ins:
                            i_.then_inc(sems[sem_], 16)
                    else:
                        ins.then_inc(sems[sem_], 16)
                elif (sem_, val_) in S.rank:
                    ins.then_inc(sems[sem_], 1)

        @block.sync
        def _(e):
            run(e, "sp")

        @block.gpsimd
        def _(e):
            run(e, "pool")

        @block.scalar
        def _(e):
            run(e, "act")

        @block.vector
        def _(e):
            run(e, "dve")

        @block.tensor
        def _(e):
            run(e, "pe")
    return nc


def _prep_shared(inp):
    f = np.float32
    sh = {}
    w_in = np.asarray(inp["w_in"][0], f)
    sh["w_in_t"] = np.ascontiguousarray(w_in.reshape(16, 128, 72, 128).transpose(2, 1, 0, 3))
    w_pool = np.asarray(inp["w_pool"][0], f)
    sh["w_pool_t"] = np.ascontiguousarray(w_pool.reshape(4, 2, 128, 256).transpose(2, 0, 1, 3)).reshape(128, 2048)
    for nm, key in (("w_pp_t", "w_proj_pool"), ("w_pc_t", "w_proj_conv")):
        w = np.asarray(inp[key][0], f)
        sh[nm] = np.ascontiguousarray(w.reshape(8, 128, 16, 128).transpose(2, 1, 0, 3))
    for nm, key in (("w_out_t", "w_out"), ("w_pg_t", "w_ple_gate")):
        w = np.asarray(inp[key][0], f)
        sh[nm] = np.ascontiguousarray(w.reshape(16, 128, 8, 256).transpose(2, 1, 0, 3))
    w_ple = np.asarray(inp["w_ple"][0], f)
    sh["w_ple_t"] = np.ascontiguousarray(w_ple.reshape(2, 128, 2048).transpose(1, 0, 2)).reshape(128, 4096)
    sh["g_post_bc"] = np.ascontiguousarray(np.broadcast_to(np.asarray(inp["g_post"][0], f)[None, :], (128, D)))
    sh["g_ple_bc"] = np.ascontiguousarray(np.broadcast_to(np.asarray(inp["g_ple"][0], f)[None, :], (128, D)))
    sh["ident"] = np.eye(128, dtype=f)
    cst = np.zeros((128, 360), f)
    cst[:, 0:16] = np.asarray(inp["g_pre"][0], f).reshape(16, 128).T
    cst[:, 16:24] = np.asarray(inp["pool_scale"][0], f).reshape(8, 128).T
    cst[:, 24:32] = np.asarray(inp["b_dw"][0], f).reshape(8, 128).T
    cst[:, 32:40] = np.asarray(inp["ln_g"][0], f).reshape(8, 128).T
    cst[:, 40:48] = np.asarray(inp["ln_b"][0], f).reshape(8, 128).T
    wdw = np.asarray(inp["w_dw"][0], f)
    cst[:, 112:360] = wdw.reshape(31, 8, 128).transpose(2, 1, 0).reshape(128, 248)
    return sh, cst


def _in_maps(inp):
    f = np.float32
    sh, cst0 = _prep_shared(inp)
    xp = np.asarray(inp["x_prompt"], f)
    xsamp = np.asarray(inp["x_sample"], f)
    pp = np.asarray(inp["p_prompt"], f)[0]
    ps = np.asarray(inp["p_sample"], f)[0]
    stc = np.asarray(inp["state_conv"], f)[0]
    stp = np.asarray(inp["state_pool"], f)[0]
    maps = []
    for r in range(NCORES):
        b, half = r // 2, r % 2
        st = half * 1024
        m = dict(sh)
        xt = np.empty((NT, 128, D), f)
        xt[0:8] = xp[b, st:st + 1024].reshape(8, 128, D)
        xt[8] = xsamp[16 * r:16 * r + 16].reshape(128, D)
        m["x_tok"] = xt
        m["x_halo"] = np.ascontiguousarray(xp[b, st - 32:st]) if half else np.zeros((32, D), f)
        pt = np.empty((NT, 128, 256), f)
        pt[0:8] = pp[b, st:st + 1024].reshape(8, 128, 256)
        pt[8] = ps[16 * r:16 * r + 16].reshape(128, 256)
        m["p_tok"] = pt
        m["st_conv"] = np.ascontiguousarray(stc[16 * r:16 * r + 16])
        m["st_pool"] = np.ascontiguousarray(stp[16 * r:16 * r + 16])
        cst = cst0.copy()
        for g, wd in enumerate((2, 4, 8, 16)):
            pos = st + np.arange(16)
            cst[:, 48 + g * 16:64 + g * 16] = (1.0 / np.minimum(pos + 1, wd)).astype(f)[None, :]
        m["cst"] = cst
        maps.append(m)
    return maps


_NC_CACHE = {}


def kernel(**inputs):
    if "nc" not in _NC_CACHE:
        _NC_CACHE["nc"] = build_program()
    nc = _NC_CACHE["nc"]
    maps = _in_maps(inputs)
    res = run_bass_kernel_spmd(nc, maps, core_ids=list(range(NCORES)))
    R = res.results
    f = np.float32
    y_prompt = np.empty((4, 2048, D), f)
    y_sample = np.empty((128, 8, D), f)
    npp = np.empty((1, 4, 15, DP), f)
    ncp = np.empty((1, 4, 30, DP), f)
    nps = np.empty((1, 128, 15, DP), f)
    ncs = np.empty((1, 128, 30, DP), f)
    for r in range(NCORES):
        b, half = r // 2, r % 2
        st = half * 1024
        yt = R[r]["y_tok"]
        y_prompt[b, st:st + 1024] = yt[0:8].reshape(1024, D)
        y_sample[16 * r:16 * r + 16] = yt[8].reshape(16, 8, D)
        ncs[0, 16 * r:16 * r + 16, 0:22] = R[r]["ncs_old"]
        ncs[0, 16 * r:16 * r + 16, 22:30] = R[r]["ncs_new"].reshape(16, 8, DP)
        nps[0, 16 * r:16 * r + 16, 0:7] = R[r]["nps_old"]
        nps[0, 16 * r:16 * r + 16, 7:15] = R[r]["nps_new"].reshape(16, 8, DP)
        if half:
            ncp[0, b] = R[r]["ncp"][2:32]
            npp[0, b] = R[r]["npp"][17:32]
    return (y_prompt, y_sample, npp, ncp, nps, ncs)
```

```python
import numpy as np
import concourse.bass as bass
import concourse.mybir as mybir
from concourse.bass_utils import run_bass_kernel_spmd

F32 = mybir.dt.float32
BF16 = mybir.dt.bfloat16
AF = mybir.ActivationFunctionType
ALU = mybir.AluOpType

NCORES = 8
D = 2048
DP = 1024
NT = 9
W = 1184
T = 1152
EPS = 1e-6
SB_BASE = 16512
SB_END = 229376

CH_U, CH_ZA, CH_A, CH_B, CH_ZB, CH_GA, CH_GB = 0, 8, 16, 24, 32, 40, 56


class Sched:
    ENGS = ("pe", "act", "dve", "pool", "sp")
    GR = 1024

    def __init__(self):
        self.prog = {e: [] for e in self.ENGS}
        self.cnt = {}
        self.waited = {e: {} for e in self.ENGS}
        self.recs = {}
        self.buckets = {}
        self.semnames = set()
        self.final_waits = {}
        self.needed = set()

    def _granules(self, space, lo, hi):
        return [(space, g) for g in range(lo // self.GR, (hi - 1) // self.GR + 1)]

    def _add(self, key, val):
        old = self.recs.get(key)
        if old is None:
            for b in self._granules(key[0], key[1], key[2]):
                self.buckets.setdefault(b, set()).add(key)
        if old is None or old < val:
            self.recs[key] = val

    def _remove(self, key):
        del self.recs[key]
        for b in self._granules(key[0], key[1], key[2]):
            self.buckets[b].discard(key)

    def _overlaps(self, space, lo, hi):
        seen = set()
        for b in self._granules(space, lo, hi):
            for key in self.buckets.get(b, ()):
                if key in seen:
                    continue
                if key[1] < hi and lo < key[2]:
                    seen.add(key)
        return seen

    def _collect(self, eng_sem, is_pe, reads, writes):
        deps = {}
        cur = self.cnt.get(eng_sem, 0)

        def need(sem, val):
            if deps.get(sem, 0) < val:
                deps[sem] = val

        for (space, lo, hi) in reads:
            for key in self._overlaps(space, lo, hi):
                sem = key[3]
                if not key[4]:
                    if space == "ps" and sem != eng_sem:
                        need(sem, self.recs[key])
                    continue
                if sem == eng_sem:
                    if is_pe:
                        continue
                need(sem, self.recs[key])
        for (space, lo, hi) in writes:
            for key in self._overlaps(space, lo, hi):
                sem = key[3]
                if sem == eng_sem:
                    continue
                need(sem, self.recs[key])
        return deps

    def _record(self, sem, val, reads, writes):
        for (space, lo, hi) in writes:
            for key in list(self._overlaps(space, lo, hi)):
                if key[1] >= lo and key[2] <= hi:
                    self._remove(key)
            self._add((space, lo, hi, sem, True), val)
        for (space, lo, hi) in reads:
            self._add((space, lo, hi, sem, False), val)

    def _waits(self, eng, deps):
        out = []
        for sem, val in deps.items():
            if self.waited[eng].get(sem, 0) < val:
                self.waited[eng][sem] = val
                out.append((sem, val))
                self.needed.add((sem, val))
        return out

    def op(self, eng, reads, writes, fn):
        sem = "E_" + eng
        self.semnames.add(sem)
        deps = self._collect(sem, eng == "pe", reads, writes)
        waits = self._waits(eng, deps)
        val = self.cnt.get(sem, 0) + 1
        self.cnt[sem] = val
        self.prog[eng].append((waits, fn, (sem, val, False)))
        self._record(sem, val, reads, writes)

    def dma(self, queue, sem, reads, writes, fn, final=False, n=1):
        self.semnames.add(sem)
        deps = self._collect(sem, False, reads, writes)
        deps.pop(sem, None)
        waits = self._waits(queue, deps)
        val = self.cnt.get(sem, 0) + 16 * n
        self.cnt[sem] = val
        self.prog[queue].append((waits, fn, (sem, val, True)))
        self._record(sem, val, reads, writes)
        if final:
            self.final_waits[sem] = val

    def finalize(self):
        self.rank = {}
        by_sem = {}
        for (sem, val) in self.needed:
            if sem.startswith("E_"):
                by_sem.setdefault(sem, []).append(val)
        for sem, vals in by_sem.items():
            for r, v in enumerate(sorted(vals)):
                self.rank[(sem, v)] = r + 1

    def wait_value(self, sem, val):
        return self.rank[(sem, val)] if sem.startswith("E_") else val


class Buf:
    def __init__(self, t, space, addr, free_shape, esz):
        self.t = t
        self.space = space
        self.addr = addr
        self.shape = tuple(free_shape)
        self.esz = esz

    def __getitem__(self, k):
        return self.t[k]

    def reg(self, *idx):
        shp = self.shape
        idx = list(idx) + [None] * (len(shp) - len(idx))
        rngs = []
        for d, i in enumerate(idx):
            if i is None:
                rngs.append((0, shp[d]))
            elif isinstance(i, int):
                rngs.append((i, i + 1))
            else:
                rngs.append(i)
        strides = [1] * len(shp)
        for d in range(len(shp) - 2, -1, -1):
            strides[d] = strides[d + 1] * shp[d + 1]
        nd = len(shp)
        cut = nd
        while cut > 1 and rngs[cut - 1] == (0, shp[cut - 1]):
            cut -= 1
        out = []

        if self.space == "ps":
            return [("ps", self.addr, self.addr + 2048)]

        def rec(d, off):
            if d == cut - 1:
                lo = off + rngs[d][0] * strides[d]
                hi = off + rngs[d][1] * strides[d]
                out.append((self.space, self.addr + lo * self.esz, self.addr + hi * self.esz))
                return
            for i in range(rngs[d][0], rngs[d][1]):
                rec(d + 1, off + i * strides[d])

        rec(0, 0)
        return out


class Arena:
    def __init__(self, nc):
        self.nc = nc
        self.n = 0

    def at(self, name, addr, free_shape, dt, parts=128):
        esz = 4 if dt == F32 else 2
        size = int(np.prod(free_shape)) * esz
        assert addr % 32 == 0, (name, addr)
        assert SB_BASE <= addr and addr + size <= SB_END, (name, addr, size)
        self.n += 1
        t = self.nc.alloc_sbuf_tensor_at(f"{name}_{self.n}", [parts] + list(free_shape), dt, offset=addr)
        return Buf(t, "sb", addr, free_shape, esz)


def build_program(stop_after=None, dbg=None):
    nc = bass.Bass("TRN2", target_bir_lowering=False)
    S = Sched()
    A = Arena(nc)
    dbg = dbg or []

    def dram_in(name, shape):
        return nc.dram_tensor(name, list(shape), F32, kind="ExternalInput").ap()

    def dram_out(name, shape):
        return nc.dram_tensor(name, list(shape), F32, kind="ExternalOutput").ap()

    x_tok = dram_in("x_tok", [NT, 128, D])
    x_halo = dram_in("x_halo", [32, D])
    p_tok = dram_in("p_tok", [NT, 128, 256])
    st_conv = dram_in("st_conv", [16, 30, DP])
    st_pool = dram_in("st_pool", [16, 15, DP])
    cst_d = dram_in("cst", [128, 360])
    ident_d = dram_in("ident", [128, 128])
    w_in_d = dram_in("w_in_t", [72, 128, 16, 128])
    w_pool_d = dram_in("w_pool_t", [128, 4 * 2 * 256])
    w_pp_d = dram_in("w_pp_t", [16, 128, 8, 128])
    w_pc_d = dram_in("w_pc_t", [16, 128, 8, 128])
    w_out_d = dram_in("w_out_t", [8, 128, 16, 256])
    w_pg_d = dram_in("w_pg_t", [8, 128, 16, 256])
    w_ple_d = dram_in("w_ple_t", [128, 2 * D])
    gpost_d = dram_in("g_post_bc", [128, D])
    gple_d = dram_in("g_ple_bc", [128, D])

    y_tok = dram_out("y_tok", [NT, 128, D])
    ncs_new = dram_out("ncs_new", [128, DP])
    ncs_old = dram_out("ncs_old", [16, 22, DP])
    nps_new = dram_out("nps_new", [128, DP])
    nps_old = dram_out("nps_old", [16, 7, DP])
    ncp_o = dram_out("ncp", [32, DP])
    npp_o = dram_out("npp", [32, DP])
    dbg_out = {}

    a = SB_BASE
    C0 = a
    ident_bf = A.at("ident_bf", a, [128], BF16); a += 256
    ident_f = A.at("ident_f", a, [128], F32); a += 512
    ones_bf = A.at("ones_bf", a, [128], BF16); a += 256
    cst = A.at("cst", a, [360], F32); a += 1440
    wpool = A.at("wpool", a, [4, 2, 256], BF16); a += 4096
    stat = A.at("stat", a, [192], F32); a += 768
    pT = A.at("pT", a, [NT, 2, 128], BF16); a += NT * 2 * 128 * 2
    exs = A.at("exs", a, [2, 16, 23], F32); a += 2 * 16 * 23 * 4
    tmpf = A.at("tmpf", a, [16], F32); a += 64
    negh = A.at("negh", a, [8], F32); a += 32
    epsb = A.at("epsb", a, [8], F32); a += 32
    a = (a + 31) // 32 * 32
    R1 = a
    hT = A.at("hT", R1, [16, W], BF16); a += 16 * W * 2
    R2 = a
    vacc = [A.at(f"vacc{i}", R2 + i * W * 4, [W], F32) for i in range(8)]
    vall = A.at("vall", R2, [10, W], F32)
    a += 10 * W * 4
    R3 = a
    mT = A.at("mT", R3, [16, T], BF16); a += 16 * T * 2
    R4 = a
    ext_s = A.at("ext_s", R4, [8, 16, 38], BF16); a += 8 * 16 * 38 * 4
    v_bf = [A.at(f"v_bf{i}", R4 + 9728 + i * 2368, [W], BF16) for i in range(2)]
    vt = [A.at(f"vt{i}", R4 + 9728 + 4736 + i * 640, [160], F32) for i in range(2)]
    dgA = A.at("dgA", R2 + 8 * W * 4, [16, 128], BF16)
    dgB = A.at("dgB", R2 + 8 * W * 4 + 4096, [16, 128], BF16)
    wdwc = A.at("wdwc", R2 + 8 * W * 4 + 8192, [32], F32)
    dgA2 = A.at("dgA2", R3, [16, 128], BF16)
    dgB2 = A.at("dgB2", R3 + 4096, [16, 128], BF16)
    wdwc2 = A.at("wdwc2", R3 + 8192, [32], F32)
    R4b = a
    ya_in = A.at("ya_in", R4b, [8, T], BF16); a += 8 * T * 2
    R5 = a
    wsl = [A.at(f"wsl{i}", R5 + i * 4096, [2048], BF16) for i in range(4)]
    a += 4 * 4096
    R6 = a
    a += 18432
    assert a <= SB_END, a

    xs = [A.at(f"xs{i}", R3 + i * 8192, [D], F32) for i in range(3)]
    xs += [A.at(f"xs{3 + i}", R4b + i * 8192, [D], F32) for i in range(2)]
    pall = A.at("pall", R4b, [NT, 256], BF16)
    xb = [A.at(f"xb{i}", R3 + 24576 + i * 4096, [D], BF16) for i in range(2)]
    sqj = A.at("sqj", R3 + 32768, [D], BF16)
    silu_za = A.at("silu_za", R3, [4, T], BF16)
    pooled = A.at("pooled", R3 + 9216, [4, T], BF16)
    ubuf = A.at("ubuf", R3 + 18432, [W], F32)
    uscr = [A.at(f"uscr{i}", R3 + 18432 + (i + 1) * W * 4, [W], F32) for i in range(2)]
    assert 18432 + 3 * W * 4 <= 16 * T * 2
    ybf = [A.at(f"ybf{i}", R3 + i * 2304, [T], BF16) for i in range(2)]
    ysq = [A.at(f"ysq{i}", R3 + 4608 + i * 2304, [T], BF16) for i in range(2)]
    mean_sb = A.at("mean", R3 + 9216, [T], F32)
    rstd_sb = A.at("rstd", R3 + 9216 + 4608, [T], F32)
    sgl = [A.at(f"sgl{i}", R3 + 18432 + i * 4608, [T], F32) for i in range(2)]
    silu_zb = A.at("silu_zb", R4, [8, T], BF16)
    ext_u = A.at("ext_u", R6, [8, 16, 23], F32)
    sgt = [A.at(f"sgt{i}", R6 + 11776 + i * 2048, [512], F32) for i in range(3)]
    stg = A.at("stg", R6 + 11776, [DP], F32)
    stgs = [A.at(f"stgs{i}", R3 + 12288 + i * 4096, [DP], F32) for i in range(6)]
    stg_n = [0]
    sga = [A.at(f"sga{i}", R6 + i * 2304, [T], BF16) for i in range(2)]
    sgb = [A.at(f"sgb{i}", R6 + 4608 + i * 2304, [T], BF16) for i in range(2)]
    tA = A.at("tA", R6 + 9216, [T], F32)
    tA2 = [tA, A.at("tA1", R2, [T], F32)]
    tB = A.at("tB", R6 + 9216 + 4608, [T], F32)
    TAIL = a
    so_s = [A.at(f"so_s{i}", TAIL + i * 512, [128], F32) for i in range(2)]
    so_p = [A.at(f"so_p{i}", TAIL + 1024 + i * 512, [128], F32, parts=32) for i in range(2)]
    assert TAIL + 2048 <= SB_END
    z = A.at("z", R1, [NT, D], F32)
    CZ = R1 + NT * D * 4
    CZ = (CZ + 31) // 32 * 32
    wple = A.at("wple", CZ, [2, D], BF16)
    assert CZ + 8192 <= R3
    c = R4
    wslc = [None, A.at("wslc1", c, [16, 256], BF16), A.at("wslc2", c + 8192, [16, 256], BF16)]; c += 2 * 8192
    x1b = [A.at(f"x1b{i}", c + i * 4096, [D], BF16) for i in range(2)]; c += 8192
    assert c == R4b + 5120
    wslc[0] = A.at("wslc0", c, [16, 256], BF16); c += 8192
    gpost = A.at("gpost", c, [D], F32); c += 8192
    gple = A.at("gple", c, [D], F32); c += 8192
    sgc = [A.at(f"sgc{i}", c + i * 1024, [256], F32) for i in range(2)]; c += 2048
    etc_ = [A.at(f"etc{i}", c + i * 1024, [256], F32) for i in range(2)]; c += 2048
    ptile = A.at("ptile", c, [256], F32); c += 1024
    pbt = A.at("pbt", c, [256], BF16); c += 512
    assert c <= SB_END, c

    banks = []
    for i in range(8):
        t = nc.alloc_psum_tensor(f"bank{i}", [128, 512], F32)
        banks.append(Buf(t, "ps", i * 2048, [512], 4))
    acc_rot = [0]

    def next_bank():
        b = banks[acc_rot[0] % 6]
        acc_rot[0] += 1
        return b

    TPB = (banks[6], banks[7])

    def bank_bf(b):
        return b.t.bitcast(BF16)

    def regs(*lists):
        out = []
        for l in lists:
            out.extend(l)
        return out

    S.dma("sp", "cst", [], regs(cst.reg(), ident_f.reg()),
          lambda e: [e.dma_start(out=cst[:], in_=cst_d), e.dma_start(out=ident_f[:], in_=ident_d)], n=2)
    S.dma("pool", "cstb", [], ident_bf.reg(), lambda e: e.dma_start(out=ident_bf[:], in_=ident_d))
    S.op("dve", [], ones_bf.reg(), lambda e: e.memset(ones_bf[:], 1.0))
    S.op("dve", [], negh.reg(), lambda e: e.memset(negh[:], -0.5))
    S.op("dve", [], epsb.reg(), lambda e: e.memset(epsb[:], EPS))
    S.op("act", epsb.reg(), negh.reg((4, 8)), lambda e: e.activation(out=negh[:, 4:8], in_=epsb[:, 0:4], func=AF.Square))
    S.op("dve", [], stat.reg(), lambda e: e.memset(stat[:], 0.0))

    GPRE = lambda k0, k1: cst[:, k0:k1]
    PSC = lambda d: cst[:, 16 + d:17 + d]
    BDW = lambda c_: cst[:, 24 + c_:25 + c_]
    LNG = lambda c_: cst[:, 32 + c_:33 + c_]
    LNB = lambda c_: cst[:, 40 + c_:41 + c_]
    INVC = lambda g: cst[:, 48 + g * 16:64 + g * 16]
    WDW = lambda c_, k: cst[:, 112 + c_ * 31 + k:113 + c_ * 31 + k]
    cst_r = cst.reg()

    wlist = []

    def w_in_src(ch):
        return (w_in_d[ch], lambda sl: sl.t[:].rearrange("p (k c) -> p k c", k=16), [16, 128])

    def w_pr_src(dram, G):
        return (dram[G], lambda sl: sl.t[:].rearrange("p (k c) -> p k c", k=8), [8, 256])

    order = []
    for c_ in range(8):
        order.append(("in", CH_B + c_)); order.append(("in", CH_A + c_))
    for g in range(4):
        order += [("in", CH_U + 2 * g), ("in", CH_ZA + 2 * g), ("in", CH_U + 2 * g + 1), ("in", CH_ZA + 2 * g + 1)]
    for c_ in range(8):
        order.append(("in", CH_ZB + c_))
    for jp in range(0, 16, 2):
        for j in (jp, jp + 1):
            order.append(("in", CH_GA + j))
            order.append(("pp", j))
            order.append(("in", CH_GB + j))
        order.append(("pc", jp))
        order.append(("pc", jp + 1))
    wpos = {}
    for n, it in enumerate(order):
        wpos[it] = n
    w_issued = [0]

    W_LIMIT = [1]

    def w_issue_upto(n):
        while w_issued[0] <= min(n, len(order) - 1, W_LIMIT[0]):
            m = w_issued[0]
            kind, idx = order[m]
            sl = wsl[m % 4]
            if kind == "in":
                src, view = w_in_d[idx], sl.t[:].rearrange("p (k c) -> p k c", k=16)
            elif kind == "pp":
                src, view = w_pp_d[idx], sl.t[:, 0:1024].rearrange("p (k c) -> p k c", k=8)
            else:
                src, view = w_pc_d[idx], sl.t[:, 0:1024].rearrange("p (k c) -> p k c", k=8)
            S.dma("pool", f"w{m % 4}", [], sl.reg(),
                  lambda e, view=view, src=src: e.dma_start(out=view, in_=src))
            w_issued[0] += 1

    def wget(kind, idx):
        n = wpos[(kind, idx)]
        w_issue_upto(n + 3)
        sl = wsl[n % 4]
        if kind == "in":
            return sl, sl.t[:].rearrange("p (k c) -> p k c", k=16)
        return sl, sl.t[:, 0:1024].rearrange("p (k c) -> p k c", k=8)

    conv_q = []

    def conv_run(n):
        for _ in range(n):
            if not conv_q:
                return
            conv_q.pop(0)()

    def after_unit():
        conv_run(CONV_RATE[0])

    CONV_RATE = [0]

    xs_n = [0]

    def stage0(src_ap, nrows, col0, statcol):
        s = xs_n[0] % 5
        sb = xs_n[0] % 2
        xs_n[0] += 1
        xsl, xbl = xs[s], xb[sb]
        S.dma("sp", f"xs{s}", [], xsl.reg(), lambda e: e.dma_start(out=xsl[0:nrows, :], in_=src_ap))
        S.op("act", xsl.reg(), regs(sqj.reg(), stat.reg((statcol, statcol + 1))),
             lambda e: e.activation(out=sqj[0:nrows, :], in_=xsl[0:nrows, :], func=AF.Square,
                                    accum_out=stat[0:nrows, statcol:statcol + 1]))
        sc_ = stat[0:nrows, statcol:statcol + 1]
        sr_ = stat.reg((statcol, statcol + 1))
        S.op("act", regs(sr_, epsb.reg()), sr_,
             lambda e: e.activation(out=sc_, in_=sc_, func=AF.Sqrt, scale=1.0 / D, bias=epsb[0:nrows, 0:1]))
        S.op("dve", sr_, sr_, lambda e: e.reciprocal(out=sc_, in_=sc_))
        S.op("dve", regs(xsl.reg(), sr_), xbl.reg(),
             lambda e: e.tensor_scalar(out=xbl[0:nrows, :], in0=xsl[0:nrows, :], scalar1=sc_, scalar2=None, op0=ALU.mult))
        evs = []
        for h in range(2):
            tb = next_bank()
            tbv = bank_bf(tb)[:, 0:8 * nrows].rearrange("p (k t) -> p k t", k=8)

            def pe_fn(e, h=h, tbv=tbv):
                ins = None
                for kk in range(8):
                    k = h * 8 + kk
                    ins = e.transpose(tbv[:, kk, :], xbl[0:nrows, k * 128:(k + 1) * 128], ident_bf[0:nrows, 0:nrows])
                return ins
            S.op("pe", regs(xbl.reg(), ident_bf.reg()), tb.reg(), pe_fn)

            def ev(h=h, tb=tb, tbv=tbv):
                S.op("dve", regs(tb.reg(), cst_r), hT.reg((h * 8, h * 8 + 8), (col0, col0 + nrows)),
                     lambda e: e.tensor_tensor(
                         out=hT[:, h * 8:h * 8 + 8, col0:col0 + nrows], in0=tbv,
                         in1=GPRE(h * 8, h * 8 + 8).unsqueeze(2).broadcast_to([128, 8, nrows]), op=ALU.mult))
            evs.append(ev)
        return evs

    def state_T(src_dram, nb, nr, ext, j):
        si_ = stg_n[0]
        stg_n[0] += 1
        xsl = stgs[si_]
        rows = nb * nr
        src = src_dram[j * nb:(j + 1) * nb].rearrange("b r c -> (b r) c")
        S.dma("sp", f"stg{si_}", [], xsl.reg((0, DP)), lambda e: e.dma_start(out=xsl[0:rows, 0:DP], in_=src))
        for h in range(2):
            tb = TPB[h]

            def pe_fn(e, h=h, tb=tb):
                ins = None
                for cc in range(4):
                    c_ = h * 4 + cc
                    ins = e.transpose(tb[:, cc * rows:(cc + 1) * rows], xsl[0:rows, c_ * 128:(c_ + 1) * 128],
                                      ident_f[0:rows, 0:rows])
                return ins
            S.op("pe", regs(xsl.reg((0, DP)), ident_f.reg()), tb.reg(), pe_fn)
            S.op("act", tb.reg(), ext.reg((h * 4, h * 4 + 4)),
                 lambda e, h=h, tb=tb: e.activation(
                     out=ext[:, h * 4:h * 4 + 4, j * nb:(j + 1) * nb, 0:nr],
                     in_=tb[:, 0:4 * rows].rearrange("p (c b r) -> p c b r", c=4, b=nb), func=AF.Copy))

    def p_load():
        S.dma("pool", "pall", [], pall.reg(), lambda e: e.dma_start(out=pall[:], in_=p_tok.rearrange("i p c -> p i c")))

    def p_T(i):
        tb = TPB[i % 2]
        tbv = bank_bf(tb)[:, 0:256].rearrange("p (k t) -> p k t", k=2)

        def pe_fn(e):
            ins = None
            for kk in range(2):
                ins = e.transpose(tbv[:, kk, :], pall[:, i, kk * 128:(kk + 1) * 128], ident_bf[:])
            return ins
        S.op("pe", regs(pall.reg(i), ident_bf.reg()), tb.reg(), pe_fn)
        S.op("dve", tb.reg(), pT.reg(i), lambda e: e.tensor_copy(out=pT[:, i, :, :], in_=tbv))

    BLK_H = [(0, 512), (512, 1024), (1024, W)]
    BLK_N = [(32, 544), (544, 1056), (1056, W)]

    def stage1(ch, halo, evac, blocks=(0, 1, 2)):
        sl, wv = wget("in", ch)
        for (c0, c1) in [(BLK_H if halo else BLK_N)[b_] for b_ in blocks]:
            n = c1 - c0
            bk = next_bank()

            def pe_fn(e, bk=bk, c0=c0, c1=c1, n=n):
                ins = None
                for k in range(16):
                    ins = e.matmul(bk[:, 0:n], wv[:, k, :], hT[:, k, c0:c1], start=(k == 0), stop=(k == 15))
                return ins
            S.op("pe", regs(sl.reg(), hT.reg(None, (c0, c1))), bk.reg((0, n)), pe_fn)
            evac(bk, c0, c1, n)
            after_unit()

    acc_of = {c_: vacc[c_] for c_ in range(8)}
    so_n = [0]
    CB = [(0, 512, 2), (512, 1024, 514), (1024, T, None)]

    def dg_set(c_):
        return (dgA2, dgB2, wdwc2) if c_ == 7 else (dgA, dgB, wdwc)

    def conv_prep(c_, slot):
        dA_, dB_, wc_ = dg_set(c_)
        S.op("dve", cst_r, wc_.reg(), lambda e: e.tensor_copy(out=wc_[:, 0:31], in_=cst[:, 112 + c_ * 31:143 + c_ * 31]))
        for (dg, k0, nk) in ((dA_, 0, 16), (dB_, 16, 15)):
            S.op("dve", regs(ident_bf.reg(), wc_.reg()), dg.reg((0, nk)),
                 lambda e, dg=dg, k0=k0, nk=nk: e.tensor_tensor(
                     out=dg[:, 0:nk, :], in0=ident_bf[:, :].unsqueeze(1).broadcast_to([128, nk, 128]),
                     in1=wc_[:, k0:k0 + nk].unsqueeze(2).broadcast_to([128, nk, 128]), op=ALU.mult))

    def conv_pe(c_, slot):
        vb, vtl, acc = v_bf[slot], vt[slot], vacc[c_]
        dgA, dgB, _ = dg_set(c_)
        S.op("act", vb.reg((1056, W)), ext_s.reg(c_),
             lambda e: e.activation(out=ext_s[:, c_, :, 30:38],
                                    in_=vb[:, 1056:W].rearrange("p (b t) -> p b t", b=16), func=AF.Copy))
        tb = TPB[c_ % 2]

        def pe_fn(e):
            e.transpose(tb[:, 0:128], vtl[:, 32:160], ident_f[:])
            return e.transpose(tb[0:32, 128:256], vtl[:, 0:32], ident_f[:])
        S.op("pe", regs(vtl.reg(), ident_f.reg()), tb.reg(), pe_fn)
        sl_ = so_n[0] % 2
        so_n[0] += 1
        ss_, sp_ = so_s[sl_], so_p[sl_]
        S.op("act", tb.reg(), ss_.reg(), lambda e: e.activation(out=ss_[:], in_=tb[:, 0:128], func=AF.Copy))
        S.op("act", tb.reg(), sp_.reg(), lambda e: e.activation(out=sp_[0:32, :], in_=tb[0:32, 128:256], func=AF.Copy))
        S.dma("sp", f"os{sl_}", ss_.reg(), [], lambda e: e.dma_start(out=ncs_new[:, c_ * 128:(c_ + 1) * 128], in_=ss_[:]),
              final=True)
        S.dma("sp", f"op{sl_}", sp_.reg(), [], lambda e: e.dma_start(out=ncp_o[:, c_ * 128:(c_ + 1) * 128],
                                                              in_=sp_[0:32, :]), final=True)
        KP = 19 if c_ < 7 else 31
        for k in range(KP, 31):
            i_ap = vb[:, 2 + k:1026 + k]
            i_rg = vb.reg((2 + k, 1026 + k))
            o_ap = acc[:, 0:1024]
            o_rg = acc.reg((0, 1024))
            if k == KP:
                S.op("dve", regs(i_rg, cst_r), o_rg,
                     lambda e, i_ap=i_ap, o_ap=o_ap, k=k: e.tensor_scalar(
                         out=o_ap, in0=i_ap, scalar1=WDW(c_, k), scalar2=BDW(c_), op0=ALU.mult, op1=ALU.add))
            else:
                S.op("dve", regs(i_rg, cst_r, o_rg), o_rg,
                     lambda e, i_ap=i_ap, o_ap=o_ap, k=k: e.scalar_tensor_tensor(
                         out=o_ap, in0=i_ap, scalar=WDW(c_, k), in1=o_ap, op0=ALU.mult, op1=ALU.add))
        for (t0, t1, voff) in CB:
            n = t1 - t0
            bk = next_bank()
            nk = KP if voff is not None else 31

            def pe_fn(e, bk=bk, t0=t0, n=n, voff=voff, nk=nk):
                ins = None
                for k in range(nk):
                    dg = dgA if k < 16 else dgB
                    if voff is not None:
                        ins = e.matmul(bk[:, 0:n], dg[:, k % 16, :], vb[:, voff + k:voff + k + n],
                                       start=(k == 0), stop=(k == nk - 1))
                    else:
                        ins = e.matmul(bk[:, 0:128].rearrange("p (b t) -> p b t", b=16), dg[:, k % 16, :],
                                       ext_s[:, c_, :, k:k + 8], start=(k == 0), stop=(k == nk - 1))
                return ins
            rd = regs(dgA.reg(), dgB.reg(), vb.reg() if voff is not None else ext_s.reg(c_))
            S.op("pe", rd, bk.reg(), pe_fn)
            if voff is not None and KP < 31:
                S.op("dve", regs(bk.reg(), acc.reg((t0, t1))), acc.reg((t0, t1)),
                     lambda e, bk=bk, t0=t0, t1=t1, n=n: e.tensor_tensor(out=acc[:, t0:t1], in0=bk[:, 0:n],
                                                                         in1=acc[:, t0:t1], op=ALU.add))
            else:
                S.op("act", regs(bk.reg(), cst_r), acc.reg((t0, t1)),
                     lambda e, bk=bk, t0=t0, t1=t1, n=n: e.activation(out=acc[:, t0:t1], in_=bk[:, 0:n],
                                                                      func=AF.Identity, bias=BDW(c_)))

    def a2_evs(c_):
        sg_ = vacc[c_]
        vb, vtl = v_bf[c_ % 2], vt[c_ % 2]

        def ev_b(bk, c0, c1, n):
            S.op("act", bk.reg(), sg_.reg((c0, c1)),
                 lambda e: e.activation(out=sg_[:, c0:c1], in_=bk[:, 0:n], func=AF.Sigmoid))

        def ev_a(bk, c0, c1, n):
            S.op("dve", regs(bk.reg(), sg_.reg((c0, c1))), vb.reg((c0, c1)),
                 lambda e: e.tensor_tensor(out=vb[:, c0:c1], in0=bk[:, 0:n], in1=sg_[:, c0:c1], op=ALU.mult))
            if c0 == 1024:
                S.op("dve", regs(bk.reg(), sg_.reg((c0, c1))), vtl.reg(),
                     lambda e: e.tensor_tensor(out=vtl[:, :], in0=bk[:, 0:n], in1=sg_[:, c0:c1], op=ALU.mult))
        return ev_b, ev_a

    early = stop_after != "A1"
    evb0, eva0 = a2_evs(0)

    def early_blocks(bl):
        if early:
            stage1(CH_B + 0, True, evb0, blocks=bl)
            stage1(CH_A + 0, True, eva0, blocks=bl)

    prev_evs = stage0(x_halo, 32, 0, 9)
    for i in range(NT):
        evs_ = stage0(x_tok[i], 128, 32 + 128 * i, i)
        for ev_ in prev_evs:
            ev_()
        prev_evs = evs_
        if early:
            if i == 4:
                stage1(CH_B + 0, True, evb0, blocks=[0])
            if i == 6:
                stage1(CH_A + 0, True, eva0, blocks=[0])
    for ev_ in prev_evs:
        ev_()
    if early:
        stage1(CH_B + 0, True, evb0, blocks=[1])
        stage1(CH_A + 0, True, eva0, blocks=[1])
    early_blocks([2])
    W_LIMIT[0] = 10 ** 6
    S.dma("pool", "wpl", [], wpool.reg(),
          lambda e: e.dma_start(out=wpool[:], in_=w_pool_d.rearrange("p (g k c) -> p g k c", g=4, k=2)))
    p_load()
    for j_ in range(4):
        conv_q.append(lambda j_=j_: state_T(st_conv, 4, 30, ext_s, j_))
    for j_ in range(2):
        conv_q.append(lambda j_=j_: state_T(st_pool, 8, 15, ext_u, j_))
    conv_q.append(lambda: S.dma("sp", "out", [], [], lambda e: e.dma_start(out=ncs_old, in_=st_conv[:, 8:30, :]), final=True))
    conv_q.append(lambda: S.dma("sp", "out", [], [], lambda e: e.dma_start(out=nps_old, in_=st_pool[:, 8:15, :]), final=True))
    CONV_RATE[0] = 1

    pending = []
    for c_ in (range(8) if stop_after != "A1" else []):
        slot = c_ % 2
        if c_ > 0:
            ev_b, ev_a = a2_evs(c_)
            stage1(CH_B + c_, True, ev_b)
            if pending:
                conv_prep(*pending[0])
            stage1(CH_A + c_, True, ev_a)
        if pending:
            conv_run(10 ** 6)
            if c_ == 7:
                conv_prep(7, slot)
            conv_pe(*pending.pop(0))
        pending.append((c_, slot))
    if stop_after != "A1":
        conv_pe(*pending.pop(0))

    if stop_after not in ("A1", "A2"):
        for i in range(NT):
            conv_q.append(lambda i=i: p_T(i))
        CONV_RATE[0] = 1
        def za_part(g, cc):
            wwin = 2 ** (g + 1)
            ch = 2 * g + cc
            slot = (g % 2) * 2 + cc

            def ev_za(bk, c0, c1, n, slot=slot):
                st = sgt[acc_rot[0] % 3]
                S.op("act", bk.reg((0, n)), st.reg((0, n)),
                     lambda e: e.activation(out=st[:, 0:n], in_=bk[:, 0:n], func=AF.Sigmoid))
                S.op("dve", regs(bk.reg((0, n)), st.reg((0, n))), silu_za.reg(slot, (c0 - 32, c1 - 32)),
                     lambda e: e.tensor_tensor(out=silu_za[:, slot, c0 - 32:c1 - 32], in0=bk[:, 0:n],
                                               in1=st[:, 0:n], op=ALU.mult))
            stage1(CH_ZA + ch, False, ev_za)

        def u_part(g, cc):
            wwin = 2 ** (g + 1)
            ch = 2 * g + cc
            slot = (g % 2) * 2 + cc

            def ev_u(bk, c0, c1, n):
                S.op("act", bk.reg((0, n)), ubuf.reg((c0, c1)),
                     lambda e: e.activation(out=ubuf[:, c0:c1], in_=bk[:, 0:n], func=AF.Copy))
            stage1(CH_U + ch, True, ev_u)
            S.op("act", ubuf.reg((1056, W)), ext_u.reg(ch),
                 lambda e, ch=ch: e.activation(out=ext_u[:, ch, :, 15:23],
                                               in_=ubuf[:, 1056:W].rearrange("p (b t) -> p b t", b=16),
                                               func=AF.Copy))
            tb = TPB[ch % 2]

            def pe_fn(e, tb=tb):
                e.transpose(tb[:, 0:128], ubuf[:, 1056:W], ident_f[:])
                return e.transpose(tb[0:32, 128:256], ubuf[:, 1024:1056], ident_f[:])
            S.op("pe", regs(ubuf.reg((1024, W)), ident_f.reg()), tb.reg((0, 256)), pe_fn)
            sl_ = so_n[0] % 2
            so_n[0] += 1
            ss_, sp_ = so_s[sl_], so_p[sl_]
            S.op("act", tb.reg((0, 128)), ss_.reg(),
                 lambda e, tb=tb, ss_=ss_: e.activation(out=ss_[:], in_=tb[:, 0:128], func=AF.Copy))
            S.op("act", tb.reg((128, 256)), sp_.reg(),
                 lambda e, tb=tb, sp_=sp_: e.activation(out=sp_[0:32, :], in_=tb[0:32, 128:256], func=AF.Copy))
            S.dma("sp", f"os{sl_}", ss_.reg(), [],
                  lambda e, ch=ch, ss_=ss_: e.dma_start(out=nps_new[:, ch * 128:(ch + 1) * 128], in_=ss_[:]), final=True)
            S.dma("sp", f"op{sl_}", sp_.reg(), [],
                  lambda e, ch=ch, sp_=sp_: e.dma_start(out=npp_o[:, ch * 128:(ch + 1) * 128], in_=sp_[0:32, :]),
                  final=True)
            src = ubuf
            for l in range(1, g + 2):
                sh = 2 ** (l - 1)
                lo = 2 ** l
                dst = uscr[(l - 1) % 2]
                S.op("dve", src.reg((lo - sh, 1056)), dst.reg((lo, 1056)),
                     lambda e, src=src, dst=dst, lo=lo, sh=sh: e.tensor_tensor(
                         out=dst[:, lo:1056], in0=src[:, lo:1056], in1=src[:, lo - sh:1056 - sh], op=ALU.add))
                src = dst
            win = src
            ssrc_ap = ext_u[:, ch, :, :]
            ssrc_reg = ext_u.reg(ch)
            for l in range(1, g + 2):
                sh = 2 ** (l - 1)
                lo = 2 ** l - 1
                d = exs[:, (l - 1) % 2, :, :]
                dreg = exs.reg((l - 1) % 2)
                S.op("dve", ssrc_reg, dreg,
                     lambda e, s_=ssrc_ap, d=d, lo=lo, sh=sh: e.tensor_tensor(
                         out=d[:, :, lo:23], in0=s_[:, :, lo:23], in1=s_[:, :, lo - sh:23 - sh], op=ALU.add))
                ssrc_ap, ssrc_reg = d, dreg
            S.op("dve", regs(win.reg((32, 1056)), ubuf.reg((32, 1056))), pooled.reg(slot, (0, 1024)),
                 lambda e, win=win, slot=slot, wwin=wwin: e.scalar_tensor_tensor(
                     out=pooled[:, slot, 0:1024], in0=win[:, 32:1056], scalar=1.0 / wwin, in1=ubuf[:, 32:1056],
                     op0=ALU.mult, op1=ALU.subtract))
            S.op("dve", regs(ssrc_reg, ext_u.reg(ch)), pooled.reg(slot, (1024, T)),
                 lambda e, s_=ssrc_ap, slot=slot, wwin=wwin, ch=ch: e.scalar_tensor_tensor(
                     out=pooled[:, slot, 1024:T].rearrange("p (b t) -> p b t", b=16), in0=s_[:, :, 15:23],
                     scalar=1.0 / wwin, in1=ext_u[:, ch, :, 15:23], op0=ALU.mult, op1=ALU.subtract))
            S.op("dve", regs(win.reg((32, 48)), cst_r), tmpf.reg(),
                 lambda e, win=win, g=g: e.tensor_tensor(out=tmpf[:], in0=win[:, 32:48], in1=INVC(g), op=ALU.mult))
            S.op("dve", regs(tmpf.reg(), ubuf.reg((32, 48))), pooled.reg(slot, (0, 16)),
                 lambda e, slot=slot: e.tensor_tensor(out=pooled[:, slot, 0:16], in0=tmpf[:], in1=ubuf[:, 32:48],
                                                      op=ALU.subtract))

        def wpool_part(g):
            conv_run(10 ** 6)
            for dd in range(2):
                d = 2 * g + dd
                for (c0, c1) in [(0, 512), (512, 1024), (1024, T)]:
                    n = c1 - c0
                    bk = next_bank()

                    def pe_fn(e, bk=bk, c0=c0, c1=c1, n=n, dd=dd, g=g):
                        ins = None
                        for kc in range(2):
                            ins = e.matmul(bk[:, 0:n], wpool[:, g, kc, dd * 128:(dd + 1) * 128],
                                           pooled[:, (g % 2) * 2 + kc, c0:c1], start=(kc == 0), stop=(kc == 1))
                        return ins
                    S.op("pe", regs(wpool.reg(g), pooled.reg(((g % 2) * 2, (g % 2) * 2 + 2), (c0, c1))),
                         bk.reg((0, n)), pe_fn)
                    S.op("dve", regs(bk.reg((0, n)), silu_za.reg((g % 2) * 2 + dd, (c0, c1)), cst_r),
                         ya_in.reg(d, (c0, c1)),
                         lambda e, bk=bk, n=n, d=d, dd=dd, g=g, c0=c0, c1=c1: e.scalar_tensor_tensor(
                             out=ya_in[:, d, c0:c1], in0=bk[:, 0:n], scalar=PSC(d),
                             in1=silu_za[:, (g % 2) * 2 + dd, c0:c1], op0=ALU.mult, op1=ALU.mult))
                    after_unit()


        for g in range(4):
            u_part(g, 0)
            za_part(g, 0)
            u_part(g, 1)
            za_part(g, 1)
            if g > 0:
                wpool_part(g - 1)
        wpool_part(3)

    if stop_after not in ("A1", "A2", "A3"):
        TB = [(0, 512), (512, 1024), (1024, T)]
        for c_ in range(8):
            acc = acc_of[c_]
            s = c_ % 2
            S.op("dve", acc.reg((0, T)), ybf[s].reg(), lambda e, acc=acc, s=s: e.tensor_copy(
                out=ybf[s][:], in_=acc[:, 0:T]))
            S.op("act", acc.reg((0, T)), ysq[s].reg(), lambda e, acc=acc, s=s: e.activation(
                out=ysq[s][:], in_=acc[:, 0:T], func=AF.Square))

            def pe_fn(e, s=s, c_=c_):
                ins = None
                for bi, (c0, c1) in enumerate(TB):
                    n = c1 - c0
                    e.matmul(banks[bi][:, 0:n], ones_bf[:], ybf[s][:, c0:c1], start=(c_ == 0), stop=(c_ == 7))
                    ins = e.matmul(banks[3 + bi][:, 0:n], ones_bf[:], ysq[s][:, c0:c1], start=(c_ == 0), stop=(c_ == 7))
                return ins
            S.op("pe", regs(ybf[s].reg(), ysq[s].reg(), ones_bf.reg()),
                 regs(*[banks[b].reg() for b in range(6)]), pe_fn)
        for bi, (c0, c1) in enumerate(TB):
            n = c1 - c0
            S.op("act", banks[bi].reg((0, n)), mean_sb.reg((c0, c1)),
                 lambda e, bi=bi, c0=c0, c1=c1, n=n: e.activation(out=mean_sb[:, c0:c1], in_=banks[bi][:, 0:n],
                                                                  func=AF.Copy, scale=1.0 / DP))
        for bi, (c0, c1) in enumerate(TB):
            n = c1 - c0
            S.op("dve", mean_sb.reg((c0, c1)), rstd_sb.reg((c0, c1)),
                 lambda e, c0=c0, c1=c1: e.tensor_tensor(out=rstd_sb[:, c0:c1], in0=mean_sb[:, c0:c1],
                                                         in1=mean_sb[:, c0:c1], op=ALU.mult))
        for bi, (c0, c1) in enumerate(TB):
            n = c1 - c0
            S.op("dve", regs(banks[3 + bi].reg((0, n)), rstd_sb.reg((c0, c1))), rstd_sb.reg((c0, c1)),
                 lambda e, bi=bi, c0=c0, c1=c1, n=n: e.scalar_tensor_tensor(
                     out=rstd_sb[:, c0:c1], in0=banks[3 + bi][:, 0:n], scalar=1.0 / DP, in1=rstd_sb[:, c0:c1],
                     op0=ALU.mult, op1=ALU.subtract))
        S.op("act", regs(rstd_sb.reg(), epsb.reg()), rstd_sb.reg(),
             lambda e: e.activation(out=rstd_sb[:], in_=rstd_sb[:], func=AF.Sqrt, bias=epsb[:, 0:1]))
        S.op("dve", rstd_sb.reg(), rstd_sb.reg(), lambda e: e.reciprocal(out=rstd_sb[:], in_=rstd_sb[:]))
        acc_rot[0] = (acc_rot[0] + 5) // 6 * 6
        def zb_part(c_):
            def ev_zb(bk, c0, c1, n, c_=c_):
                st = sgt[acc_rot[0] % 3]
                S.op("act", bk.reg((0, n)), st.reg((0, n)),
                     lambda e: e.activation(out=st[:, 0:n], in_=bk[:, 0:n], func=AF.Sigmoid))
                S.op("dve", regs(bk.reg((0, n)), st.reg((0, n))), silu_zb.reg(c_, (c0 - 32, c1 - 32)),
                     lambda e: e.tensor_tensor(out=silu_zb[:, c_, c0 - 32:c1 - 32], in0=bk[:, 0:n],
                                               in1=st[:, 0:n], op=ALU.mult))
            stage1(CH_ZB + c_, False, ev_zb)


        def norm_a(c_):
            acc = acc_of[c_]
            s = c_ % 2
            AT = acc[:, 0:T]
            S.op("dve", regs(acc.reg((0, T)), mean_sb.reg()), acc.reg((0, T)),
                 lambda e: e.tensor_tensor(out=AT, in0=AT, in1=mean_sb[:], op=ALU.subtract))
            S.op("dve", regs(acc.reg((0, T)), rstd_sb.reg()), acc.reg((0, T)),
                 lambda e: e.tensor_tensor(out=AT, in0=AT, in1=rstd_sb[:], op=ALU.mult))
            S.op("act", regs(acc.reg((0, T)), cst_r), sgl[s].reg(),
                 lambda e: e.activation(out=sgl[s][:], in_=AT, func=AF.Sigmoid, scale=LNG(c_), bias=LNB(c_)))
            S.op("act", regs(acc.reg((0, T)), cst_r), acc.reg((0, T)),
                 lambda e: e.activation(out=AT, in_=AT, func=AF.Identity, scale=LNG(c_), bias=LNB(c_)))

        def norm_b(c_):
            acc = acc_of[c_]
            s = c_ % 2
            AT = acc[:, 0:T]
            S.op("dve", regs(acc.reg((0, T)), sgl[s].reg()), acc.reg((0, T)),
                 lambda e: e.tensor_tensor(out=AT, in0=AT, in1=sgl[s][:], op=ALU.mult))
            S.op("dve", regs(acc.reg((0, T)), silu_zb.reg(c_)), silu_zb.reg(c_),
                 lambda e: e.tensor_tensor(out=silu_zb[:, c_, :], in0=AT, in1=silu_zb[:, c_, :], op=ALU.mult))

        zb_part(0)
        for c_ in range(8):
            if c_ + 1 < 8:
                zb_part(c_ + 1)
            norm_a(c_)
            if c_ >= 1:
                norm_b(c_ - 1)
        norm_b(7)
    yb_in = silu_zb

    TB = [(0, 512), (512, 1024), (1024, T)]
    SS_E = 16

    def estat_unit(i, q):
        bk = next_bank()

        def pe_fn(e):
            ins = None
            for kk in range(2):
                ins = e.matmul(bk[:, :], pT[:, i, kk, :], wple[:, kk, q * 512:(q + 1) * 512],
                               start=(kk == 0), stop=(kk == 1))
            return ins
        S.op("pe", regs(pT.reg(i), wple.reg(None, (q * 512, q * 512 + 512))), bk.reg(), pe_fn)
        col = 128 + i * 4 + q
        S.op("act", bk.reg(), regs(bk.reg(), stat.reg((col, col + 1))),
             lambda e: e.activation(out=bk[:, :], in_=bk[:, :], func=AF.Square, accum_out=stat[:, col:col + 1]))

    if stop_after not in ("A1", "A2", "A3", "LN"):
        for j in range(16):
            if j == 2 and stop_after is None:
                S.dma("pool", "ccb", [], wple.reg(),
                      lambda e: e.dma_start(out=wple[:], in_=w_ple_d.rearrange("p (k c) -> p k c", k=2)))
                for i_ in range(NT):
                    for q_ in range(4):
                        conv_q.append(lambda i_=i_, q_=q_: estat_unit(i_, q_))
                CONV_RATE[0] = 1
            s = j % 2

            def ev_gate(dstb):
                def ev(bk, c0, c1, n):
                    S.op("act", bk.reg((0, n)), dstb.reg((c0 - 32, c1 - 32)),
                         lambda e: e.activation(out=dstb[:, c0 - 32:c1 - 32], in_=bk[:, 0:n], func=AF.Sigmoid))
                return ev

            def proj(kind, src_buf, gate_buf, final, j=j):
                tA = tA2[j % 2]
                sl, wv = wget(kind, j)
                for (c0, c1) in TB:
                    n = c1 - c0
                    bk = next_bank()

                    def pe_fn(e, bk=bk, c0=c0, c1=c1, n=n):
                        ins = None
                        for k in range(8):
                            ins = e.matmul(bk[:, 0:n], wv[:, k, :],
                                           src_buf[:, k, c0:c1], start=(k == 0), stop=(k == 7))
                        return ins
                    S.op("pe", regs(sl.reg(), src_buf.reg(None, (c0, c1))), bk.reg((0, n)), pe_fn)
                    if not final:
                        S.op("dve", regs(bk.reg((0, n)), gate_buf.reg((c0, c1))), tA.reg((c0, c1)),
                             lambda e, bk=bk, c0=c0, c1=c1, n=n: e.tensor_tensor(
                                 out=tA[:, c0:c1], in0=bk[:, 0:n], in1=gate_buf[:, c0:c1], op=ALU.mult))
                    else:
                        S.op("dve", regs(bk.reg((0, n)), gate_buf.reg((c0, c1))), tB.reg((c0, c1)),
                             lambda e, bk=bk, c0=c0, c1=c1, n=n: e.tensor_tensor(
                                 out=tB[:, c0:c1], in0=bk[:, 0:n], in1=gate_buf[:, c0:c1], op=ALU.mult))
                        S.op("dve", regs(tA.reg((c0, c1)), tB.reg((c0, c1))), mT.reg(j, (c0, c1)),
                             lambda e, c0=c0, c1=c1, j=j: e.tensor_tensor(out=mT[:, j, c0:c1], in0=tA[:, c0:c1],
                                                                          in1=tB[:, c0:c1], op=ALU.add))
            stage1(CH_GA + j, False, ev_gate(sga[s]))
            proj("pp", ya_in, sga[s], False)
            stage1(CH_GB + j, False, ev_gate(sgb[s]))
            if j % 2 == 1:
                proj("pc", yb_in, sgb[0], True, j=j - 1)
                proj("pc", yb_in, sgb[1], True, j=j)

    if stop_after is None:
        S.dma("sp", "cc", [], regs(gpost.reg(), gple.reg()),
              lambda e: [e.dma_start(out=gpost[:], in_=gpost_d), e.dma_start(out=gple[:], in_=gple_d)], n=2)

        corder = [("o", G, 0) for G in range(8)] + [("g", G, 0) for G in range(8)]
        c_issued = [0]

        C_LIMIT = [10 ** 6]

        def c_issue_upto(n):
            while c_issued[0] <= min(n, len(corder) - 1, C_LIMIT[0]):
                m = c_issued[0]
                kind, G, _rep = corder[m]
                sl = wslc[m % 3]
                src = w_out_d[G] if kind == "o" else w_pg_d[G]
                S.dma("pool", f"wc{m % 3}", [], sl.reg(), lambda e, sl=sl, src=src: e.dma_start(out=sl[:], in_=src))
                c_issued[0] += 1

        def cget(kind, G, rep):
            n = corder.index((kind, G, rep))
            c_issue_upto(n + 2)
            return wslc[n % 3]

        SS_Z = 32
        conv_run(10 ** 6)
        for i in range(NT):
            S.op("dve", stat.reg((128 + i * 4, 132 + i * 4)), stat.reg((SS_E + i, SS_E + i + 1)),
                 lambda e, i=i: e.tensor_reduce(out=stat[:, SS_E + i:SS_E + i + 1], in_=stat[:, 128 + i * 4:132 + i * 4],
                                                axis=mybir.AxisListType.X, op=ALU.add))
            S.op("act", regs(stat.reg((SS_E + i, SS_E + i + 1)), epsb.reg()), stat.reg((SS_E + i, SS_E + i + 1)),
                 lambda e, i=i: e.activation(out=stat[:, SS_E + i:SS_E + i + 1], in_=stat[:, SS_E + i:SS_E + i + 1],
                                             func=AF.Sqrt, scale=1.0 / D, bias=epsb[:, 0:1]))
            S.op("dve", stat.reg((SS_E + i, SS_E + i + 1)), stat.reg((SS_E + i, SS_E + i + 1)),
                 lambda e, i=i: e.reciprocal(out=stat[:, SS_E + i:SS_E + i + 1], in_=stat[:, SS_E + i:SS_E + i + 1]))

        HA, HB = [0, 1, 2, 3, 4], [5, 6, 7, 8]

        def c1_unit(G, i, sl):
            bk = next_bank()

            def pe_fn(e):
                ins = None
                for k in range(16):
                    ins = e.matmul(bk[:, 0:256], mT[:, k, i * 128:(i + 1) * 128], sl[:, k, :],
                                   start=(k == 0), stop=(k == 15))
                return ins
            S.op("pe", regs(sl.reg(), mT.reg(None, (i * 128, i * 128 + 128))), bk.reg(), pe_fn)
            S.op("act", bk.reg(), z.reg(i, (G * 256, G * 256 + 256)),
                 lambda e: e.activation(out=z[:, i, G * 256:(G + 1) * 256], in_=bk[:, 0:256], func=AF.Copy))
            col = SS_Z + i * 8 + G
            S.op("act", bk.reg(), regs(etc_[0].reg(), stat.reg((col, col + 1))),
                 lambda e: e.activation(out=etc_[0][:], in_=bk[:, 0:256], func=AF.Square,
                                        accum_out=stat[:, col:col + 1]))

        def x1_chain(i):
            c0 = SS_Z + i * 8
            rc = 25 + (i % 2)
            S.op("dve", stat.reg((c0, c0 + 8)), stat.reg((rc, rc + 1)),
                 lambda e: e.tensor_reduce(out=stat[:, rc:rc + 1], in_=stat[:, c0:c0 + 8],
                                           axis=mybir.AxisListType.X, op=ALU.add))
            S.op("act", regs(stat.reg((rc, rc + 1)), epsb.reg()), stat.reg((rc, rc + 1)),
                 lambda e: e.activation(out=stat[:, rc:rc + 1], in_=stat[:, rc:rc + 1], func=AF.Sqrt, scale=1.0 / D,
                                        bias=epsb[:, 0:1]))
            S.op("dve", stat.reg((rc, rc + 1)), stat.reg((rc, rc + 1)),
                 lambda e: e.reciprocal(out=stat[:, rc:rc + 1], in_=stat[:, rc:rc + 1]))
            S.op("dve", regs(z.reg(i), stat.reg((rc, rc + 1)), gpost.reg()), z.reg(i),
                 lambda e: e.scalar_tensor_tensor(out=z[:, i, :], in0=z[:, i, :], scalar=stat[:, rc:rc + 1],
                                                  in1=gpost[:], op0=ALU.mult, op1=ALU.mult))
            S.dma("pool", f"xa{i}", z.reg(i), z.reg(i),
                  lambda e: e.dma_start(out=z[:, i, :], in_=x_tok[i], accum_op=ALU.add))

        def x1_cast(i):
            xbl = x1b[i % 2]
            S.op("act", z.reg(i), xbl.reg(), lambda e: e.activation(out=xbl[:], in_=z[:, i, :], func=AF.Copy))

        def x1_transposes(i):
            xbl = x1b[i % 2]
            for h in range(2):
                tb = TPB[h]
                tbv = bank_bf(tb)[:, 0:1024].rearrange("p (k t) -> p k t", k=8)

                def pe_fn(e, h=h, tbv=tbv):
                    ins = None
                    for kk in range(8):
                        k = h * 8 + kk
                        ins = e.transpose(tbv[:, kk, :], xbl[:, k * 128:(k + 1) * 128], ident_bf[:])
                    return ins
                S.op("pe", regs(xbl.reg(), ident_bf.reg()), tb.reg(), pe_fn)
                if h == 0:
                    S.op("dve", tb.reg(), mT.reg((h * 8, h * 8 + 8), (i * 128, i * 128 + 128)),
                         lambda e, h=h, tbv=tbv: e.tensor_copy(out=mT[:, h * 8:h * 8 + 8, i * 128:(i + 1) * 128], in_=tbv))
                else:
                    S.op("act", tb.reg(), mT.reg((8, 12), (i * 128, i * 128 + 128)),
                         lambda e, tbv=tbv: e.activation(out=mT[:, 8:12, i * 128:(i + 1) * 128], in_=tbv[:, 0:4, :],
                                                         func=AF.Copy))
                    S.op("dve", tb.reg(), mT.reg((12, 16), (i * 128, i * 128 + 128)),
                         lambda e, tbv=tbv: e.tensor_copy(out=mT[:, 12:16, i * 128:(i + 1) * 128], in_=tbv[:, 4:8, :]))

        def c2_unit(G, i, sl):
            bk = next_bank()
            bk2 = next_bank()
            s = i % 2

            def pe_fn(e):
                ins = None
                for k in range(16):
                    ins = e.matmul(bk[:, 0:256], mT[:, k, i * 128:(i + 1) * 128], sl[:, k, :],
                                   start=(k == 0), stop=(k == 15))
                return ins
            S.op("pe", regs(sl.reg(), mT.reg(None, (i * 128, i * 128 + 128))), bk.reg(), pe_fn)

            def pe_fn2(e):
                ins = None
                for kk in range(2):
                    ins = e.matmul(bk2[:, 0:256], pT[:, i, kk, :], wple[:, kk, G * 256:(G + 1) * 256],
                                   start=(kk == 0), stop=(kk == 1))
                return ins
            S.op("pe", regs(pT.reg(i), wple.reg(None, (G * 256, G * 256 + 256))), bk2.reg(), pe_fn2)
            S.op("act", bk.reg(), sgc[s].reg(),
                 lambda e: e.activation(out=sgc[s][:], in_=bk[:, 0:256], func=AF.Sigmoid))
            S.op("dve", regs(bk2.reg(), stat.reg((SS_E + i, SS_E + i + 1)), gple.reg((G * 256, G * 256 + 256))),
                 etc_[s].reg(),
                 lambda e: e.scalar_tensor_tensor(
                     out=etc_[s][:], in0=bk2[:, 0:256], scalar=stat[:, SS_E + i:SS_E + i + 1],
                     in1=gple[:, G * 256:(G + 1) * 256], op0=ALU.mult, op1=ALU.mult))
            S.op("dve", regs(etc_[s].reg(), sgc[s].reg()), etc_[s].reg(),
                 lambda e: e.tensor_tensor(out=etc_[s][:], in0=etc_[s][:], in1=sgc[s][:], op=ALU.mult))
            S.op("dve", regs(etc_[s].reg(), z.reg(i, (G * 256, G * 256 + 256))), z.reg(i, (G * 256, G * 256 + 256)),
                 lambda e: e.tensor_tensor(out=z[:, i, G * 256:(G + 1) * 256],
                                           in0=z[:, i, G * 256:(G + 1) * 256], in1=etc_[s][:], op=ALU.add))
            S.dma("sp", "out", z.reg(i, (G * 256, G * 256 + 256)), [],
                  lambda e: e.dma_start(out=y_tok[i, :, G * 256:(G + 1) * 256],
                                        in_=z[:, i, G * 256:(G + 1) * 256]), final=True)

        ALLT = list(range(NT))
        for G in range(6):
            sl = cget("o", G, 0)
            for i in ALLT:
                c1_unit(G, i, sl)
        C_LIMIT[0] = 8
        sl6 = cget("o", 6, 0)
        sl7 = cget("o", 7, 0)
        slg0 = cget("g", 0, 0)
        done0 = []
        for st_ in range(NT + 3):
            if st_ < NT:
                c1_unit(6, st_, sl6)
                c1_unit(7, st_, sl7)
                x1_chain(st_)
            else:
                for i_ in (2 * (st_ - NT), 2 * (st_ - NT) + 1):
                    c2_unit(0, i_, slg0)
                    done0.append(i_)
            if 0 <= st_ - 3 < NT:
                x1_transposes(st_ - 3)
            if 0 <= st_ - 2 < NT:
                x1_cast(st_ - 2)
        C_LIMIT[0] = 10 ** 6
        for i in ALLT:
            if i not in done0:
                c2_unit(0, i, slg0)
        for G in range(1, 8):
            sl = cget("g", G, 0)
            for i in ALLT:
                c2_unit(G, i, sl)

    LB = dict(locals())
    for (name, buf, shape) in dbg:
        bufobj = LB[buf] if isinstance(buf, str) else buf
        if isinstance(bufobj, (list, tuple)):
            for bi_, b_ in enumerate(bufobj):
                dt_ = dram_out(f"dbg_{name}{bi_}", [128] + list(shape))
                S.dma("sp" if b_.esz == 4 else "pool", "dbgo", b_.reg(), [],
                      lambda e, dt_=dt_, b_=b_: e.dma_start(out=dt_, in_=b_[:]), final=True)
            continue
        dt_ = dram_out("dbg_" + name, [128] + list(shape))
        S.dma("pool", "dbgo", bufobj.reg(), [], lambda e, dt_=dt_, bufobj=bufobj: e.dma_start(out=dt_, in_=bufobj[:]),
              final=True)

    fin = dict(S.final_waits)
    S.prog["sp"].append(([(s, v) for s, v in fin.items()], None, None))

    semnames = sorted(S.semnames)
    sems = {}
    import contextlib
    with contextlib.ExitStack() as es:
        for sname in semnames:
            sems[sname] = es.enter_context(nc.semaphore(sname))
        block = es.enter_context(nc.Block())

        S.finalize()

        def run(e, engname):
            for waits, fn, inc in S.prog[engname]:
                for (s_, v) in waits:
                    e.wait_ge(sems[s_], S.wait_value(s_, v))
                if fn is None:
                    continue
                ins = fn(e)
                sem_, val_, is_dma = inc
                if is_dma:
                    if isinstance(ins, (list, tuple)):
                        for i_ in ins:
                            i_.then_inc(sems[sem_], 16)
                    else:
                        ins.then_inc(sems[sem_], 16)
                elif (sem_, val_) in S.rank:
                    ins.then_inc(sems[sem_], 1)

        @block.sync
        def _(e):
            run(e, "sp")

        @block.gpsimd
        def _(e):
            run(e, "pool")

        @block.scalar
        def _(e):
            run(e, "act")

        @block.vector
        def _(e):
            run(e, "dve")

        @block.tensor
        def _(e):
            run(e, "pe")
    return nc


def _prep_shared(inp):
    f = np.float32
    sh = {}
    w_in = np.asarray(inp["w_in"][0], f)
    sh["w_in_t"] = np.ascontiguousarray(w_in.reshape(16, 128, 72, 128).transpose(2, 1, 0, 3))
    w_pool = np.asarray(inp["w_pool"][0], f)
    sh["w_pool_t"] = np.ascontiguousarray(w_pool.reshape(4, 2, 128, 256).transpose(2, 0, 1, 3)).reshape(128, 2048)
    for nm, key in (("w_pp_t", "w_proj_pool"), ("w_pc_t", "w_proj_conv")):
        w = np.asarray(inp[key][0], f)
        sh[nm] = np.ascontiguousarray(w.reshape(8, 128, 16, 128).transpose(2, 1, 0, 3))
    for nm, key in (("w_out_t", "w_out"), ("w_pg_t", "w_ple_gate")):
        w = np.asarray(inp[key][0], f)
        sh[nm] = np.ascontiguousarray(w.reshape(16, 128, 8, 256).transpose(2, 1, 0, 3))
    w_ple = np.asarray(inp["w_ple"][0], f)
    sh["w_ple_t"] = np.ascontiguousarray(w_ple.reshape(2, 128, 2048).transpose(1, 0, 2)).reshape(128, 4096)
    sh["g_post_bc"] = np.ascontiguousarray(np.broadcast_to(np.asarray(inp["g_post"][0], f)[None, :], (128, D)))
    sh["g_ple_bc"] = np.ascontiguousarray(np.broadcast_to(np.asarray(inp["g_ple"][0], f)[None, :], (128, D)))
    sh["ident"] = np.eye(128, dtype=f)
    cst = np.zeros((128, 360), f)
    cst[:, 0:16] = np.asarray(inp["g_pre"][0], f).reshape(16, 128).T
    cst[:, 16:24] = np.asarray(inp["pool_scale"][0], f).reshape(8, 128).T
    cst[:, 24:32] = np.asarray(inp["b_dw"][0], f).reshape(8, 128).T
    cst[:, 32:40] = np.asarray(inp["ln_g"][0], f).reshape(8, 128).T
    cst[:, 40:48] = np.asarray(inp["ln_b"][0], f).reshape(8, 128).T
    wdw = np.asarray(inp["w_dw"][0], f)
    cst[:, 112:360] = wdw.reshape(31, 8, 128).transpose(2, 1, 0).reshape(128, 248)
    return sh, cst


def _in_maps(inp):
    f = np.float32
    sh, cst0 = _prep_shared(inp)
    xp = np.asarray(inp["x_prompt"], f)
    xsamp = np.asarray(inp["x_sample"], f)
    pp = np.asarray(inp["p_prompt"], f)[0]
    ps = np.asarray(inp["p_sample"], f)[0]
    stc = np.asarray(inp["state_conv"], f)[0]
    stp = np.asarray(inp["state_pool"], f)[0]
    maps = []
    for r in range(NCORES):
        b, half = r // 2, r % 2
        st = half * 1024
        m = dict(sh)
        xt = np.empty((NT, 128, D), f)
        xt[0:8] = xp[b, st:st + 1024].reshape(8, 128, D)
        xt[8] = xsamp[16 * r:16 * r + 16].reshape(128, D)
        m["x_tok"] = xt
        m["x_halo"] = np.ascontiguousarray(xp[b, st - 32:st]) if half else np.zeros((32, D), f)
        pt = np.empty((NT, 128, 256), f)
        pt[0:8] = pp[b, st:st + 1024].reshape(8, 128, 256)
        pt[8] = ps[16 * r:16 * r + 16].reshape(128, 256)
        m["p_tok"] = pt
        m["st_conv"] = np.ascontiguousarray(stc[16 * r:16 * r + 16])
        m["st_pool"] = np.ascontiguousarray(stp[16 * r:16 * r + 16])
        cst = cst0.copy()
        for g, wd in enumerate((2, 4, 8, 16)):
            pos = st + np.arange(16)
            cst[:, 48 + g * 16:64 + g * 16] = (1.0 / np.minimum(pos + 1, wd)).astype(f)[None, :]
        m["cst"] = cst
        maps.append(m)
    return maps


_NC_CACHE = {}


def kernel(**inputs):
    if "nc" not in _NC_CACHE:
        _NC_CACHE["nc"] = build_program()
    nc = _NC_CACHE["nc"]
    maps = _in_maps(inputs)
    res = run_bass_kernel_spmd(nc, maps, core_ids=list(range(NCORES)))
    R = res.results
    f = np.float32
    y_prompt = np.empty((4, 2048, D), f)
    y_sample = np.empty((128, 8, D), f)
    npp = np.empty((1, 4, 15, DP), f)
    ncp = np.empty((1, 4, 30, DP), f)
    nps = np.empty((1, 128, 15, DP), f)
    ncs = np.empty((1, 128, 30, DP), f)
    for r in range(NCORES):
        b, half = r // 2, r % 2
        st = half * 1024
        yt = R[r]["y_tok"]
        y_prompt[b, st:st + 1024] = yt[0:8].reshape(1024, D)
        y_sample[16 * r:16 * r + 16] = yt[8].reshape(16, 8, D)
        ncs[0, 16 * r:16 * r + 16, 0:22] = R[r]["ncs_old"]
        ncs[0, 16 * r:16 * r + 16, 22:30] = R[r]["ncs_new"].reshape(16, 8, DP)
        nps[0, 16 * r:16 * r + 16, 0:7] = R[r]["nps_old"]
        nps[0, 16 * r:16 * r + 16, 7:15] = R[r]["nps_new"].reshape(16, 8, DP)
        if half:
            ncp[0, b] = R[r]["ncp"][2:32]
            npp[0, b] = R[r]["npp"][17:32]
    return (y_prompt, y_sample, npp, ncp, nps, ncs)
```
